# Optimizing a Trainium2 kernel written in Bass

```python
import jax, jax.numpy as jnp
from jax import lax
import numpy as np

D_MODEL = 1024
BATCH = 8
SEQ = 4096
DEPTH = 4

CHUNK = 64
N_MIXERS = 2
EPS = 1e-6

GDN_QK_HEADS = 8
GDN_V_HEADS = 16
GDN_HEAD_DIM = 128
GDN_QK_WIDTH = GDN_QK_HEADS * GDN_HEAD_DIM
GDN_V_WIDTH = GDN_V_HEADS * GDN_HEAD_DIM
GDN_CONV_CH = 2 * GDN_QK_WIDTH + GDN_V_WIDTH
GDN_IN_WIDTH = GDN_CONV_CH + GDN_V_WIDTH + 2 * GDN_V_HEADS
CONV_WIDTH = 4

FOX_HEADS = 16
FOX_HEAD_DIM = 64
FOX_WIDTH = FOX_HEADS * FOX_HEAD_DIM
FOX_IN_WIDTH = 4 * FOX_WIDTH + FOX_HEADS
Q_BLOCK = 128

N_LAYERS_A = (DEPTH + 1) // 2
N_LAYERS_B = DEPTH // 2

kernel_name = "hybrid_gdn_fox_adaln_trunk"


def rmsnorm(x, w):
    xf = x.astype(jnp.float32)
    y = xf * lax.rsqrt(jnp.mean(xf * xf, axis=-1, keepdims=True) + EPS)
    return (y * w.astype(jnp.float32)).astype(x.dtype)


def l2norm(x):
    return x * lax.rsqrt(jnp.sum(x * x, axis=-1, keepdims=True) + EPS)


def causal_depthwise_conv(x, w):
    C = x.shape[-1]
    return lax.conv_general_dilated(
        x, w[:, None, :].astype(x.dtype), window_strides=(1,),
        padding=[(CONV_WIDTH - 1, 0)], dimension_numbers=('NWC', 'WIO', 'NWC'),
        feature_group_count=C)


def gated_delta_rule(q, k, v, g, beta):
    B, S, H, DK = q.shape
    DV = v.shape[-1]
    N = S // CHUNK

    def to_chunks(t):
        t = jnp.moveaxis(t, 2, 1)
        return t.reshape(t.shape[:2] + (N, CHUNK) + t.shape[3:])

    q, k, v, g, beta = (to_chunks(t) for t in (q, k, v, g, beta))
    g = jnp.cumsum(g, axis=-1)
    idx = jnp.arange(CHUNK)
    lower = idx[:, None] >= idx[None, :]
    strict = idx[:, None] > idx[None, :]
    decay = jnp.exp(jnp.where(lower, g[..., :, None] - g[..., None, :], -jnp.inf))
    kb = k * beta[..., None]
    vb = v * beta[..., None]
    L = jnp.where(strict, jnp.einsum('bhncd,bhnsd->bhncs', kb, k) * decay, 0.0)
    eye = jnp.eye(CHUNK, dtype=q.dtype)
    T = lax.linalg.triangular_solve(eye + L, jnp.broadcast_to(eye, L.shape),
                                    left_side=True, lower=True, unit_diagonal=True)
    u = T @ vb
    w = T @ (kb * jnp.exp(g)[..., None])
    attn = jnp.einsum('bhncd,bhnsd->bhncs', q, k) * decay
    q_dec = q * jnp.exp(g)[..., None]
    k_dec = k * jnp.exp(g[..., -1:] - g)[..., None]
    g_last = jnp.exp(g[..., -1])
    xs = tuple(jnp.moveaxis(t, 2, 0) for t in (q_dec, k_dec, u, w, attn, g_last))

    def step(state, inp):
        q_n, k_n, u_n, w_n, a_n, gl_n = inp
        v_new = u_n - jnp.einsum('bhcd,bhde->bhce', w_n, state)
        o = (jnp.einsum('bhcd,bhde->bhce', q_n, state)
             + jnp.einsum('bhcs,bhse->bhce', a_n, v_new))
        state = state * gl_n[..., None, None] + jnp.einsum('bhcd,bhce->bhde', k_n, v_new)
        return state, o

    state0 = jnp.zeros((B, H, DK, DV), q.dtype)
    _, o = lax.scan(step, state0, xs)
    o = jnp.moveaxis(o, 0, 2).reshape(B, H, S, DV)
    return jnp.moveaxis(o, 1, 2)


def gdn_mixer(h, w_in, conv_w, A_log, dt_bias, norm_w, w_out):
    B, S, _ = h.shape
    proj = h @ w_in
    qkv, z, b, a = jnp.split(proj, [GDN_CONV_CH, GDN_CONV_CH + GDN_V_WIDTH,
                                    GDN_CONV_CH + GDN_V_WIDTH + GDN_V_HEADS], axis=-1)
    qkv = jax.nn.silu(causal_depthwise_conv(qkv, conv_w))
    q, k, v = jnp.split(qkv, [GDN_QK_WIDTH, 2 * GDN_QK_WIDTH], axis=-1)
    rep = GDN_V_HEADS // GDN_QK_HEADS
    q = l2norm(q.reshape(B, S, GDN_QK_HEADS, GDN_HEAD_DIM).astype(jnp.float32)) * GDN_HEAD_DIM ** -0.5
    k = l2norm(k.reshape(B, S, GDN_QK_HEADS, GDN_HEAD_DIM).astype(jnp.float32))
    q = jnp.repeat(q, rep, axis=2)
    k = jnp.repeat(k, rep, axis=2)
    v = v.reshape(B, S, GDN_V_HEADS, GDN_HEAD_DIM).astype(jnp.float32)
    beta = jax.nn.sigmoid(b.astype(jnp.float32))
    g = -jnp.exp(A_log.astype(jnp.float32)) * jax.nn.softplus(
        a.astype(jnp.float32) + dt_bias.astype(jnp.float32))
    o = gated_delta_rule(q, k, v, g, beta)
    zg = jax.nn.silu(z.reshape(B, S, GDN_V_HEADS, GDN_HEAD_DIM).astype(jnp.float32))
    o = rmsnorm(o, norm_w) * zg
    return o.reshape(B, S, GDN_V_WIDTH).astype(h.dtype) @ w_out


def forgetting_attention(q, k, v, cum):
    B, S, H, DH = q.shape
    NB = S // Q_BLOCK
    cum_k = jnp.moveaxis(cum, 1, 2)
    q_blocks = jnp.moveaxis(q.reshape(B, NB, Q_BLOCK, H, DH), 1, 0)
    c_blocks = jnp.moveaxis(cum_k.reshape(B, H, NB, Q_BLOCK), 2, 0)
    key_pos = jnp.arange(S)

    def block(inp):
        q_b, c_b, start = inp
        logits = jnp.einsum('bqhd,bkhd->bhqk', q_b, k)
        logits = logits + (c_b[..., :, None] - cum_k[..., None, :])
        q_pos = start + jnp.arange(Q_BLOCK)
        mask = key_pos[None, :] <= q_pos[:, None]
        p = jax.nn.softmax(jnp.where(mask, logits, -jnp.inf), axis=-1)
        return jnp.einsum('bhqk,bkhd->bqhd', p, v)

    o = lax.map(block, (q_blocks, c_blocks, jnp.arange(NB) * Q_BLOCK))
    return jnp.moveaxis(o, 0, 1).reshape(B, S, H, DH)


def fox_mixer(h, w_in, f_bias, qn_w, kn_w, w_out):
    B, S, _ = h.shape
    proj = h @ w_in
    q, k, v, z, f = jnp.split(proj, [FOX_WIDTH, 2 * FOX_WIDTH, 3 * FOX_WIDTH, 4 * FOX_WIDTH], axis=-1)
    shp = (B, S, FOX_HEADS, FOX_HEAD_DIM)
    q = rmsnorm(q.reshape(shp).astype(jnp.float32), qn_w) * FOX_HEAD_DIM ** -0.5
    k = rmsnorm(k.reshape(shp).astype(jnp.float32), kn_w)
    v = v.reshape(shp).astype(jnp.float32)
    log_f = jax.nn.log_sigmoid(f.astype(jnp.float32) + f_bias.astype(jnp.float32))
    cum = jnp.cumsum(log_f, axis=1)
    o = forgetting_attention(q, k, v, cum).reshape(B, S, FOX_WIDTH)
    o = o * jax.nn.silu(z.astype(jnp.float32))
    return o.astype(h.dtype) @ w_out


def setup_inputs(seed: int = 0) -> dict:
    key = jax.random.key(seed)
    ks = jax.random.split(key, 20)
    nrm = jax.random.normal
    D = D_MODEL
    dt = jnp.exp(jax.random.uniform(ks[7], (N_LAYERS_A, GDN_V_HEADS),
                                    minval=np.log(1e-3), maxval=np.log(1e-1)))
    return {
        "x": nrm(ks[0], (BATCH, SEQ, D), jnp.float32),
        "c": nrm(ks[1], (BATCH, D), jnp.float32),
        "norm_w": 1.0 + 0.1 * nrm(ks[2], (DEPTH, D), jnp.float32),
        "ada_w": 0.5 * D ** -0.5 * nrm(ks[3], (DEPTH, D, 3 * D), jnp.float32),
        "ada_b": 0.02 * nrm(ks[4], (DEPTH, 3 * D), jnp.float32),
        "a_w_in": D ** -0.5 * nrm(ks[5], (N_LAYERS_A, D, GDN_IN_WIDTH), jnp.float32),
        "a_conv_w": CONV_WIDTH ** -0.5 * nrm(ks[6], (N_LAYERS_A, CONV_WIDTH, GDN_CONV_CH), jnp.float32),
        "a_A_log": jnp.log(jax.random.uniform(ks[8], (N_LAYERS_A, GDN_V_HEADS), minval=1.0, maxval=16.0)),
        "a_dt_bias": dt + jnp.log(-jnp.expm1(-dt)),
        "a_norm_w": 1.0 + 0.1 * nrm(ks[9], (N_LAYERS_A, GDN_HEAD_DIM), jnp.float32),
        "a_w_out": GDN_V_WIDTH ** -0.5 * nrm(ks[10], (N_LAYERS_A, GDN_V_WIDTH, D), jnp.float32),
        "b_w_in": D ** -0.5 * nrm(ks[11], (N_LAYERS_B, D, FOX_IN_WIDTH), jnp.float32),
        "b_f_bias": jax.random.uniform(ks[12], (N_LAYERS_B, FOX_HEADS), minval=1.0, maxval=5.0),
        "b_qn_w": 1.0 + 0.1 * nrm(ks[13], (N_LAYERS_B, FOX_HEAD_DIM), jnp.float32),
        "b_kn_w": 1.0 + 0.1 * nrm(ks[14], (N_LAYERS_B, FOX_HEAD_DIM), jnp.float32),
        "b_w_out": FOX_WIDTH ** -0.5 * nrm(ks[15], (N_LAYERS_B, FOX_WIDTH, D), jnp.float32),
        "final_norm_w": 1.0 + 0.1 * nrm(ks[16], (D,), jnp.float32),
    }


def reference(x, c, norm_w, ada_w, ada_b, a_w_in, a_conv_w, a_A_log, a_dt_bias, a_norm_w,
              a_w_out, b_w_in, b_f_bias, b_qn_w, b_kn_w, b_w_out, final_norm_w):
    cond = jax.nn.silu(c)
    for i in range(DEPTH):
        mod = cond @ ada_w[i] + ada_b[i]
        shift, scale, gate = jnp.split(mod[:, None, :], 3, axis=-1)
        h = rmsnorm(x, norm_w[i]) * (1 + scale) + shift
        j = i // N_MIXERS
        if i % N_MIXERS == 0:
            y = gdn_mixer(h, a_w_in[j], a_conv_w[j], a_A_log[j], a_dt_bias[j], a_norm_w[j], a_w_out[j])
        else:
            y = fox_mixer(h, b_w_in[j], b_f_bias[j], b_qn_w[j], b_kn_w[j], b_w_out[j])
        x = x + gate * y
    return rmsnorm(x, final_norm_w)
```

```python
import contextlib
import numpy as np
import concourse.bass as bass
import concourse.mybir as mybir
from concourse.bass_utils import run_bass_kernel_spmd

F32 = mybir.dt.float32
BF16 = mybir.dt.bfloat16
AF = mybir.ActivationFunctionType
ALU = mybir.AluOpType
AX = mybir.AxisListType

S = 4096
D = 1024
NT = 32
EPS = 1e-6
NEG = -30000.0
DEPTH = 4
GDN_IN = 6176
FOX_IN = 4112

C_ID, C_ONES, C_UT, C_SL, C_MINCLT, C_MSTRT, C_MSTR, C_TRIBD, C_SELC, C_SELA, C_SELB, C_CAUS, C_ONESBD = range(13)
NCONST = 13


def make_consts():
    i = np.arange(128)
    r = i[:, None]
    c = i[None, :]
    same = (r // 64) == (c // 64)
    m = np.zeros((NCONST, 128, 128), np.float32)
    m[C_ID] = (r == c)
    m[C_ONES] = 1.0
    m[C_UT] = (r <= c)
    m[C_SL] = (r > c)
    m[C_MINCLT] = np.where(same & (r <= c), 0.0, NEG)
    m[C_MSTRT] = np.where(same & (r < c), 0.0, NEG)
    m[C_MSTR] = np.where(same & (r > c), 0.0, NEG)
    m[C_TRIBD] = (same & (r <= c))
    m[C_SELC] = (r == (c // 64) * 64 + 63)
    m[C_SELA] = (r == 63) * np.ones((1, 128))
    m[C_SELB] = (r == 127) * np.ones((1, 128))
    m[C_CAUS] = (r <= c)
    m[C_ONESBD] = same
    return m.astype(np.float32)


class KB:
    NS = 24

    def __init__(self, nc, es):
        self.nc = nc
        self.eng = {'pe': nc.tensor, 'act': nc.scalar, 'dve': nc.vector, 'pool': nc.gpsimd, 'sp': nc.sync}
        self.sem = {k: es.enter_context(nc.semaphore("s_" + k)) for k in ['pe', 'act', 'dve', 'pool']}
        self.cnt = {k: 0 for k in self.sem}
        self.seen = {k: {} for k in self.eng}
        self.dsem = [es.enter_context(nc.semaphore("d%d" % i)) for i in range(self.NS)]
        self.dval = [0] * self.NS
        self.dnext = 0
        self.lastw = {}
        self.readers = {}
        self.fresh = {}
        self.ninst = 0

    def _wait(self, e, tok):
        sk, v = tok
        if sk == e and e == 'pe':
            return
        if self.seen[e].get(sk, 0) >= v:
            return
        sem = self.sem[sk] if isinstance(sk, str) else self.dsem[sk[1]]
        self.eng[e].wait_ge(sem, v)
        self.seen[e][sk] = v

    def _deps(self, e, reads, writes):
        for k in reads:
            t = self.lastw.get(k)
            if t is not None:
                self._wait(e, t)
            if isinstance(k, tuple) and k[0] == 'ps':
                for sk, t in self.readers.get(k, {}).items():
                    if sk != e:
                        self._wait(e, t)
        for k in writes:
            t = self.lastw.get(k)
            if t is not None:
                self._wait(e, t)
            for t in self.readers.get(k, {}).values():
                self._wait(e, t)

    def _record(self, tok, reads, writes):
        for k in reads:
            self.readers.setdefault(k, {})[tok[0]] = tok
        for k in writes:
            self.lastw[k] = tok
            self.readers[k] = {}

    def op(self, e, fn, reads=(), writes=(), inc=True):
        self._deps(e, reads, writes)
        ins = fn(self.eng[e])
        self.ninst += 1
        if inc:
            self.cnt[e] += 1
            ins.then_inc(self.sem[e], 1)
            tok = (e, self.cnt[e])
        else:
            tok = (e, self.cnt[e] + 1)
        self._record(tok, reads, writes)
        return tok

    def dma(self, out, in_, reads=(), writes=(), q='sp'):
        i = self.dnext
        self.dnext = (self.dnext + 1) % self.NS
        if self.dval[i] > 0:
            self._wait(q, (('d', i), self.dval[i]))
        self._deps(q, reads, writes)
        ins = self.eng[q].dma_start(out=out, in_=in_)
        self.ninst += 1
        self.dval[i] += 16
        ins.then_inc(self.dsem[i], 16)
        tok = (('d', i), self.dval[i])
        self._record(tok, reads, writes)
        return tok

    def barrier(self):
        for e in self.eng:
            for o in self.sem:
                if self.cnt[o] > 0:
                    self._wait(e, (o, self.cnt[o]))
            for i in range(self.NS):
                if self.dval[i] > 0:
                    self._wait(e, (('d', i), self.dval[i]))
        self.lastw = {}
        self.readers = {}

    def newgen(self, bank):
        self.fresh[(bank, 0)] = True
        self.fresh[(bank, 1)] = True

    def mm(self, bank, out, lhsT, rhs, reads, writes, halves=(0, 1), last=True, inc=None):
        st = False
        for h in halves:
            if self.fresh.get((bank, h), True):
                st = True
            self.fresh[(bank, h)] = False
        if inc is None:
            inc = last

        def fn(e):
            return e.matmul(out, lhsT=lhsT, rhs=rhs, start=st, stop=last, skip_group_check=True)
        return self.op('pe', fn, reads, writes, inc=inc)

    def tr(self, out, in_, ident, reads, writes):
        return self.op('pe', lambda e: e.transpose(out, in_, ident), reads, writes)

    def act(self, out, in_, func, reads, writes, scale=None, bias=None):
        def fn(e):
            kw = {}
            if scale is not None:
                kw['scale'] = scale
            if bias is not None:
                kw['bias'] = bias
            return e.activation(out=out, in_=in_, func=func, **kw)
        return self.op('act', fn, reads, writes)

    def tt(self, e, out, in0, in1, op, reads, writes):
        return self.op(e, lambda g: g.tensor_tensor(out=out, in0=in0, in1=in1, op=op), reads, writes)

    def ts(self, e, out, in0, s1, s2, op0, op1, reads, writes):
        if op1 is None and e == 'pool' and op0 == ALU.mult:
            op1, s2 = ALU.add, 0.0
        if op1 is None:
            return self.op(e, lambda g: g.tensor_scalar(out=out, in0=in0, scalar1=s1, scalar2=None, op0=op0), reads, writes)
        return self.op(e, lambda g: g.tensor_scalar(out=out, in0=in0, scalar1=s1, scalar2=s2, op0=op0, op1=op1), reads, writes)

    def stt(self, out, in0, scalar, in1, op0, op1, reads, writes):
        return self.op('dve', lambda g: g.scalar_tensor_tensor(out=out, in0=in0, scalar=scalar, in1=in1, op0=op0, op1=op1), reads, writes)

    def copy(self, e, out, in_, reads, writes):
        if e == 'act':
            return self.op('act', lambda g: g.copy(out=out, in_=in_), reads, writes)
        return self.op(e, lambda g: g.tensor_copy(out=out, in_=in_), reads, writes)

    def memset(self, e, ap, val, writes):
        return self.op(e, lambda g: g.memset(ap, val), (), writes)


class _Stop(Exception):
    pass


UWB = 7


def build_program(n_layers=DEPTH, dbg=False, stop=None):
    nc = bass.Bass("TRN2", target_bir_lowering=False)

    def din(name, shape, dt=F32):
        return nc.dram_tensor(name, list(shape), dt, kind="ExternalInput").ap()

    def dscr(name, shape, dt):
        return nc.dram_tensor(name, list(shape), dt, kind="Internal").ap()

    x_in = din("x", [S, D])
    cT_in = din("cT", [128, 8])
    normw_in = din("norm_w", [DEPTH, D])
    fnw_in = din("final_norm_w", [1, D])
    adaw_in = din("ada_w", [DEPTH, D, 3 * D])
    adab_in = din("ada_b", [DEPTH, 3 * D])
    a_win = din("a_w_in", [2, D, GDN_IN])
    a_convT = din("a_convT", [2, 128, 32, 4])
    a_Alog = din("a_A_log", [2, 16])
    a_dtb = din("a_dt_bias", [2, 16])
    a_nw = din("a_norm_w", [2, 128, 1])
    a_wout = din("a_w_out", [2, 2048, D])
    b_win = din("b_w_in", [2, D, FOX_IN])
    b_fb = din("b_f_bias", [2, 16, 1])
    b_qn2 = din("b_qn2", [2, 128, 1])
    b_kn2 = din("b_kn2", [2, 128, 1])
    b_wout = din("b_w_out", [2, D, D])
    consts_in = din("consts", [NCONST, 128, 128])
    out_d = nc.dram_tensor("out", [S, D], F32, kind="ExternalOutput").ap()

    xres = dscr("xres", [S, D], F32)
    qkvz = dscr("qkvz", [48, 128, S], BF16)
    qks = dscr("qks", [16, 128, S], BF16)
    zss = dscr("zss", [S, D], BF16)
    c1s = dscr("c1s", [16, S], BF16)
    dbg_d = {}
    if dbg:
        dbg_d['hT'] = nc.dram_tensor("dbg_hT", [128, 8, S], BF16, kind="ExternalOutput").ap()
        dbg_d['xres'] = nc.dram_tensor("dbg_xres", [S, D], F32, kind="ExternalOutput").ap()
        dbg_d['qkvz'] = nc.dram_tensor("dbg_qkvz", [48, 128, S], BF16, kind="ExternalOutput").ap()
        dbg_d['gates'] = nc.dram_tensor("dbg_gates", [128, NT, 6, 16], F32, kind="ExternalOutput").ap()

    es = contextlib.ExitStack()
    with es:
        kb = KB(nc, es)

        uid = [0]

        def sb(stack, name, shape, dt=F32):
            uid[0] += 1
            return stack.enter_context(nc.sbuf_tensor("%s_%d" % (name, uid[0]), list(shape), dt))

        ps = [es.enter_context(nc.psum_tensor("ps%d" % i, [128, 512], F32)) for i in range(8)]

        def PK(bank, lo=0, hi=512):
            return [('ps', bank)]

        def psbf(i):
            return ps[i][:].bitcast(BF16)

        cst = sb(es, "cst", [128, NCONST, 128])
        ident_bf = sb(es, "ident_bf", [128, 128], BF16)
        caus_bf = sb(es, "caus_bf", [128, 128], BF16)
        condT = sb(es, "condT", [128, 8])
        gate_b = sb(es, "gate_b", [128, D])
        fnw_b = sb(es, "fnw_b", [128, D])
        xt = [sb(es, "xt%d" % i, [128, D]) for i in range(2)]
        junk = sb(es, "junk", [128, D])
        sm = sb(es, "sm", [128, 8])
        ones_row = sb(es, "ones_row", [1, 128])

        def C(i):
            return cst[:, i, :]

        kb.dma(cst[:], consts_in.rearrange("n p f -> p n f"), writes=['cst'])
        kb.copy('dve', ident_bf[:], C(C_ID), ['cst'], ['ident_bf'])
        kb.copy('dve', caus_bf[:], C(C_CAUS), ['cst'], ['caus_bf'])
        kb.memset('dve', ones_row[:], 1.0, ['ones_row'])
        kb.dma(condT[:], cT_in, writes=['condT'])
        kb.act(condT[:], condT[:], AF.Silu, ['condT'], ['condT'])
        kb.dma(fnw_b[:], fnw_in.partition_broadcast(128), writes=['fnw_b'])

        def rstd_from_ss(ss_ap, out_ap, n, rkeys, wkeys, tmp_ap):
            kb.act(tmp_ap, ss_ap, AF.Ln, rkeys, [('tmpln',)], scale=1.0 / n, bias=EPS)
            kb.act(out_ap, tmp_ap, AF.Exp, [('tmpln',)], wkeys, scale=-0.5)

        def adaln(L, A_b, B_b, stack):
            with contextlib.ExitStack() as s2:
                adaw = [sb(s2, "adaw%d" % i, [128, 8, 256]) for i in range(2)]
                modrow = sb(s2, "modrow", [1, 3 * D])
                nwrow = sb(s2, "nwrow", [1, D])
                arow = sb(s2, "arow", [1, D])
                kb.dma(modrow[:], adab_in[L:L + 1, :], writes=[('modrow', i) for i in range(6)])
                kb.dma(nwrow[:], normw_in[L:L + 1, :], writes=['nwrow'])
                wv = adaw_in[L].rearrange("(kc p) f -> p kc f", p=128)
                for fb in range(12):
                    b = fb % 2
                    kb.dma(adaw[b][:], wv[:, :, fb * 256:(fb + 1) * 256], writes=[('adaw', b)])
                    bank = fb % 2
                    kb.newgen(bank)
                    for kc in range(8):
                        kb.mm(bank, ps[bank][0:1, 0:256], condT[:, kc:kc + 1], adaw[b][:, kc, :],
                              ['condT', ('adaw', b)], PK(bank), halves=(0,), last=(kc == 7))
                    kb.tt('dve', modrow[0:1, fb * 256:(fb + 1) * 256], ps[bank][0:1, 0:256], modrow[0:1, fb * 256:(fb + 1) * 256],
                          ALU.add, PK(bank) + [('modrow', fb // 2)], [('modrow', fb // 2)])
                kb.stt(arow[0:1, :], modrow[0:1, D:2 * D], 1.0, nwrow[0:1, :], ALU.add, ALU.mult,
                       [('modrow', 2), ('modrow', 3), 'nwrow'], ['arow'])
                srcs = [(arow[0:1, :], A_b, ['arow'], 'A_b'), (modrow[0:1, 0:D], B_b, [('modrow', 0), ('modrow', 1)], 'B_b'),
                        (modrow[0:1, 2 * D:3 * D], gate_b, [('modrow', 4), ('modrow', 5)], 'gate_b')]
                n = 0
                for (src, dst, rk, dname) in srcs:
                    for half in range(2):
                        bank = 2 + (n % 2)
                        n += 1
                        kb.newgen(bank)
                        kb.mm(bank, ps[bank][:, :], ones_row[0:1, :], src[0:1, half * 512:(half + 1) * 512],
                              rk + ['ones_row'], PK(bank))
                        kb.copy('act', dst[:, half * 512:(half + 1) * 512], ps[bank][:, :], PK(bank), [(dname, half)])
                kb.barrier()

        def norm_phase(L, xsrc, xkey, hT, A_b, B_b, stack):
            hn = sb(stack, "hn", [128, D])
            hb = [sb(stack, "hb%d" % i, [128, D], BF16) for i in range(2)]
            for t in range(NT):
                b = t % 2
                kb.dma(xt[b][:], xsrc[t * 128:(t + 1) * 128, :], reads=[(xkey, t)], writes=[('xt', b)])
                kb.act(junk[:], xt[b][:], AF.Square, [('xt', b)], ['junk'])
                kb.op('dve', lambda g: g.tensor_reduce(out=sm[:, 0:1], in_=junk[:], axis=AX.X, op=ALU.add), ['junk'], [('sm', 0)])
                rstd_from_ss(sm[:, 0:1], sm[:, 2:3], D, [('sm', 0)], [('sm', 2)], sm[:, 1:2])
                kb.stt(hn[:], xt[b][:], sm[:, 2:3], A_b[:], ALU.mult, ALU.mult, [('xt', b), ('sm', 2), ('A_b', 0), ('A_b', 1)], ['hn'])
                kb.tt('pool', hb[b][:], hn[:], B_b[:], ALU.add, ['hn', ('B_b', 0), ('B_b', 1)], [('hb', b)])
                bank = 4 + b
                for kc in range(8):
                    kb.tr(psbf(bank)[:, kc * 128:(kc + 1) * 128], hb[b][:, kc * 128:(kc + 1) * 128], ident_bf[:],
                          [('hb', b), 'ident_bf'], PK(bank))
                kb.copy('act', hT[:, :, t * 128:(t + 1) * 128], psbf(bank).rearrange("p (k t) -> p k t", k=8),
                        PK(bank), [('hT', t // 4)])

        def outproj_tile(L, t, ogT, KC, wout_bf, xsrc, xkey, last_layer, ykeys):
            b = t % 2
            kb.dma(xt[b][:], xsrc[t * 128:(t + 1) * 128, :], reads=[(xkey, t)], writes=[('xt', b)])
            for fb in range(2):
                bank = 6 + fb
                kb.newgen(bank)
                for kc in range(KC):
                    kb.mm(bank, ps[bank][:, :], ogT[:, kc, :], wout_bf[:, kc, fb * 512:(fb + 1) * 512],
                          ykeys + ['wout_bf'], PK(bank), last=(kc == KC - 1))
                kb.tt('dve', junk[:, fb * 512:(fb + 1) * 512], ps[bank][:, :], gate_b[:, fb * 512:(fb + 1) * 512], ALU.mult,
                      PK(bank) + [('gate_b', fb)], [('junkh', fb)])
                kb.tt('pool', xt[b][:, fb * 512:(fb + 1) * 512], junk[:, fb * 512:(fb + 1) * 512], xt[b][:, fb * 512:(fb + 1) * 512],
                      ALU.add, [('junkh', fb), ('xt', b)], [('xt', b)])
            if not last_layer:
                kb.dma(xres[t * 128:(t + 1) * 128, :], xt[b][:], reads=[('xt', b)], writes=[('xres', t)])
            else:
                kb.act(junk[:], xt[b][:], AF.Square, [('xt', b)], [('junkh', 0), ('junkh', 1)])
                kb.op('dve', lambda g: g.tensor_reduce(out=sm[:, 4:5], in_=junk[:], axis=AX.X, op=ALU.add),
                      [('junkh', 0), ('junkh', 1)], [('sm', 4)])
                rstd_from_ss(sm[:, 4:5], sm[:, 6:7], D, [('sm', 4)], [('sm', 6)], sm[:, 5:6])
                kb.stt(xt[b][:], xt[b][:], sm[:, 6:7], fnw_b[:], ALU.mult, ALU.mult, [('xt', b), ('sm', 6), 'fnw_b'], [('xt', b)])
                kb.dma(out_d[t * 128:(t + 1) * 128, :], xt[b][:], reads=[('xt', b)], writes=[('out', t)])

        def load_wout(wout_dram, KC, wout_bf, stack):
            with contextlib.ExitStack() as s2:
                wst = [sb(s2, "wost%d" % i, [128, D]) for i in range(2)]
                wv = wout_dram.rearrange("(kc p) f -> p kc f", p=128)
                for g in range(KC):
                    b = g % 2
                    kb.dma(wst[b][:], wv[:, g, :], writes=[('wost', b)])
                    kb.copy('pool', wout_bf[:, g, :], wst[b][:], [('wost', b)], ['wout_bf'])
                kb.barrier()

        def proj_fm(w2d, col0, nch, modes, hT, scratch, stack, convw=None, pp_scalars=None, nred=128, ones_ap=None):
            with contextlib.ExitStack() as s2:
                wst = [sb(s2, "wst%d" % i, [128, 8, 256]) for i in range(2)]
                wbf = [sb(s2, "wbf%d" % i, [128, 8, 256], BF16) for i in range(2)]
                obuf = [sb(s2, "obuf%d" % i, [128, 512], BF16) for i in range(3)]
                pre = [sb(s2, "pre%d" % i, [128, 515]) for i in range(3)] if any(m.startswith('conv') for m in modes) else None
                acc = [sb(s2, "acc%d" % i, [128, 512]) for i in range(2)]
                sq2 = [sb(s2, "sq2%d" % i, [128, 512]) for i in range(2)]
                lnv = sb(s2, "lnv", [128, 512])
                rn = [sb(s2, "rn%d" % i, [128, 512]) for i in range(2)]
                wv = w2d.rearrange("(kc p) f -> p kc f", p=128)
                nslab = (nch + 1) // 2
                pcount = 0
                for sl in range(nslab):
                    sbf = sl % 2
                    ncs = min(2, nch - sl * 2)
                    kb.dma(wst[sbf][:, :, 0:ncs * 128], wv[:, :, col0 + sl * 256: col0 + sl * 256 + ncs * 128], writes=[('wst', sbf)])
                    kb.copy('pool', wbf[sbf][:, :, 0:ncs * 128], wst[sbf][:, :, 0:ncs * 128], [('wst', sbf)], [('wbf', sbf)])
                    for ci in range(ncs):
                        c = sl * 2 + ci
                        mode = modes[c]
                        for tb in range(8):
                            ob = pcount % 3
                            bank = pcount % 4
                            kb.newgen(bank)
                            for kc in range(8):
                                kb.mm(bank, ps[bank][:, :], wbf[sbf][:, kc, ci * 128:(ci + 1) * 128], hT[:, kc, tb * 512:(tb + 1) * 512],
                                      [('wbf', sbf), ('hT', tb)], PK(bank), last=(kc == 7))
                            osl = obuf[ob][:, :]
                            okey = [('obuf', ob)]
                            if mode == 'silu':
                                kb.act(osl, ps[bank][:, :], AF.Silu, PK(bank), okey)
                            elif mode == 'rms':
                                a = acc[pcount % 2]
                                ak = ('acc', pcount % 2)
                                kb.copy('act', a[:], ps[bank][:, :], PK(bank), [ak])
                                q2 = sq2[pcount % 2]
                                qk = ('sq2', pcount % 2)
                                kb.tt('pool', q2[:], a[:], a[:], ALU.mult, [ak], [qk])
                                nb = 4 + pcount % 2
                                kb.newgen(nb)
                                kb.mm(nb, ps[nb][:, :], ones_ap, q2[:], [qk, 'cst'], PK(nb))
                                r = rn[pcount % 2]
                                rk = ('rn', pcount % 2)
                                rstd_from_ss(ps[nb][:, :], r[:], nred, PK(nb), [rk], lnv[:])
                                kb.stt(osl, a[:], pp_scalars[c], r[:], ALU.mult, ALU.mult, [ak, rk, 'ppsc'], okey)
                            else:
                                p = pre[pcount % 3]
                                pk = ('pre', pcount % 3)
                                pprev = pre[(pcount - 1) % 3]
                                pkprev = ('pre', (pcount - 1) % 3)
                                kb.copy('act', p[:, 3:515], ps[bank][:, :], PK(bank), [pk])
                                if tb == 0:
                                    kb.memset('pool', p[:, 0:3], 0.0, [pk])
                                else:
                                    kb.copy('pool', p[:, 0:3], pprev[:, 512:515], [pkprev], [pk])
                                a = acc[pcount % 2]
                                ak = ('acc', pcount % 2)
                                ch = c
                                kb.ts('dve', a[:], p[:, 3:515], convw[:, ch, 3:4], None, ALU.mult, None, [pk, 'convw'], [ak])
                                for jj in (2, 1, 0):
                                    kb.stt(a[:], p[:, jj:jj + 512], convw[:, ch, jj:jj + 1], a[:], ALU.mult, ALU.add, [pk, 'convw', ak], [ak])
                                if mode == 'conv_v':
                                    kb.act(osl, a[:], AF.Silu, [ak], okey)
                                else:
                                    kb.act(a[:], a[:], AF.Silu, [ak], [ak])
                                    q2 = sq2[pcount % 2]
                                    qk = ('sq2', pcount % 2)
                                    kb.tt('pool', q2[:], a[:], a[:], ALU.mult, [ak], [qk])
                                    nb = 4 + pcount % 2
                                    kb.newgen(nb)
                                    kb.mm(nb, ps[nb][:, :], C(C_ONES), q2[:], [qk, 'cst'], PK(nb))
                                    r = rn[pcount % 2]
                                    rk = ('rn', pcount % 2)
                                    kb.act(lnv[:], ps[nb][:, :], AF.Ln, PK(nb), [('tmpln',)], bias=EPS)
                                    kb.act(r[:], lnv[:], AF.Exp, [('tmpln',)], [rk], scale=-0.5)
                                    sc = (128.0 ** -0.5) if mode == 'conv_q' else 1.0
                                    kb.stt(osl, a[:], sc, r[:], ALU.mult, ALU.mult, [ak, rk], okey)
                            pcount += 1
                            kb.dma(scratch[c][:, tb * 512:(tb + 1) * 512], obuf[ob][:], reads=[('obuf', ob)], writes=[('scr', c, tb)])
                kb.barrier()

        def gdn_layer(L, j, xsrc, xkey, last_layer):
            with contextlib.ExitStack() as sl:
                G = sb(sl, "G", [128, NT, 6, 16])
                glb = sb(sl, "glb", [128, NT, 2, 16])
                with contextlib.ExitStack() as s1:
                    A_b = sb(s1, "A_b", [128, D])
                    B_b = sb(s1, "B_b", [128, D])
                    adaln(L, A_b, B_b, s1)
                    hT = sb(s1, "hT", [128, 8, S], BF16)
                    with contextlib.ExitStack() as s2:
                        norm_phase(L, xsrc, xkey, hT, A_b, B_b, s2)
                        kb.barrier()
                    if dbg and L == 0:
                        kb.dma(dbg_d['hT'], hT[:], reads=[('hT', i) for i in range(8)], writes=['dbg_hT'])
                    if stop == 'norm':
                        kb.barrier()
                        return True
                    with contextlib.ExitStack() as s2:
                        wbast = sb(s2, "wbast", [128, 8, 32])
                        wba = sb(s2, "wba", [128, 8, 32], BF16)
                        dtb = sb(s2, "dtb", [128, 16])
                        negA = sb(s2, "negA", [128, 16])
                        gt = sb(s2, "gt", [128, 8, 16])
                        kb.dma(wbast[:], a_win[j].rearrange("(kc p) f -> p kc f", p=128)[:, :, 6144:6176], writes=['wbast'])
                        kb.copy('dve', wba[:], wbast[:], ['wbast'], ['wba'])
                        kb.dma(dtb[:], a_dtb[j:j + 1, :].partition_broadcast(128), writes=['dtb'])
                        kb.dma(negA[:], a_Alog[j:j + 1, :].partition_broadcast(128), writes=['negA'])
                        kb.act(negA[:], negA[:], AF.Exp, ['negA'], ['negA'])
                        kb.ts('dve', negA[:], negA[:], -1.0, None, ALU.mult, None, ['negA'], ['negA'])
                        for t in range(NT):
                            bank = t % 2
                            kb.newgen(bank)
                            for kc in range(8):
                                kb.mm(bank, ps[bank][:, 0:32], hT[:, kc, t * 128:(t + 1) * 128], wba[:, kc, :],
                                      [('hT', t // 4), 'wba'], PK(bank), last=(kc == 7))
                            gk = ('G', t)
                            kb.tt('dve', gt[:, 0, :], ps[bank][:, 16:32], dtb[:], ALU.add, PK(bank) + ['dtb'], ['gt0'])
                            kb.act(gt[:, 0, :], gt[:, 0, :], AF.Exp, ['gt0'], ['gt0'])
                            kb.act(gt[:, 0, :], gt[:, 0, :], AF.Ln, ['gt0'], ['gt0'], bias=1.0)
                            kb.tt('dve', G[:, t, 0, :], gt[:, 0, :], negA[:], ALU.mult, ['gt0', 'negA'], [gk])
                            kb.act(gt[:, 1, :], ps[bank][:, 0:16], AF.Exp, PK(bank), ['gt1'], scale=-1.0)
                            kb.act(gt[:, 1, :], gt[:, 1, :], AF.Ln, ['gt1'], ['gt1'], bias=1.0)
                            kb.act(G[:, t, 2, :], gt[:, 1, :], AF.Exp, ['gt1'], [gk], scale=-1.0)
                            kb.ts('dve', G[:, t, 1, :], gt[:, 1, :], -1.0, None, ALU.mult, None, ['gt1'], [gk])
                            b2 = 2 + t % 2
                            kb.newgen(b2)
                            kb.mm(b2, ps[b2][:, 0:16], C(C_TRIBD), G[:, t, 0, :], ['cst', gk], PK(b2))
                            kb.copy('dve', G[:, t, 3, :], ps[b2][:, 0:16], PK(b2), [gk])
                            kb.mm(b2, ps[b2][:, 16:32], C(C_SELC), G[:, t, 3, :], ['cst', gk], PK(b2))
                            kb.mm(b2, ps[b2][:, 32:48], C(C_SELA), G[:, t, 3, :], ['cst', gk], PK(b2))
                            kb.mm(b2, ps[b2][:, 48:64], C(C_SELB), G[:, t, 3, :], ['cst', gk], PK(b2))
                            kb.tt('dve', gt[:, 2, :], ps[b2][:, 16:32], G[:, t, 3, :], ALU.subtract, PK(b2) + [gk], ['gt2'])
                            kb.act(G[:, t, 5, :], gt[:, 2, :], AF.Exp, ['gt2'], [gk])
                            kb.act(glb[:, t, :, :], ps[b2][:, 32:64].rearrange("p (c h) -> p c h", c=2), AF.Exp, PK(b2), [('glb', t)])
                            kb.act(gt[:, 3, :], G[:, t, 3, :], AF.Exp, [gk], ['gt3'])
                            kb.tt('dve', G[:, t, 4, :], gt[:, 3, :], G[:, t, 2, :], ALU.mult, ['gt3', gk], [gk])
                        kb.barrier()
                    if dbg and L == 0:
                        kb.dma(dbg_d['gates'], G[:], reads=[('G', t) for t in range(NT)], writes=['dbg_gates'])
                    if stop == 'gates':
                        kb.barrier()
                        return True
                    with contextlib.ExitStack() as s2:
                        convw = sb(s2, "convw", [128, 32, 4])
                        kb.dma(convw[:], a_convT[j], writes=['convw'])
                        modes = ['conv_q'] * 8 + ['conv_k'] * 8 + ['conv_v'] * 16 + ['silu'] * 16
                        proj_fm(a_win[j], 0, 48, modes, hT, qkvz, s2, convw=convw)
                    kb.barrier()
                if stop == 'proj':
                    return True
                with contextlib.ExitStack() as s1:
                    wout_bf = sb(s1, "wout_bf", [128, 16, D], BF16)
                    load_wout(a_wout[j], 16, wout_bf, s1)
                    rr = gdn_tiles(L, j, G, glb, wout_bf, xsrc, xkey, last_layer, s1)
                    kb.barrier()
                    return rr

        def gdn_tiles(L, j, G, glb, wout_bf, xsrc, xkey, last_layer, st):
            HG = 8
            Sf = sb(st, "Sf", [128, 16, 128])
            Sb = sb(st, "Sb", [128, 16, 128], BF16)
            nw = sb(st, "nw", [128, 1])
            qT = [sb(st, "qT%d" % i, [128, 8, 128], BF16) for i in range(2)]
            kT = [sb(st, "kT%d" % i, [128, 8, 128], BF16) for i in range(2)]
            vT = [sb(st, "vT%d" % i, [128, 16, 128], BF16) for i in range(2)]
            zs = [sb(st, "zs%d" % i, [128, 16, 128], BF16) for i in range(2)]
            Ag = [sb(st, "Ag%d" % i, [128, 128]) for i in range(2)]
            Agp = [sb(st, "Agp%d" % i, [128, 128]) for i in range(2)]
            E3 = [sb(st, "E3%d" % i, [128, 384]) for i in range(2)]
            XY = sb(st, "XY", [128, HG, 2, 128])
            Pm = sb(st, "Pm", [128, HG, 128])
            attnT = sb(st, "attnT", [128, HG, 128], BF16)
            vb = sb(st, "vb", [128, HG, 128], BF16)
            kbg = sb(st, "kbg", [128, HG, 128], BF16)
            kdec = sb(st, "kdec", [128, HG, 128], BF16)
            qdT = sb(st, "qdT", [128, HG, 128], BF16)
            Gb = [sb(st, "Gb%d" % i, [128, 128]) for i in range(2)]
            gamb = [sb(st, "gamb%d" % i, [128, 128]) for i in range(2)]
            TT = sb(st, "TT", [128, HG, 128], BF16)
            usb = sb(st, "usb", [128, HG, 128])
            wTb = sb(st, "wTb", [128, HG, 128], BF16)
            vnew = sb(st, "vnew", [128, HG, 128], BF16)
            oT = sb(st, "oT", [128, HG, 128])
            osq = sb(st, "osq", [128, 512])
            rst = sb(st, "rst", [128, 512])
            lnt = sb(st, "lnt", [128, 512])
            ogT = [sb(st, "ogT%d" % i, [128, 16, 128], BF16) for i in range(2)]
            kb.memset('dve', Sf[:], 0.0, [('Sf', h) for h in range(16)])
            kb.memset('pool', Sb[:], 0.0, [('Sb', h) for h in range(16)])
            kb.dma(nw[:], a_nw[j], writes=['nw'])
            qv = qkvz[0:8].rearrange("c p t -> p c t")
            kv = qkvz[8:16].rearrange("c p t -> p c t")
            vv = qkvz[16:32].rearrange("c p t -> p c t")
            zv = qkvz[32:48].rearrange("c p t -> p c t")

            def load_tile(t):
                b = t % 2
                tsl = slice(t * 128, (t + 1) * 128)
                kb.dma(qT[b][:], qv[:, :, tsl], writes=[('qT', b)])
                kb.dma(kT[b][:], kv[:, :, tsl], writes=[('kT', b)])
                kb.dma(vT[b][:], vv[:, :, tsl], writes=[('vT', b)])
                kb.dma(zs[b][:], zv[:, :, tsl], writes=[('zs', b)])

            load_tile(0)
            for t in range(NT):
                b = t % 2
                if t + 1 < NT:
                    load_tile(t + 1)
                gk = ('G', t)
                for hg in range(2):
                    for hl in range(HG):
                        h = hg * HG + hl
                        hp = h // 2
                        if hl % 2 == 0:
                            gb = 4
                            off = (hp % 2) * 256
                            kb.newgen(gb) if hp % 2 == 0 else None
                            kb.mm(gb, ps[gb][:, off:off + 128], kT[b][:, hp, :], kT[b][:, hp, :], [('kT', b)], PK(4, off, off + 256))
                            kb.mm(gb, ps[gb][:, off + 128:off + 256], kT[b][:, hp, :], qT[b][:, hp, :], [('kT', b), ('qT', b)], PK(4, off, off + 256))
                            ks = hp % 4
                            kb.tr(psbf(6)[:, ks * 128:(ks + 1) * 128], kT[b][:, hp, :], ident_bf[:], [('kT', b), 'ident_bf'], PK(6, 0, 256))
                        Gps = ps[4][:, off:off + 128]
                        QKps = ps[4][:, off + 128:off + 256]
                        gkey = PK(4, off, off + 256)
                        ks = hp % 4
                        psK = psbf(6)[:, ks * 128:(ks + 1) * 128]
                        a = Ag[h % 2]
                        ap_ = Agp[h % 2]
                        kb.ts('pool', a[:], C(C_UT), G[:, t, 0, h:h + 1], None, ALU.mult, None, ['cst', gk], [('Ag', h % 2)])
                        kb.stt(ap_[:], C(C_ID), G[:, t, 1, h:h + 1], a[:], ALU.mult, ALU.add, ['cst', gk, ('Ag', h % 2)], [('Agp', h % 2)])
                        db = 5
                        kb.newgen(db)
                        dk = PK(5, 0, 384)
                        kb.mm(db, ps[db][:, 0:128], C(C_SL), a[:], ['cst', ('Ag', h % 2)], dk, last=False, inc=False)
                        kb.mm(db, ps[db][:, 0:128], C(C_ID), C(C_MINCLT), ['cst'], dk)
                        kb.mm(db, ps[db][:, 128:256], C(C_SL), ap_[:], ['cst', ('Agp', h % 2)], dk, last=False, inc=False)
                        kb.mm(db, ps[db][:, 128:256], C(C_ID), C(C_MSTRT), ['cst'], dk)
                        kb.mm(db, ps[db][:, 256:384], ap_[:], C(C_SL), ['cst', ('Agp', h % 2)], dk, last=False, inc=False)
                        kb.mm(db, ps[db][:, 256:384], C(C_ID), C(C_MSTR), ['cst'], dk)
                        g_ = Gb[h % 2]
                        kb.ts('pool', g_[:], C(C_ONES), G[:, t, 3, h:h + 1], None, ALU.mult, None, ['cst', gk], [('Gb', h % 2)])
                        kb.mm(db, ps[db][:, 384:512], g_[:], C(C_ID), ['cst', ('Gb', h % 2)], PK(5, 384, 512))
                        e3 = E3[h % 2]
                        kb.act(e3[:], ps[db][:, 0:384], AF.Exp, dk, [('E3', h % 2)])
                        gm = gamb[h % 2]
                        kb.act(gm[:], ps[db][:, 384:512], AF.Exp, PK(5, 384, 512), [('gamb', h % 2)])
                        kb.tt('dve', XY[:, hl, 0, :], e3[:, 128:256], Gps, ALU.mult, [('E3', h % 2)] + gkey, [('XY', hl)])
                        kb.tt('dve', XY[:, hl, 1, :], e3[:, 256:384], Gps, ALU.mult, [('E3', h % 2)] + gkey, [('XY', hl)])
                        kb.tt('dve', attnT[:, hl, :], e3[:, 0:128], QKps, ALU.mult, [('E3', h % 2)] + gkey, [('attnT', hl)])
                        kb.stt(Pm[:, hl, :], XY[:, hl, 0, :], -1.0, C(C_ID), ALU.mult, ALU.add, [('XY', hl), 'cst'], [('Pm', hl)])
                        kb.tt('pool', qdT[:, hl, :], qT[b][:, hp, :], gm[:], ALU.mult, [('qT', b), ('gamb', h % 2)], [('qdT', hl)])
                        vs = 4 + h % 4
                        psV = psbf(6)[:, vs * 128:(vs + 1) * 128]
                        kb.tr(psV, vT[b][:, h, :], ident_bf[:], [('vT', b), 'ident_bf'], PK(6, 256, 512))
                        kb.act(vb[:, hl, :], psV, AF.Identity, PK(6, 256, 512) + [gk], [('vb', hl)], scale=G[:, t, 2, h:h + 1])
                        kb.act(kbg[:, hl, :], psK, AF.Identity, PK(6, 0, 256) + [gk], [('kbg', hl)], scale=G[:, t, 4, h:h + 1])
                        kb.ts('dve', kdec[:, hl, :], psK, G[:, t, 5, h:h + 1], None, ALU.mult, None, PK(6, 0, 256) + [gk], [('kdec', hl)])
                    if stop == 't1':
                        return True
                    for lvl in range(1, 6):
                        for pr in range(HG // 2):
                            bank = pr % 2
                            kb.newgen(bank)
                            for u_ in range(2):
                                hl = pr * 2 + u_
                                X = XY[:, hl, 0, :]
                                Y = XY[:, hl, 1, :]
                                o0 = u_ * 256
                                if lvl < 5:
                                    kb.mm(bank, ps[bank][:, o0:o0 + 128], Y, X, [('XY', hl)], PK(bank, o0, o0 + 256), inc=False)
                                kb.mm(bank, ps[bank][:, o0 + 128:o0 + 256], X, Y, [('XY', hl)], PK(bank, o0, o0 + 256))
                            if lvl < 5:
                                kb.copy('act', XY[:, pr * 2:pr * 2 + 2, :, :], ps[bank][:, :].rearrange("p (h x c) -> p h x c", h=2, x=2),
                                        PK(bank), [('XY', pr * 2), ('XY', pr * 2 + 1)])
                            else:
                                kb.copy('act', XY[:, pr * 2:pr * 2 + 2, 1, :], ps[bank][:, :].rearrange("p (h x c) -> p h x c", h=2, x=2)[:, :, 1, :],
                                        PK(bank), [('XY', pr * 2), ('XY', pr * 2 + 1)])
                        for q4 in range(HG // 4):
                            bank = 2 + q4 % 2
                            kb.newgen(bank)
                            for u_ in range(4):
                                hl = q4 * 4 + u_
                                kb.mm(bank, ps[bank][:, u_ * 128:(u_ + 1) * 128], XY[:, hl, 1, :], Pm[:, hl, :], [('XY', hl), ('Pm', hl)], PK(bank, u_ * 128, (u_ + 1) * 128))
                            hs = slice(q4 * 4, q4 * 4 + 4)
                            pk = [('Pm', q4 * 4 + u_) for u_ in range(4)]
                            if lvl < 5:
                                kb.tt('dve', Pm[:, hs, :], Pm[:, hs, :], ps[bank][:, :].rearrange("p (h c) -> p h c", h=4), ALU.add,
                                      PK(bank) + pk, pk)
                            else:
                                kb.tt('dve', TT[:, hs, :], Pm[:, hs, :], ps[bank][:, :].rearrange("p (h c) -> p h c", h=4), ALU.add,
                                      PK(bank) + pk, [('TT', q4 * 4 + u_) for u_ in range(4)])
                    if stop == 't2':
                        return True
                    for pr in range(HG // 2):
                        bank = UWB
                        kb.newgen(bank)
                        for u_ in range(2):
                            hl = pr * 2 + u_
                            if True:
                                kb.mm(bank, ps[bank][:, u_ * 128:(u_ + 1) * 128], TT[:, hl, :], vb[:, hl, :], [('TT', hl), ('vb', hl)], PK(UWB, 0, 256))
                            if True:
                                kb.mm(bank, ps[bank][:, 256 + u_ * 128:256 + (u_ + 1) * 128], kbg[:, hl, :], TT[:, hl, :], [('TT', hl), ('kbg', hl)], PK(UWB, 256, 512))
                        if True:
                            kb.copy('act', usb[:, pr * 2:pr * 2 + 2, :], ps[bank][:, 0:256].rearrange("p (h c) -> p h c", h=2), PK(UWB, 0, 256),
                                    [('usb', pr * 2), ('usb', pr * 2 + 1)])
                            kb.copy('dve', wTb[:, pr * 2:pr * 2 + 2, :], ps[bank][:, 256:512].rearrange("p (h c) -> p h c", h=2), PK(UWB, 256, 512),
                                    [('wTb', pr * 2), ('wTb', pr * 2 + 1)])
                    if stop == 't3':
                        return True
                    for ch in range(2):
                        rs = slice(ch * 64, (ch + 1) * 64)
                        for q4 in range(HG // 4):
                            wb = q4 % 2
                            kb.newgen(wb)
                            for u_ in range(4):
                                hl = q4 * 4 + u_
                                h = hg * HG + hl
                                kb.mm(wb, ps[wb][rs, u_ * 128:(u_ + 1) * 128], wTb[:, hl, rs], Sb[:, h, :], [('wTb', hl), ('Sb', h)], PK(wb), halves=(ch,))
                            hs = slice(q4 * 4, q4 * 4 + 4)
                            vk = [('vnew', q4 * 4 + u_) for u_ in range(4)]
                            kb.tt('dve', vnew[rs, hs, :], usb[rs, hs, :], ps[wb][rs, :].rearrange("p (h c) -> p h c", h=4), ALU.subtract,
                                  PK(wb) + [('usb', q4 * 4 + u_) for u_ in range(4)], vk)
                            ob_ = 2 + q4 % 2
                            sbk = 4 + q4 % 2
                            kb.newgen(ob_)
                            kb.newgen(sbk)
                            for u_ in range(4):
                                hl = q4 * 4 + u_
                                h = hg * HG + hl
                                kb.mm(ob_, ps[ob_][:, u_ * 64:(u_ + 1) * 64], Sb[:, h, :], qdT[:, hl, rs], [('Sb', h), ('qdT', hl)], PK(ob_, 0, 256), last=False, inc=False)
                                kb.mm(ob_, ps[ob_][:, u_ * 64:(u_ + 1) * 64], vnew[rs, hl, :], attnT[rs, hl, rs], [('vnew', hl), ('attnT', hl)], PK(ob_, 0, 256))
                                kb.mm(sbk, ps[sbk][:, u_ * 128:(u_ + 1) * 128], kdec[rs, hl, :], vnew[rs, hl, :], [('kdec', hl), ('vnew', hl)], PK(sbk, u_ * 128, (u_ + 1) * 128))
                            kb.copy('act', oT[:, hs, rs], ps[ob_][:, 0:256].rearrange("p (h c) -> p h c", h=4), PK(ob_, 0, 256),
                                    [('oT', q4 * 4 + u_) for u_ in range(4)])
                            for u_ in range(4):
                                hl = q4 * 4 + u_
                                h = hg * HG + hl
                                kb.stt(Sf[:, h, :], Sf[:, h, :], glb[:, t, ch, h:h + 1], ps[sbk][:, u_ * 128:(u_ + 1) * 128], ALU.mult, ALU.add,
                                       [('Sf', h), ('glb', t)] + PK(sbk, u_ * 128, (u_ + 1) * 128), [('Sf', h)])
                            h0 = hg * HG + q4 * 4
                            kb.copy('act', Sb[:, h0:h0 + 4, :], Sf[:, h0:h0 + 4, :], [('Sf', h0 + u_) for u_ in range(4)], [('Sb', h0 + u_) for u_ in range(4)])
                    if stop == 't4':
                        return True
                    for q4 in range(HG // 4):
                        hs = slice(q4 * 4, q4 * 4 + 4)
                        h0 = hg * HG + q4 * 4
                        ok = [('oT', q4 * 4 + u_) for u_ in range(4)]
                        kb.act(osq[:].rearrange("p (h c) -> p h c", h=4), oT[:, hs, :], AF.Square, ok, ['osq'])
                        nb = 6
                        kb.newgen(nb)
                        kb.mm(nb, ps[nb][:, :], C(C_ONES), osq[:], ['cst', 'osq'], PK(6))
                        rstd_from_ss(ps[nb][:, :], rst[:], 128, PK(6), ['rst'], lnt[:])
                        kb.tt('dve', osq[:].rearrange("p (h c) -> p h c", h=4), oT[:, hs, :], rst[:].rearrange("p (h c) -> p h c", h=4), ALU.mult,
                              ok + ['rst', 'osq'], ['osq'])
                        kb.stt(ogT[b][:, h0:h0 + 4, :], osq[:].rearrange("p (h c) -> p h c", h=4), nw[:, 0:1], zs[b][:, h0:h0 + 4, :], ALU.mult, ALU.mult,
                               ['osq', 'nw', ('zs', b)], [('ogT', b)])
                if stop == 't5':
                    return True
                outproj_tile(L, t, ogT[b], 16, wout_bf, xsrc, xkey, last_layer, [('ogT', b)])
                if stop == 't6':
                    return True

        def fox_layer(L, j, xsrc, xkey, last_layer):
            with contextlib.ExitStack() as sl:
                Vall = sb(sl, "Vall", [128, NT, 16, 65], BF16)
                cumT = sb(sl, "cumT", [128, NT, 16])
                with contextlib.ExitStack() as s1:
                    hT = sb(s1, "hT", [128, 8, S], BF16)
                    with contextlib.ExitStack() as s2:
                        A_b = sb(s2, "A_b", [128, D])
                        B_b = sb(s2, "B_b", [128, D])
                        adaln(L, A_b, B_b, s2)
                        norm_phase(L, xsrc, xkey, hT, A_b, B_b, s2)
                        kb.barrier()
                    with contextlib.ExitStack() as s2:
                        wfst = sb(s2, "wfst", [128, 8, 16])
                        wf = sb(s2, "wf", [128, 8, 16], BF16)
                        nfb = sb(s2, "nfb", [16, 1])
                        spl = sb(s2, "spl", [16, 2048])
                        cums = sb(s2, "cums", [16, S])
                        onesr = sb(s2, "onesr", [16, 2048], BF16)
                        c1b = sb(s2, "c1b", [16, S], BF16)
                        kb.dma(wfst[:], b_win[j].rearrange("(kc p) f -> p kc f", p=128)[:, :, 4096:4112], writes=['wfst'])
                        kb.copy('dve', wf[:], wfst[:], ['wfst'], ['wf'])
                        kb.dma(nfb[:], b_fb[j], writes=['nfb'])
                        kb.ts('dve', nfb[:], nfb[:], -1.0, None, ALU.mult, None, ['nfb'], ['nfb'])
                        kb.memset('pool', onesr[:], 1.0, ['onesr'])
                        for half in range(2):
                            for tl in range(4):
                                tb = half * 4 + tl
                                bank = tb % 2
                                kb.newgen(bank)
                                for kc in range(8):
                                    kb.mm(bank, ps[bank][0:16, :], wf[:, kc, :], hT[:, kc, tb * 512:(tb + 1) * 512], ['wf', ('hT', tb)], PK(bank),
                                          halves=(0,), last=(kc == 7))
                                kb.act(spl[:, tl * 512:(tl + 1) * 512], ps[bank][0:16, :], AF.Exp, PK(bank) + ['nfb'], [('spl', tl)], scale=-1.0, bias=nfb[:, 0:1])
                                kb.act(spl[:, tl * 512:(tl + 1) * 512], spl[:, tl * 512:(tl + 1) * 512], AF.Ln, [('spl', tl)], [('spl', tl)], bias=1.0)
                            init = 0.0 if half == 0 else cums[:, 2047:2048]
                            kb.op('dve', lambda g: g.tensor_tensor_scan(out=cums[:, half * 2048:(half + 1) * 2048], data0=onesr[:], data1=spl[:], initial=init,
                                                                      op0=ALU.mult, op1=ALU.add),
                                  [('spl', tl) for tl in range(4)] + ['onesr', 'cums'], ['cums'])
                        kb.ts('dve', c1b[:], cums[:], -1.0, None, ALU.mult, None, ['cums'], ['c1b'])
                        kb.dma(c1s, c1b[:], reads=['c1b'], writes=['c1s'])
                        for t in range(NT):
                            bank = 2 + t % 2
                            kb.newgen(bank)
                            kb.mm(bank, ps[bank][:, 0:16], cums[:, t * 128:(t + 1) * 128], cst[0:16, C_ID, 0:16], ['cums', 'cst'], PK(bank))
                            kb.copy('dve', cumT[:, t, :], ps[bank][:, 0:16], PK(bank), [('cumT', t)])
                        kb.barrier()
                    with contextlib.ExitStack() as s2:
                        qn = sb(s2, "qn", [128, 1])
                        kn = sb(s2, "kn", [128, 1])
                        kb.dma(qn[:], b_qn2[j], writes=['ppsc'])
                        kb.dma(kn[:], b_kn2[j], writes=['ppsc'])
                        kb.ts('dve', qn[:], qn[:], 0.125, None, ALU.mult, None, ['ppsc'], ['ppsc'])
                        proj_fm(b_win[j], 0, 16, ['rms'] * 16, hT, qks, s2, pp_scalars=[qn[:, 0:1]] * 8 + [kn[:, 0:1]] * 8, nred=64, ones_ap=C(C_ONESBD))
                    with contextlib.ExitStack() as s2:
                        wst = [sb(s2, "wvst%d" % i, [128, 8, 128]) for i in range(2)]
                        wvz = sb(s2, "wvz", [128, 8, 2048], BF16)
                        zt = [sb(s2, "zt%d" % i, [128, D], BF16) for i in range(2)]
                        wv = b_win[j].rearrange("(kc p) f -> p kc f", p=128)
                        for g in range(16):
                            b = g % 2
                            kb.dma(wst[b][:], wv[:, :, 2048 + g * 128:2048 + (g + 1) * 128], writes=[('wvst', b)])
                            kb.copy('pool', wvz[:, :, g * 128:(g + 1) * 128], wst[b][:], [('wvst', b)], [('wvz', g // 4)])
                        kb.memset('dve', Vall[:, :, :, 64:65], 1.0, [('Vall1',)])
                        for t in range(NT):
                            for fb in range(4):
                                bank = (t * 4 + fb) % 4
                                kb.newgen(bank)
                                for kc in range(8):
                                    kb.mm(bank, ps[bank][:, :], hT[:, kc, t * 128:(t + 1) * 128], wvz[:, kc, fb * 512:(fb + 1) * 512],
                                          [('hT', t // 4), ('wvz', fb)], PK(bank), last=(kc == 7))
                                if fb < 2:
                                    kb.copy('dve', Vall[:, t, fb * 8:(fb + 1) * 8, 0:64], ps[bank][:, :].rearrange("p (h d) -> p h d", h=8), PK(bank), [('Vall', t)])
                                else:
                                    kb.act(zt[t % 2][:, (fb - 2) * 512:(fb - 1) * 512], ps[bank][:, :], AF.Silu, PK(bank), [('zt', t % 2)])
                            kb.dma(zss[t * 128:(t + 1) * 128, :], zt[t % 2][:], reads=[('zt', t % 2)], writes=[('zss', t)])
                        kb.barrier()
                with contextlib.ExitStack() as s1:
                    Oall = sb(s1, "Oall", [128, NT, D], BF16)
                    with contextlib.ExitStack() as s2:
                        QA = [sb(s2, "QA%d" % i, [65, S], BF16) for i in range(2)]
                        KA = [sb(s2, "KA%d" % i, [65, S], BF16) for i in range(2)]
                        PT = [sb(s2, "PT%d" % i, [128, 512], BF16) for i in range(3)]
                        rl = sb(s2, "rl", [128, 4])
                        for i in range(2):
                            kb.memset('dve', KA[i][64:65, :], 1.0, [('KA1', i)])
                        pcount = 0
                        ocount = 0

                        def load_head(h):
                            b = h % 2
                            r0 = (h % 2) * 64
                            kb.dma(QA[b][0:64, :], qks[h // 2][r0:r0 + 64, :], writes=[('QA', b)])
                            kb.dma(QA[b][64:65, :], c1s[h:h + 1, :], reads=['c1s'], writes=[('QA', b)])
                            kb.dma(KA[b][0:64, :], qks[8 + h // 2][r0:r0 + 64, :], writes=[('KA', b)])

                        load_head(0)
                        for h in range(16):
                            b = h % 2
                            if h + 1 < 16:
                                load_head(h + 1)
                            for qb in range(8):
                                obk = 3 + ocount % 2
                                ocount += 1
                                kb.newgen(obk)
                                nkt = 4 * (qb + 1)
                                for kt in range(nkt):
                                    jd = kt - 4 * qb
                                    i0 = max(0, jd)
                                    sbk = pcount % 3
                                    pt = PT[pcount % 3]
                                    ptk = ('PT', pcount % 3)
                                    pcount += 1
                                    kb.newgen(sbk)
                                    kb.mm(sbk, ps[sbk][:, i0 * 128:512], KA[b][0:65, kt * 128:(kt + 1) * 128], QA[b][0:65, qb * 512 + i0 * 128:(qb + 1) * 512],
                                          [('KA', b), ('KA1', b), ('QA', b)], PK(sbk))
                                    kb.act(pt[:, i0 * 128:512], ps[sbk][:, i0 * 128:512], AF.Exp, PK(sbk) + [('cumT', kt)], [ptk], bias=cumT[:, kt, h:h + 1])
                                    if jd >= 0:
                                        kb.tt('pool', pt[:, jd * 128:(jd + 1) * 128], pt[:, jd * 128:(jd + 1) * 128], caus_bf[:], ALU.mult, [ptk, 'caus_bf'], [ptk])
                                    for i in range(i0, 4):
                                        kb.mm(obk, ps[obk][:, i * 65:(i + 1) * 65], pt[:, i * 128:(i + 1) * 128], Vall[:, kt, h, :],
                                              [ptk, ('Vall', kt), ('Vall1',)], PK(obk), last=(kt == nkt - 1), inc=(i == 3))
                                ov = ps[obk][:, 0:260].rearrange("p (i d) -> p i d", i=4)
                                kb.op('dve', lambda g: g.reciprocal(out=rl[:], in_=ov[:, :, 64]), PK(obk), ['rl'])
                                kb.tt('dve', Oall[:, qb * 4:(qb + 1) * 4, h * 64:(h + 1) * 64], ov[:, :, 0:64], rl[:].unsqueeze(2).broadcast_to([128, 4, 64]),
                                      ALU.mult, PK(obk) + ['rl'], [('Oall', qb)])
                        kb.barrier()
                    with contextlib.ExitStack() as s2:
                        wout_bf = sb(s2, "wout_bf", [128, 8, D], BF16)
                        load_wout(b_wout[j], 8, wout_bf, s2)
                        zt = [sb(s2, "zt%d" % i, [128, D], BF16) for i in range(2)]
                        og = [sb(s2, "og%d" % i, [128, D], BF16) for i in range(2)]
                        ogT = [sb(s2, "ogT%d" % i, [128, 8, 128], BF16) for i in range(2)]
                        for t in range(NT):
                            b = t % 2
                            kb.dma(zt[b][:], zss[t * 128:(t + 1) * 128, :], reads=[('zss', t)], writes=[('zt', b)])
                            kb.tt('pool', og[b][:], Oall[:, t, :], zt[b][:], ALU.mult, [('Oall', t // 4), ('zt', b)], [('og', b)])
                            bank = 4 + b
                            for kc in range(8):
                                kb.tr(psbf(bank)[:, kc * 128:(kc + 1) * 128], og[b][:, kc * 128:(kc + 1) * 128], ident_bf[:], [('og', b), 'ident_bf'], PK(bank))
                            kb.copy('act', ogT[b][:], psbf(bank).rearrange("p (k t) -> p k t", k=8), PK(bank), [('ogT', b)])
                            outproj_tile(L, t, ogT[b], 8, wout_bf, xsrc, xkey, last_layer, [('ogT', b)])
                        kb.barrier()

        xsrc, xkey = x_in, 'xin'
        for L in range(n_layers):
            last = (L == n_layers - 1)
            if L % 2 == 0:
                stopped = gdn_layer(L, L // 2, xsrc, xkey, last)
            else:
                stopped = fox_layer(L, L // 2, xsrc, xkey, last)
            xsrc, xkey = xres, 'xres'
            kb.barrier()
            if stopped:
                break
        if dbg:
            kb.dma(dbg_d['xres'], xres, reads=[('xres', t) for t in range(NT)], writes=['dbg_x'])
            kb.dma(dbg_d['qkvz'], qkvz, writes=['dbg_q'])
        kb.barrier()
        print("instructions emitted:", kb.ninst, {k: v for k, v in kb.cnt.items()})
    return nc


def make_in_maps(inputs):
    consts = make_consts()
    f = lambda a: np.ascontiguousarray(np.asarray(a, dtype=np.float32))
    x = f(inputs["x"])
    c = f(inputs["c"])
    shared = {
        "norm_w": f(inputs["norm_w"]),
        "final_norm_w": f(inputs["final_norm_w"]).reshape(1, D),
        "ada_w": f(inputs["ada_w"]),
        "ada_b": f(inputs["ada_b"]),
        "a_w_in": f(inputs["a_w_in"]),
        "a_convT": f(np.transpose(f(inputs["a_conv_w"]), (0, 2, 1)).reshape(2, 32, 128, 4).transpose(0, 2, 1, 3)),
        "a_A_log": f(inputs["a_A_log"]),
        "a_dt_bias": f(inputs["a_dt_bias"]),
        "a_norm_w": f(inputs["a_norm_w"]).reshape(2, 128, 1),
        "a_w_out": f(inputs["a_w_out"]),
        "b_w_in": f(inputs["b_w_in"]),
        "b_f_bias": f(inputs["b_f_bias"]).reshape(2, 16, 1),
        "b_qn2": f(np.tile(f(inputs["b_qn_w"]), (1, 2))).reshape(2, 128, 1),
        "b_kn2": f(np.tile(f(inputs["b_kn_w"]), (1, 2))).reshape(2, 128, 1),
        "b_w_out": f(inputs["b_w_out"]),
        "consts": consts,
    }
    maps = []
    for b in range(8):
        m = dict(shared)
        m["x"] = x[b]
        m["cT"] = f(c[b].reshape(8, 128).T)
        maps.append(m)
    return maps


_NC_CACHE = {}


def kernel(**inputs):
    if 'nc' not in _NC_CACHE:
        _NC_CACHE['nc'] = build_program()
    nc = _NC_CACHE['nc']
    in_maps = make_in_maps(inputs)
    res = run_bass_kernel_spmd(nc, in_maps, core_ids=list(range(8)))
    out = np.stack([np.asarray(r["out"], dtype=np.float32) for r in res.results], axis=0)
    return out
```

```python
import contextlib
import numpy as np
import concourse.bass as bass
import concourse.mybir as mybir
from concourse.bass_utils import run_bass_kernel_spmd

F32 = mybir.dt.float32
BF16 = mybir.dt.bfloat16
AF = mybir.ActivationFunctionType
ALU = mybir.AluOpType
AX = mybir.AxisListType

S = 4096
D = 1024
NT = 32
EPS = 1e-6
NEG = -30000.0
DEPTH = 4
GDN_IN = 6176
FOX_IN = 4112

C_ID, C_ONES, C_UT, C_SL, C_MINCLT, C_MSTRT, C_MSTR, C_TRIBD, C_SELC, C_SELA, C_SELB, C_CAUS, C_ONESBD = range(13)
NCONST = 13


def make_consts():
    i = np.arange(128)
    r = i[:, None]
    c = i[None, :]
    same = (r // 64) == (c // 64)
    m = np.zeros((NCONST, 128, 128), np.float32)
    m[C_ID] = (r == c)
    m[C_ONES] = 1.0
    m[C_UT] = (r <= c)
    m[C_SL] = (r > c)
    m[C_MINCLT] = np.where(same & (r <= c), 0.0, NEG)
    m[C_MSTRT] = np.where(same & (r < c), 0.0, NEG)
    m[C_MSTR] = np.where(same & (r > c), 0.0, NEG)
    m[C_TRIBD] = (same & (r <= c))
    m[C_SELC] = (r == (c // 64) * 64 + 63)
    m[C_SELA] = (r == 63) * np.ones((1, 128))
    m[C_SELB] = (r == 127) * np.ones((1, 128))
    m[C_CAUS] = (r <= c)
    m[C_ONESBD] = same
    return m.astype(np.float32)


class KB:
    NS = 24

    def __init__(self, nc, es):
        self.nc = nc
        self.eng = {'pe': nc.tensor, 'act': nc.scalar, 'dve': nc.vector, 'pool': nc.gpsimd, 'sp': nc.sync}
        self.sem = {k: es.enter_context(nc.semaphore("s_" + k)) for k in ['pe', 'act', 'dve', 'pool']}
        self.cnt = {k: 0 for k in self.sem}
        self.seen = {k: {} for k in self.eng}
        self.dsem = [es.enter_context(nc.semaphore("d%d" % i)) for i in range(self.NS)]
        self.dval = [0] * self.NS
        self.dnext = 0
        self.lastw = {}
        self.readers = {}
        self.fresh = {}
        self.ninst = 0

    def _wait(self, e, tok):
        sk, v = tok
        if sk == e and e == 'pe':
            return
        if self.seen[e].get(sk, 0) >= v:
            return
        sem = self.sem[sk] if isinstance(sk, str) else self.dsem[sk[1]]
        self.eng[e].wait_ge(sem, v)
        self.seen[e][sk] = v

    def _deps(self, e, reads, writes):
        for k in reads:
            t = self.lastw.get(k)
            if t is not None:
                self._wait(e, t)
            if isinstance(k, tuple) and k[0] == 'ps':
                for sk, t in self.readers.get(k, {}).items():
                    if sk != e:
                        self._wait(e, t)
        for k in writes:
            t = self.lastw.get(k)
            if t is not None:
                self._wait(e, t)
            for t in self.readers.get(k, {}).values():
                self._wait(e, t)

    def _record(self, tok, reads, writes):
        for k in reads:
            self.readers.setdefault(k, {})[tok[0]] = tok
        for k in writes:
            self.lastw[k] = tok
            self.readers[k] = {}

    def op(self, e, fn, reads=(), writes=(), inc=True):
        self._deps(e, reads, writes)
        ins = fn(self.eng[e])
        self.ninst += 1
        if inc:
            self.cnt[e] += 1
            ins.then_inc(self.sem[e], 1)
            tok = (e, self.cnt[e])
        else:
            tok = (e, self.cnt[e] + 1)
        self._record(tok, reads, writes)
        return tok

    def dma(self, out, in_, reads=(), writes=(), q='sp'):
        i = self.dnext
        self.dnext = (self.dnext + 1) % self.NS
        if self.dval[i] > 0:
            self._wait(q, (('d', i), self.dval[i]))
        self._deps(q, reads, writes)
        ins = self.eng[q].dma_start(out=out, in_=in_)
        self.ninst += 1
        self.dval[i] += 16
        ins.then_inc(self.dsem[i], 16)
        tok = (('d', i), self.dval[i])
        self._record(tok, reads, writes)
        return tok

    def barrier(self):
        for e in self.eng:
            for o in self.sem:
                if self.cnt[o] > 0:
                    self._wait(e, (o, self.cnt[o]))
            for i in range(self.NS):
                if self.dval[i] > 0:
                    self._wait(e, (('d', i), self.dval[i]))
        self.lastw = {}
        self.readers = {}

    def newgen(self, bank):
        self.fresh[(bank, 0)] = True
        self.fresh[(bank, 1)] = True

    def mm(self, bank, out, lhsT, rhs, reads, writes, halves=(0, 1), last=True, inc=None):
        st = False
        for h in halves:
            if self.fresh.get((bank, h), True):
                st = True
            self.fresh[(bank, h)] = False
        if inc is None:
            inc = last

        def fn(e):
            return e.matmul(out, lhsT=lhsT, rhs=rhs, start=st, stop=last, skip_group_check=True)
        return self.op('pe', fn, reads, writes, inc=inc)

    def tr(self, out, in_, ident, reads, writes):
        return self.op('pe', lambda e: e.transpose(out, in_, ident), reads, writes)

    def act(self, out, in_, func, reads, writes, scale=None, bias=None):
        def fn(e):
            kw = {}
            if scale is not None:
                kw['scale'] = scale
            if bias is not None:
                kw['bias'] = bias
            return e.activation(out=out, in_=in_, func=func, **kw)
        return self.op('act', fn, reads, writes)

    def tt(self, e, out, in0, in1, op, reads, writes):
        return self.op(e, lambda g: g.tensor_tensor(out=out, in0=in0, in1=in1, op=op), reads, writes)

    def ts(self, e, out, in0, s1, s2, op0, op1, reads, writes):
        if op1 is None and e == 'pool' and op0 == ALU.mult:
            op1, s2 = ALU.add, 0.0
        if op1 is None:
            return self.op(e, lambda g: g.tensor_scalar(out=out, in0=in0, scalar1=s1, scalar2=None, op0=op0), reads, writes)
        return self.op(e, lambda g: g.tensor_scalar(out=out, in0=in0, scalar1=s1, scalar2=s2, op0=op0, op1=op1), reads, writes)

    def stt(self, out, in0, scalar, in1, op0, op1, reads, writes):
        return self.op('dve', lambda g: g.scalar_tensor_tensor(out=out, in0=in0, scalar=scalar, in1=in1, op0=op0, op1=op1), reads, writes)

    def copy(self, e, out, in_, reads, writes):
        if e == 'act':
            return self.op('act', lambda g: g.copy(out=out, in_=in_), reads, writes)
        return self.op(e, lambda g: g.tensor_copy(out=out, in_=in_), reads, writes)

    def memset(self, e, ap, val, writes):
        return self.op(e, lambda g: g.memset(ap, val), (), writes)


class _Stop(Exception):
    pass


UWB = 7


def build_program(n_layers=DEPTH, dbg=False, stop=None):
    nc = bass.Bass("TRN2", target_bir_lowering=False)

    def din(name, shape, dt=F32):
        return nc.dram_tensor(name, list(shape), dt, kind="ExternalInput").ap()

    def dscr(name, shape, dt):
        return nc.dram_tensor(name, list(shape), dt, kind="Internal").ap()

    x_in = din("x", [S, D])
    cT_in = din("cT", [128, 8])
    normw_in = din("norm_w", [DEPTH, D])
    fnw_in = din("final_norm_w", [1, D])
    adaw_in = din("ada_w", [DEPTH, D, 3 * D])
    adab_in = din("ada_b", [DEPTH, 3 * D])
    a_win = din("a_w_in", [2, D, GDN_IN])
    a_convT = din("a_convT", [2, 128, 32, 4])
    a_Alog = din("a_A_log", [2, 16])
    a_dtb = din("a_dt_bias", [2, 16])
    a_nw = din("a_norm_w", [2, 128, 1])
    a_wout = din("a_w_out", [2, 2048, D])
    b_win = din("b_w_in", [2, D, FOX_IN])
    b_fb = din("b_f_bias", [2, 16, 1])
    b_qn2 = din("b_qn2", [2, 128, 1])
    b_kn2 = din("b_kn2", [2, 128, 1])
    b_wout = din("b_w_out", [2, D, D])
    consts_in = din("consts", [NCONST, 128, 128])
    out_d = nc.dram_tensor("out", [S, D], F32, kind="ExternalOutput").ap()

    xres = dscr("xres", [S, D], F32)
    qkvz = dscr("qkvz", [48, 128, S], BF16)
    qks = dscr("qks", [16, 128, S], BF16)
    zss = dscr("zss", [S, D], BF16)
    c1s = dscr("c1s", [16, S], BF16)
    dbg_d = {}
    if dbg:
        dbg_d['hT'] = nc.dram_tensor("dbg_hT", [128, 8, S], BF16, kind="ExternalOutput").ap()
        dbg_d['xres'] = nc.dram_tensor("dbg_xres", [S, D], F32, kind="ExternalOutput").ap()
        dbg_d['qkvz'] = nc.dram_tensor("dbg_qkvz", [48, 128, S], BF16, kind="ExternalOutput").ap()
        dbg_d['gates'] = nc.dram_tensor("dbg_gates", [128, NT, 6, 16], F32, kind="ExternalOutput").ap()

    es = contextlib.ExitStack()
    with es:
        kb = KB(nc, es)

        uid = [0]

        def sb(stack, name, shape, dt=F32):
            uid[0] += 1
            return stack.enter_context(nc.sbuf_tensor("%s_%d" % (name, uid[0]), list(shape), dt))

        ps = [es.enter_context(nc.psum_tensor("ps%d" % i, [128, 512], F32)) for i in range(8)]

        def PK(bank, lo=0, hi=512):
            return [('ps', bank)]

        def psbf(i):
            return ps[i][:].bitcast(BF16)

        cst = sb(es, "cst", [128, NCONST, 128])
        ident_bf = sb(es, "ident_bf", [128, 128], BF16)
        caus_bf = sb(es, "caus_bf", [128, 128], BF16)
        condT = sb(es, "condT", [128, 8])
        gate_b = sb(es, "gate_b", [128, D])
        fnw_b = sb(es, "fnw_b", [128, D])
        xt = [sb(es, "xt%d" % i, [128, D]) for i in range(2)]
        junk = sb(es, "junk", [128, D])
        sm = sb(es, "sm", [128, 8])
        ones_row = sb(es, "ones_row", [1, 128])

        def C(i):
            return cst[:, i, :]

        kb.dma(cst[:], consts_in.rearrange("n p f -> p n f"), writes=['cst'])
        kb.copy('dve', ident_bf[:], C(C_ID), ['cst'], ['ident_bf'])
        kb.copy('dve', caus_bf[:], C(C_CAUS), ['cst'], ['caus_bf'])
        kb.memset('dve', ones_row[:], 1.0, ['ones_row'])
        kb.dma(condT[:], cT_in, writes=['condT'])
        kb.act(condT[:], condT[:], AF.Silu, ['condT'], ['condT'])
        kb.dma(fnw_b[:], fnw_in.partition_broadcast(128), writes=['fnw_b'])

        def rstd_from_ss(ss_ap, out_ap, n, rkeys, wkeys, tmp_ap):
            kb.act(tmp_ap, ss_ap, AF.Ln, rkeys, [('tmpln',)], scale=1.0 / n, bias=EPS)
            kb.act(out_ap, tmp_ap, AF.Exp, [('tmpln',)], wkeys, scale=-0.5)

        def adaln(L, A_b, B_b, stack):
            with contextlib.ExitStack() as s2:
                adaw = [sb(s2, "adaw%d" % i, [128, 8, 256]) for i in range(2)]
                modrow = sb(s2, "modrow", [1, 3 * D])
                nwrow = sb(s2, "nwrow", [1, D])
                arow = sb(s2, "arow", [1, D])
                kb.dma(modrow[:], adab_in[L:L + 1, :], writes=[('modrow', i) for i in range(6)])
                kb.dma(nwrow[:], normw_in[L:L + 1, :], writes=['nwrow'])
                wv = adaw_in[L].rearrange("(kc p) f -> p kc f", p=128)
                for fb in range(12):
                    b = fb % 2
                    kb.dma(adaw[b][:], wv[:, :, fb * 256:(fb + 1) * 256], writes=[('adaw', b)])
                    bank = fb % 2
                    kb.newgen(bank)
                    for kc in range(8):
                        kb.mm(bank, ps[bank][0:1, 0:256], condT[:, kc:kc + 1], adaw[b][:, kc, :],
                              ['condT', ('adaw', b)], PK(bank), halves=(0,), last=(kc == 7))
                    kb.tt('dve', modrow[0:1, fb * 256:(fb + 1) * 256], ps[bank][0:1, 0:256], modrow[0:1, fb * 256:(fb + 1) * 256],
                          ALU.add, PK(bank) + [('modrow', fb // 2)], [('modrow', fb // 2)])
                kb.stt(arow[0:1, :], modrow[0:1, D:2 * D], 1.0, nwrow[0:1, :], ALU.add, ALU.mult,
                       [('modrow', 2), ('modrow', 3), 'nwrow'], ['arow'])
                srcs = [(arow[0:1, :], A_b, ['arow'], 'A_b'), (modrow[0:1, 0:D], B_b, [('modrow', 0), ('modrow', 1)], 'B_b'),
                        (modrow[0:1, 2 * D:3 * D], gate_b, [('modrow', 4), ('modrow', 5)], 'gate_b')]
                n = 0
                for (src, dst, rk, dname) in srcs:
                    for half in range(2):
                        bank = 2 + (n % 2)
                        n += 1
                        kb.newgen(bank)
                        kb.mm(bank, ps[bank][:, :], ones_row[0:1, :], src[0:1, half * 512:(half + 1) * 512],
                              rk + ['ones_row'], PK(bank))
                        kb.copy('act', dst[:, half * 512:(half + 1) * 512], ps[bank][:, :], PK(bank), [(dname, half)])
                kb.barrier()

        def norm_phase(L, xsrc, xkey, hT, A_b, B_b, stack):
            hn = sb(stack, "hn", [128, D])
            hb = [sb(stack, "hb%d" % i, [128, D], BF16) for i in range(2)]
            for t in range(NT):
                b = t % 2
                kb.dma(xt[b][:], xsrc[t * 128:(t + 1) * 128, :], reads=[(xkey, t)], writes=[('xt', b)])
                kb.act(junk[:], xt[b][:], AF.Square, [('xt', b)], ['junk'])
                kb.op('dve', lambda g: g.tensor_reduce(out=sm[:, 0:1], in_=junk[:], axis=AX.X, op=ALU.add), ['junk'], [('sm', 0)])
                rstd_from_ss(sm[:, 0:1], sm[:, 2:3], D, [('sm', 0)], [('sm', 2)], sm[:, 1:2])
                kb.stt(hn[:], xt[b][:], sm[:, 2:3], A_b[:], ALU.mult, ALU.mult, [('xt', b), ('sm', 2), ('A_b', 0), ('A_b', 1)], ['hn'])
                kb.tt('pool', hb[b][:], hn[:], B_b[:], ALU.add, ['hn', ('B_b', 0), ('B_b', 1)], [('hb', b)])
                bank = 4 + b
                for kc in range(8):
                    kb.tr(psbf(bank)[:, kc * 128:(kc + 1) * 128], hb[b][:, kc * 128:(kc + 1) * 128], ident_bf[:],
                          [('hb', b), 'ident_bf'], PK(bank))
                kb.copy('act', hT[:, :, t * 128:(t + 1) * 128], psbf(bank).rearrange("p (k t) -> p k t", k=8),
                        PK(bank), [('hT', t // 4)])

        def outproj_tile(L, t, ogT, KC, wout_bf, xsrc, xkey, last_layer, ykeys):
            b = t % 2
            kb.dma(xt[b][:], xsrc[t * 128:(t + 1) * 128, :], reads=[(xkey, t)], writes=[('xt', b)])
            for fb in range(2):
                bank = 6 + fb
                kb.newgen(bank)
                for kc in range(KC):
                    kb.mm(bank, ps[bank][:, :], ogT[:, kc, :], wout_bf[:, kc, fb * 512:(fb + 1) * 512],
                          ykeys + ['wout_bf'], PK(bank), last=(kc == KC - 1))
                kb.tt('dve', junk[:, fb * 512:(fb + 1) * 512], ps[bank][:, :], gate_b[:, fb * 512:(fb + 1) * 512], ALU.mult,
                      PK(bank) + [('gate_b', fb)], [('junkh', fb)])
                kb.tt('pool', xt[b][:, fb * 512:(fb + 1) * 512], junk[:, fb * 512:(fb + 1) * 512], xt[b][:, fb * 512:(fb + 1) * 512],
                      ALU.add, [('junkh', fb), ('xt', b)], [('xt', b)])
            if not last_layer:
                kb.dma(xres[t * 128:(t + 1) * 128, :], xt[b][:], reads=[('xt', b)], writes=[('xres', t)])
            else:
                kb.act(junk[:], xt[b][:], AF.Square, [('xt', b)], [('junkh', 0), ('junkh', 1)])
                kb.op('dve', lambda g: g.tensor_reduce(out=sm[:, 4:5], in_=junk[:], axis=AX.X, op=ALU.add),
                      [('junkh', 0), ('junkh', 1)], [('sm', 4)])
                rstd_from_ss(sm[:, 4:5], sm[:, 6:7], D, [('sm', 4)], [('sm', 6)], sm[:, 5:6])
                kb.stt(xt[b][:], xt[b][:], sm[:, 6:7], fnw_b[:], ALU.mult, ALU.mult, [('xt', b), ('sm', 6), 'fnw_b'], [('xt', b)])
                kb.dma(out_d[t * 128:(t + 1) * 128, :], xt[b][:], reads=[('xt', b)], writes=[('out', t)])

        def load_wout(wout_dram, KC, wout_bf, stack):
            with contextlib.ExitStack() as s2:
                wst = [sb(s2, "wost%d" % i, [128, D]) for i in range(2)]
                wv = wout_dram.rearrange("(kc p) f -> p kc f", p=128)
                for g in range(KC):
                    b = g % 2
                    kb.dma(wst[b][:], wv[:, g, :], writes=[('wost', b)])
                    kb.copy('pool', wout_bf[:, g, :], wst[b][:], [('wost', b)], ['wout_bf'])
                kb.barrier()

        def proj_fm(w2d, col0, nch, modes, hT, scratch, stack, convw=None, pp_scalars=None, nred=128, ones_ap=None):
            with contextlib.ExitStack() as s2:
                wst = [sb(s2, "wst%d" % i, [128, 8, 256]) for i in range(2)]
                wbf = [sb(s2, "wbf%d" % i, [128, 8, 256], BF16) for i in range(2)]
                obuf = [sb(s2, "obuf%d" % i, [128, 512], BF16) for i in range(3)]
                pre = [sb(s2, "pre%d" % i, [128, 515]) for i in range(3)] if any(m.startswith('conv') for m in modes) else None
                acc = [sb(s2, "acc%d" % i, [128, 512]) for i in range(3)]
                sq2 = [sb(s2, "sq2%d" % i, [128, 512]) for i in range(3)]
                lnv = sb(s2, "lnv", [128, 512])
                rn = [sb(s2, "rn%d" % i, [128, 512]) for i in range(2)]
                wv = w2d.rearrange("(kc p) f -> p kc f", p=128)
                nslab = (nch + 1) // 2
                blocks = [(c, tb) for c in range(nch) for tb in range(8)]
                NB = len(blocks)

                def load_slab(sl):
                    sbf = sl % 2
                    ncs = min(2, nch - sl * 2)
                    kb.dma(wst[sbf][:, :, 0:ncs * 128], wv[:, :, col0 + sl * 256: col0 + sl * 256 + ncs * 128], writes=[('wst', sbf)])
                    kb.copy('pool', wbf[sbf][:, :, 0:ncs * 128], wst[sbf][:, :, 0:ncs * 128], [('wst', sbf)], [('wbf', sbf)])

                def stageA(n):
                    c, tb = blocks[n]
                    sl, ci = c // 2, c % 2
                    sbf = sl % 2
                    if ci == 0 and tb == 0 and sl + 1 < nslab:
                        load_slab(sl + 1)
                    bank = n % 4
                    kb.newgen(bank)
                    for kc in range(8):
                        kb.mm(bank, ps[bank][:, :], wbf[sbf][:, kc, ci * 128:(ci + 1) * 128], hT[:, kc, tb * 512:(tb + 1) * 512],
                              [('wbf', sbf), ('hT', tb)], PK(bank), last=(kc == 7))

                def out_dma(n):
                    c, tb = blocks[n]
                    ob = n % 3
                    kb.dma(scratch[c][:, tb * 512:(tb + 1) * 512], obuf[ob][:], reads=[('obuf', ob)], writes=[('scr', c, tb)])

                def stageB1(n):
                    c, tb = blocks[n]
                    mode = modes[c]
                    bank = n % 4
                    ob = n % 3
                    osl = obuf[ob][:, :]
                    okey = [('obuf', ob)]
                    a = acc[n % 3]
                    ak = ('acc', n % 3)
                    q2 = sq2[n % 3]
                    qk = ('sq2', n % 3)
                    if mode == 'silu':
                        kb.act(osl, ps[bank][:, :], AF.Silu, PK(bank), okey)
                        out_dma(n)
                    elif mode == 'rms':
                        kb.copy('act', a[:], ps[bank][:, :], PK(bank), [ak])
                        kb.tt('pool', q2[:], a[:], a[:], ALU.mult, [ak], [qk])
                    else:
                        p = pre[n % 3]
                        pk = ('pre', n % 3)
                        pprev = pre[(n - 1) % 3]
                        pkprev = ('pre', (n - 1) % 3)
                        kb.copy('act', p[:, 3:515], ps[bank][:, :], PK(bank), [pk])
                        kb.act(a[:], ps[bank][:, :], AF.Identity, PK(bank) + ['convw'], [ak], scale=convw[:, c, 3:4])
                        if tb == 0:
                            kb.memset('pool', p[:, 0:3], 0.0, [pk])
                        else:
                            kb.copy('pool', p[:, 0:3], pprev[:, 512:515], [pkprev], [pk])
                        for jj in (2, 1, 0):
                            kb.stt(a[:], p[:, jj:jj + 512], convw[:, c, jj:jj + 1], a[:], ALU.mult, ALU.add, [pk, 'convw', ak], [ak])
                        if mode == 'conv_v':
                            kb.act(osl, a[:], AF.Silu, [ak], okey)
                            out_dma(n)
                        else:
                            kb.act(a[:], a[:], AF.Silu, [ak], [ak])
                            kb.tt('pool', q2[:], a[:], a[:], ALU.mult, [ak], [qk])

                def stageB2(n):
                    c, tb = blocks[n]
                    mode = modes[c]
                    if mode in ('silu', 'conv_v'):
                        return
                    ob = n % 3
                    osl = obuf[ob][:, :]
                    okey = [('obuf', ob)]
                    a = acc[n % 3]
                    ak = ('acc', n % 3)
                    q2 = sq2[n % 3]
                    qk = ('sq2', n % 3)
                    nb = 4 + n % 2
                    r = rn[n % 2]
                    rk = ('rn', n % 2)
                    kb.newgen(nb)
                    if mode == 'rms':
                        kb.mm(nb, ps[nb][:, :], ones_ap, q2[:], [qk, 'cst'], PK(nb))
                        rstd_from_ss(ps[nb][:, :], r[:], nred, PK(nb), [rk], lnv[:])
                        kb.stt(osl, a[:], pp_scalars[c], r[:], ALU.mult, ALU.mult, [ak, rk, 'ppsc'], okey)
                    else:
                        kb.mm(nb, ps[nb][:, :], C(C_ONES), q2[:], [qk, 'cst'], PK(nb))
                        kb.act(lnv[:], ps[nb][:, :], AF.Ln, PK(nb), [('tmpln',)], bias=EPS)
                        kb.act(r[:], lnv[:], AF.Exp, [('tmpln',)], [rk], scale=-0.5)
                        sc = (128.0 ** -0.5) if mode == 'conv_q' else 1.0
                        kb.stt(osl, a[:], sc, r[:], ALU.mult, ALU.mult, [ak, rk], okey)
                    out_dma(n)

                load_slab(0)
                for n in range(NB + 3):
                    if n < NB:
                        stageA(n)
                    if 0 <= n - 3 < NB:
                        stageB2(n - 3)
                    if 0 <= n - 1 < NB:
                        stageB1(n - 1)
                kb.barrier()

        def gdn_layer(L, j, xsrc, xkey, last_layer):
            with contextlib.ExitStack() as sl:
                G = sb(sl, "G", [128, NT, 6, 16])
                glb = sb(sl, "glb", [128, NT, 2, 16])
                with contextlib.ExitStack() as s1:
                    A_b = sb(s1, "A_b", [128, D])
                    B_b = sb(s1, "B_b", [128, D])
                    adaln(L, A_b, B_b, s1)
                    hT = sb(s1, "hT", [128, 8, S], BF16)
                    with contextlib.ExitStack() as s2:
                        norm_phase(L, xsrc, xkey, hT, A_b, B_b, s2)
                        kb.barrier()
                    if dbg and L == 0:
                        kb.dma(dbg_d['hT'], hT[:], reads=[('hT', i) for i in range(8)], writes=['dbg_hT'])
                    if stop == 'norm':
                        kb.barrier()
                        return True
                    with contextlib.ExitStack() as s2:
                        wbast = sb(s2, "wbast", [128, 8, 32])
                        wba = sb(s2, "wba", [128, 8, 32], BF16)
                        dtb = sb(s2, "dtb", [128, 16])
                        negA = sb(s2, "negA", [128, 16])
                        gt = sb(s2, "gt", [128, 8, 16])
                        kb.dma(wbast[:], a_win[j].rearrange("(kc p) f -> p kc f", p=128)[:, :, 6144:6176], writes=['wbast'])
                        kb.copy('dve', wba[:], wbast[:], ['wbast'], ['wba'])
                        kb.dma(dtb[:], a_dtb[j:j + 1, :].partition_broadcast(128), writes=['dtb'])
                        kb.dma(negA[:], a_Alog[j:j + 1, :].partition_broadcast(128), writes=['negA'])
                        kb.act(negA[:], negA[:], AF.Exp, ['negA'], ['negA'])
                        kb.ts('dve', negA[:], negA[:], -1.0, None, ALU.mult, None, ['negA'], ['negA'])
                        for t in range(NT):
                            bank = t % 2
                            kb.newgen(bank)
                            for kc in range(8):
                                kb.mm(bank, ps[bank][:, 0:32], hT[:, kc, t * 128:(t + 1) * 128], wba[:, kc, :],
                                      [('hT', t // 4), 'wba'], PK(bank), last=(kc == 7))
                            gk = ('G', t)
                            kb.tt('dve', gt[:, 0, :], ps[bank][:, 16:32], dtb[:], ALU.add, PK(bank) + ['dtb'], ['gt0'])
                            kb.act(gt[:, 0, :], gt[:, 0, :], AF.Exp, ['gt0'], ['gt0'])
                            kb.act(gt[:, 0, :], gt[:, 0, :], AF.Ln, ['gt0'], ['gt0'], bias=1.0)
                            kb.tt('dve', G[:, t, 0, :], gt[:, 0, :], negA[:], ALU.mult, ['gt0', 'negA'], [gk])
                            kb.act(gt[:, 1, :], ps[bank][:, 0:16], AF.Exp, PK(bank), ['gt1'], scale=-1.0)
                            kb.act(gt[:, 1, :], gt[:, 1, :], AF.Ln, ['gt1'], ['gt1'], bias=1.0)
                            kb.act(G[:, t, 2, :], gt[:, 1, :], AF.Exp, ['gt1'], [gk], scale=-1.0)
                            kb.ts('dve', G[:, t, 1, :], gt[:, 1, :], -1.0, None, ALU.mult, None, ['gt1'], [gk])
                            b2 = 2 + t % 2
                            kb.newgen(b2)
                            kb.mm(b2, ps[b2][:, 0:16], C(C_TRIBD), G[:, t, 0, :], ['cst', gk], PK(b2))
                            kb.copy('dve', G[:, t, 3, :], ps[b2][:, 0:16], PK(b2), [gk])
                            kb.mm(b2, ps[b2][:, 16:32], C(C_SELC), G[:, t, 3, :], ['cst', gk], PK(b2))
                            kb.mm(b2, ps[b2][:, 32:48], C(C_SELA), G[:, t, 3, :], ['cst', gk], PK(b2))
                            kb.mm(b2, ps[b2][:, 48:64], C(C_SELB), G[:, t, 3, :], ['cst', gk], PK(b2))
                            kb.tt('dve', gt[:, 2, :], ps[b2][:, 16:32], G[:, t, 3, :], ALU.subtract, PK(b2) + [gk], ['gt2'])
                            kb.act(G[:, t, 5, :], gt[:, 2, :], AF.Exp, ['gt2'], [gk])
                            kb.act(glb[:, t, :, :], ps[b2][:, 32:64].rearrange("p (c h) -> p c h", c=2), AF.Exp, PK(b2), [('glb', t)])
                            kb.act(gt[:, 3, :], G[:, t, 3, :], AF.Exp, [gk], ['gt3'])
                            kb.tt('dve', G[:, t, 4, :], gt[:, 3, :], G[:, t, 2, :], ALU.mult, ['gt3', gk], [gk])
                        kb.barrier()
                    if dbg and L == 0:
                        kb.dma(dbg_d['gates'], G[:], reads=[('G', t) for t in range(NT)], writes=['dbg_gates'])
                    if stop == 'gates':
                        kb.barrier()
                        return True
                    with contextlib.ExitStack() as s2:
                        convw = sb(s2, "convw", [128, 32, 4])
                        kb.dma(convw[:], a_convT[j], writes=['convw'])
                        modes = ['conv_q'] * 8 + ['conv_k'] * 8 + ['conv_v'] * 16 + ['silu'] * 16
                        proj_fm(a_win[j], 0, 48, modes, hT, qkvz, s2, convw=convw)
                    kb.barrier()
                if stop == 'proj':
                    return True
                with contextlib.ExitStack() as s1:
                    wout_bf = sb(s1, "wout_bf", [128, 16, D], BF16)
                    load_wout(a_wout[j], 16, wout_bf, s1)
                    rr = gdn_tiles(L, j, G, glb, wout_bf, xsrc, xkey, last_layer, s1)
                    kb.barrier()
                    return rr

        def gdn_tiles(L, j, G, glb, wout_bf, xsrc, xkey, last_layer, st):
            HG = 8
            Sf = sb(st, "Sf", [128, 16, 128])
            Sb = sb(st, "Sb", [128, 16, 128], BF16)
            nw = sb(st, "nw", [128, 1])
            qT = [sb(st, "qT%d" % i, [128, 8, 128], BF16) for i in range(2)]
            kT = [sb(st, "kT%d" % i, [128, 8, 128], BF16) for i in range(2)]
            vT = [sb(st, "vT%d" % i, [128, 16, 128], BF16) for i in range(2)]
            zs = [sb(st, "zs%d" % i, [128, 16, 128], BF16) for i in range(2)]
            Ag = [sb(st, "Ag%d" % i, [128, 128]) for i in range(2)]
            Agp = [sb(st, "Agp%d" % i, [128, 128]) for i in range(2)]
            E3 = [sb(st, "E3%d" % i, [128, 384]) for i in range(2)]
            XY = sb(st, "XY", [128, HG, 2, 128])
            Pm = sb(st, "Pm", [128, HG, 128])
            attnT = sb(st, "attnT", [128, HG, 128], BF16)
            vb = sb(st, "vb", [128, HG, 128], BF16)
            kbg = sb(st, "kbg", [128, HG, 128], BF16)
            kdec = sb(st, "kdec", [128, HG, 128], BF16)
            qdT = sb(st, "qdT", [128, HG, 128], BF16)
            Gb = [sb(st, "Gb%d" % i, [128, 128]) for i in range(2)]
            gamb = [sb(st, "gamb%d" % i, [128, 128]) for i in range(2)]
            TT = sb(st, "TT", [128, HG, 128], BF16)
            usb = sb(st, "usb", [128, HG, 128])
            wTb = sb(st, "wTb", [128, HG, 128], BF16)
            vnew = sb(st, "vnew", [128, HG, 128], BF16)
            oT = sb(st, "oT", [128, HG, 128])
            osq = sb(st, "osq", [128, 512])
            rst = sb(st, "rst", [128, 512])
            lnt = sb(st, "lnt", [128, 512])
            ogT = [sb(st, "ogT%d" % i, [128, 16, 128], BF16) for i in range(2)]
            kb.memset('dve', Sf[:], 0.0, [('Sf', h) for h in range(16)])
            kb.memset('pool', Sb[:], 0.0, [('Sb', h) for h in range(16)])
            kb.dma(nw[:], a_nw[j], writes=['nw'])
            qv = qkvz[0:8].rearrange("c p t -> p c t")
            kv = qkvz[8:16].rearrange("c p t -> p c t")
            vv = qkvz[16:32].rearrange("c p t -> p c t")
            zv = qkvz[32:48].rearrange("c p t -> p c t")

            def load_tile(t):
                b = t % 2
                tsl = slice(t * 128, (t + 1) * 128)
                kb.dma(qT[b][:], qv[:, :, tsl], writes=[('qT', b)])
                kb.dma(kT[b][:], kv[:, :, tsl], writes=[('kT', b)])
                kb.dma(vT[b][:], vv[:, :, tsl], writes=[('vT', b)])
                kb.dma(zs[b][:], zv[:, :, tsl], writes=[('zs', b)])

            load_tile(0)
            for t in range(NT):
                b = t % 2
                if t + 1 < NT:
                    load_tile(t + 1)
                gk = ('G', t)
                for hg in range(2):
                    for hl in range(HG):
                        h = hg * HG + hl
                        hp = h // 2
                        if hl % 2 == 0:
                            gb = 4
                            off = (hp % 2) * 256
                            kb.newgen(gb) if hp % 2 == 0 else None
                            kb.mm(gb, ps[gb][:, off:off + 128], kT[b][:, hp, :], kT[b][:, hp, :], [('kT', b)], PK(4, off, off + 256))
                            kb.mm(gb, ps[gb][:, off + 128:off + 256], kT[b][:, hp, :], qT[b][:, hp, :], [('kT', b), ('qT', b)], PK(4, off, off + 256))
                            ks = hp % 4
                            kb.tr(psbf(6)[:, ks * 128:(ks + 1) * 128], kT[b][:, hp, :], ident_bf[:], [('kT', b), 'ident_bf'], PK(6, 0, 256))
                        Gps = ps[4][:, off:off + 128]
                        QKps = ps[4][:, off + 128:off + 256]
                        gkey = PK(4, off, off + 256)
                        ks = hp % 4
                        psK = psbf(6)[:, ks * 128:(ks + 1) * 128]
                        a = Ag[h % 2]
                        ap_ = Agp[h % 2]
                        kb.ts('pool', a[:], C(C_UT), G[:, t, 0, h:h + 1], None, ALU.mult, None, ['cst', gk], [('Ag', h % 2)])
                        kb.stt(ap_[:], C(C_ID), G[:, t, 1, h:h + 1], a[:], ALU.mult, ALU.add, ['cst', gk, ('Ag', h % 2)], [('Agp', h % 2)])
                        db = 5
                        kb.newgen(db)
                        dk = PK(5, 0, 384)
                        kb.mm(db, ps[db][:, 0:128], C(C_SL), a[:], ['cst', ('Ag', h % 2)], dk, last=False, inc=False)
                        kb.mm(db, ps[db][:, 0:128], C(C_ID), C(C_MINCLT), ['cst'], dk)
                        kb.mm(db, ps[db][:, 128:256], C(C_SL), ap_[:], ['cst', ('Agp', h % 2)], dk, last=False, inc=False)
                        kb.mm(db, ps[db][:, 128:256], C(C_ID), C(C_MSTRT), ['cst'], dk)
                        kb.mm(db, ps[db][:, 256:384], ap_[:], C(C_SL), ['cst', ('Agp', h % 2)], dk, last=False, inc=False)
                        kb.mm(db, ps[db][:, 256:384], C(C_ID), C(C_MSTR), ['cst'], dk)
                        g_ = Gb[h % 2]
                        kb.ts('pool', g_[:], C(C_ONES), G[:, t, 3, h:h + 1], None, ALU.mult, None, ['cst', gk], [('Gb', h % 2)])
                        kb.mm(db, ps[db][:, 384:512], g_[:], C(C_ID), ['cst', ('Gb', h % 2)], PK(5, 384, 512))
                        e3 = E3[h % 2]
                        kb.act(e3[:], ps[db][:, 0:384], AF.Exp, dk, [('E3', h % 2)])
                        gm = gamb[h % 2]
                        kb.act(gm[:], ps[db][:, 384:512], AF.Exp, PK(5, 384, 512), [('gamb', h % 2)])
                        kb.tt('dve', XY[:, hl, 0, :], e3[:, 128:256], Gps, ALU.mult, [('E3', h % 2)] + gkey, [('XY', hl)])
                        kb.tt('dve', XY[:, hl, 1, :], e3[:, 256:384], Gps, ALU.mult, [('E3', h % 2)] + gkey, [('XY', hl)])
                        kb.tt('dve', attnT[:, hl, :], e3[:, 0:128], QKps, ALU.mult, [('E3', h % 2)] + gkey, [('attnT', hl)])
                        kb.stt(Pm[:, hl, :], XY[:, hl, 0, :], -1.0, C(C_ID), ALU.mult, ALU.add, [('XY', hl), 'cst'], [('Pm', hl)])
                        kb.tt('pool', qdT[:, hl, :], qT[b][:, hp, :], gm[:], ALU.mult, [('qT', b), ('gamb', h % 2)], [('qdT', hl)])
                        vs = 4 + h % 4
                        psV = psbf(6)[:, vs * 128:(vs + 1) * 128]
                        kb.tr(psV, vT[b][:, h, :], ident_bf[:], [('vT', b), 'ident_bf'], PK(6, 256, 512))
                        kb.act(vb[:, hl, :], psV, AF.Identity, PK(6, 256, 512) + [gk], [('vb', hl)], scale=G[:, t, 2, h:h + 1])
                        kb.act(kbg[:, hl, :], psK, AF.Identity, PK(6, 0, 256) + [gk], [('kbg', hl)], scale=G[:, t, 4, h:h + 1])
                        kb.ts('dve', kdec[:, hl, :], psK, G[:, t, 5, h:h + 1], None, ALU.mult, None, PK(6, 0, 256) + [gk], [('kdec', hl)])
                    if stop == 't1':
                        return True
                    for lvl in range(1, 6):
                        for pr in range(HG // 2):
                            bank = pr % 2
                            kb.newgen(bank)
                            for u_ in range(2):
                                hl = pr * 2 + u_
                                X = XY[:, hl, 0, :]
                                Y = XY[:, hl, 1, :]
                                o0 = u_ * 256
                                if lvl < 5:
                                    kb.mm(bank, ps[bank][:, o0:o0 + 128], Y, X, [('XY', hl)], PK(bank, o0, o0 + 256), inc=False)
                                kb.mm(bank, ps[bank][:, o0 + 128:o0 + 256], X, Y, [('XY', hl)], PK(bank, o0, o0 + 256))
                            if lvl < 5:
                                kb.copy('act', XY[:, pr * 2:pr * 2 + 2, :, :], ps[bank][:, :].rearrange("p (h x c) -> p h x c", h=2, x=2),
                                        PK(bank), [('XY', pr * 2), ('XY', pr * 2 + 1)])
                            else:
                                kb.copy('act', XY[:, pr * 2:pr * 2 + 2, 1, :], ps[bank][:, :].rearrange("p (h x c) -> p h x c", h=2, x=2)[:, :, 1, :],
                                        PK(bank), [('XY', pr * 2), ('XY', pr * 2 + 1)])
                        for q4 in range(HG // 4):
                            bank = 2 + q4 % 2
                            kb.newgen(bank)
                            for u_ in range(4):
                                hl = q4 * 4 + u_
                                kb.mm(bank, ps[bank][:, u_ * 128:(u_ + 1) * 128], XY[:, hl, 1, :], Pm[:, hl, :], [('XY', hl), ('Pm', hl)], PK(bank, u_ * 128, (u_ + 1) * 128))
                            hs = slice(q4 * 4, q4 * 4 + 4)
                            pk = [('Pm', q4 * 4 + u_) for u_ in range(4)]
                            if lvl < 5:
                                kb.tt('dve', Pm[:, hs, :], Pm[:, hs, :], ps[bank][:, :].rearrange("p (h c) -> p h c", h=4), ALU.add,
                                      PK(bank) + pk, pk)
                            else:
                                kb.tt('dve', TT[:, hs, :], Pm[:, hs, :], ps[bank][:, :].rearrange("p (h c) -> p h c", h=4), ALU.add,
                                      PK(bank) + pk, [('TT', q4 * 4 + u_) for u_ in range(4)])
                    if stop == 't2':
                        return True
                    for pr in range(HG // 2):
                        bank = UWB
                        kb.newgen(bank)
                        for u_ in range(2):
                            hl = pr * 2 + u_
                            if True:
                                kb.mm(bank, ps[bank][:, u_ * 128:(u_ + 1) * 128], TT[:, hl, :], vb[:, hl, :], [('TT', hl), ('vb', hl)], PK(UWB, 0, 256))
                            if True:
                                kb.mm(bank, ps[bank][:, 256 + u_ * 128:256 + (u_ + 1) * 128], kbg[:, hl, :], TT[:, hl, :], [('TT', hl), ('kbg', hl)], PK(UWB, 256, 512))
                        if True:
                            kb.copy('act', usb[:, pr * 2:pr * 2 + 2, :], ps[bank][:, 0:256].rearrange("p (h c) -> p h c", h=2), PK(UWB, 0, 256),
                                    [('usb', pr * 2), ('usb', pr * 2 + 1)])
                            kb.copy('dve', wTb[:, pr * 2:pr * 2 + 2, :], ps[bank][:, 256:512].rearrange("p (h c) -> p h c", h=2), PK(UWB, 256, 512),
                                    [('wTb', pr * 2), ('wTb', pr * 2 + 1)])
                    if stop == 't3':
                        return True
                    for ch in range(2):
                        rs = slice(ch * 64, (ch + 1) * 64)
                        for q4 in range(HG // 4):
                            wb = q4 % 2
                            kb.newgen(wb)
                            for u_ in range(4):
                                hl = q4 * 4 + u_
                                h = hg * HG + hl
                                kb.mm(wb, ps[wb][rs, u_ * 128:(u_ + 1) * 128], wTb[:, hl, rs], Sb[:, h, :], [('wTb', hl), ('Sb', h)], PK(wb), halves=(ch,))
                            hs = slice(q4 * 4, q4 * 4 + 4)
                            vk = [('vnew', q4 * 4 + u_) for u_ in range(4)]
                            kb.tt('dve', vnew[rs, hs, :], usb[rs, hs, :], ps[wb][rs, :].rearrange("p (h c) -> p h c", h=4), ALU.subtract,
                                  PK(wb) + [('usb', q4 * 4 + u_) for u_ in range(4)], vk)
                            ob_ = 2 + q4 % 2
                            sbk = 4 + q4 % 2
                            kb.newgen(ob_)
                            kb.newgen(sbk)
                            for u_ in range(4):
                                hl = q4 * 4 + u_
                                h = hg * HG + hl
                                kb.mm(ob_, ps[ob_][:, u_ * 64:(u_ + 1) * 64], Sb[:, h, :], qdT[:, hl, rs], [('Sb', h), ('qdT', hl)], PK(ob_, 0, 256), last=False, inc=False)
                                kb.mm(ob_, ps[ob_][:, u_ * 64:(u_ + 1) * 64], vnew[rs, hl, :], attnT[rs, hl, rs], [('vnew', hl), ('attnT', hl)], PK(ob_, 0, 256))
                                kb.mm(sbk, ps[sbk][:, u_ * 128:(u_ + 1) * 128], kdec[rs, hl, :], vnew[rs, hl, :], [('kdec', hl), ('vnew', hl)], PK(sbk, u_ * 128, (u_ + 1) * 128))
                            kb.copy('act', oT[:, hs, rs], ps[ob_][:, 0:256].rearrange("p (h c) -> p h c", h=4), PK(ob_, 0, 256),
                                    [('oT', q4 * 4 + u_) for u_ in range(4)])
                            for u_ in range(4):
                                hl = q4 * 4 + u_
                                h = hg * HG + hl
                                kb.stt(Sf[:, h, :], Sf[:, h, :], glb[:, t, ch, h:h + 1], ps[sbk][:, u_ * 128:(u_ + 1) * 128], ALU.mult, ALU.add,
                                       [('Sf', h), ('glb', t)] + PK(sbk, u_ * 128, (u_ + 1) * 128), [('Sf', h)])
                            h0 = hg * HG + q4 * 4
                            kb.copy('act', Sb[:, h0:h0 + 4, :], Sf[:, h0:h0 + 4, :], [('Sf', h0 + u_) for u_ in range(4)], [('Sb', h0 + u_) for u_ in range(4)])
                    if stop == 't4':
                        return True
                    for q4 in range(HG // 4):
                        hs = slice(q4 * 4, q4 * 4 + 4)
                        h0 = hg * HG + q4 * 4
                        ok = [('oT', q4 * 4 + u_) for u_ in range(4)]
                        kb.act(osq[:].rearrange("p (h c) -> p h c", h=4), oT[:, hs, :], AF.Square, ok, ['osq'])
                        nb = 6
                        kb.newgen(nb)
                        kb.mm(nb, ps[nb][:, :], C(C_ONES), osq[:], ['cst', 'osq'], PK(6))
                        rstd_from_ss(ps[nb][:, :], rst[:], 128, PK(6), ['rst'], lnt[:])
                        kb.tt('dve', osq[:].rearrange("p (h c) -> p h c", h=4), oT[:, hs, :], rst[:].rearrange("p (h c) -> p h c", h=4), ALU.mult,
                              ok + ['rst', 'osq'], ['osq'])
                        kb.stt(ogT[b][:, h0:h0 + 4, :], osq[:].rearrange("p (h c) -> p h c", h=4), nw[:, 0:1], zs[b][:, h0:h0 + 4, :], ALU.mult, ALU.mult,
                               ['osq', 'nw', ('zs', b)], [('ogT', b)])
                if stop == 't5':
                    return True
                outproj_tile(L, t, ogT[b], 16, wout_bf, xsrc, xkey, last_layer, [('ogT', b)])
                if stop == 't6':
                    return True

        def fox_layer(L, j, xsrc, xkey, last_layer):
            with contextlib.ExitStack() as sl:
                Vall = sb(sl, "Vall", [128, NT, 16, 65], BF16)
                cumT = sb(sl, "cumT", [128, NT, 16])
                with contextlib.ExitStack() as s1:
                    hT = sb(s1, "hT", [128, 8, S], BF16)
                    with contextlib.ExitStack() as s2:
                        A_b = sb(s2, "A_b", [128, D])
                        B_b = sb(s2, "B_b", [128, D])
                        adaln(L, A_b, B_b, s2)
                        norm_phase(L, xsrc, xkey, hT, A_b, B_b, s2)
                        kb.barrier()
                    with contextlib.ExitStack() as s2:
                        wfst = sb(s2, "wfst", [128, 8, 16])
                        wf = sb(s2, "wf", [128, 8, 16], BF16)
                        nfb = sb(s2, "nfb", [16, 1])
                        spl = sb(s2, "spl", [16, 2048])
                        cums = sb(s2, "cums", [16, S])
                        onesr = sb(s2, "onesr", [16, 2048], BF16)
                        c1b = sb(s2, "c1b", [16, S], BF16)
                        kb.dma(wfst[:], b_win[j].rearrange("(kc p) f -> p kc f", p=128)[:, :, 4096:4112], writes=['wfst'])
                        kb.copy('dve', wf[:], wfst[:], ['wfst'], ['wf'])
                        kb.dma(nfb[:], b_fb[j], writes=['nfb'])
                        kb.ts('dve', nfb[:], nfb[:], -1.0, None, ALU.mult, None, ['nfb'], ['nfb'])
                        kb.memset('pool', onesr[:], 1.0, ['onesr'])
                        for half in range(2):
                            for tl in range(4):
                                tb = half * 4 + tl
                                bank = tb % 2
                                kb.newgen(bank)
                                for kc in range(8):
                                    kb.mm(bank, ps[bank][0:16, :], wf[:, kc, :], hT[:, kc, tb * 512:(tb + 1) * 512], ['wf', ('hT', tb)], PK(bank),
                                          halves=(0,), last=(kc == 7))
                                kb.act(spl[:, tl * 512:(tl + 1) * 512], ps[bank][0:16, :], AF.Exp, PK(bank) + ['nfb'], [('spl', tl)], scale=-1.0, bias=nfb[:, 0:1])
                                kb.act(spl[:, tl * 512:(tl + 1) * 512], spl[:, tl * 512:(tl + 1) * 512], AF.Ln, [('spl', tl)], [('spl', tl)], bias=1.0)
                            init = 0.0 if half == 0 else cums[:, 2047:2048]
                            kb.op('dve', lambda g: g.tensor_tensor_scan(out=cums[:, half * 2048:(half + 1) * 2048], data0=onesr[:], data1=spl[:], initial=init,
                                                                      op0=ALU.mult, op1=ALU.add),
                                  [('spl', tl) for tl in range(4)] + ['onesr', 'cums'], ['cums'])
                        kb.ts('dve', c1b[:], cums[:], -1.0, None, ALU.mult, None, ['cums'], ['c1b'])
                        kb.dma(c1s, c1b[:], reads=['c1b'], writes=['c1s'])
                        for t in range(NT):
                            bank = 2 + t % 2
                            kb.newgen(bank)
                            kb.mm(bank, ps[bank][:, 0:16], cums[:, t * 128:(t + 1) * 128], cst[0:16, C_ID, 0:16], ['cums', 'cst'], PK(bank))
                            kb.copy('dve', cumT[:, t, :], ps[bank][:, 0:16], PK(bank), [('cumT', t)])
                        kb.barrier()
                    if stop == 'f1':
                        return True
                    with contextlib.ExitStack() as s2:
                        qn = sb(s2, "qn", [128, 1])
                        kn = sb(s2, "kn", [128, 1])
                        kb.dma(qn[:], b_qn2[j], writes=['ppsc'])
                        kb.dma(kn[:], b_kn2[j], writes=['ppsc'])
                        kb.ts('dve', qn[:], qn[:], 0.125, None, ALU.mult, None, ['ppsc'], ['ppsc'])
                        proj_fm(b_win[j], 0, 16, ['rms'] * 16, hT, qks, s2, pp_scalars=[qn[:, 0:1]] * 8 + [kn[:, 0:1]] * 8, nred=64, ones_ap=C(C_ONESBD))
                    if stop == 'f2':
                        return True
                    with contextlib.ExitStack() as s2:
                        wst = [sb(s2, "wvst%d" % i, [128, 8, 128]) for i in range(2)]
                        wvz = sb(s2, "wvz", [128, 8, 2048], BF16)
                        zt = [sb(s2, "zt%d" % i, [128, D], BF16) for i in range(2)]
                        wv = b_win[j].rearrange("(kc p) f -> p kc f", p=128)
                        for g in range(16):
                            b = g % 2
                            kb.dma(wst[b][:], wv[:, :, 2048 + g * 128:2048 + (g + 1) * 128], writes=[('wvst', b)])
                            kb.copy('pool', wvz[:, :, g * 128:(g + 1) * 128], wst[b][:], [('wvst', b)], [('wvz', g // 4)])
                        kb.memset('dve', Vall[:, :, :, 64:65], 1.0, [('Vall1',)])
                        for t in range(NT):
                            for fb in range(4):
                                bank = (t * 4 + fb) % 4
                                kb.newgen(bank)
                                for kc in range(8):
                                    kb.mm(bank, ps[bank][:, :], hT[:, kc, t * 128:(t + 1) * 128], wvz[:, kc, fb * 512:(fb + 1) * 512],
                                          [('hT', t // 4), ('wvz', fb)], PK(bank), last=(kc == 7))
                                if fb < 2:
                                    kb.copy('dve', Vall[:, t, fb * 8:(fb + 1) * 8, 0:64], ps[bank][:, :].rearrange("p (h d) -> p h d", h=8), PK(bank), [('Vall', t)])
                                else:
                                    kb.act(zt[t % 2][:, (fb - 2) * 512:(fb - 1) * 512], ps[bank][:, :], AF.Silu, PK(bank), [('zt', t % 2)])
                            kb.dma(zss[t * 128:(t + 1) * 128, :], zt[t % 2][:], reads=[('zt', t % 2)], writes=[('zss', t)])
                        kb.barrier()
                if stop == 'f3':
                    return True
                with contextlib.ExitStack() as s1:
                    Oall = sb(s1, "Oall", [128, NT, D], BF16)
                    with contextlib.ExitStack() as s2:
                        QA = [sb(s2, "QA%d" % i, [65, S], BF16) for i in range(2)]
                        KA = [sb(s2, "KA%d" % i, [65, S], BF16) for i in range(2)]
                        PT = [sb(s2, "PT%d" % i, [128, 512], BF16) for i in range(4)]
                        rl = sb(s2, "rl", [128, 4])
                        for i in range(2):
                            kb.memset('dve', KA[i][64:65, :], 1.0, [('KA1', i)])

                        def load_head(h):
                            b = h % 2
                            r0 = (h % 2) * 64
                            kb.dma(QA[b][0:64, :], qks[h // 2][r0:r0 + 64, :], writes=[('QA', b)])
                            kb.dma(QA[b][64:65, :], c1s[h:h + 1, :], reads=['c1s'], writes=[('QA', b)])
                            kb.dma(KA[b][0:64, :], qks[8 + h // 2][r0:r0 + 64, :], writes=[('KA', b)])

                        load_head(0)
                        pairs = [(h, qb, kt) for h in range(16) for qb in range(8) for kt in range(4 * (qb + 1))]
                        NSB = 4
                        LA = 2

                        def stageA(n):
                            h, qb, kt = pairs[n]
                            b = h % 2
                            if qb == 0 and kt == 0 and h + 1 < 16:
                                load_head(h + 1)
                            i0 = max(0, kt - 4 * qb)
                            sbk = n % NSB
                            kb.newgen(sbk)
                            kb.mm(sbk, ps[sbk][:, i0 * 128:512], KA[b][0:65, kt * 128:(kt + 1) * 128], QA[b][0:65, qb * 512 + i0 * 128:(qb + 1) * 512],
                                  [('KA', b), ('KA1', b), ('QA', b)], PK(sbk))

                        def stageB(n):
                            h, qb, kt = pairs[n]
                            nkt = 4 * (qb + 1)
                            jd = kt - 4 * qb
                            i0 = max(0, jd)
                            sbk = n % NSB
                            pt = PT[n % 4]
                            ptk = ('PT', n % 4)
                            obk = 4 + (h * 8 + qb) % 2
                            if kt == 0:
                                kb.newgen(obk)
                            kb.act(pt[:, i0 * 128:512], ps[sbk][:, i0 * 128:512], AF.Exp, PK(sbk) + [('cumT', kt)], [ptk], bias=cumT[:, kt, h:h + 1])
                            if jd >= 0:
                                kb.tt('pool', pt[:, jd * 128:(jd + 1) * 128], pt[:, jd * 128:(jd + 1) * 128], caus_bf[:], ALU.mult, [ptk, 'caus_bf'], [ptk])
                            for i in range(i0, 4):
                                kb.mm(obk, ps[obk][:, i * 65:(i + 1) * 65], pt[:, i * 128:(i + 1) * 128], Vall[:, kt, h, :],
                                      [ptk, ('Vall', kt), ('Vall1',)], PK(obk), last=(kt == nkt - 1), inc=(i == 3))
                            if kt == nkt - 1:
                                ov = ps[obk][:, 0:260].rearrange("p (i d) -> p i d", i=4)
                                kb.op('dve', lambda g: g.reciprocal(out=rl[:], in_=ov[:, :, 64]), PK(obk), ['rl'])
                                kb.tt('dve', Oall[:, qb * 4:(qb + 1) * 4, h * 64:(h + 1) * 64], ov[:, :, 0:64], rl[:].unsqueeze(2).broadcast_to([128, 4, 64]),
                                      ALU.mult, PK(obk) + ['rl'], [('Oall', qb)])

                        for n in range(len(pairs) + LA):
                            if n < len(pairs):
                                stageA(n)
                            if n - LA >= 0:
                                stageB(n - LA)
                        kb.barrier()
                    if stop == 'f4':
                        return True
                    with contextlib.ExitStack() as s2:
                        wout_bf = sb(s2, "wout_bf", [128, 8, D], BF16)
                        load_wout(b_wout[j], 8, wout_bf, s2)
                        zt = [sb(s2, "zt%d" % i, [128, D], BF16) for i in range(2)]
                        og = [sb(s2, "og%d" % i, [128, D], BF16) for i in range(2)]
                        ogT = [sb(s2, "ogT%d" % i, [128, 8, 128], BF16) for i in range(2)]
                        for t in range(NT):
                            b = t % 2
                            kb.dma(zt[b][:], zss[t * 128:(t + 1) * 128, :], reads=[('zss', t)], writes=[('zt', b)])
                            kb.tt('pool', og[b][:], Oall[:, t, :], zt[b][:], ALU.mult, [('Oall', t // 4), ('zt', b)], [('og', b)])
                            bank = 4 + b
                            for kc in range(8):
                                kb.tr(psbf(bank)[:, kc * 128:(kc + 1) * 128], og[b][:, kc * 128:(kc + 1) * 128], ident_bf[:], [('og', b), 'ident_bf'], PK(bank))
                            kb.copy('act', ogT[b][:], psbf(bank).rearrange("p (k t) -> p k t", k=8), PK(bank), [('ogT', b)])
                            outproj_tile(L, t, ogT[b], 8, wout_bf, xsrc, xkey, last_layer, [('ogT', b)])
                        kb.barrier()

        xsrc, xkey = x_in, 'xin'
        for L in range(n_layers):
            last = (L == n_layers - 1)
            if L % 2 == 0:
                stopped = gdn_layer(L, L // 2, xsrc, xkey, last)
            else:
                stopped = fox_layer(L, L // 2, xsrc, xkey, last)
            xsrc, xkey = xres, 'xres'
            kb.barrier()
            if stopped:
                break
        if dbg:
            kb.dma(dbg_d['xres'], xres, reads=[('xres', t) for t in range(NT)], writes=['dbg_x'])
            kb.dma(dbg_d['qkvz'], qkvz, writes=['dbg_q'])
        kb.barrier()
        print("instructions emitted:", kb.ninst, {k: v for k, v in kb.cnt.items()})
    return nc


def make_in_maps(inputs):
    consts = make_consts()
    f = lambda a: np.ascontiguousarray(np.asarray(a, dtype=np.float32))
    x = f(inputs["x"])
    c = f(inputs["c"])
    shared = {
        "norm_w": f(inputs["norm_w"]),
        "final_norm_w": f(inputs["final_norm_w"]).reshape(1, D),
        "ada_w": f(inputs["ada_w"]),
        "ada_b": f(inputs["ada_b"]),
        "a_w_in": f(inputs["a_w_in"]),
        "a_convT": f(np.transpose(f(inputs["a_conv_w"]), (0, 2, 1)).reshape(2, 32, 128, 4).transpose(0, 2, 1, 3)),
        "a_A_log": f(inputs["a_A_log"]),
        "a_dt_bias": f(inputs["a_dt_bias"]),
        "a_norm_w": f(inputs["a_norm_w"]).reshape(2, 128, 1),
        "a_w_out": f(inputs["a_w_out"]),
        "b_w_in": f(inputs["b_w_in"]),
        "b_f_bias": f(inputs["b_f_bias"]).reshape(2, 16, 1),
        "b_qn2": f(np.tile(f(inputs["b_qn_w"]), (1, 2))).reshape(2, 128, 1),
        "b_kn2": f(np.tile(f(inputs["b_kn_w"]), (1, 2))).reshape(2, 128, 1),
        "b_w_out": f(inputs["b_w_out"]),
        "consts": consts,
    }
    maps = []
    for b in range(8):
        m = dict(shared)
        m["x"] = x[b]
        m["cT"] = f(c[b].reshape(8, 128).T)
        maps.append(m)
    return maps


_NC_CACHE = {}


def kernel(**inputs):
    if 'nc' not in _NC_CACHE:
        _NC_CACHE['nc'] = build_program()
    nc = _NC_CACHE['nc']
    in_maps = make_in_maps(inputs)
    res = run_bass_kernel_spmd(nc, in_maps, core_ids=list(range(8)))
    out = np.stack([np.asarray(r["out"], dtype=np.float32) for r in res.results], axis=0)
    return out
```

```python
import contextlib
import numpy as np
import concourse.bass as bass
import concourse.mybir as mybir
from concourse.bass_utils import run_bass_kernel_spmd

F32 = mybir.dt.float32
BF16 = mybir.dt.bfloat16
AF = mybir.ActivationFunctionType
ALU = mybir.AluOpType
AX = mybir.AxisListType

S = 4096
D = 1024
NT = 32
EPS = 1e-6
NEG = -30000.0
DEPTH = 4
GDN_IN = 6176
FOX_IN = 4112

C_ID, C_ONES, C_UT, C_SL, C_MINCLT, C_MSTRT, C_MSTR, C_TRIBD, C_SELC, C_SELA, C_SELB, C_CAUS, C_ONESBD = range(13)
NCONST = 13


def make_consts():
    i = np.arange(128)
    r = i[:, None]
    c = i[None, :]
    same = (r // 64) == (c // 64)
    m = np.zeros((NCONST, 128, 128), np.float32)
    m[C_ID] = (r == c)
    m[C_ONES] = 1.0
    m[C_UT] = (r <= c)
    m[C_SL] = (r > c)
    m[C_MINCLT] = np.where(same & (r <= c), 0.0, NEG)
    m[C_MSTRT] = np.where(same & (r < c), 0.0, NEG)
    m[C_MSTR] = np.where(same & (r > c), 0.0, NEG)
    m[C_TRIBD] = (same & (r <= c))
    m[C_SELC] = (r == (c // 64) * 64 + 63)
    m[C_SELA] = (r == 63) * np.ones((1, 128))
    m[C_SELB] = (r == 127) * np.ones((1, 128))
    m[C_CAUS] = (r <= c)
    m[C_ONESBD] = same
    return m.astype(np.float32)


class KB:
    NS = 24

    def __init__(self, nc, es):
        self.nc = nc
        self.eng = {'pe': nc.tensor, 'act': nc.scalar, 'dve': nc.vector, 'pool': nc.gpsimd, 'sp': nc.sync}
        self.sem = {k: es.enter_context(nc.semaphore("s_" + k)) for k in ['pe', 'act', 'dve', 'pool']}
        self.cnt = {k: 0 for k in self.sem}
        self.seen = {k: {} for k in self.eng}
        self.dsem = [es.enter_context(nc.semaphore("d%d" % i)) for i in range(self.NS)]
        self.dval = [0] * self.NS
        self.dnext = 0
        self.lastw = {}
        self.readers = {}
        self.fresh = {}
        self.ninst = 0

    def _wait(self, e, tok):
        sk, v = tok
        if sk == e and e == 'pe':
            return
        if self.seen[e].get(sk, 0) >= v:
            return
        sem = self.sem[sk] if isinstance(sk, str) else self.dsem[sk[1]]
        self.eng[e].wait_ge(sem, v)
        self.seen[e][sk] = v

    def _deps(self, e, reads, writes):
        for k in reads:
            t = self.lastw.get(k)
            if t is not None:
                self._wait(e, t)
            if isinstance(k, tuple) and k[0] == 'ps':
                for sk, t in self.readers.get(k, {}).items():
                    if sk != e:
                        self._wait(e, t)
        for k in writes:
            t = self.lastw.get(k)
            if t is not None:
                self._wait(e, t)
            for t in self.readers.get(k, {}).values():
                self._wait(e, t)

    def _record(self, tok, reads, writes):
        for k in reads:
            self.readers.setdefault(k, {})[tok[0]] = tok
        for k in writes:
            self.lastw[k] = tok
            self.readers[k] = {}

    def op(self, e, fn, reads=(), writes=(), inc=True):
        self._deps(e, reads, writes)
        ins = fn(self.eng[e])
        self.ninst += 1
        if inc:
            self.cnt[e] += 1
            ins.then_inc(self.sem[e], 1)
            tok = (e, self.cnt[e])
        else:
            tok = (e, self.cnt[e] + 1)
        self._record(tok, reads, writes)
        return tok

    def dma(self, out, in_, reads=(), writes=(), q='sp'):
        i = self.dnext
        self.dnext = (self.dnext + 1) % self.NS
        if self.dval[i] > 0:
            self._wait(q, (('d', i), self.dval[i]))
        self._deps(q, reads, writes)
        ins = self.eng[q].dma_start(out=out, in_=in_)
        self.ninst += 1
        self.dval[i] += 16
        ins.then_inc(self.dsem[i], 16)
        tok = (('d', i), self.dval[i])
        self._record(tok, reads, writes)
        return tok

    def barrier(self):
        for e in self.eng:
            for o in self.sem:
                if self.cnt[o] > 0:
                    self._wait(e, (o, self.cnt[o]))
            for i in range(self.NS):
                if self.dval[i] > 0:
                    self._wait(e, (('d', i), self.dval[i]))
        self.lastw = {}
        self.readers = {}

    def newgen(self, bank):
        self.fresh[(bank, 0)] = True
        self.fresh[(bank, 1)] = True

    def mm(self, bank, out, lhsT, rhs, reads, writes, halves=(0, 1), last=True, inc=None):
        st = False
        for h in halves:
            if self.fresh.get((bank, h), True):
                st = True
            self.fresh[(bank, h)] = False
        if inc is None:
            inc = last

        def fn(e):
            return e.matmul(out, lhsT=lhsT, rhs=rhs, start=st, stop=last, skip_group_check=True)
        return self.op('pe', fn, reads, writes, inc=inc)

    def tr(self, out, in_, ident, reads, writes):
        return self.op('pe', lambda e: e.transpose(out, in_, ident), reads, writes)

    def act(self, out, in_, func, reads, writes, scale=None, bias=None):
        def fn(e):
            kw = {}
            if scale is not None:
                kw['scale'] = scale
            if bias is not None:
                kw['bias'] = bias
            return e.activation(out=out, in_=in_, func=func, **kw)
        return self.op('act', fn, reads, writes)

    def tt(self, e, out, in0, in1, op, reads, writes):
        return self.op(e, lambda g: g.tensor_tensor(out=out, in0=in0, in1=in1, op=op), reads, writes)

    def ts(self, e, out, in0, s1, s2, op0, op1, reads, writes):
        if op1 is None and e == 'pool' and op0 == ALU.mult:
            op1, s2 = ALU.add, 0.0
        if op1 is None:
            return self.op(e, lambda g: g.tensor_scalar(out=out, in0=in0, scalar1=s1, scalar2=None, op0=op0), reads, writes)
        return self.op(e, lambda g: g.tensor_scalar(out=out, in0=in0, scalar1=s1, scalar2=s2, op0=op0, op1=op1), reads, writes)

    def stt(self, out, in0, scalar, in1, op0, op1, reads, writes):
        return self.op('dve', lambda g: g.scalar_tensor_tensor(out=out, in0=in0, scalar=scalar, in1=in1, op0=op0, op1=op1), reads, writes)

    def copy(self, e, out, in_, reads, writes):
        if e == 'act':
            return self.op('act', lambda g: g.copy(out=out, in_=in_), reads, writes)
        return self.op(e, lambda g: g.tensor_copy(out=out, in_=in_), reads, writes)

    def memset(self, e, ap, val, writes):
        return self.op(e, lambda g: g.memset(ap, val), (), writes)


class _Stop(Exception):
    pass


UWB = 7


def build_program(n_layers=DEPTH, dbg=False, stop=None):
    nc = bass.Bass("TRN2", target_bir_lowering=False)

    def din(name, shape, dt=F32):
        return nc.dram_tensor(name, list(shape), dt, kind="ExternalInput").ap()

    def dscr(name, shape, dt):
        return nc.dram_tensor(name, list(shape), dt, kind="Internal").ap()

    x_in = din("x", [S, D])
    cT_in = din("cT", [128, 8])
    normw_in = din("norm_w", [DEPTH, D])
    fnw_in = din("final_norm_w", [1, D])
    adaw_in = din("ada_w", [DEPTH, D, 3 * D])
    adab_in = din("ada_b", [DEPTH, 3 * D])
    a_win = din("a_w_in", [2, D, GDN_IN])
    a_convT = din("a_convT", [2, 128, 32, 4])
    a_Alog = din("a_A_log", [2, 16])
    a_dtb = din("a_dt_bias", [2, 16])
    a_nw = din("a_norm_w", [2, 128, 1])
    a_wout = din("a_w_out", [2, 2048, D])
    b_win = din("b_w_in", [2, D, FOX_IN])
    b_fb = din("b_f_bias", [2, 16, 1])
    b_qn2 = din("b_qn2", [2, 128, 1])
    b_kn2 = din("b_kn2", [2, 128, 1])
    b_wout = din("b_w_out", [2, D, D])
    consts_in = din("consts", [NCONST, 128, 128])
    out_d = nc.dram_tensor("out", [S, D], F32, kind="ExternalOutput").ap()

    xres = dscr("xres", [S, D], F32)
    qkvz = dscr("qkvz", [48, 128, S], BF16)
    qks = dscr("qks", [16, 128, S], BF16)
    zss = dscr("zss", [S, D], BF16)
    c1s = dscr("c1s", [16, S], BF16)
    dbg_d = {}
    if dbg:
        dbg_d['hT'] = nc.dram_tensor("dbg_hT", [128, 8, S], BF16, kind="ExternalOutput").ap()
        dbg_d['xres'] = nc.dram_tensor("dbg_xres", [S, D], F32, kind="ExternalOutput").ap()
        dbg_d['qkvz'] = nc.dram_tensor("dbg_qkvz", [48, 128, S], BF16, kind="ExternalOutput").ap()
        dbg_d['gates'] = nc.dram_tensor("dbg_gates", [128, NT, 6, 16], F32, kind="ExternalOutput").ap()

    es = contextlib.ExitStack()
    with es:
        kb = KB(nc, es)

        uid = [0]

        def sb(stack, name, shape, dt=F32):
            uid[0] += 1
            return stack.enter_context(nc.sbuf_tensor("%s_%d" % (name, uid[0]), list(shape), dt))

        ps = [es.enter_context(nc.psum_tensor("ps%d" % i, [128, 512], F32)) for i in range(8)]

        def PK(bank, lo=0, hi=512):
            return [('ps', bank)]

        def psbf(i):
            return ps[i][:].bitcast(BF16)

        cst = sb(es, "cst", [128, NCONST, 128])
        ident_bf = sb(es, "ident_bf", [128, 128], BF16)
        caus_bf = sb(es, "caus_bf", [128, 128], BF16)
        condT = sb(es, "condT", [128, 8])
        gate_b = sb(es, "gate_b", [128, D])
        fnw_b = sb(es, "fnw_b", [128, D])
        xt = [sb(es, "xt%d" % i, [128, D]) for i in range(2)]
        junk = sb(es, "junk", [128, D])
        sm = sb(es, "sm", [128, 8])
        ones_row = sb(es, "ones_row", [1, 128])

        def C(i):
            return cst[:, i, :]

        kb.dma(cst[:], consts_in.rearrange("n p f -> p n f"), writes=['cst'])
        kb.copy('dve', ident_bf[:], C(C_ID), ['cst'], ['ident_bf'])
        kb.copy('dve', caus_bf[:], C(C_CAUS), ['cst'], ['caus_bf'])
        kb.memset('dve', ones_row[:], 1.0, ['ones_row'])
        kb.dma(condT[:], cT_in, writes=['condT'])
        kb.act(condT[:], condT[:], AF.Silu, ['condT'], ['condT'])
        kb.dma(fnw_b[:], fnw_in.partition_broadcast(128), writes=['fnw_b'])

        def rstd_from_ss(ss_ap, out_ap, n, rkeys, wkeys, tmp_ap):
            kb.act(tmp_ap, ss_ap, AF.Ln, rkeys, [('tmpln',)], scale=1.0 / n, bias=EPS)
            kb.act(out_ap, tmp_ap, AF.Exp, [('tmpln',)], wkeys, scale=-0.5)

        def adaln(L, A_b, B_b, stack):
            with contextlib.ExitStack() as s2:
                adaw = [sb(s2, "adaw%d" % i, [128, 8, 256]) for i in range(2)]
                modrow = sb(s2, "modrow", [1, 3 * D])
                nwrow = sb(s2, "nwrow", [1, D])
                arow = sb(s2, "arow", [1, D])
                kb.dma(modrow[:], adab_in[L:L + 1, :], writes=[('modrow', i) for i in range(6)])
                kb.dma(nwrow[:], normw_in[L:L + 1, :], writes=['nwrow'])
                wv = adaw_in[L].rearrange("(kc p) f -> p kc f", p=128)
                for fb in range(12):
                    b = fb % 2
                    kb.dma(adaw[b][:], wv[:, :, fb * 256:(fb + 1) * 256], writes=[('adaw', b)])
                    bank = fb % 2
                    kb.newgen(bank)
                    for kc in range(8):
                        kb.mm(bank, ps[bank][0:1, 0:256], condT[:, kc:kc + 1], adaw[b][:, kc, :],
                              ['condT', ('adaw', b)], PK(bank), halves=(0,), last=(kc == 7))
                    kb.tt('dve', modrow[0:1, fb * 256:(fb + 1) * 256], ps[bank][0:1, 0:256], modrow[0:1, fb * 256:(fb + 1) * 256],
                          ALU.add, PK(bank) + [('modrow', fb // 2)], [('modrow', fb // 2)])
                kb.stt(arow[0:1, :], modrow[0:1, D:2 * D], 1.0, nwrow[0:1, :], ALU.add, ALU.mult,
                       [('modrow', 2), ('modrow', 3), 'nwrow'], ['arow'])
                srcs = [(arow[0:1, :], A_b, ['arow'], 'A_b'), (modrow[0:1, 0:D], B_b, [('modrow', 0), ('modrow', 1)], 'B_b'),
                        (modrow[0:1, 2 * D:3 * D], gate_b, [('modrow', 4), ('modrow', 5)], 'gate_b')]
                n = 0
                for (src, dst, rk, dname) in srcs:
                    for half in range(2):
                        bank = 2 + (n % 2)
                        n += 1
                        kb.newgen(bank)
                        kb.mm(bank, ps[bank][:, :], ones_row[0:1, :], src[0:1, half * 512:(half + 1) * 512],
                              rk + ['ones_row'], PK(bank))
                        kb.copy('act', dst[:, half * 512:(half + 1) * 512], ps[bank][:, :], PK(bank), [(dname, half)])
                kb.barrier()

        def norm_phase(L, xsrc, xkey, hT, A_b, B_b, stack):
            hn = sb(stack, "hn", [128, D])
            hb = [sb(stack, "hb%d" % i, [128, D], BF16) for i in range(2)]
            for t in range(NT):
                b = t % 2
                kb.dma(xt[b][:], xsrc[t * 128:(t + 1) * 128, :], reads=[(xkey, t)], writes=[('xt', b)])
                kb.act(junk[:], xt[b][:], AF.Square, [('xt', b)], ['junk'])
                kb.op('dve', lambda g: g.tensor_reduce(out=sm[:, 0:1], in_=junk[:], axis=AX.X, op=ALU.add), ['junk'], [('sm', 0)])
                rstd_from_ss(sm[:, 0:1], sm[:, 2:3], D, [('sm', 0)], [('sm', 2)], sm[:, 1:2])
                kb.stt(hn[:], xt[b][:], sm[:, 2:3], A_b[:], ALU.mult, ALU.mult, [('xt', b), ('sm', 2), ('A_b', 0), ('A_b', 1)], ['hn'])
                kb.tt('pool', hb[b][:], hn[:], B_b[:], ALU.add, ['hn', ('B_b', 0), ('B_b', 1)], [('hb', b)])
                bank = 4 + b
                for kc in range(8):
                    kb.tr(psbf(bank)[:, kc * 128:(kc + 1) * 128], hb[b][:, kc * 128:(kc + 1) * 128], ident_bf[:],
                          [('hb', b), 'ident_bf'], PK(bank))
                kb.copy('act', hT[:, :, t * 128:(t + 1) * 128], psbf(bank).rearrange("p (k t) -> p k t", k=8),
                        PK(bank), [('hT', t // 4)])

        def outproj_tile(L, t, ogT, KC, wout_bf, xsrc, xkey, last_layer, ykeys):
            b = t % 2
            kb.dma(xt[b][:], xsrc[t * 128:(t + 1) * 128, :], reads=[(xkey, t)], writes=[('xt', b)])
            for fb in range(2):
                bank = 6 + fb
                kb.newgen(bank)
                for kc in range(KC):
                    kb.mm(bank, ps[bank][:, :], ogT[:, kc, :], wout_bf[:, kc, fb * 512:(fb + 1) * 512],
                          ykeys + ['wout_bf'], PK(bank), last=(kc == KC - 1))
                kb.tt('dve', junk[:, fb * 512:(fb + 1) * 512], ps[bank][:, :], gate_b[:, fb * 512:(fb + 1) * 512], ALU.mult,
                      PK(bank) + [('gate_b', fb)], [('junkh', fb)])
                kb.tt('pool', xt[b][:, fb * 512:(fb + 1) * 512], junk[:, fb * 512:(fb + 1) * 512], xt[b][:, fb * 512:(fb + 1) * 512],
                      ALU.add, [('junkh', fb), ('xt', b)], [('xt', b)])
            if not last_layer:
                kb.dma(xres[t * 128:(t + 1) * 128, :], xt[b][:], reads=[('xt', b)], writes=[('xres', t)])
            else:
                kb.act(junk[:], xt[b][:], AF.Square, [('xt', b)], [('junkh', 0), ('junkh', 1)])
                kb.op('dve', lambda g: g.tensor_reduce(out=sm[:, 4:5], in_=junk[:], axis=AX.X, op=ALU.add),
                      [('junkh', 0), ('junkh', 1)], [('sm', 4)])
                rstd_from_ss(sm[:, 4:5], sm[:, 6:7], D, [('sm', 4)], [('sm', 6)], sm[:, 5:6])
                kb.stt(xt[b][:], xt[b][:], sm[:, 6:7], fnw_b[:], ALU.mult, ALU.mult, [('xt', b), ('sm', 6), 'fnw_b'], [('xt', b)])
                kb.dma(out_d[t * 128:(t + 1) * 128, :], xt[b][:], reads=[('xt', b)], writes=[('out', t)])

        def load_wout(wout_dram, KC, wout_bf, stack):
            with contextlib.ExitStack() as s2:
                wst = [sb(s2, "wost%d" % i, [128, D]) for i in range(2)]
                wv = wout_dram.rearrange("(kc p) f -> p kc f", p=128)
                for g in range(KC):
                    b = g % 2
                    kb.dma(wst[b][:], wv[:, g, :], writes=[('wost', b)])
                    kb.copy('pool', wout_bf[:, g, :], wst[b][:], [('wost', b)], ['wout_bf'])
                kb.barrier()

        def proj_fm(w2d, col0, nch, modes, hT, scratch, stack, convw=None, pp_scalars=None, nred=128, ones_ap=None):
            with contextlib.ExitStack() as s2:
                wst = [sb(s2, "wst%d" % i, [128, 8, 256]) for i in range(2)]
                wbf = [sb(s2, "wbf%d" % i, [128, 8, 256], BF16) for i in range(2)]
                obuf = [sb(s2, "obuf%d" % i, [128, 512], BF16) for i in range(3)]
                pre = [sb(s2, "pre%d" % i, [128, 515]) for i in range(3)] if any(m.startswith('conv') for m in modes) else None
                acc = [sb(s2, "acc%d" % i, [128, 512]) for i in range(3)]
                sq2 = [sb(s2, "sq2%d" % i, [128, 512]) for i in range(3)]
                lnv = sb(s2, "lnv", [128, 512])
                rn = [sb(s2, "rn%d" % i, [128, 512]) for i in range(2)]
                wv = w2d.rearrange("(kc p) f -> p kc f", p=128)
                nslab = (nch + 1) // 2
                blocks = [(c, tb) for c in range(nch) for tb in range(8)]
                NB = len(blocks)

                def load_slab(sl):
                    sbf = sl % 2
                    ncs = min(2, nch - sl * 2)
                    kb.dma(wst[sbf][:, :, 0:ncs * 128], wv[:, :, col0 + sl * 256: col0 + sl * 256 + ncs * 128], writes=[('wst', sbf)])
                    kb.copy('pool', wbf[sbf][:, :, 0:ncs * 128], wst[sbf][:, :, 0:ncs * 128], [('wst', sbf)], [('wbf', sbf)])

                def stageA(n):
                    c, tb = blocks[n]
                    sl, ci = c // 2, c % 2
                    sbf = sl % 2
                    if ci == 0 and tb == 0 and sl + 1 < nslab:
                        load_slab(sl + 1)
                    bank = n % 4
                    kb.newgen(bank)
                    for kc in range(8):
                        kb.mm(bank, ps[bank][:, :], wbf[sbf][:, kc, ci * 128:(ci + 1) * 128], hT[:, kc, tb * 512:(tb + 1) * 512],
                              [('wbf', sbf), ('hT', tb)], PK(bank), last=(kc == 7))

                def out_dma(n):
                    c, tb = blocks[n]
                    ob = n % 3
                    kb.dma(scratch[c][:, tb * 512:(tb + 1) * 512], obuf[ob][:], reads=[('obuf', ob)], writes=[('scr', c, tb)])

                def stageB1(n):
                    c, tb = blocks[n]
                    mode = modes[c]
                    bank = n % 4
                    ob = n % 3
                    osl = obuf[ob][:, :]
                    okey = [('obuf', ob)]
                    a = acc[n % 3]
                    ak = ('acc', n % 3)
                    q2 = sq2[n % 3]
                    qk = ('sq2', n % 3)
                    if mode == 'silu':
                        kb.act(osl, ps[bank][:, :], AF.Silu, PK(bank), okey)
                        out_dma(n)
                    elif mode == 'rms':
                        kb.copy('act', a[:], ps[bank][:, :], PK(bank), [ak])
                        kb.tt('pool', q2[:], a[:], a[:], ALU.mult, [ak], [qk])
                    else:
                        p = pre[n % 3]
                        pk = ('pre', n % 3)
                        pprev = pre[(n - 1) % 3]
                        pkprev = ('pre', (n - 1) % 3)
                        kb.copy('act', p[:, 3:515], ps[bank][:, :], PK(bank), [pk])
                        kb.act(a[:], ps[bank][:, :], AF.Identity, PK(bank) + ['convw'], [ak], scale=convw[:, c, 3:4])
                        if tb == 0:
                            kb.memset('pool', p[:, 0:3], 0.0, [pk])
                        else:
                            kb.copy('pool', p[:, 0:3], pprev[:, 512:515], [pkprev], [pk])
                        for jj in (2, 1, 0):
                            kb.stt(a[:], p[:, jj:jj + 512], convw[:, c, jj:jj + 1], a[:], ALU.mult, ALU.add, [pk, 'convw', ak], [ak])
                        if mode == 'conv_v':
                            kb.act(osl, a[:], AF.Silu, [ak], okey)
                            out_dma(n)
                        else:
                            kb.act(a[:], a[:], AF.Silu, [ak], [ak])
                            kb.tt('pool', q2[:], a[:], a[:], ALU.mult, [ak], [qk])

                def stageB2(n):
                    c, tb = blocks[n]
                    mode = modes[c]
                    if mode in ('silu', 'conv_v'):
                        return
                    ob = n % 3
                    osl = obuf[ob][:, :]
                    okey = [('obuf', ob)]
                    a = acc[n % 3]
                    ak = ('acc', n % 3)
                    q2 = sq2[n % 3]
                    qk = ('sq2', n % 3)
                    nb = 4 + n % 2
                    r = rn[n % 2]
                    rk = ('rn', n % 2)
                    kb.newgen(nb)
                    if mode == 'rms':
                        kb.mm(nb, ps[nb][:, :], ones_ap, q2[:], [qk, 'cst'], PK(nb))
                        rstd_from_ss(ps[nb][:, :], r[:], nred, PK(nb), [rk], lnv[:])
                        kb.stt(osl, a[:], pp_scalars[c], r[:], ALU.mult, ALU.mult, [ak, rk, 'ppsc'], okey)
                    else:
                        kb.mm(nb, ps[nb][:, :], C(C_ONES), q2[:], [qk, 'cst'], PK(nb))
                        kb.act(lnv[:], ps[nb][:, :], AF.Ln, PK(nb), [('tmpln',)], bias=EPS)
                        kb.act(r[:], lnv[:], AF.Exp, [('tmpln',)], [rk], scale=-0.5)
                        sc = (128.0 ** -0.5) if mode == 'conv_q' else 1.0
                        kb.stt(osl, a[:], sc, r[:], ALU.mult, ALU.mult, [ak, rk], okey)
                    out_dma(n)

                load_slab(0)
                for n in range(NB + 3):
                    if n < NB:
                        stageA(n)
                    if 0 <= n - 3 < NB:
                        stageB2(n - 3)
                    if 0 <= n - 1 < NB:
                        stageB1(n - 1)
                kb.barrier()

        def gdn_layer(L, j, xsrc, xkey, last_layer):
            with contextlib.ExitStack() as sl:
                G = sb(sl, "G", [128, NT, 6, 16])
                glb = sb(sl, "glb", [128, NT, 2, 16])
                with contextlib.ExitStack() as s1:
                    A_b = sb(s1, "A_b", [128, D])
                    B_b = sb(s1, "B_b", [128, D])
                    adaln(L, A_b, B_b, s1)
                    hT = sb(s1, "hT", [128, 8, S], BF16)
                    with contextlib.ExitStack() as s2:
                        norm_phase(L, xsrc, xkey, hT, A_b, B_b, s2)
                        kb.barrier()
                    if dbg and L == 0:
                        kb.dma(dbg_d['hT'], hT[:], reads=[('hT', i) for i in range(8)], writes=['dbg_hT'])
                    if stop == 'norm':
                        kb.barrier()
                        return True
                    with contextlib.ExitStack() as s2:
                        wbast = sb(s2, "wbast", [128, 8, 32])
                        wba = sb(s2, "wba", [128, 8, 32], BF16)
                        dtb = sb(s2, "dtb", [128, 16])
                        negA = sb(s2, "negA", [128, 16])
                        gt = sb(s2, "gt", [128, 8, 16])
                        kb.dma(wbast[:], a_win[j].rearrange("(kc p) f -> p kc f", p=128)[:, :, 6144:6176], writes=['wbast'])
                        kb.copy('dve', wba[:], wbast[:], ['wbast'], ['wba'])
                        kb.dma(dtb[:], a_dtb[j:j + 1, :].partition_broadcast(128), writes=['dtb'])
                        kb.dma(negA[:], a_Alog[j:j + 1, :].partition_broadcast(128), writes=['negA'])
                        kb.act(negA[:], negA[:], AF.Exp, ['negA'], ['negA'])
                        kb.ts('dve', negA[:], negA[:], -1.0, None, ALU.mult, None, ['negA'], ['negA'])
                        for t in range(NT):
                            bank = t % 2
                            kb.newgen(bank)
                            for kc in range(8):
                                kb.mm(bank, ps[bank][:, 0:32], hT[:, kc, t * 128:(t + 1) * 128], wba[:, kc, :],
                                      [('hT', t // 4), 'wba'], PK(bank), last=(kc == 7))
                            gk = ('G', t)
                            kb.tt('dve', gt[:, 0, :], ps[bank][:, 16:32], dtb[:], ALU.add, PK(bank) + ['dtb'], ['gt0'])
                            kb.act(gt[:, 0, :], gt[:, 0, :], AF.Exp, ['gt0'], ['gt0'])
                            kb.act(gt[:, 0, :], gt[:, 0, :], AF.Ln, ['gt0'], ['gt0'], bias=1.0)
                            kb.tt('dve', G[:, t, 0, :], gt[:, 0, :], negA[:], ALU.mult, ['gt0', 'negA'], [gk])
                            kb.act(gt[:, 1, :], ps[bank][:, 0:16], AF.Exp, PK(bank), ['gt1'], scale=-1.0)
                            kb.act(gt[:, 1, :], gt[:, 1, :], AF.Ln, ['gt1'], ['gt1'], bias=1.0)
                            kb.act(G[:, t, 2, :], gt[:, 1, :], AF.Exp, ['gt1'], [gk], scale=-1.0)
                            kb.ts('dve', G[:, t, 1, :], gt[:, 1, :], -1.0, None, ALU.mult, None, ['gt1'], [gk])
                            b2 = 2 + t % 2
                            kb.newgen(b2)
                            kb.mm(b2, ps[b2][:, 0:16], C(C_TRIBD), G[:, t, 0, :], ['cst', gk], PK(b2))
                            kb.copy('dve', G[:, t, 3, :], ps[b2][:, 0:16], PK(b2), [gk])
                            kb.mm(b2, ps[b2][:, 16:32], C(C_SELC), G[:, t, 3, :], ['cst', gk], PK(b2))
                            kb.mm(b2, ps[b2][:, 32:48], C(C_SELA), G[:, t, 3, :], ['cst', gk], PK(b2))
                            kb.mm(b2, ps[b2][:, 48:64], C(C_SELB), G[:, t, 3, :], ['cst', gk], PK(b2))
                            kb.tt('dve', gt[:, 2, :], ps[b2][:, 16:32], G[:, t, 3, :], ALU.subtract, PK(b2) + [gk], ['gt2'])
                            kb.act(G[:, t, 5, :], gt[:, 2, :], AF.Exp, ['gt2'], [gk])
                            kb.act(glb[:, t, :, :], ps[b2][:, 32:64].rearrange("p (c h) -> p c h", c=2), AF.Exp, PK(b2), [('glb', t)])
                            kb.act(gt[:, 3, :], G[:, t, 3, :], AF.Exp, [gk], ['gt3'])
                            kb.tt('dve', G[:, t, 4, :], gt[:, 3, :], G[:, t, 2, :], ALU.mult, ['gt3', gk], [gk])
                        kb.barrier()
                    if dbg and L == 0:
                        kb.dma(dbg_d['gates'], G[:], reads=[('G', t) for t in range(NT)], writes=['dbg_gates'])
                    if stop == 'gates':
                        kb.barrier()
                        return True
                    with contextlib.ExitStack() as s2:
                        convw = sb(s2, "convw", [128, 32, 4])
                        kb.dma(convw[:], a_convT[j], writes=['convw'])
                        modes = ['conv_q'] * 8 + ['conv_k'] * 8 + ['conv_v'] * 16 + ['silu'] * 16
                        proj_fm(a_win[j], 0, 48, modes, hT, qkvz, s2, convw=convw)
                    kb.barrier()
                if stop == 'proj':
                    return True
                with contextlib.ExitStack() as s1:
                    wout_bf = sb(s1, "wout_bf", [128, 16, D], BF16)
                    load_wout(a_wout[j], 16, wout_bf, s1)
                    rr = gdn_tiles(L, j, G, glb, wout_bf, xsrc, xkey, last_layer, s1)
                    kb.barrier()
                    return rr

        def gdn_tiles(L, j, G, glb, wout_bf, xsrc, xkey, last_layer, st):
            HG = 8
            Sf = sb(st, "Sf", [128, 16, 128])
            Sb = sb(st, "Sb", [128, 16, 128], BF16)
            nw = sb(st, "nw", [128, 1])
            maskbf = sb(st, "maskbf", [128, 3, 128], BF16)
            qT = [sb(st, "qT%d" % i, [128, 8, 128], BF16) for i in range(2)]
            kT = [sb(st, "kT%d" % i, [128, 8, 128], BF16) for i in range(2)]
            vT = [sb(st, "vT%d" % i, [128, 16, 128], BF16) for i in range(2)]
            zs = [sb(st, "zs%d" % i, [128, 16, 128], BF16) for i in range(2)]
            Ag = [sb(st, "Ag%d" % i, [128, 128]) for i in range(2)]
            Agp = [sb(st, "Agp%d" % i, [128, 128]) for i in range(2)]
            E3 = [sb(st, "E3%d" % i, [128, 384]) for i in range(2)]
            XY = sb(st, "XY", [128, HG, 2, 128])
            Pm = sb(st, "Pm", [128, HG, 128])
            vb = sb(st, "vb", [128, HG, 128], BF16)
            kbg = sb(st, "kbg", [128, HG, 128], BF16)
            Gb = [sb(st, "Gb%d" % i, [128, 128]) for i in range(2)]
            gamb = [sb(st, "gamb%d" % i, [128, 128]) for i in range(2)]
            TT = sb(st, "TT", [128, HG, 128], BF16)
            attnT = [sb(st, "attnT%d" % i, [128, HG, 128], BF16) for i in range(2)]
            kdec = [sb(st, "kdec%d" % i, [128, HG, 128], BF16) for i in range(2)]
            qdT = [sb(st, "qdT%d" % i, [128, HG, 128], BF16) for i in range(2)]
            usb = [sb(st, "usb%d" % i, [128, HG, 128]) for i in range(2)]
            wTb = [sb(st, "wTb%d" % i, [128, HG, 128], BF16) for i in range(2)]
            vnew = sb(st, "vnew", [128, HG, 128], BF16)
            oT = sb(st, "oT", [128, HG, 128])
            osq = sb(st, "osq", [128, 512])
            rst = sb(st, "rst", [128, 512])
            lnt = sb(st, "lnt", [128, 512])
            ogT = [sb(st, "ogT%d" % i, [128, 16, 128], BF16) for i in range(2)]
            kb.memset('dve', Sf[:], 0.0, [('Sf', h) for h in range(16)])
            kb.memset('pool', Sb[:], 0.0, [('Sb', h) for h in range(16)])
            kb.dma(nw[:], a_nw[j], writes=['nw'])
            kb.copy('dve', maskbf[:, 0, :], C(C_MINCLT), ['cst'], ['maskbf'])
            kb.copy('dve', maskbf[:, 1, :], C(C_MSTRT), ['cst'], ['maskbf'])
            kb.copy('dve', maskbf[:, 2, :], C(C_MSTR), ['cst'], ['maskbf'])
            qv = qkvz[0:8].rearrange("c p t -> p c t")
            kv = qkvz[8:16].rearrange("c p t -> p c t")
            vv = qkvz[16:32].rearrange("c p t -> p c t")
            zv = qkvz[32:48].rearrange("c p t -> p c t")

            def load_qkv(t):
                b = t % 2
                tsl = slice(t * 128, (t + 1) * 128)
                kb.dma(qT[b][:], qv[:, :, tsl], writes=[('qT', b)])
                kb.dma(kT[b][:], kv[:, :, tsl], writes=[('kT', b)])
                kb.dma(vT[b][:], vv[:, :, tsl], writes=[('vT', b)])

            def load_zs(t):
                b = t % 2
                kb.dma(zs[b][:], zv[:, :, t * 128:(t + 1) * 128], writes=[('zs', b)])

            B_G, B_T = 0, 2
            B_D = (1, 3)

            def s1(t, hg, bf):
                b = t % 2
                gk = ('G', t)
                if hg == 0 and t + 1 < NT:
                    load_qkv(t + 1)
                for hl in range(HG):
                    h = hg * HG + hl
                    hp = h // 2
                    if hl % 2 == 0:
                        kb.newgen(B_G)
                        kb.mm(B_G, ps[B_G][:, 0:128], kT[b][:, hp, :], kT[b][:, hp, :], [('kT', b)], PK(B_G))
                        kb.mm(B_G, ps[B_G][:, 128:256], kT[b][:, hp, :], qT[b][:, hp, :], [('kT', b), ('qT', b)], PK(B_G))
                        kb.tr(psbf(B_T)[:, 0:128], kT[b][:, hp, :], ident_bf[:], [('kT', b), 'ident_bf'], PK(B_T))
                    Gps = ps[B_G][:, 0:128]
                    QKps = ps[B_G][:, 128:256]
                    psK = psbf(B_T)[:, 0:128]
                    a = Ag[h % 2]
                    ap_ = Agp[h % 2]
                    kb.ts('pool', a[:], C(C_UT), G[:, t, 0, h:h + 1], None, ALU.mult, None, ['cst', gk], [('Ag', h % 2)])
                    kb.stt(ap_[:], C(C_ID), G[:, t, 1, h:h + 1], a[:], ALU.mult, ALU.add, ['cst', gk, ('Ag', h % 2)], [('Agp', h % 2)])
                    g_ = Gb[h % 2]
                    kb.ts('pool', g_[:], C(C_ONES), G[:, t, 3, h:h + 1], None, ALU.mult, None, ['cst', gk], [('Gb', h % 2)])
                    db = B_D[h % 2]
                    kb.newgen(db)
                    dk = PK(db)
                    kb.mm(db, ps[db][:, 0:128], C(C_SL), a[:], ['cst', ('Ag', h % 2)], dk, last=False, inc=False)
                    kb.mm(db, ps[db][:, 0:128], ident_bf[:], maskbf[:, 0, :], ['ident_bf', 'maskbf'], dk, inc=False)
                    kb.mm(db, ps[db][:, 128:256], C(C_SL), ap_[:], ['cst', ('Agp', h % 2)], dk, last=False, inc=False)
                    kb.mm(db, ps[db][:, 128:256], ident_bf[:], maskbf[:, 1, :], ['ident_bf', 'maskbf'], dk, inc=False)
                    kb.mm(db, ps[db][:, 256:384], ap_[:], C(C_SL), ['cst', ('Agp', h % 2)], dk, last=False, inc=False)
                    kb.mm(db, ps[db][:, 256:384], ident_bf[:], maskbf[:, 2, :], ['ident_bf', 'maskbf'], dk, inc=False)
                    kb.mm(db, ps[db][:, 384:512], g_[:], C(C_ID), ['cst', ('Gb', h % 2)], dk)
                    vs = 1 + h % 2
                    psV = psbf(B_T)[:, vs * 128:(vs + 1) * 128]
                    kb.tr(psV, vT[b][:, h, :], ident_bf[:], [('vT', b), 'ident_bf'], PK(B_T))
                    e3 = E3[h % 2]
                    kb.act(e3[:], ps[db][:, 0:384], AF.Exp, dk, [('E3', h % 2)])
                    gm = gamb[h % 2]
                    kb.act(gm[:], ps[db][:, 384:512], AF.Exp, dk, [('gamb', h % 2)])
                    kb.tt('dve', XY[:, hl, 0, :], e3[:, 128:256], Gps, ALU.mult, [('E3', h % 2)] + PK(B_G), [('XY', hl)])
                    kb.tt('dve', XY[:, hl, 1, :], e3[:, 256:384], Gps, ALU.mult, [('E3', h % 2)] + PK(B_G), [('XY', hl)])
                    kb.tt('dve', attnT[bf][:, hl, :], e3[:, 0:128], QKps, ALU.mult, [('E3', h % 2)] + PK(B_G), [('attnT', bf, hl)])
                    kb.stt(Pm[:, hl, :], XY[:, hl, 0, :], -1.0, C(C_ID), ALU.mult, ALU.add, [('XY', hl), 'cst'], [('Pm', hl)])
                    kb.tt('pool', qdT[bf][:, hl, :], qT[b][:, hp, :], gm[:], ALU.mult, [('qT', b), ('gamb', h % 2)], [('qdT', bf, hl)])
                    kb.act(vb[:, hl, :], psV, AF.Identity, PK(B_T) + [gk], [('vb', hl)], scale=G[:, t, 2, h:h + 1])
                    kb.act(kbg[:, hl, :], psK, AF.Identity, PK(B_T) + [gk], [('kbg', hl)], scale=G[:, t, 4, h:h + 1])
                    kb.ts('dve', kdec[bf][:, hl, :], psK, G[:, t, 5, h:h + 1], None, ALU.mult, None, PK(B_T) + [gk], [('kdec', bf, hl)])
                    yield
                for lvl in range(1, 6):
                    for pr in range(HG // 2):
                        bank = pr % 2
                        kb.newgen(bank)
                        for u_ in range(2):
                            hl = pr * 2 + u_
                            X = XY[:, hl, 0, :]
                            Y = XY[:, hl, 1, :]
                            o0 = u_ * 256
                            if lvl < 5:
                                kb.mm(bank, ps[bank][:, o0:o0 + 128], Y, X, [('XY', hl)], PK(bank), inc=False)
                            kb.mm(bank, ps[bank][:, o0 + 128:o0 + 256], X, Y, [('XY', hl)], PK(bank), inc=(u_ == 1))
                        if lvl < 5:
                            kb.copy('act', XY[:, pr * 2:pr * 2 + 2, :, :], ps[bank][:, :].rearrange("p (h x c) -> p h x c", h=2, x=2),
                                    PK(bank), [('XY', pr * 2), ('XY', pr * 2 + 1)])
                        else:
                            kb.copy('act', XY[:, pr * 2:pr * 2 + 2, 1, :], ps[bank][:, :].rearrange("p (h x c) -> p h x c", h=2, x=2)[:, :, 1, :],
                                    PK(bank), [('XY', pr * 2), ('XY', pr * 2 + 1)])
                        yield
                    for q4 in range(HG // 4):
                        bank = 2 + q4 % 2
                        kb.newgen(bank)
                        for u_ in range(4):
                            hl = q4 * 4 + u_
                            kb.mm(bank, ps[bank][:, u_ * 128:(u_ + 1) * 128], XY[:, hl, 1, :], Pm[:, hl, :], [('XY', hl), ('Pm', hl)], PK(bank), inc=(u_ == 3))
                        hs = slice(q4 * 4, q4 * 4 + 4)
                        pk = [('Pm', q4 * 4 + u_) for u_ in range(4)]
                        if lvl < 5:
                            kb.tt('dve', Pm[:, hs, :], Pm[:, hs, :], ps[bank][:, :].rearrange("p (h c) -> p h c", h=4), ALU.add, PK(bank) + pk, pk)
                        else:
                            kb.tt('dve', TT[:, hs, :], Pm[:, hs, :], ps[bank][:, :].rearrange("p (h c) -> p h c", h=4), ALU.add,
                                  PK(bank) + pk, [('TT', q4 * 4 + u_) for u_ in range(4)])
                        yield
                for pr in range(HG // 2):
                    bank = pr % 2
                    kb.newgen(bank)
                    for u_ in range(2):
                        hl = pr * 2 + u_
                        kb.mm(bank, ps[bank][:, u_ * 128:(u_ + 1) * 128], TT[:, hl, :], vb[:, hl, :], [('TT', hl), ('vb', hl)], PK(bank), inc=False)
                        kb.mm(bank, ps[bank][:, 256 + u_ * 128:256 + (u_ + 1) * 128], kbg[:, hl, :], TT[:, hl, :], [('TT', hl), ('kbg', hl)], PK(bank), inc=(u_ == 1))
                    kb.copy('act', usb[bf][:, pr * 2:pr * 2 + 2, :], ps[bank][:, 0:256].rearrange("p (h c) -> p h c", h=2), PK(bank),
                            [('usb', bf, pr * 2), ('usb', bf, pr * 2 + 1)])
                    kb.copy('dve', wTb[bf][:, pr * 2:pr * 2 + 2, :], ps[bank][:, 256:512].rearrange("p (h c) -> p h c", h=2), PK(bank),
                            [('wTb', bf, pr * 2), ('wTb', bf, pr * 2 + 1)])
                    yield

            B_W, B_O, B_S, B_N = 4, 5, 6, 7

            def s2(t, hg, bf):
                b = t % 2
                if hg == 0 and t + 1 < NT:
                    load_zs(t + 1)
                for ch in range(2):
                    rs = slice(ch * 64, (ch + 1) * 64)
                    for q4 in range(HG // 4):
                        kb.newgen(B_W)
                        for u_ in range(4):
                            hl = q4 * 4 + u_
                            h = hg * HG + hl
                            kb.mm(B_W, ps[B_W][rs, u_ * 128:(u_ + 1) * 128], wTb[bf][:, hl, rs], Sb[:, h, :], [('wTb', bf, hl), ('Sb', h)], PK(B_W),
                                  halves=(ch,), inc=(u_ == 3))
                        hs = slice(q4 * 4, q4 * 4 + 4)
                        vk = [('vnew', q4 * 4 + u_) for u_ in range(4)]
                        kb.tt('dve', vnew[rs, hs, :], usb[bf][rs, hs, :], ps[B_W][rs, :].rearrange("p (h c) -> p h c", h=4), ALU.subtract,
                              PK(B_W) + [('usb', bf, q4 * 4 + u_) for u_ in range(4)], vk)
                        yield
                        kb.newgen(B_O)
                        kb.newgen(B_S)
                        for u_ in range(4):
                            hl = q4 * 4 + u_
                            h = hg * HG + hl
                            kb.mm(B_O, ps[B_O][:, u_ * 64:(u_ + 1) * 64], Sb[:, h, :], qdT[bf][:, hl, rs], [('Sb', h), ('qdT', bf, hl)], PK(B_O), last=False, inc=False)
                            kb.mm(B_O, ps[B_O][:, u_ * 64:(u_ + 1) * 64], vnew[rs, hl, :], attnT[bf][rs, hl, rs], [('vnew', hl), ('attnT', bf, hl)], PK(B_O), inc=(u_ == 3))
                        for u_ in range(4):
                            hl = q4 * 4 + u_
                            kb.mm(B_S, ps[B_S][:, u_ * 128:(u_ + 1) * 128], kdec[bf][rs, hl, :], vnew[rs, hl, :], [('kdec', bf, hl), ('vnew', hl)], PK(B_S), inc=(u_ == 3))
                        kb.copy('act', oT[:, hs, rs], ps[B_O][:, 0:256].rearrange("p (h c) -> p h c", h=4), PK(B_O),
                                [('oT', q4 * 4 + u_) for u_ in range(4)])
                        yield
                        for u_ in range(4):
                            hl = q4 * 4 + u_
                            h = hg * HG + hl
                            kb.stt(Sf[:, h, :], Sf[:, h, :], glb[:, t, ch, h:h + 1], ps[B_S][:, u_ * 128:(u_ + 1) * 128], ALU.mult, ALU.add,
                                   [('Sf', h), ('glb', t)] + PK(B_S), [('Sf', h)])
                        h0 = hg * HG + q4 * 4
                        kb.copy('act', Sb[:, h0:h0 + 4, :], Sf[:, h0:h0 + 4, :], [('Sf', h0 + u_) for u_ in range(4)], [('Sb', h0 + u_) for u_ in range(4)])
                        yield
                for q4 in range(HG // 4):
                    hs = slice(q4 * 4, q4 * 4 + 4)
                    h0 = hg * HG + q4 * 4
                    ok = [('oT', q4 * 4 + u_) for u_ in range(4)]
                    kb.act(osq[:].rearrange("p (h c) -> p h c", h=4), oT[:, hs, :], AF.Square, ok, ['osq'])
                    kb.newgen(B_N)
                    kb.mm(B_N, ps[B_N][:, :], C(C_ONES), osq[:], ['cst', 'osq'], PK(B_N))
                    rstd_from_ss(ps[B_N][:, :], rst[:], 128, PK(B_N), ['rst'], lnt[:])
                    kb.tt('dve', osq[:].rearrange("p (h c) -> p h c", h=4), oT[:, hs, :], rst[:].rearrange("p (h c) -> p h c", h=4), ALU.mult,
                          ok + ['rst', 'osq'], ['osq'])
                    kb.stt(ogT[b][:, h0:h0 + 4, :], osq[:].rearrange("p (h c) -> p h c", h=4), nw[:, 0:1], zs[b][:, h0:h0 + 4, :], ALU.mult, ALU.mult,
                           ['osq', 'nw', ('zs', b)], [('ogT', b)])
                    yield
                if hg == 1:
                    outproj_tile(L, t, ogT[b], 16, wout_bf, xsrc, xkey, last_layer, [('ogT', b)])
                    yield

            steps = [(t, hg) for t in range(NT) for hg in range(2)]
            load_qkv(0)
            load_zs(0)
            for _ in s1(0, 0, 0):
                pass
            RATIO = 3
            for k in range(len(steps)):
                g1 = s1(steps[k + 1][0], steps[k + 1][1], (k + 1) % 2) if k + 1 < len(steps) else None
                g2 = s2(steps[k][0], steps[k][1], k % 2)
                while g1 is not None or g2 is not None:
                    if g1 is not None:
                        for _ in range(RATIO):
                            try:
                                next(g1)
                            except StopIteration:
                                g1 = None
                                break
                    if g2 is not None:
                        try:
                            next(g2)
                        except StopIteration:
                            g2 = None

        def fox_layer(L, j, xsrc, xkey, last_layer):
            with contextlib.ExitStack() as sl:
                Vall = sb(sl, "Vall", [128, NT, 16, 65], BF16)
                cumT = sb(sl, "cumT", [128, NT, 16])
                with contextlib.ExitStack() as s1:
                    hT = sb(s1, "hT", [128, 8, S], BF16)
                    with contextlib.ExitStack() as s2:
                        A_b = sb(s2, "A_b", [128, D])
                        B_b = sb(s2, "B_b", [128, D])
                        adaln(L, A_b, B_b, s2)
                        norm_phase(L, xsrc, xkey, hT, A_b, B_b, s2)
                        kb.barrier()
                    with contextlib.ExitStack() as s2:
                        wfst = sb(s2, "wfst", [128, 8, 16])
                        wf = sb(s2, "wf", [128, 8, 16], BF16)
                        nfb = sb(s2, "nfb", [16, 1])
                        spl = sb(s2, "spl", [16, 2048])
                        cums = sb(s2, "cums", [16, S])
                        onesr = sb(s2, "onesr", [16, 2048], BF16)
                        c1b = sb(s2, "c1b", [16, S], BF16)
                        kb.dma(wfst[:], b_win[j].rearrange("(kc p) f -> p kc f", p=128)[:, :, 4096:4112], writes=['wfst'])
                        kb.copy('dve', wf[:], wfst[:], ['wfst'], ['wf'])
                        kb.dma(nfb[:], b_fb[j], writes=['nfb'])
                        kb.ts('dve', nfb[:], nfb[:], -1.0, None, ALU.mult, None, ['nfb'], ['nfb'])
                        kb.memset('pool', onesr[:], 1.0, ['onesr'])
                        for half in range(2):
                            for tl in range(4):
                                tb = half * 4 + tl
                                bank = tb % 2
                                kb.newgen(bank)
                                for kc in range(8):
                                    kb.mm(bank, ps[bank][0:16, :], wf[:, kc, :], hT[:, kc, tb * 512:(tb + 1) * 512], ['wf', ('hT', tb)], PK(bank),
                                          halves=(0,), last=(kc == 7))
                                kb.act(spl[:, tl * 512:(tl + 1) * 512], ps[bank][0:16, :], AF.Exp, PK(bank) + ['nfb'], [('spl', tl)], scale=-1.0, bias=nfb[:, 0:1])
                                kb.act(spl[:, tl * 512:(tl + 1) * 512], spl[:, tl * 512:(tl + 1) * 512], AF.Ln, [('spl', tl)], [('spl', tl)], bias=1.0)
                            init = 0.0 if half == 0 else cums[:, 2047:2048]
                            kb.op('dve', lambda g: g.tensor_tensor_scan(out=cums[:, half * 2048:(half + 1) * 2048], data0=onesr[:], data1=spl[:], initial=init,
                                                                      op0=ALU.mult, op1=ALU.add),
                                  [('spl', tl) for tl in range(4)] + ['onesr', 'cums'], ['cums'])
                        kb.ts('dve', c1b[:], cums[:], -1.0, None, ALU.mult, None, ['cums'], ['c1b'])
                        kb.dma(c1s, c1b[:], reads=['c1b'], writes=['c1s'])
                        for t in range(NT):
                            bank = 2 + t % 2
                            kb.newgen(bank)
                            kb.mm(bank, ps[bank][:, 0:16], cums[:, t * 128:(t + 1) * 128], cst[0:16, C_ID, 0:16], ['cums', 'cst'], PK(bank))
                            kb.copy('dve', cumT[:, t, :], ps[bank][:, 0:16], PK(bank), [('cumT', t)])
                        kb.barrier()
                    if stop == 'f1':
                        return True
                    with contextlib.ExitStack() as s2:
                        qn = sb(s2, "qn", [128, 1])
                        kn = sb(s2, "kn", [128, 1])
                        kb.dma(qn[:], b_qn2[j], writes=['ppsc'])
                        kb.dma(kn[:], b_kn2[j], writes=['ppsc'])
                        kb.ts('dve', qn[:], qn[:], 0.125, None, ALU.mult, None, ['ppsc'], ['ppsc'])
                        proj_fm(b_win[j], 0, 16, ['rms'] * 16, hT, qks, s2, pp_scalars=[qn[:, 0:1]] * 8 + [kn[:, 0:1]] * 8, nred=64, ones_ap=C(C_ONESBD))
                    if stop == 'f2':
                        return True
                    with contextlib.ExitStack() as s2:
                        wst = [sb(s2, "wvst%d" % i, [128, 8, 128]) for i in range(2)]
                        wvz = sb(s2, "wvz", [128, 8, 2048], BF16)
                        zt = [sb(s2, "zt%d" % i, [128, D], BF16) for i in range(2)]
                        wv = b_win[j].rearrange("(kc p) f -> p kc f", p=128)
                        for g in range(16):
                            b = g % 2
                            kb.dma(wst[b][:], wv[:, :, 2048 + g * 128:2048 + (g + 1) * 128], writes=[('wvst', b)])
                            kb.copy('pool', wvz[:, :, g * 128:(g + 1) * 128], wst[b][:], [('wvst', b)], [('wvz', g // 4)])
                        kb.memset('dve', Vall[:, :, :, 64:65], 1.0, [('Vall1',)])
                        for t in range(NT):
                            for fb in range(4):
                                bank = (t * 4 + fb) % 4
                                kb.newgen(bank)
                                for kc in range(8):
                                    kb.mm(bank, ps[bank][:, :], hT[:, kc, t * 128:(t + 1) * 128], wvz[:, kc, fb * 512:(fb + 1) * 512],
                                          [('hT', t // 4), ('wvz', fb)], PK(bank), last=(kc == 7))
                                if fb < 2:
                                    kb.copy('dve', Vall[:, t, fb * 8:(fb + 1) * 8, 0:64], ps[bank][:, :].rearrange("p (h d) -> p h d", h=8), PK(bank), [('Vall', t)])
                                else:
                                    kb.act(zt[t % 2][:, (fb - 2) * 512:(fb - 1) * 512], ps[bank][:, :], AF.Silu, PK(bank), [('zt', t % 2)])
                            kb.dma(zss[t * 128:(t + 1) * 128, :], zt[t % 2][:], reads=[('zt', t % 2)], writes=[('zss', t)])
                        kb.barrier()
                if stop == 'f3':
                    return True
                with contextlib.ExitStack() as s1:
                    Oall = sb(s1, "Oall", [128, NT, D], BF16)
                    with contextlib.ExitStack() as s2:
                        QA = [sb(s2, "QA%d" % i, [65, S], BF16) for i in range(2)]
                        KA = [sb(s2, "KA%d" % i, [65, S], BF16) for i in range(2)]
                        PT = [sb(s2, "PT%d" % i, [128, 512], BF16) for i in range(4)]
                        rl = sb(s2, "rl", [128, 4])
                        for i in range(2):
                            kb.memset('dve', KA[i][64:65, :], 1.0, [('KA1', i)])

                        def load_head(h):
                            b = h % 2
                            r0 = (h % 2) * 64
                            kb.dma(QA[b][0:64, :], qks[h // 2][r0:r0 + 64, :], writes=[('QA', b)])
                            kb.dma(QA[b][64:65, :], c1s[h:h + 1, :], reads=['c1s'], writes=[('QA', b)])
                            kb.dma(KA[b][0:64, :], qks[8 + h // 2][r0:r0 + 64, :], writes=[('KA', b)])

                        load_head(0)
                        pairs = [(h, qb, kt) for h in range(16) for qb in range(8) for kt in range(4 * (qb + 1))]
                        NSB = 4
                        LA = 2

                        def stageA(n):
                            h, qb, kt = pairs[n]
                            b = h % 2
                            if qb == 0 and kt == 0 and h + 1 < 16:
                                load_head(h + 1)
                            i0 = max(0, kt - 4 * qb)
                            sbk = n % NSB
                            kb.newgen(sbk)
                            kb.mm(sbk, ps[sbk][:, i0 * 128:512], KA[b][0:65, kt * 128:(kt + 1) * 128], QA[b][0:65, qb * 512 + i0 * 128:(qb + 1) * 512],
                                  [('KA', b), ('KA1', b), ('QA', b)], PK(sbk))

                        def stageB(n):
                            h, qb, kt = pairs[n]
                            nkt = 4 * (qb + 1)
                            jd = kt - 4 * qb
                            i0 = max(0, jd)
                            sbk = n % NSB
                            pt = PT[n % 4]
                            ptk = ('PT', n % 4)
                            obk = 4 + (h * 8 + qb) % 2
                            if kt == 0:
                                kb.newgen(obk)
                            kb.act(pt[:, i0 * 128:512], ps[sbk][:, i0 * 128:512], AF.Exp, PK(sbk) + [('cumT', kt)], [ptk], bias=cumT[:, kt, h:h + 1])
                            if jd >= 0:
                                kb.tt('pool', pt[:, jd * 128:(jd + 1) * 128], pt[:, jd * 128:(jd + 1) * 128], caus_bf[:], ALU.mult, [ptk, 'caus_bf'], [ptk])
                            for i in range(i0, 4):
                                kb.mm(obk, ps[obk][:, i * 65:(i + 1) * 65], pt[:, i * 128:(i + 1) * 128], Vall[:, kt, h, :],
                                      [ptk, ('Vall', kt), ('Vall1',)], PK(obk), last=(kt == nkt - 1), inc=(i == 3))
                            if kt == nkt - 1:
                                ov = ps[obk][:, 0:260].rearrange("p (i d) -> p i d", i=4)
                                kb.op('dve', lambda g: g.reciprocal(out=rl[:], in_=ov[:, :, 64]), PK(obk), ['rl'])
                                kb.tt('dve', Oall[:, qb * 4:(qb + 1) * 4, h * 64:(h + 1) * 64], ov[:, :, 0:64], rl[:].unsqueeze(2).broadcast_to([128, 4, 64]),
                                      ALU.mult, PK(obk) + ['rl'], [('Oall', qb)])

                        for n in range(len(pairs) + LA):
                            if n < len(pairs):
                                stageA(n)
                            if n - LA >= 0:
                                stageB(n - LA)
                        kb.barrier()
                    if stop == 'f4':
                        return True
                    with contextlib.ExitStack() as s2:
                        wout_bf = sb(s2, "wout_bf", [128, 8, D], BF16)
                        load_wout(b_wout[j], 8, wout_bf, s2)
                        zt = [sb(s2, "zt%d" % i, [128, D], BF16) for i in range(2)]
                        og = [sb(s2, "og%d" % i, [128, D], BF16) for i in range(2)]
                        ogT = [sb(s2, "ogT%d" % i, [128, 8, 128], BF16) for i in range(2)]
                        for t in range(NT):
                            b = t % 2
                            kb.dma(zt[b][:], zss[t * 128:(t + 1) * 128, :], reads=[('zss', t)], writes=[('zt', b)])
                            kb.tt('pool', og[b][:], Oall[:, t, :], zt[b][:], ALU.mult, [('Oall', t // 4), ('zt', b)], [('og', b)])
                            bank = 4 + b
                            for kc in range(8):
                                kb.tr(psbf(bank)[:, kc * 128:(kc + 1) * 128], og[b][:, kc * 128:(kc + 1) * 128], ident_bf[:], [('og', b), 'ident_bf'], PK(bank))
                            kb.copy('act', ogT[b][:], psbf(bank).rearrange("p (k t) -> p k t", k=8), PK(bank), [('ogT', b)])
                            outproj_tile(L, t, ogT[b], 8, wout_bf, xsrc, xkey, last_layer, [('ogT', b)])
                        kb.barrier()

        xsrc, xkey = x_in, 'xin'
        for L in range(n_layers):
            last = (L == n_layers - 1)
            if L % 2 == 0:
                stopped = gdn_layer(L, L // 2, xsrc, xkey, last)
            else:
                stopped = fox_layer(L, L // 2, xsrc, xkey, last)
            xsrc, xkey = xres, 'xres'
            kb.barrier()
            if stopped:
                break
        if dbg:
            kb.dma(dbg_d['xres'], xres, reads=[('xres', t) for t in range(NT)], writes=['dbg_x'])
            kb.dma(dbg_d['qkvz'], qkvz, writes=['dbg_q'])
        kb.barrier()
        print("instructions emitted:", kb.ninst, {k: v for k, v in kb.cnt.items()})
    return nc


def make_in_maps(inputs):
    consts = make_consts()
    f = lambda a: np.ascontiguousarray(np.asarray(a, dtype=np.float32))
    x = f(inputs["x"])
    c = f(inputs["c"])
    shared = {
        "norm_w": f(inputs["norm_w"]),
        "final_norm_w": f(inputs["final_norm_w"]).reshape(1, D),
        "ada_w": f(inputs["ada_w"]),
        "ada_b": f(inputs["ada_b"]),
        "a_w_in": f(inputs["a_w_in"]),
        "a_convT": f(np.transpose(f(inputs["a_conv_w"]), (0, 2, 1)).reshape(2, 32, 128, 4).transpose(0, 2, 1, 3)),
        "a_A_log": f(inputs["a_A_log"]),
        "a_dt_bias": f(inputs["a_dt_bias"]),
        "a_norm_w": f(inputs["a_norm_w"]).reshape(2, 128, 1),
        "a_w_out": f(inputs["a_w_out"]),
        "b_w_in": f(inputs["b_w_in"]),
        "b_f_bias": f(inputs["b_f_bias"]).reshape(2, 16, 1),
        "b_qn2": f(np.tile(f(inputs["b_qn_w"]), (1, 2))).reshape(2, 128, 1),
        "b_kn2": f(np.tile(f(inputs["b_kn_w"]), (1, 2))).reshape(2, 128, 1),
        "b_w_out": f(inputs["b_w_out"]),
        "consts": consts,
    }
    maps = []
    for b in range(8):
        m = dict(shared)
        m["x"] = x[b]
        m["cT"] = f(c[b].reshape(8, 128).T)
        maps.append(m)
    return maps


_NC_CACHE = {}


def kernel(**inputs):
    if 'nc' not in _NC_CACHE:
        _NC_CACHE['nc'] = build_program()
    nc = _NC_CACHE['nc']
    in_maps = make_in_maps(inputs)
    res = run_bass_kernel_spmd(nc, in_maps, core_ids=list(range(8)))
    out = np.stack([np.asarray(r["out"], dtype=np.float32) for r in res.results], axis=0)
    return out
```

```python
import contextlib
import numpy as np
import concourse.bass as bass
import concourse.mybir as mybir
from concourse.bass_utils import run_bass_kernel_spmd

F32 = mybir.dt.float32
BF16 = mybir.dt.bfloat16
AF = mybir.ActivationFunctionType
ALU = mybir.AluOpType
AX = mybir.AxisListType

S = 4096
D = 1024
NT = 32
EPS = 1e-6
NEG = -30000.0
DEPTH = 4
GDN_IN = 6176
FOX_IN = 4112

C_ID, C_ONES, C_UT, C_SL, C_MINCLT, C_MSTRT, C_MSTR, C_TRIBD, C_SELC, C_SELA, C_SELB, C_CAUS, C_ONESBD = range(13)
NCONST = 13


def make_consts():
    i = np.arange(128)
    r = i[:, None]
    c = i[None, :]
    same = (r // 64) == (c // 64)
    m = np.zeros((NCONST, 128, 128), np.float32)
    m[C_ID] = (r == c)
    m[C_ONES] = 1.0
    m[C_UT] = (r <= c)
    m[C_SL] = (r > c)
    m[C_MINCLT] = np.where(same & (r <= c), 0.0, NEG)
    m[C_MSTRT] = np.where(same & (r < c), 0.0, NEG)
    m[C_MSTR] = np.where(same & (r > c), 0.0, NEG)
    m[C_TRIBD] = (same & (r <= c))
    m[C_SELC] = (r == (c // 64) * 64 + 63)
    m[C_SELA] = (r == 63) * np.ones((1, 128))
    m[C_SELB] = (r == 127) * np.ones((1, 128))
    m[C_CAUS] = (r <= c)
    m[C_ONESBD] = same
    return m.astype(np.float32)


class KB:
    NS = 24

    def __init__(self, nc, es):
        self.nc = nc
        self.eng = {'pe': nc.tensor, 'act': nc.scalar, 'dve': nc.vector, 'pool': nc.gpsimd, 'sp': nc.sync}
        self.sem = {k: es.enter_context(nc.semaphore("s_" + k)) for k in ['pe', 'act', 'dve', 'pool']}
        self.cnt = {k: 0 for k in self.sem}
        self.seen = {k: {} for k in self.eng}
        self.dsem = [es.enter_context(nc.semaphore("d%d" % i)) for i in range(self.NS)]
        self.dval = [0] * self.NS
        self.dnext = 0
        self.lastw = {}
        self.readers = {}
        self.fresh = {}
        self.ninst = 0

    def _wait(self, e, tok):
        sk, v = tok
        if sk == e and e == 'pe':
            return
        if self.seen[e].get(sk, 0) >= v:
            return
        sem = self.sem[sk] if isinstance(sk, str) else self.dsem[sk[1]]
        self.eng[e].wait_ge(sem, v)
        self.seen[e][sk] = v

    def _deps(self, e, reads, writes):
        for k in reads:
            t = self.lastw.get(k)
            if t is not None:
                self._wait(e, t)
            if isinstance(k, tuple) and k[0] == 'ps':
                for sk, t in self.readers.get(k, {}).items():
                    if sk != e:
                        self._wait(e, t)
        for k in writes:
            t = self.lastw.get(k)
            if t is not None:
                self._wait(e, t)
            for t in self.readers.get(k, {}).values():
                self._wait(e, t)

    def _record(self, tok, reads, writes):
        for k in reads:
            self.readers.setdefault(k, {})[tok[0]] = tok
        for k in writes:
            self.lastw[k] = tok
            self.readers[k] = {}

    def op(self, e, fn, reads=(), writes=(), inc=True):
        self._deps(e, reads, writes)
        ins = fn(self.eng[e])
        self.ninst += 1
        if inc:
            self.cnt[e] += 1
            ins.then_inc(self.sem[e], 1)
            tok = (e, self.cnt[e])
        else:
            tok = (e, self.cnt[e] + 1)
        self._record(tok, reads, writes)
        return tok

    def dma(self, out, in_, reads=(), writes=(), q='sp'):
        i = self.dnext
        self.dnext = (self.dnext + 1) % self.NS
        if self.dval[i] > 0:
            self._wait(q, (('d', i), self.dval[i]))
        self._deps(q, reads, writes)
        ins = self.eng[q].dma_start(out=out, in_=in_)
        self.ninst += 1
        self.dval[i] += 16
        ins.then_inc(self.dsem[i], 16)
        tok = (('d', i), self.dval[i])
        self._record(tok, reads, writes)
        return tok

    def barrier(self):
        for e in self.eng:
            for o in self.sem:
                if self.cnt[o] > 0:
                    self._wait(e, (o, self.cnt[o]))
            for i in range(self.NS):
                if self.dval[i] > 0:
                    self._wait(e, (('d', i), self.dval[i]))
        self.lastw = {}
        self.readers = {}

    def newgen(self, bank):
        self.fresh[(bank, 0)] = True
        self.fresh[(bank, 1)] = True

    def mm(self, bank, out, lhsT, rhs, reads, writes, halves=(0, 1), last=True, inc=None):
        st = False
        for h in halves:
            if self.fresh.get((bank, h), True):
                st = True
            self.fresh[(bank, h)] = False
        if inc is None:
            inc = last

        def fn(e):
            return e.matmul(out, lhsT=lhsT, rhs=rhs, start=st, stop=last, skip_group_check=True)
        return self.op('pe', fn, reads, writes, inc=inc)

    def tr(self, out, in_, ident, reads, writes):
        return self.op('pe', lambda e: e.transpose(out, in_, ident), reads, writes)

    def act(self, out, in_, func, reads, writes, scale=None, bias=None):
        def fn(e):
            kw = {}
            if scale is not None:
                kw['scale'] = scale
            if bias is not None:
                kw['bias'] = bias
            return e.activation(out=out, in_=in_, func=func, **kw)
        return self.op('act', fn, reads, writes)

    def tt(self, e, out, in0, in1, op, reads, writes):
        return self.op(e, lambda g: g.tensor_tensor(out=out, in0=in0, in1=in1, op=op), reads, writes)

    def ts(self, e, out, in0, s1, s2, op0, op1, reads, writes):
        if op1 is None and e == 'pool' and op0 == ALU.mult:
            op1, s2 = ALU.add, 0.0
        if op1 is None:
            return self.op(e, lambda g: g.tensor_scalar(out=out, in0=in0, scalar1=s1, scalar2=None, op0=op0), reads, writes)
        return self.op(e, lambda g: g.tensor_scalar(out=out, in0=in0, scalar1=s1, scalar2=s2, op0=op0, op1=op1), reads, writes)

    def stt(self, out, in0, scalar, in1, op0, op1, reads, writes):
        return self.op('dve', lambda g: g.scalar_tensor_tensor(out=out, in0=in0, scalar=scalar, in1=in1, op0=op0, op1=op1), reads, writes)

    def copy(self, e, out, in_, reads, writes):
        if e == 'act':
            return self.op('act', lambda g: g.copy(out=out, in_=in_), reads, writes)
        return self.op(e, lambda g: g.tensor_copy(out=out, in_=in_), reads, writes)

    def memset(self, e, ap, val, writes):
        return self.op(e, lambda g: g.memset(ap, val), (), writes)


class _Stop(Exception):
    pass


UWB = 7
NEU_LO = 1


def build_program(n_layers=DEPTH, dbg=False, stop=None):
    nc = bass.Bass("TRN2", target_bir_lowering=False)

    def din(name, shape, dt=F32):
        return nc.dram_tensor(name, list(shape), dt, kind="ExternalInput").ap()

    def dscr(name, shape, dt):
        return nc.dram_tensor(name, list(shape), dt, kind="Internal").ap()

    x_in = din("x", [S, D])
    cT_in = din("cT", [128, 8])
    normw_in = din("norm_w", [DEPTH, D])
    fnw_in = din("final_norm_w", [1, D])
    adaw_in = din("ada_w", [DEPTH, D, 3 * D])
    adab_in = din("ada_b", [DEPTH, 3 * D])
    a_win = din("a_w_in", [2, D, GDN_IN])
    a_convT = din("a_convT", [2, 128, 32, 4])
    a_Alog = din("a_A_log", [2, 16])
    a_dtb = din("a_dt_bias", [2, 16])
    a_nw = din("a_norm_w", [2, 128, 1])
    a_wout = din("a_w_out", [2, 2048, D])
    b_win = din("b_w_in", [2, D, FOX_IN])
    b_fb = din("b_f_bias", [2, 16, 1])
    b_qn2 = din("b_qn2", [2, 128, 1])
    b_kn2 = din("b_kn2", [2, 128, 1])
    b_wout = din("b_w_out", [2, D, D])
    consts_in = din("consts", [NCONST, 128, 128])
    out_d = nc.dram_tensor("out", [S, D], F32, kind="ExternalOutput").ap()

    xres = dscr("xres", [S, D], F32)
    qkvz = dscr("qkvz", [48, 128, S], BF16)
    qks = dscr("qks", [16, 128, S], BF16)
    zss = dscr("zss", [S, D], BF16)
    c1s = dscr("c1s", [16, S], BF16)
    dbg_d = {}
    if dbg:
        dbg_d['hT'] = nc.dram_tensor("dbg_hT", [128, 8, S], BF16, kind="ExternalOutput").ap()
        dbg_d['xres'] = nc.dram_tensor("dbg_xres", [S, D], F32, kind="ExternalOutput").ap()
        dbg_d['qkvz'] = nc.dram_tensor("dbg_qkvz", [48, 128, S], BF16, kind="ExternalOutput").ap()
        dbg_d['gates'] = nc.dram_tensor("dbg_gates", [128, NT, 6, 16], F32, kind="ExternalOutput").ap()

    es = contextlib.ExitStack()
    with es:
        kb = KB(nc, es)

        uid = [0]

        def sb(stack, name, shape, dt=F32):
            uid[0] += 1
            return stack.enter_context(nc.sbuf_tensor("%s_%d" % (name, uid[0]), list(shape), dt))

        ps = [es.enter_context(nc.psum_tensor("ps%d" % i, [128, 512], F32)) for i in range(8)]

        def PK(bank, lo=0, hi=512):
            return [('ps', bank)]

        def psbf(i):
            return ps[i][:].bitcast(BF16)

        cst = sb(es, "cst", [128, NCONST, 128])
        ident_bf = sb(es, "ident_bf", [128, 128], BF16)
        caus_bf = sb(es, "caus_bf", [128, 128], BF16)
        condT = sb(es, "condT", [128, 8])
        gate_b = sb(es, "gate_b", [128, D])
        fnw_b = sb(es, "fnw_b", [128, D])
        xt = [sb(es, "xt%d" % i, [128, D]) for i in range(2)]
        junk = sb(es, "junk", [128, D])
        sm = sb(es, "sm", [128, 8])
        ones_row = sb(es, "ones_row", [1, 128])

        def C(i):
            return cst[:, i, :]

        kb.dma(cst[:], consts_in.rearrange("n p f -> p n f"), writes=['cst'])
        kb.copy('dve', ident_bf[:], C(C_ID), ['cst'], ['ident_bf'])
        kb.copy('dve', caus_bf[:], C(C_CAUS), ['cst'], ['caus_bf'])
        kb.memset('dve', ones_row[:], 1.0, ['ones_row'])
        kb.dma(condT[:], cT_in, writes=['condT'])
        kb.act(condT[:], condT[:], AF.Silu, ['condT'], ['condT'])
        kb.dma(fnw_b[:], fnw_in.partition_broadcast(128), writes=['fnw_b'])

        def rstd_from_ss(ss_ap, out_ap, n, rkeys, wkeys, tmp_ap):
            kb.act(tmp_ap, ss_ap, AF.Ln, rkeys, [('tmpln',)], scale=1.0 / n, bias=EPS)
            kb.act(out_ap, tmp_ap, AF.Exp, [('tmpln',)], wkeys, scale=-0.5)

        def adaln(L, A_b, B_b, stack):
            with contextlib.ExitStack() as s2:
                adaw = [sb(s2, "adaw%d" % i, [128, 8, 256]) for i in range(2)]
                modrow = sb(s2, "modrow", [1, 3 * D])
                nwrow = sb(s2, "nwrow", [1, D])
                arow = sb(s2, "arow", [1, D])
                kb.dma(modrow[:], adab_in[L:L + 1, :], writes=[('modrow', i) for i in range(6)])
                kb.dma(nwrow[:], normw_in[L:L + 1, :], writes=['nwrow'])
                wv = adaw_in[L].rearrange("(kc p) f -> p kc f", p=128)
                for fb in range(12):
                    b = fb % 2
                    kb.dma(adaw[b][:], wv[:, :, fb * 256:(fb + 1) * 256], writes=[('adaw', b)])
                    bank = fb % 2
                    kb.newgen(bank)
                    for kc in range(8):
                        kb.mm(bank, ps[bank][0:1, 0:256], condT[:, kc:kc + 1], adaw[b][:, kc, :],
                              ['condT', ('adaw', b)], PK(bank), halves=(0,), last=(kc == 7))
                    kb.tt('dve', modrow[0:1, fb * 256:(fb + 1) * 256], ps[bank][0:1, 0:256], modrow[0:1, fb * 256:(fb + 1) * 256],
                          ALU.add, PK(bank) + [('modrow', fb // 2)], [('modrow', fb // 2)])
                kb.stt(arow[0:1, :], modrow[0:1, D:2 * D], 1.0, nwrow[0:1, :], ALU.add, ALU.mult,
                       [('modrow', 2), ('modrow', 3), 'nwrow'], ['arow'])
                srcs = [(arow[0:1, :], A_b, ['arow'], 'A_b'), (modrow[0:1, 0:D], B_b, [('modrow', 0), ('modrow', 1)], 'B_b'),
                        (modrow[0:1, 2 * D:3 * D], gate_b, [('modrow', 4), ('modrow', 5)], 'gate_b')]
                n = 0
                for (src, dst, rk, dname) in srcs:
                    for half in range(2):
                        bank = 2 + (n % 2)
                        n += 1
                        kb.newgen(bank)
                        kb.mm(bank, ps[bank][:, :], ones_row[0:1, :], src[0:1, half * 512:(half + 1) * 512],
                              rk + ['ones_row'], PK(bank))
                        kb.copy('act', dst[:, half * 512:(half + 1) * 512], ps[bank][:, :], PK(bank), [(dname, half)])
                kb.barrier()

        def norm_phase(L, xsrc, xkey, hT, A_b, B_b, stack):
            hn = sb(stack, "hn", [128, D])
            hb = [sb(stack, "hb%d" % i, [128, D], BF16) for i in range(2)]
            for t in range(NT):
                b = t % 2
                kb.dma(xt[b][:], xsrc[t * 128:(t + 1) * 128, :], reads=[(xkey, t)], writes=[('xt', b)])
                kb.act(junk[:], xt[b][:], AF.Square, [('xt', b)], ['junk'])
                kb.op('dve', lambda g: g.tensor_reduce(out=sm[:, 0:1], in_=junk[:], axis=AX.X, op=ALU.add), ['junk'], [('sm', 0)])
                rstd_from_ss(sm[:, 0:1], sm[:, 2:3], D, [('sm', 0)], [('sm', 2)], sm[:, 1:2])
                kb.stt(hn[:], xt[b][:], sm[:, 2:3], A_b[:], ALU.mult, ALU.mult, [('xt', b), ('sm', 2), ('A_b', 0), ('A_b', 1)], ['hn'])
                kb.tt('pool', hb[b][:], hn[:], B_b[:], ALU.add, ['hn', ('B_b', 0), ('B_b', 1)], [('hb', b)])
                bank = 4 + b
                for kc in range(8):
                    kb.tr(psbf(bank)[:, kc * 128:(kc + 1) * 128], hb[b][:, kc * 128:(kc + 1) * 128], ident_bf[:],
                          [('hb', b), 'ident_bf'], PK(bank))
                kb.copy('act', hT[:, :, t * 128:(t + 1) * 128], psbf(bank).rearrange("p (k t) -> p k t", k=8),
                        PK(bank), [('hT', t // 4)])

        def outproj_tile(L, t, ogT, KC, wout_bf, xsrc, xkey, last_layer, ykeys):
            b = t % 2
            kb.dma(xt[b][:], xsrc[t * 128:(t + 1) * 128, :], reads=[(xkey, t)], writes=[('xt', b)])
            for fb in range(2):
                bank = 6 + fb
                kb.newgen(bank)
                for kc in range(KC):
                    kb.mm(bank, ps[bank][:, :], ogT[:, kc, :], wout_bf[:, kc, fb * 512:(fb + 1) * 512],
                          ykeys + ['wout_bf'], PK(bank), last=(kc == KC - 1))
                kb.tt('dve', junk[:, fb * 512:(fb + 1) * 512], ps[bank][:, :], gate_b[:, fb * 512:(fb + 1) * 512], ALU.mult,
                      PK(bank) + [('gate_b', fb)], [('junkh', fb)])
                kb.tt('pool', xt[b][:, fb * 512:(fb + 1) * 512], junk[:, fb * 512:(fb + 1) * 512], xt[b][:, fb * 512:(fb + 1) * 512],
                      ALU.add, [('junkh', fb), ('xt', b)], [('xt', b)])
            if not last_layer:
                kb.dma(xres[t * 128:(t + 1) * 128, :], xt[b][:], reads=[('xt', b)], writes=[('xres', t)])
            else:
                kb.act(junk[:], xt[b][:], AF.Square, [('xt', b)], [('junkh', 0), ('junkh', 1)])
                kb.op('dve', lambda g: g.tensor_reduce(out=sm[:, 4:5], in_=junk[:], axis=AX.X, op=ALU.add),
                      [('junkh', 0), ('junkh', 1)], [('sm', 4)])
                rstd_from_ss(sm[:, 4:5], sm[:, 6:7], D, [('sm', 4)], [('sm', 6)], sm[:, 5:6])
                kb.stt(xt[b][:], xt[b][:], sm[:, 6:7], fnw_b[:], ALU.mult, ALU.mult, [('xt', b), ('sm', 6), 'fnw_b'], [('xt', b)])
                kb.dma(out_d[t * 128:(t + 1) * 128, :], xt[b][:], reads=[('xt', b)], writes=[('out', t)])

        def load_wout(wout_dram, KC, wout_bf, stack):
            with contextlib.ExitStack() as s2:
                wst = [sb(s2, "wost%d" % i, [128, D]) for i in range(2)]
                wv = wout_dram.rearrange("(kc p) f -> p kc f", p=128)
                for g in range(KC):
                    b = g % 2
                    kb.dma(wst[b][:], wv[:, g, :], writes=[('wost', b)])
                    kb.copy('pool', wout_bf[:, g, :], wst[b][:], [('wost', b)], ['wout_bf'])
                kb.barrier()

        def proj_fm(w2d, col0, nch, modes, hT, scratch, stack, convw=None, pp_scalars=None, nred=128, ones_ap=None):
            with contextlib.ExitStack() as s2:
                wst = [sb(s2, "wst%d" % i, [128, 8, 256]) for i in range(2)]
                wbf = [sb(s2, "wbf%d" % i, [128, 8, 256], BF16) for i in range(2)]
                obuf = [sb(s2, "obuf%d" % i, [128, 512], BF16) for i in range(3)]
                pre = [sb(s2, "pre%d" % i, [128, 515]) for i in range(3)] if any(m.startswith('conv') for m in modes) else None
                acc = [sb(s2, "acc%d" % i, [128, 512]) for i in range(3)]
                sq2 = [sb(s2, "sq2%d" % i, [128, 512]) for i in range(3)]
                lnv = sb(s2, "lnv", [128, 512])
                rn = [sb(s2, "rn%d" % i, [128, 512]) for i in range(2)]
                wv = w2d.rearrange("(kc p) f -> p kc f", p=128)
                nslab = (nch + 1) // 2
                blocks = [(c, tb) for c in range(nch) for tb in range(8)]
                NB = len(blocks)

                def load_slab(sl):
                    sbf = sl % 2
                    ncs = min(2, nch - sl * 2)
                    kb.dma(wst[sbf][:, :, 0:ncs * 128], wv[:, :, col0 + sl * 256: col0 + sl * 256 + ncs * 128], writes=[('wst', sbf)])
                    kb.copy('pool', wbf[sbf][:, :, 0:ncs * 128], wst[sbf][:, :, 0:ncs * 128], [('wst', sbf)], [('wbf', sbf)])

                def stageA(n):
                    c, tb = blocks[n]
                    sl, ci = c // 2, c % 2
                    sbf = sl % 2
                    if ci == 0 and tb == 0 and sl + 1 < nslab:
                        load_slab(sl + 1)
                    bank = n % 4
                    kb.newgen(bank)
                    for kc in range(8):
                        kb.mm(bank, ps[bank][:, :], wbf[sbf][:, kc, ci * 128:(ci + 1) * 128], hT[:, kc, tb * 512:(tb + 1) * 512],
                              [('wbf', sbf), ('hT', tb)], PK(bank), last=(kc == 7))

                def out_dma(n):
                    c, tb = blocks[n]
                    ob = n % 3
                    kb.dma(scratch[c][:, tb * 512:(tb + 1) * 512], obuf[ob][:], reads=[('obuf', ob)], writes=[('scr', c, tb)])

                def stageB1(n):
                    c, tb = blocks[n]
                    mode = modes[c]
                    bank = n % 4
                    ob = n % 3
                    osl = obuf[ob][:, :]
                    okey = [('obuf', ob)]
                    a = acc[n % 3]
                    ak = ('acc', n % 3)
                    q2 = sq2[n % 3]
                    qk = ('sq2', n % 3)
                    if mode == 'silu':
                        kb.act(osl, ps[bank][:, :], AF.Silu, PK(bank), okey)
                        out_dma(n)
                    elif mode == 'rms':
                        kb.copy('act', a[:], ps[bank][:, :], PK(bank), [ak])
                        kb.tt('pool', q2[:], a[:], a[:], ALU.mult, [ak], [qk])
                    else:
                        p = pre[n % 3]
                        pk = ('pre', n % 3)
                        pprev = pre[(n - 1) % 3]
                        pkprev = ('pre', (n - 1) % 3)
                        kb.copy('act', p[:, 3:515], ps[bank][:, :], PK(bank), [pk])
                        kb.act(a[:], ps[bank][:, :], AF.Identity, PK(bank) + ['convw'], [ak], scale=convw[:, c, 3:4])
                        if tb == 0:
                            kb.memset('pool', p[:, 0:3], 0.0, [pk])
                        else:
                            kb.copy('pool', p[:, 0:3], pprev[:, 512:515], [pkprev], [pk])
                        for jj in (2, 1, 0):
                            kb.stt(a[:], p[:, jj:jj + 512], convw[:, c, jj:jj + 1], a[:], ALU.mult, ALU.add, [pk, 'convw', ak], [ak])
                        if mode == 'conv_v':
                            kb.act(osl, a[:], AF.Silu, [ak], okey)
                            out_dma(n)
                        else:
                            kb.act(a[:], a[:], AF.Silu, [ak], [ak])
                            kb.tt('pool', q2[:], a[:], a[:], ALU.mult, [ak], [qk])

                def stageB2(n):
                    c, tb = blocks[n]
                    mode = modes[c]
                    if mode in ('silu', 'conv_v'):
                        return
                    ob = n % 3
                    osl = obuf[ob][:, :]
                    okey = [('obuf', ob)]
                    a = acc[n % 3]
                    ak = ('acc', n % 3)
                    q2 = sq2[n % 3]
                    qk = ('sq2', n % 3)
                    nb = 4 + n % 2
                    r = rn[n % 2]
                    rk = ('rn', n % 2)
                    kb.newgen(nb)
                    if mode == 'rms':
                        kb.mm(nb, ps[nb][:, :], ones_ap, q2[:], [qk, 'cst'], PK(nb))
                        rstd_from_ss(ps[nb][:, :], r[:], nred, PK(nb), [rk], lnv[:])
                        kb.stt(osl, a[:], pp_scalars[c], r[:], ALU.mult, ALU.mult, [ak, rk, 'ppsc'], okey)
                    else:
                        kb.mm(nb, ps[nb][:, :], C(C_ONES), q2[:], [qk, 'cst'], PK(nb))
                        kb.act(lnv[:], ps[nb][:, :], AF.Ln, PK(nb), [('tmpln',)], bias=EPS)
                        kb.act(r[:], lnv[:], AF.Exp, [('tmpln',)], [rk], scale=-0.5)
                        sc = (128.0 ** -0.5) if mode == 'conv_q' else 1.0
                        kb.stt(osl, a[:], sc, r[:], ALU.mult, ALU.mult, [ak, rk], okey)
                    out_dma(n)

                load_slab(0)
                for n in range(NB + 3):
                    if n < NB:
                        stageA(n)
                    if 0 <= n - 3 < NB:
                        stageB2(n - 3)
                    if 0 <= n - 1 < NB:
                        stageB1(n - 1)
                kb.barrier()

        def gdn_layer(L, j, xsrc, xkey, last_layer):
            with contextlib.ExitStack() as sl:
                G = sb(sl, "G", [128, NT, 6, 16])
                glb = sb(sl, "glb", [128, NT, 2, 16])
                with contextlib.ExitStack() as s1:
                    A_b = sb(s1, "A_b", [128, D])
                    B_b = sb(s1, "B_b", [128, D])
                    adaln(L, A_b, B_b, s1)
                    hT = sb(s1, "hT", [128, 8, S], BF16)
                    with contextlib.ExitStack() as s2:
                        norm_phase(L, xsrc, xkey, hT, A_b, B_b, s2)
                        kb.barrier()
                    if dbg and L == 0:
                        kb.dma(dbg_d['hT'], hT[:], reads=[('hT', i) for i in range(8)], writes=['dbg_hT'])
                    if stop == 'norm':
                        kb.barrier()
                        return True
                    with contextlib.ExitStack() as s2:
                        wbast = sb(s2, "wbast", [128, 8, 32])
                        wba = sb(s2, "wba", [128, 8, 32], BF16)
                        dtb = sb(s2, "dtb", [128, 16])
                        negA = sb(s2, "negA", [128, 16])
                        gt = sb(s2, "gt", [128, 8, 16])
                        kb.dma(wbast[:], a_win[j].rearrange("(kc p) f -> p kc f", p=128)[:, :, 6144:6176], writes=['wbast'])
                        kb.copy('dve', wba[:], wbast[:], ['wbast'], ['wba'])
                        kb.dma(dtb[:], a_dtb[j:j + 1, :].partition_broadcast(128), writes=['dtb'])
                        kb.dma(negA[:], a_Alog[j:j + 1, :].partition_broadcast(128), writes=['negA'])
                        kb.act(negA[:], negA[:], AF.Exp, ['negA'], ['negA'])
                        kb.ts('dve', negA[:], negA[:], -1.0, None, ALU.mult, None, ['negA'], ['negA'])
                        for t in range(NT):
                            bank = t % 2
                            kb.newgen(bank)
                            for kc in range(8):
                                kb.mm(bank, ps[bank][:, 0:32], hT[:, kc, t * 128:(t + 1) * 128], wba[:, kc, :],
                                      [('hT', t // 4), 'wba'], PK(bank), last=(kc == 7))
                            gk = ('G', t)
                            kb.tt('dve', gt[:, 0, :], ps[bank][:, 16:32], dtb[:], ALU.add, PK(bank) + ['dtb'], ['gt0'])
                            kb.act(gt[:, 0, :], gt[:, 0, :], AF.Exp, ['gt0'], ['gt0'])
                            kb.act(gt[:, 0, :], gt[:, 0, :], AF.Ln, ['gt0'], ['gt0'], bias=1.0)
                            kb.tt('dve', G[:, t, 0, :], gt[:, 0, :], negA[:], ALU.mult, ['gt0', 'negA'], [gk])
                            kb.act(gt[:, 1, :], ps[bank][:, 0:16], AF.Exp, PK(bank), ['gt1'], scale=-1.0)
                            kb.act(gt[:, 1, :], gt[:, 1, :], AF.Ln, ['gt1'], ['gt1'], bias=1.0)
                            kb.act(G[:, t, 2, :], gt[:, 1, :], AF.Exp, ['gt1'], [gk], scale=-1.0)
                            kb.ts('dve', G[:, t, 1, :], gt[:, 1, :], -1.0, None, ALU.mult, None, ['gt1'], [gk])
                            b2 = 2 + t % 2
                            kb.newgen(b2)
                            kb.mm(b2, ps[b2][:, 0:16], C(C_TRIBD), G[:, t, 0, :], ['cst', gk], PK(b2))
                            kb.copy('dve', G[:, t, 3, :], ps[b2][:, 0:16], PK(b2), [gk])
                            kb.mm(b2, ps[b2][:, 16:32], C(C_SELC), G[:, t, 3, :], ['cst', gk], PK(b2))
                            kb.mm(b2, ps[b2][:, 32:48], C(C_SELA), G[:, t, 3, :], ['cst', gk], PK(b2))
                            kb.mm(b2, ps[b2][:, 48:64], C(C_SELB), G[:, t, 3, :], ['cst', gk], PK(b2))
                            kb.tt('dve', gt[:, 2, :], ps[b2][:, 16:32], G[:, t, 3, :], ALU.subtract, PK(b2) + [gk], ['gt2'])
                            kb.act(G[:, t, 5, :], gt[:, 2, :], AF.Exp, ['gt2'], [gk])
                            kb.act(glb[:, t, :, :], ps[b2][:, 32:64].rearrange("p (c h) -> p c h", c=2), AF.Exp, PK(b2), [('glb', t)])
                            kb.act(gt[:, 3, :], G[:, t, 3, :], AF.Exp, [gk], ['gt3'])
                            kb.tt('dve', G[:, t, 4, :], gt[:, 3, :], G[:, t, 2, :], ALU.mult, ['gt3', gk], [gk])
                        kb.barrier()
                    if dbg and L == 0:
                        kb.dma(dbg_d['gates'], G[:], reads=[('G', t) for t in range(NT)], writes=['dbg_gates'])
                    if stop == 'gates':
                        kb.barrier()
                        return True
                    with contextlib.ExitStack() as s2:
                        convw = sb(s2, "convw", [128, 32, 4])
                        kb.dma(convw[:], a_convT[j], writes=['convw'])
                        modes = ['conv_q'] * 8 + ['conv_k'] * 8 + ['conv_v'] * 16 + ['silu'] * 16
                        proj_fm(a_win[j], 0, 48, modes, hT, qkvz, s2, convw=convw)
                    kb.barrier()
                if stop == 'proj':
                    return True
                with contextlib.ExitStack() as s1:
                    wout_bf = sb(s1, "wout_bf", [128, 16, D], BF16)
                    load_wout(a_wout[j], 16, wout_bf, s1)
                    rr = gdn_tiles(L, j, G, glb, wout_bf, xsrc, xkey, last_layer, s1)
                    kb.barrier()
                    return rr

        def gdn_tiles(L, j, G, glb, wout_bf, xsrc, xkey, last_layer, st):
            HG = 8
            Sf = sb(st, "Sf", [128, 16, 128])
            Sb = sb(st, "Sb", [128, 16, 128], BF16)
            nw = sb(st, "nw", [128, 1])
            maskbf = sb(st, "maskbf", [128, 3, 128], BF16)
            qT = [sb(st, "qT%d" % i, [128, 8, 128], BF16) for i in range(2)]
            kT = [sb(st, "kT%d" % i, [128, 8, 128], BF16) for i in range(2)]
            vT = [sb(st, "vT%d" % i, [128, 16, 128], BF16) for i in range(2)]
            zs = [sb(st, "zs%d" % i, [128, 16, 128], BF16) for i in range(2)]
            Ag = [sb(st, "Ag%d" % i, [128, 128]) for i in range(2)]
            Agp = [sb(st, "Agp%d" % i, [128, 128]) for i in range(2)]
            E3 = [sb(st, "E3%d" % i, [128, 384]) for i in range(2)]
            XY = sb(st, "XY", [128, HG, 2, 128])
            Pm = sb(st, "Pm", [128, HG, 128])
            vb = sb(st, "vb", [128, HG, 128], BF16)
            kbg = sb(st, "kbg", [128, HG, 128], BF16)
            Gb = [sb(st, "Gb%d" % i, [128, 128]) for i in range(2)]
            gamb = [sb(st, "gamb%d" % i, [128, 128]) for i in range(2)]
            TT = sb(st, "TT", [128, HG, 128], BF16)
            attnT = [sb(st, "attnT%d" % i, [128, HG, 128], BF16) for i in range(2)]
            kdec = [sb(st, "kdec%d" % i, [128, HG, 128], BF16) for i in range(2)]
            qdT = [sb(st, "qdT%d" % i, [128, HG, 128], BF16) for i in range(2)]
            usb = [sb(st, "usb%d" % i, [128, HG, 128]) for i in range(2)]
            wTb = [sb(st, "wTb%d" % i, [128, HG, 128], BF16) for i in range(2)]
            vnew = sb(st, "vnew", [128, HG, 128], BF16)
            oT = sb(st, "oT", [128, HG, 128])
            osq = sb(st, "osq", [128, 512])
            rst = sb(st, "rst", [128, 512])
            lnt = sb(st, "lnt", [128, 512])
            ogT = [sb(st, "ogT%d" % i, [128, 16, 128], BF16) for i in range(2)]
            kb.memset('dve', Sf[:], 0.0, [('Sf', h) for h in range(16)])
            kb.memset('pool', Sb[:], 0.0, [('Sb', h) for h in range(16)])
            kb.dma(nw[:], a_nw[j], writes=['nw'])
            kb.copy('dve', maskbf[:, 0, :], C(C_MINCLT), ['cst'], ['maskbf'])
            kb.copy('dve', maskbf[:, 1, :], C(C_MSTRT), ['cst'], ['maskbf'])
            kb.copy('dve', maskbf[:, 2, :], C(C_MSTR), ['cst'], ['maskbf'])
            qv = qkvz[0:8].rearrange("c p t -> p c t")
            kv = qkvz[8:16].rearrange("c p t -> p c t")
            vv = qkvz[16:32].rearrange("c p t -> p c t")
            zv = qkvz[32:48].rearrange("c p t -> p c t")

            def load_qkv(t):
                b = t % 2
                tsl = slice(t * 128, (t + 1) * 128)
                kb.dma(qT[b][:], qv[:, :, tsl], writes=[('qT', b)])
                kb.dma(kT[b][:], kv[:, :, tsl], writes=[('kT', b)])
                kb.dma(vT[b][:], vv[:, :, tsl], writes=[('vT', b)])

            def load_zs(t):
                b = t % 2
                kb.dma(zs[b][:], zv[:, :, t * 128:(t + 1) * 128], writes=[('zs', b)])

            B_G, B_T = 0, 2
            B_D = (1, 3)

            def s1(t, hg, bf):
                b = t % 2
                gk = ('G', t)
                if hg == 0 and t + 1 < NT:
                    load_qkv(t + 1)
                def g_stage(hp):
                    kb.newgen(B_G)
                    kb.mm(B_G, ps[B_G][:, 0:128], kT[b][:, hp, :], kT[b][:, hp, :], [('kT', b)], PK(B_G), inc=False)
                    kb.mm(B_G, ps[B_G][:, 128:256], kT[b][:, hp, :], qT[b][:, hp, :], [('kT', b), ('qT', b)], PK(B_G))
                    ksl = hp % 2
                    kb.tr(psbf(B_T)[:, ksl * 128:(ksl + 1) * 128], kT[b][:, hp, :], ident_bf[:], [('kT', b), 'ident_bf'], PK(B_T))

                def alpha(hl):
                    h = hg * HG + hl
                    a = Ag[h % 2]
                    ap_ = Agp[h % 2]
                    kb.ts('pool', a[:], C(C_UT), G[:, t, 0, h:h + 1], None, ALU.mult, None, ['cst', gk], [('Ag', h % 2)])
                    kb.stt(ap_[:], C(C_ID), G[:, t, 1, h:h + 1], a[:], ALU.mult, ALU.add, ['cst', gk, ('Ag', h % 2)], [('Agp', h % 2)])
                    g_ = Gb[h % 2]
                    kb.ts('pool', g_[:], C(C_ONES), G[:, t, 3, h:h + 1], None, ALU.mult, None, ['cst', gk], [('Gb', h % 2)])
                    db = B_D[h % 2]
                    kb.newgen(db)
                    dk = PK(db)
                    kb.mm(db, ps[db][:, 0:128], C(C_SL), a[:], ['cst', ('Ag', h % 2)], dk, last=False, inc=False)
                    kb.mm(db, ps[db][:, 0:128], ident_bf[:], maskbf[:, 0, :], ['ident_bf', 'maskbf'], dk, inc=False)
                    kb.mm(db, ps[db][:, 128:256], C(C_SL), ap_[:], ['cst', ('Agp', h % 2)], dk, last=False, inc=False)
                    kb.mm(db, ps[db][:, 128:256], ident_bf[:], maskbf[:, 1, :], ['ident_bf', 'maskbf'], dk, inc=False)
                    kb.mm(db, ps[db][:, 256:384], ap_[:], C(C_SL), ['cst', ('Agp', h % 2)], dk, last=False, inc=False)
                    kb.mm(db, ps[db][:, 256:384], ident_bf[:], maskbf[:, 2, :], ['ident_bf', 'maskbf'], dk, inc=False)
                    kb.mm(db, ps[db][:, 384:512], g_[:], C(C_ID), ['cst', ('Gb', h % 2)], dk)
                    vs = 2 + h % 2
                    kb.tr(psbf(B_T)[:, vs * 128:(vs + 1) * 128], vT[b][:, h, :], ident_bf[:], [('vT', b), 'ident_bf'], PK(B_T))

                def beta(hl):
                    h = hg * HG + hl
                    hp = h // 2
                    db = B_D[h % 2]
                    dk = PK(db)
                    Gps = ps[B_G][:, 0:128]
                    QKps = ps[B_G][:, 128:256]
                    ksl = hp % 2
                    psK = psbf(B_T)[:, ksl * 128:(ksl + 1) * 128]
                    vs = 2 + h % 2
                    psV = psbf(B_T)[:, vs * 128:(vs + 1) * 128]
                    e3 = E3[h % 2]
                    kb.act(e3[:], ps[db][:, 0:384], AF.Exp, dk, [('E3', h % 2)])
                    gm = gamb[h % 2]
                    kb.act(gm[:], ps[db][:, 384:512], AF.Exp, dk, [('gamb', h % 2)])
                    kb.tt('dve', XY[:, hl, 0, :], e3[:, 128:256], Gps, ALU.mult, [('E3', h % 2)] + PK(B_G), [('XY', hl)])
                    kb.tt('dve', XY[:, hl, 1, :], e3[:, 256:384], Gps, ALU.mult, [('E3', h % 2)] + PK(B_G), [('XY', hl)])
                    kb.tt('dve', attnT[bf][:, hl, :], e3[:, 0:128], QKps, ALU.mult, [('E3', h % 2)] + PK(B_G), [('attnT', bf, hl)])
                    kb.stt(Pm[:, hl, :], XY[:, hl, 0, :], -1.0, C(C_ID), ALU.mult, ALU.add, [('XY', hl), 'cst'], [('Pm', hl)])
                    kb.tt('pool', qdT[bf][:, hl, :], qT[b][:, hp, :], gm[:], ALU.mult, [('qT', b), ('gamb', h % 2)], [('qdT', bf, hl)])
                    kb.act(vb[:, hl, :], psV, AF.Identity, PK(B_T) + [gk], [('vb', hl)], scale=G[:, t, 2, h:h + 1])
                    kb.act(kbg[:, hl, :], psK, AF.Identity, PK(B_T) + [gk], [('kbg', hl)], scale=G[:, t, 4, h:h + 1])
                    kb.ts('dve', kdec[bf][:, hl, :], psK, G[:, t, 5, h:h + 1], None, ALU.mult, None, PK(B_T) + [gk], [('kdec', bf, hl)])

                g_stage((hg * HG) // 2)
                alpha(0)
                for hl in range(HG):
                    if hl + 1 < HG and (hl + 1) % 2 == 1:
                        alpha(hl + 1)
                    beta(hl)
                    if hl + 1 < HG and (hl + 1) % 2 == 0:
                        g_stage((hg * HG + hl + 1) // 2)
                        alpha(hl + 1)
                    yield
                for lvl in range(NEU_LO, 6):
                    for pr in range(HG // 2):
                        bank = pr % 2
                        kb.newgen(bank)
                        for u_ in range(2):
                            hl = pr * 2 + u_
                            X = XY[:, hl, 0, :]
                            Y = XY[:, hl, 1, :]
                            o0 = u_ * 256
                            if lvl < 5:
                                kb.mm(bank, ps[bank][:, o0:o0 + 128], Y, X, [('XY', hl)], PK(bank), inc=False)
                            kb.mm(bank, ps[bank][:, o0 + 128:o0 + 256], X, Y, [('XY', hl)], PK(bank), inc=(u_ == 1))
                        if lvl < 5:
                            kb.copy('act', XY[:, pr * 2:pr * 2 + 2, :, :], ps[bank][:, :].rearrange("p (h x c) -> p h x c", h=2, x=2),
                                    PK(bank), [('XY', pr * 2), ('XY', pr * 2 + 1)])
                        else:
                            kb.copy('act', XY[:, pr * 2:pr * 2 + 2, 1, :], ps[bank][:, :].rearrange("p (h x c) -> p h x c", h=2, x=2)[:, :, 1, :],
                                    PK(bank), [('XY', pr * 2), ('XY', pr * 2 + 1)])
                        yield
                    for q4 in range(HG // 4):
                        bank = 2 + q4 % 2
                        kb.newgen(bank)
                        for u_ in range(4):
                            hl = q4 * 4 + u_
                            kb.mm(bank, ps[bank][:, u_ * 128:(u_ + 1) * 128], XY[:, hl, 1, :], Pm[:, hl, :], [('XY', hl), ('Pm', hl)], PK(bank), inc=(u_ == 3))
                        hs = slice(q4 * 4, q4 * 4 + 4)
                        pk = [('Pm', q4 * 4 + u_) for u_ in range(4)]
                        if lvl < 5:
                            kb.tt('dve', Pm[:, hs, :], Pm[:, hs, :], ps[bank][:, :].rearrange("p (h c) -> p h c", h=4), ALU.add, PK(bank) + pk, pk)
                        else:
                            kb.tt('dve', TT[:, hs, :], Pm[:, hs, :], ps[bank][:, :].rearrange("p (h c) -> p h c", h=4), ALU.add,
                                  PK(bank) + pk, [('TT', q4 * 4 + u_) for u_ in range(4)])
                        yield
                for pr in range(HG // 2):
                    bank = pr % 2
                    kb.newgen(bank)
                    for u_ in range(2):
                        hl = pr * 2 + u_
                        kb.mm(bank, ps[bank][:, u_ * 128:(u_ + 1) * 128], TT[:, hl, :], vb[:, hl, :], [('TT', hl), ('vb', hl)], PK(bank), inc=False)
                        kb.mm(bank, ps[bank][:, 256 + u_ * 128:256 + (u_ + 1) * 128], kbg[:, hl, :], TT[:, hl, :], [('TT', hl), ('kbg', hl)], PK(bank), inc=(u_ == 1))
                    kb.copy('act', usb[bf][:, pr * 2:pr * 2 + 2, :], ps[bank][:, 0:256].rearrange("p (h c) -> p h c", h=2), PK(bank),
                            [('usb', bf, pr * 2), ('usb', bf, pr * 2 + 1)])
                    kb.copy('dve', wTb[bf][:, pr * 2:pr * 2 + 2, :], ps[bank][:, 256:512].rearrange("p (h c) -> p h c", h=2), PK(bank),
                            [('wTb', bf, pr * 2), ('wTb', bf, pr * 2 + 1)])
                    yield

            B_WO = (4, 5)
            B_SS = (6, 7)
            B_N = 4

            def s2(t, hg, bf):
                b = t % 2
                if hg == 0 and t + 1 < NT:
                    load_zs(t + 1)
                for ch in range(2):
                    rs = slice(ch * 64, (ch + 1) * 64)
                    for q4 in range(HG // 4):
                        bw = B_WO[q4]
                        kb.newgen(bw)
                        for u_ in range(4):
                            hl = q4 * 4 + u_
                            h = hg * HG + hl
                            kb.mm(bw, ps[bw][rs, u_ * 128:(u_ + 1) * 128], wTb[bf][:, hl, rs], Sb[:, h, :], [('wTb', bf, hl), ('Sb', h)], PK(bw),
                                  halves=(ch,), inc=(u_ == 3))
                    for q4 in range(HG // 4):
                        bw = B_WO[q4]
                        hs = slice(q4 * 4, q4 * 4 + 4)
                        vk = [('vnew', q4 * 4 + u_) for u_ in range(4)]
                        kb.tt('dve', vnew[rs, hs, :], usb[bf][rs, hs, :], ps[bw][rs, :].rearrange("p (h c) -> p h c", h=4), ALU.subtract,
                              PK(bw) + [('usb', bf, q4 * 4 + u_) for u_ in range(4)], vk)
                    yield
                    for q4 in range(HG // 4):
                        bw = B_WO[q4]
                        bs = B_SS[q4]
                        kb.newgen(bw)
                        kb.newgen(bs)
                        for u_ in range(4):
                            hl = q4 * 4 + u_
                            h = hg * HG + hl
                            kb.mm(bw, ps[bw][:, u_ * 64:(u_ + 1) * 64], Sb[:, h, :], qdT[bf][:, hl, rs], [('Sb', h), ('qdT', bf, hl)], PK(bw), last=False, inc=False)
                            kb.mm(bw, ps[bw][:, u_ * 64:(u_ + 1) * 64], vnew[rs, hl, :], attnT[bf][rs, hl, rs], [('vnew', hl), ('attnT', bf, hl)], PK(bw), inc=(u_ == 3))
                        for u_ in range(4):
                            hl = q4 * 4 + u_
                            kb.mm(bs, ps[bs][:, u_ * 128:(u_ + 1) * 128], kdec[bf][rs, hl, :], vnew[rs, hl, :], [('kdec', bf, hl), ('vnew', hl)], PK(bs), inc=(u_ == 3))
                    yield
                    for q4 in range(HG // 4):
                        bw = B_WO[q4]
                        bs = B_SS[q4]
                        hs = slice(q4 * 4, q4 * 4 + 4)
                        kb.copy('act', oT[:, hs, rs], ps[bw][:, 0:256].rearrange("p (h c) -> p h c", h=4), PK(bw),
                                [('oT', q4 * 4 + u_) for u_ in range(4)])
                        for u_ in range(4):
                            hl = q4 * 4 + u_
                            h = hg * HG + hl
                            kb.stt(Sf[:, h, :], Sf[:, h, :], glb[:, t, ch, h:h + 1], ps[bs][:, u_ * 128:(u_ + 1) * 128], ALU.mult, ALU.add,
                                   [('Sf', h), ('glb', t)] + PK(bs), [('Sf', h)])
                        h0 = hg * HG + q4 * 4
                        kb.copy('act', Sb[:, h0:h0 + 4, :], Sf[:, h0:h0 + 4, :], [('Sf', h0 + u_) for u_ in range(4)], [('Sb', h0 + u_) for u_ in range(4)])
                    yield
                for q4 in range(HG // 4):
                    hs = slice(q4 * 4, q4 * 4 + 4)
                    h0 = hg * HG + q4 * 4
                    ok = [('oT', q4 * 4 + u_) for u_ in range(4)]
                    kb.act(osq[:].rearrange("p (h c) -> p h c", h=4), oT[:, hs, :], AF.Square, ok, ['osq'])
                    kb.newgen(B_N)
                    kb.mm(B_N, ps[B_N][:, :], C(C_ONES), osq[:], ['cst', 'osq'], PK(B_N))
                    rstd_from_ss(ps[B_N][:, :], rst[:], 128, PK(B_N), ['rst'], lnt[:])
                    kb.tt('dve', osq[:].rearrange("p (h c) -> p h c", h=4), oT[:, hs, :], rst[:].rearrange("p (h c) -> p h c", h=4), ALU.mult,
                          ok + ['rst', 'osq'], ['osq'])
                    kb.stt(ogT[b][:, h0:h0 + 4, :], osq[:].rearrange("p (h c) -> p h c", h=4), nw[:, 0:1], zs[b][:, h0:h0 + 4, :], ALU.mult, ALU.mult,
                           ['osq', 'nw', ('zs', b)], [('ogT', b)])
                    yield
                if hg == 1:
                    outproj_tile(L, t, ogT[b], 16, wout_bf, xsrc, xkey, last_layer, [('ogT', b)])
                    yield

            steps = [(t, hg) for t in range(NT) for hg in range(2)]
            load_qkv(0)
            load_zs(0)
            for _ in s1(0, 0, 0):
                pass
            RATIO = 3
            for k in range(len(steps)):
                g1 = s1(steps[k + 1][0], steps[k + 1][1], (k + 1) % 2) if k + 1 < len(steps) else None
                g2 = s2(steps[k][0], steps[k][1], k % 2)
                while g1 is not None or g2 is not None:
                    if g1 is not None:
                        for _ in range(RATIO):
                            try:
                                next(g1)
                            except StopIteration:
                                g1 = None
                                break
                    if g2 is not None:
                        try:
                            next(g2)
                        except StopIteration:
                            g2 = None

        def fox_layer(L, j, xsrc, xkey, last_layer):
            with contextlib.ExitStack() as sl:
                Vall = sb(sl, "Vall", [128, NT, 16, 65], BF16)
                cumT = sb(sl, "cumT", [128, NT, 16])
                with contextlib.ExitStack() as s1:
                    hT = sb(s1, "hT", [128, 8, S], BF16)
                    with contextlib.ExitStack() as s2:
                        A_b = sb(s2, "A_b", [128, D])
                        B_b = sb(s2, "B_b", [128, D])
                        adaln(L, A_b, B_b, s2)
                        norm_phase(L, xsrc, xkey, hT, A_b, B_b, s2)
                        kb.barrier()
                    with contextlib.ExitStack() as s2:
                        wfst = sb(s2, "wfst", [128, 8, 16])
                        wf = sb(s2, "wf", [128, 8, 16], BF16)
                        nfb = sb(s2, "nfb", [16, 1])
                        spl = sb(s2, "spl", [16, 2048])
                        cums = sb(s2, "cums", [16, S])
                        onesr = sb(s2, "onesr", [16, 2048], BF16)
                        c1b = sb(s2, "c1b", [16, S], BF16)
                        kb.dma(wfst[:], b_win[j].rearrange("(kc p) f -> p kc f", p=128)[:, :, 4096:4112], writes=['wfst'])
                        kb.copy('dve', wf[:], wfst[:], ['wfst'], ['wf'])
                        kb.dma(nfb[:], b_fb[j], writes=['nfb'])
                        kb.ts('dve', nfb[:], nfb[:], -1.0, None, ALU.mult, None, ['nfb'], ['nfb'])
                        kb.memset('pool', onesr[:], 1.0, ['onesr'])
                        for half in range(2):
                            for tl in range(4):
                                tb = half * 4 + tl
                                bank = tb % 2
                                kb.newgen(bank)
                                for kc in range(8):
                                    kb.mm(bank, ps[bank][0:16, :], wf[:, kc, :], hT[:, kc, tb * 512:(tb + 1) * 512], ['wf', ('hT', tb)], PK(bank),
                                          halves=(0,), last=(kc == 7))
                                kb.act(spl[:, tl * 512:(tl + 1) * 512], ps[bank][0:16, :], AF.Exp, PK(bank) + ['nfb'], [('spl', tl)], scale=-1.0, bias=nfb[:, 0:1])
                                kb.act(spl[:, tl * 512:(tl + 1) * 512], spl[:, tl * 512:(tl + 1) * 512], AF.Ln, [('spl', tl)], [('spl', tl)], bias=1.0)
                            init = 0.0 if half == 0 else cums[:, 2047:2048]
                            kb.op('dve', lambda g: g.tensor_tensor_scan(out=cums[:, half * 2048:(half + 1) * 2048], data0=onesr[:], data1=spl[:], initial=init,
                                                                      op0=ALU.mult, op1=ALU.add),
                                  [('spl', tl) for tl in range(4)] + ['onesr', 'cums'], ['cums'])
                        kb.ts('dve', c1b[:], cums[:], -1.0, None, ALU.mult, None, ['cums'], ['c1b'])
                        kb.dma(c1s, c1b[:], reads=['c1b'], writes=['c1s'])
                        for t in range(NT):
                            bank = 2 + t % 2
                            kb.newgen(bank)
                            kb.mm(bank, ps[bank][:, 0:16], cums[:, t * 128:(t + 1) * 128], cst[0:16, C_ID, 0:16], ['cums', 'cst'], PK(bank))
                            kb.copy('dve', cumT[:, t, :], ps[bank][:, 0:16], PK(bank), [('cumT', t)])
                        kb.barrier()
                    if stop == 'f1':
                        return True
                    with contextlib.ExitStack() as s2:
                        qn = sb(s2, "qn", [128, 1])
                        kn = sb(s2, "kn", [128, 1])
                        kb.dma(qn[:], b_qn2[j], writes=['ppsc'])
                        kb.dma(kn[:], b_kn2[j], writes=['ppsc'])
                        kb.ts('dve', qn[:], qn[:], 0.125, None, ALU.mult, None, ['ppsc'], ['ppsc'])
                        proj_fm(b_win[j], 0, 16, ['rms'] * 16, hT, qks, s2, pp_scalars=[qn[:, 0:1]] * 8 + [kn[:, 0:1]] * 8, nred=64, ones_ap=C(C_ONESBD))
                    if stop == 'f2':
                        return True
                    with contextlib.ExitStack() as s2:
                        wst = [sb(s2, "wvst%d" % i, [128, 8, 128]) for i in range(2)]
                        wvz = sb(s2, "wvz", [128, 8, 2048], BF16)
                        zt = [sb(s2, "zt%d" % i, [128, D], BF16) for i in range(2)]
                        wv = b_win[j].rearrange("(kc p) f -> p kc f", p=128)
                        for g in range(16):
                            b = g % 2
                            kb.dma(wst[b][:], wv[:, :, 2048 + g * 128:2048 + (g + 1) * 128], writes=[('wvst', b)])
                            kb.copy('pool', wvz[:, :, g * 128:(g + 1) * 128], wst[b][:], [('wvst', b)], [('wvz', g // 4)])
                        kb.memset('dve', Vall[:, :, :, 64:65], 1.0, [('Vall1',)])
                        for t in range(NT):
                            for fb in range(4):
                                bank = (t * 4 + fb) % 4
                                kb.newgen(bank)
                                for kc in range(8):
                                    kb.mm(bank, ps[bank][:, :], hT[:, kc, t * 128:(t + 1) * 128], wvz[:, kc, fb * 512:(fb + 1) * 512],
                                          [('hT', t // 4), ('wvz', fb)], PK(bank), last=(kc == 7))
                                if fb < 2:
                                    kb.copy('dve', Vall[:, t, fb * 8:(fb + 1) * 8, 0:64], ps[bank][:, :].rearrange("p (h d) -> p h d", h=8), PK(bank), [('Vall', t)])
                                else:
                                    kb.act(zt[t % 2][:, (fb - 2) * 512:(fb - 1) * 512], ps[bank][:, :], AF.Silu, PK(bank), [('zt', t % 2)])
                            kb.dma(zss[t * 128:(t + 1) * 128, :], zt[t % 2][:], reads=[('zt', t % 2)], writes=[('zss', t)])
                        kb.barrier()
                if stop == 'f3':
                    return True
                with contextlib.ExitStack() as s1:
                    Oall = sb(s1, "Oall", [128, NT, D], BF16)
                    with contextlib.ExitStack() as s2:
                        QA = [sb(s2, "QA%d" % i, [65, S], BF16) for i in range(2)]
                        KA = [sb(s2, "KA%d" % i, [65, S], BF16) for i in range(2)]
                        PT = [sb(s2, "PT%d" % i, [128, 512], BF16) for i in range(4)]
                        rl = sb(s2, "rl", [128, 4])
                        for i in range(2):
                            kb.memset('dve', KA[i][64:65, :], 1.0, [('KA1', i)])

                        def load_head(h):
                            b = h % 2
                            r0 = (h % 2) * 64
                            kb.dma(QA[b][0:64, :], qks[h // 2][r0:r0 + 64, :], writes=[('QA', b)])
                            kb.dma(QA[b][64:65, :], c1s[h:h + 1, :], reads=['c1s'], writes=[('QA', b)])
                            kb.dma(KA[b][0:64, :], qks[8 + h // 2][r0:r0 + 64, :], writes=[('KA', b)])

                        load_head(0)
                        pairs = [(h, qb, kt) for h in range(16) for qb in range(8) for kt in range(4 * (qb + 1))]
                        NSB = 4
                        LA = 2

                        def stageA(n):
                            h, qb, kt = pairs[n]
                            b = h % 2
                            if qb == 0 and kt == 0 and h + 1 < 16:
                                load_head(h + 1)
                            i0 = max(0, kt - 4 * qb)
                            sbk = n % NSB
                            kb.newgen(sbk)
                            kb.mm(sbk, ps[sbk][:, i0 * 128:512], KA[b][0:65, kt * 128:(kt + 1) * 128], QA[b][0:65, qb * 512 + i0 * 128:(qb + 1) * 512],
                                  [('KA', b), ('KA1', b), ('QA', b)], PK(sbk))

                        def stageB(n):
                            h, qb, kt = pairs[n]
                            nkt = 4 * (qb + 1)
                            jd = kt - 4 * qb
                            i0 = max(0, jd)
                            sbk = n % NSB
                            pt = PT[n % 4]
                            ptk = ('PT', n % 4)
                            obk = 4 + (h * 8 + qb) % 2
                            if kt == 0:
                                kb.newgen(obk)
                            kb.act(pt[:, i0 * 128:512], ps[sbk][:, i0 * 128:512], AF.Exp, PK(sbk) + [('cumT', kt)], [ptk], bias=cumT[:, kt, h:h + 1])
                            if jd >= 0:
                                kb.tt('pool', pt[:, jd * 128:(jd + 1) * 128], pt[:, jd * 128:(jd + 1) * 128], caus_bf[:], ALU.mult, [ptk, 'caus_bf'], [ptk])
                            for i in range(i0, 4):
                                kb.mm(obk, ps[obk][:, i * 65:(i + 1) * 65], pt[:, i * 128:(i + 1) * 128], Vall[:, kt, h, :],
                                      [ptk, ('Vall', kt), ('Vall1',)], PK(obk), last=(kt == nkt - 1), inc=(i == 3))
                            if kt == nkt - 1:
                                ov = ps[obk][:, 0:260].rearrange("p (i d) -> p i d", i=4)
                                kb.op('dve', lambda g: g.reciprocal(out=rl[:], in_=ov[:, :, 64]), PK(obk), ['rl'])
                                kb.tt('dve', Oall[:, qb * 4:(qb + 1) * 4, h * 64:(h + 1) * 64], ov[:, :, 0:64], rl[:].unsqueeze(2).broadcast_to([128, 4, 64]),
                                      ALU.mult, PK(obk) + ['rl'], [('Oall', qb)])

                        for n in range(len(pairs) + LA):
                            if n < len(pairs):
                                stageA(n)
                            if n - LA >= 0:
                                stageB(n - LA)
                        kb.barrier()
                    if stop == 'f4':
                        return True
                    with contextlib.ExitStack() as s2:
                        wout_bf = sb(s2, "wout_bf", [128, 8, D], BF16)
                        load_wout(b_wout[j], 8, wout_bf, s2)
                        zt = [sb(s2, "zt%d" % i, [128, D], BF16) for i in range(2)]
                        og = [sb(s2, "og%d" % i, [128, D], BF16) for i in range(2)]
                        ogT = [sb(s2, "ogT%d" % i, [128, 8, 128], BF16) for i in range(2)]
                        for t in range(NT):
                            b = t % 2
                            kb.dma(zt[b][:], zss[t * 128:(t + 1) * 128, :], reads=[('zss', t)], writes=[('zt', b)])
                            kb.tt('pool', og[b][:], Oall[:, t, :], zt[b][:], ALU.mult, [('Oall', t // 4), ('zt', b)], [('og', b)])
                            bank = 4 + b
                            for kc in range(8):
                                kb.tr(psbf(bank)[:, kc * 128:(kc + 1) * 128], og[b][:, kc * 128:(kc + 1) * 128], ident_bf[:], [('og', b), 'ident_bf'], PK(bank))
                            kb.copy('act', ogT[b][:], psbf(bank).rearrange("p (k t) -> p k t", k=8), PK(bank), [('ogT', b)])
                            outproj_tile(L, t, ogT[b], 8, wout_bf, xsrc, xkey, last_layer, [('ogT', b)])
                        kb.barrier()

        xsrc, xkey = x_in, 'xin'
        for L in range(n_layers):
            last = (L == n_layers - 1)
            if L % 2 == 0:
                stopped = gdn_layer(L, L // 2, xsrc, xkey, last)
            else:
                stopped = fox_layer(L, L // 2, xsrc, xkey, last)
            xsrc, xkey = xres, 'xres'
            kb.barrier()
            if stopped:
                break
        if dbg:
            kb.dma(dbg_d['xres'], xres, reads=[('xres', t) for t in range(NT)], writes=['dbg_x'])
            kb.dma(dbg_d['qkvz'], qkvz, writes=['dbg_q'])
        kb.barrier()
        print("instructions emitted:", kb.ninst, {k: v for k, v in kb.cnt.items()})
    return nc


def make_in_maps(inputs):
    consts = make_consts()
    f = lambda a: np.ascontiguousarray(np.asarray(a, dtype=np.float32))
    x = f(inputs["x"])
    c = f(inputs["c"])
    shared = {
        "norm_w": f(inputs["norm_w"]),
        "final_norm_w": f(inputs["final_norm_w"]).reshape(1, D),
        "ada_w": f(inputs["ada_w"]),
        "ada_b": f(inputs["ada_b"]),
        "a_w_in": f(inputs["a_w_in"]),
        "a_convT": f(np.transpose(f(inputs["a_conv_w"]), (0, 2, 1)).reshape(2, 32, 128, 4).transpose(0, 2, 1, 3)),
        "a_A_log": f(inputs["a_A_log"]),
        "a_dt_bias": f(inputs["a_dt_bias"]),
        "a_norm_w": f(inputs["a_norm_w"]).reshape(2, 128, 1),
        "a_w_out": f(inputs["a_w_out"]),
        "b_w_in": f(inputs["b_w_in"]),
        "b_f_bias": f(inputs["b_f_bias"]).reshape(2, 16, 1),
        "b_qn2": f(np.tile(f(inputs["b_qn_w"]), (1, 2))).reshape(2, 128, 1),
        "b_kn2": f(np.tile(f(inputs["b_kn_w"]), (1, 2))).reshape(2, 128, 1),
        "b_w_out": f(inputs["b_w_out"]),
        "consts": consts,
    }
    maps = []
    for b in range(8):
        m = dict(shared)
        m["x"] = x[b]
        m["cT"] = f(c[b].reshape(8, 128).T)
        maps.append(m)
    return maps


_NC_CACHE = {}


def kernel(**inputs):
    if 'nc' not in _NC_CACHE:
        _NC_CACHE['nc'] = build_program()
    nc = _NC_CACHE['nc']
    in_maps = make_in_maps(inputs)
    res = run_bass_kernel_spmd(nc, in_maps, core_ids=list(range(8)))
    out = np.stack([np.asarray(r["out"], dtype=np.float32) for r in res.results], axis=0)
    return out
```

```python
import contextlib
import numpy as np
import concourse.bass as bass
import concourse.mybir as mybir
from concourse.bass_utils import run_bass_kernel_spmd

F32 = mybir.dt.float32
BF16 = mybir.dt.bfloat16
AF = mybir.ActivationFunctionType
ALU = mybir.AluOpType
AX = mybir.AxisListType

S = 4096
D = 1024
NT = 32
EPS = 1e-6
NEG = -30000.0
DEPTH = 4
GDN_IN = 6176
FOX_IN = 4112

C_ID, C_ONES, C_UT, C_SL, C_MINCLT, C_MSTRT, C_MSTR, C_TRIBD, C_SELC, C_SELA, C_SELB, C_CAUS, C_ONESBD = range(13)
NCONST = 13


def make_consts():
    i = np.arange(128)
    r = i[:, None]
    c = i[None, :]
    same = (r // 64) == (c // 64)
    m = np.zeros((NCONST, 128, 128), np.float32)
    m[C_ID] = (r == c)
    m[C_ONES] = 1.0
    m[C_UT] = (r <= c)
    m[C_SL] = (r > c)
    m[C_MINCLT] = np.where(same & (r <= c), 0.0, NEG)
    m[C_MSTRT] = np.where(same & (r < c), 0.0, NEG)
    m[C_MSTR] = np.where(same & (r > c), 0.0, NEG)
    m[C_TRIBD] = (same & (r <= c))
    m[C_SELC] = (r == (c // 64) * 64 + 63)
    m[C_SELA] = (r == 63) * np.ones((1, 128))
    m[C_SELB] = (r == 127) * np.ones((1, 128))
    m[C_CAUS] = (r <= c)
    m[C_ONESBD] = same
    return m.astype(np.float32)


class KB:
    NS = 24

    def __init__(self, nc, es):
        self.nc = nc
        self.eng = {'pe': nc.tensor, 'act': nc.scalar, 'dve': nc.vector, 'pool': nc.gpsimd, 'sp': nc.sync}
        self.sem = {k: es.enter_context(nc.semaphore("s_" + k)) for k in ['pe', 'act', 'dve', 'pool']}
        self.cnt = {k: 0 for k in self.sem}
        self.seen = {k: {} for k in self.eng}
        self.dsem = [es.enter_context(nc.semaphore("d%d" % i)) for i in range(self.NS)]
        self.dval = [0] * self.NS
        self.dnext = 0
        self.lastw = {}
        self.readers = {}
        self.fresh = {}
        self.ninst = 0

    def _wait(self, e, tok):
        sk, v = tok
        if sk == e and e == 'pe':
            return
        if self.seen[e].get(sk, 0) >= v:
            return
        sem = self.sem[sk] if isinstance(sk, str) else self.dsem[sk[1]]
        self.eng[e].wait_ge(sem, v)
        self.seen[e][sk] = v

    def _deps(self, e, reads, writes):
        for k in reads:
            t = self.lastw.get(k)
            if t is not None:
                self._wait(e, t)
            if isinstance(k, tuple) and k[0] == 'ps':
                for sk, t in self.readers.get(k, {}).items():
                    if sk != e:
                        self._wait(e, t)
        for k in writes:
            t = self.lastw.get(k)
            if t is not None:
                self._wait(e, t)
            for t in self.readers.get(k, {}).values():
                self._wait(e, t)

    def _record(self, tok, reads, writes):
        for k in reads:
            self.readers.setdefault(k, {})[tok[0]] = tok
        for k in writes:
            self.lastw[k] = tok
            self.readers[k] = {}

    def op(self, e, fn, reads=(), writes=(), inc=True):
        self._deps(e, reads, writes)
        ins = fn(self.eng[e])
        self.ninst += 1
        if inc:
            self.cnt[e] += 1
            ins.then_inc(self.sem[e], 1)
            tok = (e, self.cnt[e])
        else:
            tok = (e, self.cnt[e] + 1)
        self._record(tok, reads, writes)
        return tok

    def dma(self, out, in_, reads=(), writes=(), q='sp'):
        i = self.dnext
        self.dnext = (self.dnext + 1) % self.NS
        if self.dval[i] > 0:
            self._wait(q, (('d', i), self.dval[i]))
        self._deps(q, reads, writes)
        ins = self.eng[q].dma_start(out=out, in_=in_)
        self.ninst += 1
        self.dval[i] += 16
        ins.then_inc(self.dsem[i], 16)
        tok = (('d', i), self.dval[i])
        self._record(tok, reads, writes)
        return tok

    def barrier(self):
        for e in self.eng:
            for o in self.sem:
                if self.cnt[o] > 0:
                    self._wait(e, (o, self.cnt[o]))
            for i in range(self.NS):
                if self.dval[i] > 0:
                    self._wait(e, (('d', i), self.dval[i]))
        self.lastw = {}
        self.readers = {}

    def newgen(self, bank):
        self.fresh[(bank, 0)] = True
        self.fresh[(bank, 1)] = True

    def mm(self, bank, out, lhsT, rhs, reads, writes, halves=(0, 1), last=True, inc=None):
        st = False
        for h in halves:
            if self.fresh.get((bank, h), True):
                st = True
            self.fresh[(bank, h)] = False
        if inc is None:
            inc = last

        def fn(e):
            return e.matmul(out, lhsT=lhsT, rhs=rhs, start=st, stop=last, skip_group_check=True)
        return self.op('pe', fn, reads, writes, inc=inc)

    def tr(self, out, in_, ident, reads, writes):
        return self.op('pe', lambda e: e.transpose(out, in_, ident), reads, writes)

    def act(self, out, in_, func, reads, writes, scale=None, bias=None):
        def fn(e):
            kw = {}
            if scale is not None:
                kw['scale'] = scale
            if bias is not None:
                kw['bias'] = bias
            return e.activation(out=out, in_=in_, func=func, **kw)
        return self.op('act', fn, reads, writes)

    def tt(self, e, out, in0, in1, op, reads, writes):
        return self.op(e, lambda g: g.tensor_tensor(out=out, in0=in0, in1=in1, op=op), reads, writes)

    def ts(self, e, out, in0, s1, s2, op0, op1, reads, writes):
        if op1 is None and e == 'pool' and op0 == ALU.mult:
            op1, s2 = ALU.add, 0.0
        if op1 is None:
            return self.op(e, lambda g: g.tensor_scalar(out=out, in0=in0, scalar1=s1, scalar2=None, op0=op0), reads, writes)
        return self.op(e, lambda g: g.tensor_scalar(out=out, in0=in0, scalar1=s1, scalar2=s2, op0=op0, op1=op1), reads, writes)

    def stt(self, out, in0, scalar, in1, op0, op1, reads, writes):
        return self.op('dve', lambda g: g.scalar_tensor_tensor(out=out, in0=in0, scalar=scalar, in1=in1, op0=op0, op1=op1), reads, writes)

    def copy(self, e, out, in_, reads, writes):
        if e == 'act':
            return self.op('act', lambda g: g.copy(out=out, in_=in_), reads, writes)
        return self.op(e, lambda g: g.tensor_copy(out=out, in_=in_), reads, writes)

    def memset(self, e, ap, val, writes):
        return self.op(e, lambda g: g.memset(ap, val), (), writes)


class _Stop(Exception):
    pass


UWB = 7
GRATIO = 3
NEU_LO = 1


def build_program(n_layers=DEPTH, dbg=False, stop=None):
    nc = bass.Bass("TRN2", target_bir_lowering=False)

    def din(name, shape, dt=F32):
        return nc.dram_tensor(name, list(shape), dt, kind="ExternalInput").ap()

    def dscr(name, shape, dt):
        return nc.dram_tensor(name, list(shape), dt, kind="Internal").ap()

    x_in = din("x", [S, D])
    cT_in = din("cT", [128, 8])
    normw_in = din("norm_w", [DEPTH, D])
    fnw_in = din("final_norm_w", [1, D])
    adaw_in = din("ada_w", [DEPTH, D, 3 * D])
    adab_in = din("ada_b", [DEPTH, 3 * D])
    a_win = din("a_w_in", [2, D, GDN_IN])
    a_convT = din("a_convT", [2, 128, 32, 4])
    a_Alog = din("a_A_log", [2, 16])
    a_dtb = din("a_dt_bias", [2, 16])
    a_nw = din("a_norm_w", [2, 128, 1])
    a_wout = din("a_w_out", [2, 2048, D])
    b_win = din("b_w_in", [2, D, FOX_IN])
    b_fb = din("b_f_bias", [2, 16, 1])
    b_qn2 = din("b_qn2", [2, 128, 1])
    b_kn2 = din("b_kn2", [2, 128, 1])
    b_wout = din("b_w_out", [2, D, D])
    consts_in = din("consts", [NCONST, 128, 128])
    out_d = nc.dram_tensor("out", [S, D], F32, kind="ExternalOutput").ap()

    xres = dscr("xres", [S, D], F32)
    qkvz = dscr("qkvz", [48, 128, S], BF16)
    qks = dscr("qks", [16, 128, S], BF16)
    zss = dscr("zss", [S, D], BF16)
    c1s = dscr("c1s", [16, S], BF16)
    dbg_d = {}
    if dbg:
        dbg_d['hT'] = nc.dram_tensor("dbg_hT", [128, 8, S], BF16, kind="ExternalOutput").ap()
        dbg_d['xres'] = nc.dram_tensor("dbg_xres", [S, D], F32, kind="ExternalOutput").ap()
        dbg_d['qkvz'] = nc.dram_tensor("dbg_qkvz", [48, 128, S], BF16, kind="ExternalOutput").ap()
        dbg_d['gates'] = nc.dram_tensor("dbg_gates", [128, NT, 6, 16], F32, kind="ExternalOutput").ap()

    es = contextlib.ExitStack()
    with es:
        kb = KB(nc, es)

        uid = [0]

        def sb(stack, name, shape, dt=F32):
            uid[0] += 1
            return stack.enter_context(nc.sbuf_tensor("%s_%d" % (name, uid[0]), list(shape), dt))

        ps = [es.enter_context(nc.psum_tensor("ps%d" % i, [128, 512], F32)) for i in range(8)]

        def PK(bank, lo=0, hi=512):
            return [('ps', bank)]

        def psbf(i):
            return ps[i][:].bitcast(BF16)

        cst = sb(es, "cst", [128, NCONST, 128])
        ident_bf = sb(es, "ident_bf", [128, 128], BF16)
        caus_bf = sb(es, "caus_bf", [128, 128], BF16)
        condT = sb(es, "condT", [128, 8])
        gate_b = sb(es, "gate_b", [128, D])
        fnw_b = sb(es, "fnw_b", [128, D])
        xt = [sb(es, "xt%d" % i, [128, D]) for i in range(2)]
        junk = sb(es, "junk", [128, D])
        sm = sb(es, "sm", [128, 8])
        ones_row = sb(es, "ones_row", [1, 128])

        def C(i):
            return cst[:, i, :]

        kb.dma(cst[:], consts_in.rearrange("n p f -> p n f"), writes=['cst'])
        kb.copy('dve', ident_bf[:], C(C_ID), ['cst'], ['ident_bf'])
        kb.copy('dve', caus_bf[:], C(C_CAUS), ['cst'], ['caus_bf'])
        kb.memset('dve', ones_row[:], 1.0, ['ones_row'])
        kb.dma(condT[:], cT_in, writes=['condT'])
        kb.act(condT[:], condT[:], AF.Silu, ['condT'], ['condT'])
        kb.dma(fnw_b[:], fnw_in.partition_broadcast(128), writes=['fnw_b'])

        def rstd_from_ss(ss_ap, out_ap, n, rkeys, wkeys, tmp_ap):
            kb.act(tmp_ap, ss_ap, AF.Ln, rkeys, [('tmpln',)], scale=1.0 / n, bias=EPS)
            kb.act(out_ap, tmp_ap, AF.Exp, [('tmpln',)], wkeys, scale=-0.5)

        def adaln(L, A_b, B_b, stack):
            with contextlib.ExitStack() as s2:
                adaw = [sb(s2, "adaw%d" % i, [128, 8, 256]) for i in range(2)]
                modrow = sb(s2, "modrow", [1, 3 * D])
                nwrow = sb(s2, "nwrow", [1, D])
                arow = sb(s2, "arow", [1, D])
                kb.dma(modrow[:], adab_in[L:L + 1, :], writes=[('modrow', i) for i in range(6)])
                kb.dma(nwrow[:], normw_in[L:L + 1, :], writes=['nwrow'])
                wv = adaw_in[L].rearrange("(kc p) f -> p kc f", p=128)
                for fb in range(12):
                    b = fb % 2
                    kb.dma(adaw[b][:], wv[:, :, fb * 256:(fb + 1) * 256], writes=[('adaw', b)])
                    bank = fb % 2
                    kb.newgen(bank)
                    for kc in range(8):
                        kb.mm(bank, ps[bank][0:1, 0:256], condT[:, kc:kc + 1], adaw[b][:, kc, :],
                              ['condT', ('adaw', b)], PK(bank), halves=(0,), last=(kc == 7))
                    kb.tt('dve', modrow[0:1, fb * 256:(fb + 1) * 256], ps[bank][0:1, 0:256], modrow[0:1, fb * 256:(fb + 1) * 256],
                          ALU.add, PK(bank) + [('modrow', fb // 2)], [('modrow', fb // 2)])
                kb.stt(arow[0:1, :], modrow[0:1, D:2 * D], 1.0, nwrow[0:1, :], ALU.add, ALU.mult,
                       [('modrow', 2), ('modrow', 3), 'nwrow'], ['arow'])
                srcs = [(arow[0:1, :], A_b, ['arow'], 'A_b'), (modrow[0:1, 0:D], B_b, [('modrow', 0), ('modrow', 1)], 'B_b'),
                        (modrow[0:1, 2 * D:3 * D], gate_b, [('modrow', 4), ('modrow', 5)], 'gate_b')]
                n = 0
                for (src, dst, rk, dname) in srcs:
                    for half in range(2):
                        bank = 2 + (n % 2)
                        n += 1
                        kb.newgen(bank)
                        kb.mm(bank, ps[bank][:, :], ones_row[0:1, :], src[0:1, half * 512:(half + 1) * 512],
                              rk + ['ones_row'], PK(bank))
                        kb.copy('act', dst[:, half * 512:(half + 1) * 512], ps[bank][:, :], PK(bank), [(dname, half)])
                kb.barrier()

        def norm_phase(L, xsrc, xkey, hT, A_b, B_b, stack):
            hn = sb(stack, "hn", [128, D])
            hb = [sb(stack, "hb%d" % i, [128, D], BF16) for i in range(2)]
            for t in range(NT):
                b = t % 2
                kb.dma(xt[b][:], xsrc[t * 128:(t + 1) * 128, :], reads=[(xkey, t)], writes=[('xt', b)])
                kb.act(junk[:], xt[b][:], AF.Square, [('xt', b)], ['junk'])
                kb.op('dve', lambda g: g.tensor_reduce(out=sm[:, 0:1], in_=junk[:], axis=AX.X, op=ALU.add), ['junk'], [('sm', 0)])
                rstd_from_ss(sm[:, 0:1], sm[:, 2:3], D, [('sm', 0)], [('sm', 2)], sm[:, 1:2])
                kb.stt(hn[:], xt[b][:], sm[:, 2:3], A_b[:], ALU.mult, ALU.mult, [('xt', b), ('sm', 2), ('A_b', 0), ('A_b', 1)], ['hn'])
                kb.tt('pool', hb[b][:], hn[:], B_b[:], ALU.add, ['hn', ('B_b', 0), ('B_b', 1)], [('hb', b)])
                bank = 4 + b
                for kc in range(8):
                    kb.tr(psbf(bank)[:, kc * 128:(kc + 1) * 128], hb[b][:, kc * 128:(kc + 1) * 128], ident_bf[:],
                          [('hb', b), 'ident_bf'], PK(bank))
                kb.copy('act', hT[:, :, t * 128:(t + 1) * 128], psbf(bank).rearrange("p (k t) -> p k t", k=8),
                        PK(bank), [('hT', t // 4)])

        def outproj_tile(L, t, ogT, KC, wout_bf, xsrc, xkey, last_layer, ykeys):
            b = t % 2
            kb.dma(xt[b][:], xsrc[t * 128:(t + 1) * 128, :], reads=[(xkey, t)], writes=[('xt', b)])
            for fb in range(2):
                bank = 6 + fb
                kb.newgen(bank)
                for kc in range(KC):
                    kb.mm(bank, ps[bank][:, :], ogT[:, kc, :], wout_bf[:, kc, fb * 512:(fb + 1) * 512],
                          ykeys + ['wout_bf'], PK(bank), last=(kc == KC - 1))
                kb.tt('dve', junk[:, fb * 512:(fb + 1) * 512], ps[bank][:, :], gate_b[:, fb * 512:(fb + 1) * 512], ALU.mult,
                      PK(bank) + [('gate_b', fb)], [('junkh', fb)])
                kb.tt('pool', xt[b][:, fb * 512:(fb + 1) * 512], junk[:, fb * 512:(fb + 1) * 512], xt[b][:, fb * 512:(fb + 1) * 512],
                      ALU.add, [('junkh', fb), ('xt', b)], [('xt', b)])
            if not last_layer:
                kb.dma(xres[t * 128:(t + 1) * 128, :], xt[b][:], reads=[('xt', b)], writes=[('xres', t)])
            else:
                kb.act(junk[:], xt[b][:], AF.Square, [('xt', b)], [('junkh', 0), ('junkh', 1)])
                kb.op('dve', lambda g: g.tensor_reduce(out=sm[:, 4:5], in_=junk[:], axis=AX.X, op=ALU.add),
                      [('junkh', 0), ('junkh', 1)], [('sm', 4)])
                rstd_from_ss(sm[:, 4:5], sm[:, 6:7], D, [('sm', 4)], [('sm', 6)], sm[:, 5:6])
                kb.stt(xt[b][:], xt[b][:], sm[:, 6:7], fnw_b[:], ALU.mult, ALU.mult, [('xt', b), ('sm', 6), 'fnw_b'], [('xt', b)])
                kb.dma(out_d[t * 128:(t + 1) * 128, :], xt[b][:], reads=[('xt', b)], writes=[('out', t)])

        def load_wout(wout_dram, KC, wout_bf, stack):
            with contextlib.ExitStack() as s2:
                wst = [sb(s2, "wost%d" % i, [128, D]) for i in range(2)]
                wv = wout_dram.rearrange("(kc p) f -> p kc f", p=128)
                for g in range(KC):
                    b = g % 2
                    kb.dma(wst[b][:], wv[:, g, :], writes=[('wost', b)])
                    kb.copy('pool', wout_bf[:, g, :], wst[b][:], [('wost', b)], ['wout_bf'])
                kb.barrier()

        def proj_fm(w2d, col0, nch, modes, hT, scratch, stack, convw=None, pp_scalars=None, nred=128, ones_ap=None):
            with contextlib.ExitStack() as s2:
                wst = [sb(s2, "wst%d" % i, [128, 8, 256]) for i in range(2)]
                wbf = [sb(s2, "wbf%d" % i, [128, 8, 256], BF16) for i in range(2)]
                NBUF = 6 if any(m.startswith('conv') for m in modes) else 3
                obuf = [sb(s2, "obuf%d" % i, [128, 512], BF16) for i in range(NBUF)]
                pre = [sb(s2, "pre%d" % i, [128, 515]) for i in range(3)] if any(m.startswith('conv') for m in modes) else None
                acc = [sb(s2, "acc%d" % i, [128, 512]) for i in range(NBUF)]
                sq2 = [sb(s2, "sq2%d" % i, [128, 512]) for i in range(NBUF)]
                lnv = sb(s2, "lnv", [128, 512])
                rn = [sb(s2, "rn%d" % i, [128, 512]) for i in range(2)]
                wv = w2d.rearrange("(kc p) f -> p kc f", p=128)
                nslab = (nch + 1) // 2
                blocks = [(c, tb) for c in range(nch) for tb in range(8)]
                NB = len(blocks)

                def load_slab(sl):
                    sbf = sl % 2
                    ncs = min(2, nch - sl * 2)
                    kb.dma(wst[sbf][:, :, 0:ncs * 128], wv[:, :, col0 + sl * 256: col0 + sl * 256 + ncs * 128], writes=[('wst', sbf)])
                    kb.copy('pool', wbf[sbf][:, :, 0:ncs * 128], wst[sbf][:, :, 0:ncs * 128], [('wst', sbf)], [('wbf', sbf)])

                def stageA(n):
                    c, tb = blocks[n]
                    sl, ci = c // 2, c % 2
                    sbf = sl % 2
                    if ci == 0 and tb == 0 and sl + 1 < nslab:
                        load_slab(sl + 1)
                    bank = n % 4
                    kb.newgen(bank)
                    for kc in range(8):
                        kb.mm(bank, ps[bank][:, :], wbf[sbf][:, kc, ci * 128:(ci + 1) * 128], hT[:, kc, tb * 512:(tb + 1) * 512],
                              [('wbf', sbf), ('hT', tb)], PK(bank), last=(kc == 7))

                def out_dma(n):
                    c, tb = blocks[n]
                    ob = n % NBUF
                    kb.dma(scratch[c][:, tb * 512:(tb + 1) * 512], obuf[ob][:], reads=[('obuf', ob)], writes=[('scr', c, tb)])

                def stageB1(n):
                    c, tb = blocks[n]
                    mode = modes[c]
                    bank = n % 4
                    ob = n % NBUF
                    osl = obuf[ob][:, :]
                    okey = [('obuf', ob)]
                    a = acc[n % NBUF]
                    ak = ('acc', n % NBUF)
                    q2 = sq2[n % NBUF]
                    qk = ('sq2', n % NBUF)
                    if mode == 'silu':
                        kb.act(osl, ps[bank][:, :], AF.Silu, PK(bank), okey)
                        out_dma(n)
                    elif mode == 'rms':
                        kb.copy('act', a[:], ps[bank][:, :], PK(bank), [ak])
                        kb.tt('pool', q2[:], a[:], a[:], ALU.mult, [ak], [qk])
                    else:
                        p = pre[n % 3]
                        pk = ('pre', n % 3)
                        pprev = pre[(n - 1) % 3]
                        pkprev = ('pre', (n - 1) % 3)
                        kb.copy('act', p[:, 3:515], ps[bank][:, :], PK(bank), [pk])
                        kb.act(a[:], ps[bank][:, :], AF.Identity, PK(bank) + ['convw'], [ak], scale=convw[:, c, 3:4])
                        if tb == 0:
                            kb.memset('pool', p[:, 0:3], 0.0, [pk])
                        else:
                            kb.copy('pool', p[:, 0:3], pprev[:, 512:515], [pkprev], [pk])
                        for jj in (2, 1, 0):
                            kb.stt(a[:], p[:, jj:jj + 512], convw[:, c, jj:jj + 1], a[:], ALU.mult, ALU.add, [pk, 'convw', ak], [ak])
                        if mode == 'conv_v':
                            kb.act(osl, a[:], AF.Silu, [ak], okey)
                            out_dma(n)
                        else:
                            kb.act(a[:], a[:], AF.Silu, [ak], [ak])
                            kb.tt('pool', q2[:], a[:], a[:], ALU.mult, [ak], [qk])

                def stageB2(n):
                    c, tb = blocks[n]
                    mode = modes[c]
                    if mode in ('silu', 'conv_v'):
                        return
                    ob = n % NBUF
                    osl = obuf[ob][:, :]
                    okey = [('obuf', ob)]
                    a = acc[n % NBUF]
                    ak = ('acc', n % NBUF)
                    q2 = sq2[n % NBUF]
                    qk = ('sq2', n % NBUF)
                    nb = 4 + n % 2
                    r = rn[n % 2]
                    rk = ('rn', n % 2)
                    kb.newgen(nb)
                    if mode == 'rms':
                        kb.mm(nb, ps[nb][:, :], ones_ap, q2[:], [qk, 'cst'], PK(nb))
                        rstd_from_ss(ps[nb][:, :], r[:], nred, PK(nb), [rk], lnv[:])
                        kb.stt(osl, a[:], pp_scalars[c], r[:], ALU.mult, ALU.mult, [ak, rk, 'ppsc'], okey)
                    else:
                        kb.mm(nb, ps[nb][:, :], C(C_ONES), q2[:], [qk, 'cst'], PK(nb))
                        kb.act(lnv[:], ps[nb][:, :], AF.Ln, PK(nb), [('tmpln',)], bias=EPS)
                        kb.act(r[:], lnv[:], AF.Exp, [('tmpln',)], [rk], scale=-0.5)
                        sc = (128.0 ** -0.5) if mode == 'conv_q' else 1.0
                        kb.stt(osl, a[:], sc, r[:], ALU.mult, ALU.mult, [ak, rk], okey)
                    out_dma(n)

                load_slab(0)
                for n in range(NB + 7):
                    if n < NB:
                        stageA(n)
                    if NBUF == 6:
                        if n >= 6 and (n - 6) % 4 == 0:
                            for m in range(n - 6, n - 2):
                                if 0 <= m < NB:
                                    stageB2(m)
                    elif 0 <= n - 3 < NB:
                        stageB2(n - 3)
                    if 0 <= n - 1 < NB:
                        stageB1(n - 1)
                kb.barrier()

        def gdn_layer(L, j, xsrc, xkey, last_layer):
            with contextlib.ExitStack() as sl:
                G = sb(sl, "G", [128, NT, 6, 16])
                glb = sb(sl, "glb", [128, NT, 2, 16])
                with contextlib.ExitStack() as s1:
                    A_b = sb(s1, "A_b", [128, D])
                    B_b = sb(s1, "B_b", [128, D])
                    adaln(L, A_b, B_b, s1)
                    hT = sb(s1, "hT", [128, 8, S], BF16)
                    with contextlib.ExitStack() as s2:
                        norm_phase(L, xsrc, xkey, hT, A_b, B_b, s2)
                        kb.barrier()
                    if dbg and L == 0:
                        kb.dma(dbg_d['hT'], hT[:], reads=[('hT', i) for i in range(8)], writes=['dbg_hT'])
                    if stop == 'norm':
                        kb.barrier()
                        return True
                    with contextlib.ExitStack() as s2:
                        wbast = sb(s2, "wbast", [128, 8, 32])
                        wba = sb(s2, "wba", [128, 8, 32], BF16)
                        dtb = sb(s2, "dtb", [128, 16])
                        negA = sb(s2, "negA", [128, 16])
                        gt = sb(s2, "gt", [128, 8, 16])
                        kb.dma(wbast[:], a_win[j].rearrange("(kc p) f -> p kc f", p=128)[:, :, 6144:6176], writes=['wbast'])
                        kb.copy('dve', wba[:], wbast[:], ['wbast'], ['wba'])
                        kb.dma(dtb[:], a_dtb[j:j + 1, :].partition_broadcast(128), writes=['dtb'])
                        kb.dma(negA[:], a_Alog[j:j + 1, :].partition_broadcast(128), writes=['negA'])
                        kb.act(negA[:], negA[:], AF.Exp, ['negA'], ['negA'])
                        kb.ts('dve', negA[:], negA[:], -1.0, None, ALU.mult, None, ['negA'], ['negA'])
                        for t in range(NT):
                            bank = t % 2
                            kb.newgen(bank)
                            for kc in range(8):
                                kb.mm(bank, ps[bank][:, 0:32], hT[:, kc, t * 128:(t + 1) * 128], wba[:, kc, :],
                                      [('hT', t // 4), 'wba'], PK(bank), last=(kc == 7))
                            gk = ('G', t)
                            kb.tt('dve', gt[:, 0, :], ps[bank][:, 16:32], dtb[:], ALU.add, PK(bank) + ['dtb'], ['gt0'])
                            kb.act(gt[:, 0, :], gt[:, 0, :], AF.Exp, ['gt0'], ['gt0'])
                            kb.act(gt[:, 0, :], gt[:, 0, :], AF.Ln, ['gt0'], ['gt0'], bias=1.0)
                            kb.tt('dve', G[:, t, 0, :], gt[:, 0, :], negA[:], ALU.mult, ['gt0', 'negA'], [gk])
                            kb.act(gt[:, 1, :], ps[bank][:, 0:16], AF.Exp, PK(bank), ['gt1'], scale=-1.0)
                            kb.act(gt[:, 1, :], gt[:, 1, :], AF.Ln, ['gt1'], ['gt1'], bias=1.0)
                            kb.act(G[:, t, 2, :], gt[:, 1, :], AF.Exp, ['gt1'], [gk], scale=-1.0)
                            kb.ts('dve', G[:, t, 1, :], gt[:, 1, :], -1.0, None, ALU.mult, None, ['gt1'], [gk])
                            b2 = 2 + t % 2
                            kb.newgen(b2)
                            kb.mm(b2, ps[b2][:, 0:16], C(C_TRIBD), G[:, t, 0, :], ['cst', gk], PK(b2))
                            kb.copy('dve', G[:, t, 3, :], ps[b2][:, 0:16], PK(b2), [gk])
                            kb.mm(b2, ps[b2][:, 16:32], C(C_SELC), G[:, t, 3, :], ['cst', gk], PK(b2))
                            kb.mm(b2, ps[b2][:, 32:48], C(C_SELA), G[:, t, 3, :], ['cst', gk], PK(b2))
                            kb.mm(b2, ps[b2][:, 48:64], C(C_SELB), G[:, t, 3, :], ['cst', gk], PK(b2))
                            kb.tt('dve', gt[:, 2, :], ps[b2][:, 16:32], G[:, t, 3, :], ALU.subtract, PK(b2) + [gk], ['gt2'])
                            kb.act(G[:, t, 5, :], gt[:, 2, :], AF.Exp, ['gt2'], [gk])
                            kb.act(glb[:, t, :, :], ps[b2][:, 32:64].rearrange("p (c h) -> p c h", c=2), AF.Exp, PK(b2), [('glb', t)])
                            kb.act(gt[:, 3, :], G[:, t, 3, :], AF.Exp, [gk], ['gt3'])
                            kb.tt('dve', G[:, t, 4, :], gt[:, 3, :], G[:, t, 2, :], ALU.mult, ['gt3', gk], [gk])
                        kb.barrier()
                    if dbg and L == 0:
                        kb.dma(dbg_d['gates'], G[:], reads=[('G', t) for t in range(NT)], writes=['dbg_gates'])
                    if stop == 'gates':
                        kb.barrier()
                        return True
                    with contextlib.ExitStack() as s2:
                        convw = sb(s2, "convw", [128, 32, 4])
                        kb.dma(convw[:], a_convT[j], writes=['convw'])
                        modes = ['conv_q'] * 8 + ['conv_k'] * 8 + ['conv_v'] * 16 + ['silu'] * 16
                        proj_fm(a_win[j], 0, 48, modes, hT, qkvz, s2, convw=convw)
                    kb.barrier()
                if stop == 'proj':
                    return True
                with contextlib.ExitStack() as s1:
                    wout_bf = sb(s1, "wout_bf", [128, 16, D], BF16)
                    load_wout(a_wout[j], 16, wout_bf, s1)
                    rr = gdn_tiles(L, j, G, glb, wout_bf, xsrc, xkey, last_layer, s1)
                    kb.barrier()
                    return rr

        def gdn_tiles(L, j, G, glb, wout_bf, xsrc, xkey, last_layer, st):
            HG = 8
            Sf = sb(st, "Sf", [128, 16, 128])
            Sb = sb(st, "Sb", [128, 16, 128], BF16)
            nw = sb(st, "nw", [128, 1])
            maskbf = sb(st, "maskbf", [128, 3, 128], BF16)
            qT = [sb(st, "qT%d" % i, [128, 8, 128], BF16) for i in range(2)]
            kT = [sb(st, "kT%d" % i, [128, 8, 128], BF16) for i in range(2)]
            vT = [sb(st, "vT%d" % i, [128, 16, 128], BF16) for i in range(2)]
            zs = [sb(st, "zs%d" % i, [128, 16, 128], BF16) for i in range(2)]
            Ag = [sb(st, "Ag%d" % i, [128, 128]) for i in range(2)]
            Agp = [sb(st, "Agp%d" % i, [128, 128]) for i in range(2)]
            E3 = [sb(st, "E3%d" % i, [128, 384]) for i in range(2)]
            XY = sb(st, "XY", [128, HG, 2, 128])
            Pm = sb(st, "Pm", [128, HG, 128])
            vb = sb(st, "vb", [128, HG, 128], BF16)
            kbg = sb(st, "kbg", [128, HG, 128], BF16)
            Gb = [sb(st, "Gb%d" % i, [128, 128]) for i in range(2)]
            gamb = [sb(st, "gamb%d" % i, [128, 128]) for i in range(2)]
            TT = sb(st, "TT", [128, HG, 128], BF16)
            attnT = [sb(st, "attnT%d" % i, [128, HG, 128], BF16) for i in range(2)]
            kdec = [sb(st, "kdec%d" % i, [128, HG, 128], BF16) for i in range(2)]
            qdT = [sb(st, "qdT%d" % i, [128, HG, 128], BF16) for i in range(2)]
            usb = [sb(st, "usb%d" % i, [128, HG, 128]) for i in range(2)]
            wTb = [sb(st, "wTb%d" % i, [128, HG, 128], BF16) for i in range(2)]
            vnew = sb(st, "vnew", [128, HG, 128], BF16)
            oT = sb(st, "oT", [128, HG, 128])
            osq = sb(st, "osq", [128, 512])
            rst = sb(st, "rst", [128, 512])
            lnt = sb(st, "lnt", [128, 512])
            ogT = [sb(st, "ogT%d" % i, [128, 16, 128], BF16) for i in range(2)]
            kb.memset('dve', Sf[:], 0.0, [('Sf', h) for h in range(16)])
            kb.memset('pool', Sb[:], 0.0, [('Sb', h) for h in range(16)])
            kb.dma(nw[:], a_nw[j], writes=['nw'])
            kb.copy('dve', maskbf[:, 0, :], C(C_MINCLT), ['cst'], ['maskbf'])
            kb.copy('dve', maskbf[:, 1, :], C(C_MSTRT), ['cst'], ['maskbf'])
            kb.copy('dve', maskbf[:, 2, :], C(C_MSTR), ['cst'], ['maskbf'])
            qv = qkvz[0:8].rearrange("c p t -> p c t")
            kv = qkvz[8:16].rearrange("c p t -> p c t")
            vv = qkvz[16:32].rearrange("c p t -> p c t")
            zv = qkvz[32:48].rearrange("c p t -> p c t")

            def load_qkv(t):
                b = t % 2
                tsl = slice(t * 128, (t + 1) * 128)
                kb.dma(qT[b][:], qv[:, :, tsl], writes=[('qT', b)])
                kb.dma(kT[b][:], kv[:, :, tsl], writes=[('kT', b)])
                kb.dma(vT[b][:], vv[:, :, tsl], writes=[('vT', b)])

            def load_zs(t):
                b = t % 2
                kb.dma(zs[b][:], zv[:, :, t * 128:(t + 1) * 128], writes=[('zs', b)])

            B_G, B_T = 0, 2
            B_D = (1, 3)

            def s1(t, hg, bf):
                b = t % 2
                gk = ('G', t)
                if hg == 0 and t + 1 < NT:
                    load_qkv(t + 1)
                def g_stage(hp):
                    kb.newgen(B_G)
                    kb.mm(B_G, ps[B_G][:, 0:128], kT[b][:, hp, :], kT[b][:, hp, :], [('kT', b)], PK(B_G), inc=False)
                    kb.mm(B_G, ps[B_G][:, 128:256], kT[b][:, hp, :], qT[b][:, hp, :], [('kT', b), ('qT', b)], PK(B_G))
                    ksl = hp % 2
                    kb.tr(psbf(B_T)[:, ksl * 128:(ksl + 1) * 128], kT[b][:, hp, :], ident_bf[:], [('kT', b), 'ident_bf'], PK(B_T))

                def alpha(hl):
                    h = hg * HG + hl
                    a = Ag[h % 2]
                    ap_ = Agp[h % 2]
                    kb.ts('pool', a[:], C(C_UT), G[:, t, 0, h:h + 1], None, ALU.mult, None, ['cst', gk], [('Ag', h % 2)])
                    kb.stt(ap_[:], C(C_ID), G[:, t, 1, h:h + 1], a[:], ALU.mult, ALU.add, ['cst', gk, ('Ag', h % 2)], [('Agp', h % 2)])
                    g_ = Gb[h % 2]
                    kb.ts('pool', g_[:], C(C_ONES), G[:, t, 3, h:h + 1], None, ALU.mult, None, ['cst', gk], [('Gb', h % 2)])
                    db = B_D[h % 2]
                    kb.newgen(db)
                    dk = PK(db)
                    kb.mm(db, ps[db][:, 0:128], C(C_SL), a[:], ['cst', ('Ag', h % 2)], dk, last=False, inc=False)
                    kb.mm(db, ps[db][:, 0:128], ident_bf[:], maskbf[:, 0, :], ['ident_bf', 'maskbf'], dk, inc=False)
                    kb.mm(db, ps[db][:, 128:256], C(C_SL), ap_[:], ['cst', ('Agp', h % 2)], dk, last=False, inc=False)
                    kb.mm(db, ps[db][:, 128:256], ident_bf[:], maskbf[:, 1, :], ['ident_bf', 'maskbf'], dk, inc=False)
                    kb.mm(db, ps[db][:, 256:384], ap_[:], C(C_SL), ['cst', ('Agp', h % 2)], dk, last=False, inc=False)
                    kb.mm(db, ps[db][:, 256:384], ident_bf[:], maskbf[:, 2, :], ['ident_bf', 'maskbf'], dk, inc=False)
                    kb.mm(db, ps[db][:, 384:512], g_[:], C(C_ID), ['cst', ('Gb', h % 2)], dk)
                    vs = 2 + h % 2
                    kb.tr(psbf(B_T)[:, vs * 128:(vs + 1) * 128], vT[b][:, h, :], ident_bf[:], [('vT', b), 'ident_bf'], PK(B_T))

                def beta(hl):
                    h = hg * HG + hl
                    hp = h // 2
                    db = B_D[h % 2]
                    dk = PK(db)
                    Gps = ps[B_G][:, 0:128]
                    QKps = ps[B_G][:, 128:256]
                    ksl = hp % 2
                    psK = psbf(B_T)[:, ksl * 128:(ksl + 1) * 128]
                    vs = 2 + h % 2
                    psV = psbf(B_T)[:, vs * 128:(vs + 1) * 128]
                    e3 = E3[h % 2]
                    kb.act(e3[:], ps[db][:, 0:384], AF.Exp, dk, [('E3', h % 2)])
                    gm = gamb[h % 2]
                    kb.act(gm[:], ps[db][:, 384:512], AF.Exp, dk, [('gamb', h % 2)])
                    kb.tt('dve', XY[:, hl, 0, :], e3[:, 128:256], Gps, ALU.mult, [('E3', h % 2)] + PK(B_G), [('XY', hl)])
                    kb.tt('dve', XY[:, hl, 1, :], e3[:, 256:384], Gps, ALU.mult, [('E3', h % 2)] + PK(B_G), [('XY', hl)])
                    kb.tt('dve', attnT[bf][:, hl, :], e3[:, 0:128], QKps, ALU.mult, [('E3', h % 2)] + PK(B_G), [('attnT', bf, hl)])
                    kb.stt(Pm[:, hl, :], XY[:, hl, 0, :], -1.0, C(C_ID), ALU.mult, ALU.add, [('XY', hl), 'cst'], [('Pm', hl)])
                    kb.tt('pool', qdT[bf][:, hl, :], qT[b][:, hp, :], gm[:], ALU.mult, [('qT', b), ('gamb', h % 2)], [('qdT', bf, hl)])
                    kb.act(vb[:, hl, :], psV, AF.Identity, PK(B_T) + [gk], [('vb', hl)], scale=G[:, t, 2, h:h + 1])
                    kb.act(kbg[:, hl, :], psK, AF.Identity, PK(B_T) + [gk], [('kbg', hl)], scale=G[:, t, 4, h:h + 1])
                    kb.ts('dve', kdec[bf][:, hl, :], psK, G[:, t, 5, h:h + 1], None, ALU.mult, None, PK(B_T) + [gk], [('kdec', bf, hl)])

                g_stage((hg * HG) // 2)
                alpha(0)
                for hl in range(HG):
                    if hl + 1 < HG and (hl + 1) % 2 == 1:
                        alpha(hl + 1)
                    beta(hl)
                    if hl + 1 < HG and (hl + 1) % 2 == 0:
                        g_stage((hg * HG + hl + 1) // 2)
                        alpha(hl + 1)
                    yield
                for lvl in range(NEU_LO, 6):
                    for pr in range(HG // 2):
                        bank = pr % 2
                        kb.newgen(bank)
                        for u_ in range(2):
                            hl = pr * 2 + u_
                            X = XY[:, hl, 0, :]
                            Y = XY[:, hl, 1, :]
                            o0 = u_ * 256
                            if lvl < 5:
                                kb.mm(bank, ps[bank][:, o0:o0 + 128], Y, X, [('XY', hl)], PK(bank), inc=False)
                            kb.mm(bank, ps[bank][:, o0 + 128:o0 + 256], X, Y, [('XY', hl)], PK(bank), inc=(u_ == 1))
                        if lvl < 5:
                            kb.copy('act', XY[:, pr * 2:pr * 2 + 2, :, :], ps[bank][:, :].rearrange("p (h x c) -> p h x c", h=2, x=2),
                                    PK(bank), [('XY', pr * 2), ('XY', pr * 2 + 1)])
                        else:
                            kb.copy('act', XY[:, pr * 2:pr * 2 + 2, 1, :], ps[bank][:, :].rearrange("p (h x c) -> p h x c", h=2, x=2)[:, :, 1, :],
                                    PK(bank), [('XY', pr * 2), ('XY', pr * 2 + 1)])
                        yield
                    for q4 in range(HG // 4):
                        bank = 2 + q4 % 2
                        kb.newgen(bank)
                        for u_ in range(4):
                            hl = q4 * 4 + u_
                            kb.mm(bank, ps[bank][:, u_ * 128:(u_ + 1) * 128], XY[:, hl, 1, :], Pm[:, hl, :], [('XY', hl), ('Pm', hl)], PK(bank), inc=(u_ == 3))
                        hs = slice(q4 * 4, q4 * 4 + 4)
                        pk = [('Pm', q4 * 4 + u_) for u_ in range(4)]
                        if lvl < 5:
                            kb.tt('dve', Pm[:, hs, :], Pm[:, hs, :], ps[bank][:, :].rearrange("p (h c) -> p h c", h=4), ALU.add, PK(bank) + pk, pk)
                        else:
                            kb.tt('dve', TT[:, hs, :], Pm[:, hs, :], ps[bank][:, :].rearrange("p (h c) -> p h c", h=4), ALU.add,
                                  PK(bank) + pk, [('TT', q4 * 4 + u_) for u_ in range(4)])
                        yield
                for pr in range(HG // 2):
                    bank = pr % 2
                    kb.newgen(bank)
                    for u_ in range(2):
                        hl = pr * 2 + u_
                        kb.mm(bank, ps[bank][:, u_ * 128:(u_ + 1) * 128], TT[:, hl, :], vb[:, hl, :], [('TT', hl), ('vb', hl)], PK(bank), inc=False)
                        kb.mm(bank, ps[bank][:, 256 + u_ * 128:256 + (u_ + 1) * 128], kbg[:, hl, :], TT[:, hl, :], [('TT', hl), ('kbg', hl)], PK(bank), inc=(u_ == 1))
                    kb.copy('act', usb[bf][:, pr * 2:pr * 2 + 2, :], ps[bank][:, 0:256].rearrange("p (h c) -> p h c", h=2), PK(bank),
                            [('usb', bf, pr * 2), ('usb', bf, pr * 2 + 1)])
                    kb.copy('dve', wTb[bf][:, pr * 2:pr * 2 + 2, :], ps[bank][:, 256:512].rearrange("p (h c) -> p h c", h=2), PK(bank),
                            [('wTb', bf, pr * 2), ('wTb', bf, pr * 2 + 1)])
                    yield

            B_WO = (4, 5)
            B_SS = (6, 7)
            B_N = 4

            def s2(t, hg, bf):
                b = t % 2
                if hg == 0 and t + 1 < NT:
                    load_zs(t + 1)
                for ch in range(2):
                    rs = slice(ch * 64, (ch + 1) * 64)
                    for q4 in range(HG // 4):
                        bw = B_WO[q4]
                        kb.newgen(bw)
                        for u_ in range(4):
                            hl = q4 * 4 + u_
                            h = hg * HG + hl
                            kb.mm(bw, ps[bw][rs, u_ * 128:(u_ + 1) * 128], wTb[bf][:, hl, rs], Sb[:, h, :], [('wTb', bf, hl), ('Sb', h)], PK(bw),
                                  halves=(ch,), inc=(u_ == 3))
                    for q4 in range(HG // 4):
                        bw = B_WO[q4]
                        hs = slice(q4 * 4, q4 * 4 + 4)
                        vk = [('vnew', q4 * 4 + u_) for u_ in range(4)]
                        kb.tt('dve', vnew[rs, hs, :], usb[bf][rs, hs, :], ps[bw][rs, :].rearrange("p (h c) -> p h c", h=4), ALU.subtract,
                              PK(bw) + [('usb', bf, q4 * 4 + u_) for u_ in range(4)], vk)
                    yield
                    for q4 in range(HG // 4):
                        bw = B_WO[q4]
                        bs = B_SS[q4]
                        kb.newgen(bw)
                        kb.newgen(bs)
                        for u_ in range(4):
                            hl = q4 * 4 + u_
                            h = hg * HG + hl
                            kb.mm(bw, ps[bw][:, u_ * 64:(u_ + 1) * 64], Sb[:, h, :], qdT[bf][:, hl, rs], [('Sb', h), ('qdT', bf, hl)], PK(bw), last=False, inc=False)
                            kb.mm(bw, ps[bw][:, u_ * 64:(u_ + 1) * 64], vnew[rs, hl, :], attnT[bf][rs, hl, rs], [('vnew', hl), ('attnT', bf, hl)], PK(bw), inc=(u_ == 3))
                        for u_ in range(4):
                            hl = q4 * 4 + u_
                            kb.mm(bs, ps[bs][:, u_ * 128:(u_ + 1) * 128], kdec[bf][rs, hl, :], vnew[rs, hl, :], [('kdec', bf, hl), ('vnew', hl)], PK(bs), inc=(u_ == 3))
                    yield
                    for q4 in range(HG // 4):
                        bw = B_WO[q4]
                        bs = B_SS[q4]
                        hs = slice(q4 * 4, q4 * 4 + 4)
                        kb.copy('act', oT[:, hs, rs], ps[bw][:, 0:256].rearrange("p (h c) -> p h c", h=4), PK(bw),
                                [('oT', q4 * 4 + u_) for u_ in range(4)])
                        for u_ in range(4):
                            hl = q4 * 4 + u_
                            h = hg * HG + hl
                            kb.stt(Sf[:, h, :], Sf[:, h, :], glb[:, t, ch, h:h + 1], ps[bs][:, u_ * 128:(u_ + 1) * 128], ALU.mult, ALU.add,
                                   [('Sf', h), ('glb', t)] + PK(bs), [('Sf', h)])
                        h0 = hg * HG + q4 * 4
                        kb.copy('act', Sb[:, h0:h0 + 4, :], Sf[:, h0:h0 + 4, :], [('Sf', h0 + u_) for u_ in range(4)], [('Sb', h0 + u_) for u_ in range(4)])
                    yield
                for q4 in range(HG // 4):
                    hs = slice(q4 * 4, q4 * 4 + 4)
                    h0 = hg * HG + q4 * 4
                    ok = [('oT', q4 * 4 + u_) for u_ in range(4)]
                    kb.act(osq[:].rearrange("p (h c) -> p h c", h=4), oT[:, hs, :], AF.Square, ok, ['osq'])
                    kb.newgen(B_N)
                    kb.mm(B_N, ps[B_N][:, :], C(C_ONES), osq[:], ['cst', 'osq'], PK(B_N))
                    rstd_from_ss(ps[B_N][:, :], rst[:], 128, PK(B_N), ['rst'], lnt[:])
                    kb.tt('dve', osq[:].rearrange("p (h c) -> p h c", h=4), oT[:, hs, :], rst[:].rearrange("p (h c) -> p h c", h=4), ALU.mult,
                          ok + ['rst', 'osq'], ['osq'])
                    kb.stt(ogT[b][:, h0:h0 + 4, :], osq[:].rearrange("p (h c) -> p h c", h=4), nw[:, 0:1], zs[b][:, h0:h0 + 4, :], ALU.mult, ALU.mult,
                           ['osq', 'nw', ('zs', b)], [('ogT', b)])
                    yield
                if hg == 1:
                    outproj_tile(L, t, ogT[b], 16, wout_bf, xsrc, xkey, last_layer, [('ogT', b)])
                    yield

            steps = [(t, hg) for t in range(NT) for hg in range(2)]
            load_qkv(0)
            load_zs(0)
            for _ in s1(0, 0, 0):
                pass
            RATIO = GRATIO
            for k in range(len(steps)):
                g1 = s1(steps[k + 1][0], steps[k + 1][1], (k + 1) % 2) if k + 1 < len(steps) else None
                g2 = s2(steps[k][0], steps[k][1], k % 2)
                while g1 is not None or g2 is not None:
                    if g1 is not None:
                        for _ in range(RATIO):
                            try:
                                next(g1)
                            except StopIteration:
                                g1 = None
                                break
                    if g2 is not None:
                        try:
                            next(g2)
                        except StopIteration:
                            g2 = None

        def fox_layer(L, j, xsrc, xkey, last_layer):
            with contextlib.ExitStack() as sl:
                Vall = sb(sl, "Vall", [128, NT, 16, 65], BF16)
                cumT = sb(sl, "cumT", [128, NT, 16])
                with contextlib.ExitStack() as s1:
                    hT = sb(s1, "hT", [128, 8, S], BF16)
                    with contextlib.ExitStack() as s2:
                        A_b = sb(s2, "A_b", [128, D])
                        B_b = sb(s2, "B_b", [128, D])
                        adaln(L, A_b, B_b, s2)
                        norm_phase(L, xsrc, xkey, hT, A_b, B_b, s2)
                        kb.barrier()
                    with contextlib.ExitStack() as s2:
                        wfst = sb(s2, "wfst", [128, 8, 16])
                        wf = sb(s2, "wf", [128, 8, 16], BF16)
                        nfb = sb(s2, "nfb", [16, 1])
                        spl = sb(s2, "spl", [16, 2048])
                        cums = sb(s2, "cums", [16, S])
                        onesr = sb(s2, "onesr", [16, 2048], BF16)
                        c1b = sb(s2, "c1b", [16, S], BF16)
                        kb.dma(wfst[:], b_win[j].rearrange("(kc p) f -> p kc f", p=128)[:, :, 4096:4112], writes=['wfst'])
                        kb.copy('dve', wf[:], wfst[:], ['wfst'], ['wf'])
                        kb.dma(nfb[:], b_fb[j], writes=['nfb'])
                        kb.ts('dve', nfb[:], nfb[:], -1.0, None, ALU.mult, None, ['nfb'], ['nfb'])
                        kb.memset('pool', onesr[:], 1.0, ['onesr'])
                        for half in range(2):
                            for tl in range(4):
                                tb = half * 4 + tl
                                bank = tb % 2
                                kb.newgen(bank)
                                for kc in range(8):
                                    kb.mm(bank, ps[bank][0:16, :], wf[:, kc, :], hT[:, kc, tb * 512:(tb + 1) * 512], ['wf', ('hT', tb)], PK(bank),
                                          halves=(0,), last=(kc == 7))
                                kb.act(spl[:, tl * 512:(tl + 1) * 512], ps[bank][0:16, :], AF.Exp, PK(bank) + ['nfb'], [('spl', tl)], scale=-1.0, bias=nfb[:, 0:1])
                                kb.act(spl[:, tl * 512:(tl + 1) * 512], spl[:, tl * 512:(tl + 1) * 512], AF.Ln, [('spl', tl)], [('spl', tl)], bias=1.0)
                            init = 0.0 if half == 0 else cums[:, 2047:2048]
                            kb.op('dve', lambda g: g.tensor_tensor_scan(out=cums[:, half * 2048:(half + 1) * 2048], data0=onesr[:], data1=spl[:], initial=init,
                                                                      op0=ALU.mult, op1=ALU.add),
                                  [('spl', tl) for tl in range(4)] + ['onesr', 'cums'], ['cums'])
                        kb.ts('dve', c1b[:], cums[:], -1.0, None, ALU.mult, None, ['cums'], ['c1b'])
                        kb.dma(c1s, c1b[:], reads=['c1b'], writes=['c1s'])
                        for t in range(NT):
                            bank = 2 + t % 2
                            kb.newgen(bank)
                            kb.mm(bank, ps[bank][:, 0:16], cums[:, t * 128:(t + 1) * 128], cst[0:16, C_ID, 0:16], ['cums', 'cst'], PK(bank))
                            kb.copy('dve', cumT[:, t, :], ps[bank][:, 0:16], PK(bank), [('cumT', t)])
                        kb.barrier()
                    if stop == 'f1':
                        return True
                    with contextlib.ExitStack() as s2:
                        qn = sb(s2, "qn", [128, 1])
                        kn = sb(s2, "kn", [128, 1])
                        kb.dma(qn[:], b_qn2[j], writes=['ppsc'])
                        kb.dma(kn[:], b_kn2[j], writes=['ppsc'])
                        kb.ts('dve', qn[:], qn[:], 0.125, None, ALU.mult, None, ['ppsc'], ['ppsc'])
                        proj_fm(b_win[j], 0, 16, ['rms'] * 16, hT, qks, s2, pp_scalars=[qn[:, 0:1]] * 8 + [kn[:, 0:1]] * 8, nred=64, ones_ap=C(C_ONESBD))
                    if stop == 'f2':
                        return True
                    with contextlib.ExitStack() as s2:
                        wst = [sb(s2, "wvst%d" % i, [128, 8, 128]) for i in range(2)]
                        wvz = sb(s2, "wvz", [128, 8, 2048], BF16)
                        zt = [sb(s2, "zt%d" % i, [128, D], BF16) for i in range(2)]
                        wv = b_win[j].rearrange("(kc p) f -> p kc f", p=128)
                        for g in range(16):
                            b = g % 2
                            kb.dma(wst[b][:], wv[:, :, 2048 + g * 128:2048 + (g + 1) * 128], writes=[('wvst', b)])
                            kb.copy('pool', wvz[:, :, g * 128:(g + 1) * 128], wst[b][:], [('wvst', b)], [('wvz', g // 4)])
                        kb.memset('dve', Vall[:, :, :, 64:65], 1.0, [('Vall1',)])
                        for t in range(NT):
                            for fb in range(4):
                                bank = (t * 4 + fb) % 4
                                kb.newgen(bank)
                                for kc in range(8):
                                    kb.mm(bank, ps[bank][:, :], hT[:, kc, t * 128:(t + 1) * 128], wvz[:, kc, fb * 512:(fb + 1) * 512],
                                          [('hT', t // 4), ('wvz', fb)], PK(bank), last=(kc == 7))
                                if fb < 2:
                                    kb.copy('dve', Vall[:, t, fb * 8:(fb + 1) * 8, 0:64], ps[bank][:, :].rearrange("p (h d) -> p h d", h=8), PK(bank), [('Vall', t)])
                                else:
                                    kb.act(zt[t % 2][:, (fb - 2) * 512:(fb - 1) * 512], ps[bank][:, :], AF.Silu, PK(bank), [('zt', t % 2)])
                            kb.dma(zss[t * 128:(t + 1) * 128, :], zt[t % 2][:], reads=[('zt', t % 2)], writes=[('zss', t)])
                        kb.barrier()
                if stop == 'f3':
                    return True
                with contextlib.ExitStack() as s1:
                    Oall = sb(s1, "Oall", [128, NT, D], BF16)
                    with contextlib.ExitStack() as s2:
                        QA = [sb(s2, "QA%d" % i, [65, S], BF16) for i in range(2)]
                        KA = [sb(s2, "KA%d" % i, [65, S], BF16) for i in range(2)]
                        PT = [sb(s2, "PT%d" % i, [128, 512], BF16) for i in range(4)]
                        rl = sb(s2, "rl", [128, 4])
                        for i in range(2):
                            kb.memset('dve', KA[i][64:65, :], 1.0, [('KA1', i)])

                        def load_head(h):
                            b = h % 2
                            r0 = (h % 2) * 64
                            kb.dma(QA[b][0:64, :], qks[h // 2][r0:r0 + 64, :], writes=[('QA', b)])
                            kb.dma(QA[b][64:65, :], c1s[h:h + 1, :], reads=['c1s'], writes=[('QA', b)])
                            kb.dma(KA[b][0:64, :], qks[8 + h // 2][r0:r0 + 64, :], writes=[('KA', b)])

                        load_head(0)
                        pairs = [(h, qb, kt) for h in range(16) for qb in range(8) for kt in range(4 * (qb + 1))]
                        NSB = 4
                        LA = 2

                        def stageA(n):
                            h, qb, kt = pairs[n]
                            b = h % 2
                            if qb == 0 and kt == 0 and h + 1 < 16:
                                load_head(h + 1)
                            i0 = max(0, kt - 4 * qb)
                            sbk = n % NSB
                            kb.newgen(sbk)
                            kb.mm(sbk, ps[sbk][:, i0 * 128:512], KA[b][0:65, kt * 128:(kt + 1) * 128], QA[b][0:65, qb * 512 + i0 * 128:(qb + 1) * 512],
                                  [('KA', b), ('KA1', b), ('QA', b)], PK(sbk))

                        def stageB(n):
                            h, qb, kt = pairs[n]
                            nkt = 4 * (qb + 1)
                            jd = kt - 4 * qb
                            i0 = max(0, jd)
                            sbk = n % NSB
                            pt = PT[n % 4]
                            ptk = ('PT', n % 4)
                            obk = 4 + (h * 8 + qb) % 2
                            if kt == 0:
                                kb.newgen(obk)
                            kb.act(pt[:, i0 * 128:512], ps[sbk][:, i0 * 128:512], AF.Exp, PK(sbk) + [('cumT', kt)], [ptk], bias=cumT[:, kt, h:h + 1])
                            if jd >= 0:
                                kb.tt('pool', pt[:, jd * 128:(jd + 1) * 128], pt[:, jd * 128:(jd + 1) * 128], caus_bf[:], ALU.mult, [ptk, 'caus_bf'], [ptk])
                            for i in range(i0, 4):
                                kb.mm(obk, ps[obk][:, i * 65:(i + 1) * 65], pt[:, i * 128:(i + 1) * 128], Vall[:, kt, h, :],
                                      [ptk, ('Vall', kt), ('Vall1',)], PK(obk), last=(kt == nkt - 1), inc=(i == 3))
                            if kt == nkt - 1:
                                ov = ps[obk][:, 0:260].rearrange("p (i d) -> p i d", i=4)
                                kb.op('dve', lambda g: g.reciprocal(out=rl[:], in_=ov[:, :, 64]), PK(obk), ['rl'])
                                kb.tt('dve', Oall[:, qb * 4:(qb + 1) * 4, h * 64:(h + 1) * 64], ov[:, :, 0:64], rl[:].unsqueeze(2).broadcast_to([128, 4, 64]),
                                      ALU.mult, PK(obk) + ['rl'], [('Oall', qb)])

                        for n in range(len(pairs) + LA):
                            if n < len(pairs):
                                stageA(n)
                            if n - LA >= 0:
                                stageB(n - LA)
                        kb.barrier()
                    if stop == 'f4':
                        return True
                    with contextlib.ExitStack() as s2:
                        wout_bf = sb(s2, "wout_bf", [128, 8, D], BF16)
                        load_wout(b_wout[j], 8, wout_bf, s2)
                        zt = [sb(s2, "zt%d" % i, [128, D], BF16) for i in range(2)]
                        og = [sb(s2, "og%d" % i, [128, D], BF16) for i in range(2)]
                        ogT = [sb(s2, "ogT%d" % i, [128, 8, 128], BF16) for i in range(2)]
                        for t in range(NT):
                            b = t % 2
                            kb.dma(zt[b][:], zss[t * 128:(t + 1) * 128, :], reads=[('zss', t)], writes=[('zt', b)])
                            kb.tt('pool', og[b][:], Oall[:, t, :], zt[b][:], ALU.mult, [('Oall', t // 4), ('zt', b)], [('og', b)])
                            bank = 4 + b
                            for kc in range(8):
                                kb.tr(psbf(bank)[:, kc * 128:(kc + 1) * 128], og[b][:, kc * 128:(kc + 1) * 128], ident_bf[:], [('og', b), 'ident_bf'], PK(bank))
                            kb.copy('act', ogT[b][:], psbf(bank).rearrange("p (k t) -> p k t", k=8), PK(bank), [('ogT', b)])
                            outproj_tile(L, t, ogT[b], 8, wout_bf, xsrc, xkey, last_layer, [('ogT', b)])
                        kb.barrier()

        xsrc, xkey = x_in, 'xin'
        for L in range(n_layers):
            last = (L == n_layers - 1)
            if L % 2 == 0:
                stopped = gdn_layer(L, L // 2, xsrc, xkey, last)
            else:
                stopped = fox_layer(L, L // 2, xsrc, xkey, last)
            xsrc, xkey = xres, 'xres'
            kb.barrier()
            if stopped:
                break
        if dbg:
            kb.dma(dbg_d['xres'], xres, reads=[('xres', t) for t in range(NT)], writes=['dbg_x'])
            kb.dma(dbg_d['qkvz'], qkvz, writes=['dbg_q'])
        kb.barrier()
        print("instructions emitted:", kb.ninst, {k: v for k, v in kb.cnt.items()})
    return nc


def make_in_maps(inputs):
    consts = make_consts()
    f = lambda a: np.ascontiguousarray(np.asarray(a, dtype=np.float32))
    x = f(inputs["x"])
    c = f(inputs["c"])
    shared = {
        "norm_w": f(inputs["norm_w"]),
        "final_norm_w": f(inputs["final_norm_w"]).reshape(1, D),
        "ada_w": f(inputs["ada_w"]),
        "ada_b": f(inputs["ada_b"]),
        "a_w_in": f(inputs["a_w_in"]),
        "a_convT": f(np.transpose(f(inputs["a_conv_w"]), (0, 2, 1)).reshape(2, 32, 128, 4).transpose(0, 2, 1, 3)),
        "a_A_log": f(inputs["a_A_log"]),
        "a_dt_bias": f(inputs["a_dt_bias"]),
        "a_norm_w": f(inputs["a_norm_w"]).reshape(2, 128, 1),
        "a_w_out": f(inputs["a_w_out"]),
        "b_w_in": f(inputs["b_w_in"]),
        "b_f_bias": f(inputs["b_f_bias"]).reshape(2, 16, 1),
        "b_qn2": f(np.tile(f(inputs["b_qn_w"]), (1, 2))).reshape(2, 128, 1),
        "b_kn2": f(np.tile(f(inputs["b_kn_w"]), (1, 2))).reshape(2, 128, 1),
        "b_w_out": f(inputs["b_w_out"]),
        "consts": consts,
    }
    maps = []
    for b in range(8):
        m = dict(shared)
        m["x"] = x[b]
        m["cT"] = f(c[b].reshape(8, 128).T)
        maps.append(m)
    return maps


_NC_CACHE = {}


def kernel(**inputs):
    if 'nc' not in _NC_CACHE:
        _NC_CACHE['nc'] = build_program()
    nc = _NC_CACHE['nc']
    in_maps = make_in_maps(inputs)
    res = run_bass_kernel_spmd(nc, in_maps, core_ids=list(range(8)))
    out = np.stack([np.asarray(r["out"], dtype=np.float32) for r in res.results], axis=0)
    return out
```

```python
import contextlib
import numpy as np
import concourse.bass as bass
import concourse.mybir as mybir
from concourse.bass_utils import run_bass_kernel_spmd

F32 = mybir.dt.float32
BF16 = mybir.dt.bfloat16
AF = mybir.ActivationFunctionType
ALU = mybir.AluOpType
AX = mybir.AxisListType

S = 4096
D = 1024
NT = 32
EPS = 1e-6
NEG = -30000.0
DEPTH = 4
GDN_IN = 6176
FOX_IN = 4112

C_ID, C_ONES, C_UT, C_SL, C_MINCLT, C_MSTRT, C_MSTR, C_TRIBD, C_SELC, C_SELA, C_SELB, C_CAUS, C_ONESBD = range(13)
NCONST = 13


def make_consts():
    i = np.arange(128)
    r = i[:, None]
    c = i[None, :]
    same = (r // 64) == (c // 64)
    m = np.zeros((NCONST, 128, 128), np.float32)
    m[C_ID] = (r == c)
    m[C_ONES] = 1.0
    m[C_UT] = (r <= c)
    m[C_SL] = (r > c)
    m[C_MINCLT] = np.where(same & (r <= c), 0.0, NEG)
    m[C_MSTRT] = np.where(same & (r < c), 0.0, NEG)
    m[C_MSTR] = np.where(same & (r > c), 0.0, NEG)
    m[C_TRIBD] = (same & (r <= c))
    m[C_SELC] = (r == (c // 64) * 64 + 63)
    m[C_SELA] = (r == 63) * np.ones((1, 128))
    m[C_SELB] = (r == 127) * np.ones((1, 128))
    m[C_CAUS] = (r <= c)
    m[C_ONESBD] = same
    return m.astype(np.float32)


class KB:
    NS = 24

    def __init__(self, nc, es):
        self.nc = nc
        self.eng = {'pe': nc.tensor, 'act': nc.scalar, 'dve': nc.vector, 'pool': nc.gpsimd, 'sp': nc.sync}
        self.sem = {k: es.enter_context(nc.semaphore("s_" + k)) for k in ['pe', 'act', 'dve', 'pool']}
        self.cnt = {k: 0 for k in self.sem}
        self.seen = {k: {} for k in self.eng}
        self.dsem = [es.enter_context(nc.semaphore("d%d" % i)) for i in range(self.NS)]
        self.dval = [0] * self.NS
        self.dnext = 0
        self.lastw = {}
        self.readers = {}
        self.fresh = {}
        self.ninst = 0

    def _wait(self, e, tok):
        sk, v = tok
        if sk == e and e == 'pe':
            return
        if self.seen[e].get(sk, 0) >= v:
            return
        sem = self.sem[sk] if isinstance(sk, str) else self.dsem[sk[1]]
        self.eng[e].wait_ge(sem, v)
        self.seen[e][sk] = v

    def _deps(self, e, reads, writes):
        for k in reads:
            t = self.lastw.get(k)
            if t is not None:
                self._wait(e, t)
            if isinstance(k, tuple) and k[0] == 'ps':
                for sk, t in self.readers.get(k, {}).items():
                    if sk != e:
                        self._wait(e, t)
        for k in writes:
            t = self.lastw.get(k)
            if t is not None:
                self._wait(e, t)
            for t in self.readers.get(k, {}).values():
                self._wait(e, t)

    def _record(self, tok, reads, writes):
        for k in reads:
            self.readers.setdefault(k, {})[tok[0]] = tok
        for k in writes:
            self.lastw[k] = tok
            self.readers[k] = {}

    def op(self, e, fn, reads=(), writes=(), inc=True):
        self._deps(e, reads, writes)
        ins = fn(self.eng[e])
        self.ninst += 1
        if inc:
            self.cnt[e] += 1
            ins.then_inc(self.sem[e], 1)
            tok = (e, self.cnt[e])
        else:
            tok = (e, self.cnt[e] + 1)
        self._record(tok, reads, writes)
        return tok

    def dma(self, out, in_, reads=(), writes=(), q='sp'):
        i = self.dnext
        self.dnext = (self.dnext + 1) % self.NS
        if self.dval[i] > 0:
            self._wait(q, (('d', i), self.dval[i]))
        self._deps(q, reads, writes)
        ins = self.eng[q].dma_start(out=out, in_=in_)
        self.ninst += 1
        self.dval[i] += 16
        ins.then_inc(self.dsem[i], 16)
        tok = (('d', i), self.dval[i])
        self._record(tok, reads, writes)
        return tok

    def barrier(self):
        for e in self.eng:
            for o in self.sem:
                if self.cnt[o] > 0:
                    self._wait(e, (o, self.cnt[o]))
            for i in range(self.NS):
                if self.dval[i] > 0:
                    self._wait(e, (('d', i), self.dval[i]))
        self.lastw = {}
        self.readers = {}

    def newgen(self, bank):
        self.fresh[(bank, 0)] = True
        self.fresh[(bank, 1)] = True

    def mm(self, bank, out, lhsT, rhs, reads, writes, halves=(0, 1), last=True, inc=None):
        st = False
        for h in halves:
            if self.fresh.get((bank, h), True):
                st = True
            self.fresh[(bank, h)] = False
        if inc is None:
            inc = last

        def fn(e):
            return e.matmul(out, lhsT=lhsT, rhs=rhs, start=st, stop=last, skip_group_check=True)
        return self.op('pe', fn, reads, writes, inc=inc)

    def tr(self, out, in_, ident, reads, writes):
        return self.op('pe', lambda e: e.transpose(out, in_, ident), reads, writes)

    def act(self, out, in_, func, reads, writes, scale=None, bias=None):
        def fn(e):
            kw = {}
            if scale is not None:
                kw['scale'] = scale
            if bias is not None:
                kw['bias'] = bias
            return e.activation(out=out, in_=in_, func=func, **kw)
        return self.op('act', fn, reads, writes)

    def tt(self, e, out, in0, in1, op, reads, writes):
        return self.op(e, lambda g: g.tensor_tensor(out=out, in0=in0, in1=in1, op=op), reads, writes)

    def ts(self, e, out, in0, s1, s2, op0, op1, reads, writes):
        if op1 is None and e == 'pool' and op0 == ALU.mult:
            op1, s2 = ALU.add, 0.0
        if op1 is None:
            return self.op(e, lambda g: g.tensor_scalar(out=out, in0=in0, scalar1=s1, scalar2=None, op0=op0), reads, writes)
        return self.op(e, lambda g: g.tensor_scalar(out=out, in0=in0, scalar1=s1, scalar2=s2, op0=op0, op1=op1), reads, writes)

    def stt(self, out, in0, scalar, in1, op0, op1, reads, writes):
        return self.op('dve', lambda g: g.scalar_tensor_tensor(out=out, in0=in0, scalar=scalar, in1=in1, op0=op0, op1=op1), reads, writes)

    def copy(self, e, out, in_, reads, writes):
        if e == 'act':
            return self.op('act', lambda g: g.copy(out=out, in_=in_), reads, writes)
        return self.op(e, lambda g: g.tensor_copy(out=out, in_=in_), reads, writes)

    def memset(self, e, ap, val, writes):
        return self.op(e, lambda g: g.memset(ap, val), (), writes)


class _Stop(Exception):
    pass


UWB = 7
GRATIO = 3
NEU_LO = 1


def build_program(n_layers=DEPTH, dbg=False, stop=None):
    nc = bass.Bass("TRN2", target_bir_lowering=False)

    def din(name, shape, dt=F32):
        return nc.dram_tensor(name, list(shape), dt, kind="ExternalInput").ap()

    def dscr(name, shape, dt):
        return nc.dram_tensor(name, list(shape), dt, kind="Internal").ap()

    x_in = din("x", [S, D])
    cT_in = din("cT", [128, 8])
    normw_in = din("norm_w", [DEPTH, D])
    fnw_in = din("final_norm_w", [1, D])
    adaw_in = din("ada_w", [DEPTH, D, 3 * D])
    adab_in = din("ada_b", [DEPTH, 3 * D])
    a_win = din("a_w_in", [2, D, GDN_IN])
    a_convT = din("a_convT", [2, 128, 32, 4])
    a_Alog = din("a_A_log", [2, 16])
    a_dtb = din("a_dt_bias", [2, 16])
    a_nw = din("a_norm_w", [2, 128, 1])
    a_wout = din("a_w_out", [2, 2048, D])
    b_win = din("b_w_in", [2, D, FOX_IN])
    b_fb = din("b_f_bias", [2, 16, 1])
    b_qn2 = din("b_qn2", [2, 128, 1])
    b_kn2 = din("b_kn2", [2, 128, 1])
    b_wout = din("b_w_out", [2, D, D])
    consts_in = din("consts", [NCONST, 128, 128])
    out_d = nc.dram_tensor("out", [S, D], F32, kind="ExternalOutput").ap()

    xres = dscr("xres", [S, D], F32)
    qkvz = dscr("qkvz", [48, 128, S], BF16)
    qks = dscr("qks", [16, 128, S], BF16)
    zss = dscr("zss", [S, D], BF16)
    c1s = dscr("c1s", [16, S], BF16)
    dbg_d = {}
    if dbg:
        dbg_d['hT'] = nc.dram_tensor("dbg_hT", [128, 8, S], BF16, kind="ExternalOutput").ap()
        dbg_d['xres'] = nc.dram_tensor("dbg_xres", [S, D], F32, kind="ExternalOutput").ap()
        dbg_d['qkvz'] = nc.dram_tensor("dbg_qkvz", [48, 128, S], BF16, kind="ExternalOutput").ap()
        dbg_d['gates'] = nc.dram_tensor("dbg_gates", [128, NT, 6, 16], F32, kind="ExternalOutput").ap()

    es = contextlib.ExitStack()
    with es:
        kb = KB(nc, es)

        uid = [0]

        def sb(stack, name, shape, dt=F32):
            uid[0] += 1
            return stack.enter_context(nc.sbuf_tensor("%s_%d" % (name, uid[0]), list(shape), dt))

        ps = [es.enter_context(nc.psum_tensor("ps%d" % i, [128, 512], F32)) for i in range(8)]

        def PK(bank, lo=0, hi=512):
            return [('ps', bank)]

        def psbf(i):
            return ps[i][:].bitcast(BF16)

        cst = sb(es, "cst", [128, NCONST, 128])
        ident_bf = sb(es, "ident_bf", [128, 128], BF16)
        caus_bf = sb(es, "caus_bf", [128, 128], BF16)
        condT = sb(es, "condT", [128, 8])
        gate_b = sb(es, "gate_b", [128, D])
        fnw_b = sb(es, "fnw_b", [128, D])
        xt = [sb(es, "xt%d" % i, [128, D]) for i in range(2)]
        junk = sb(es, "junk", [128, D])
        sm = sb(es, "sm", [128, 8])
        ones_row = sb(es, "ones_row", [1, 128])

        def C(i):
            return cst[:, i, :]

        kb.dma(cst[:], consts_in.rearrange("n p f -> p n f"), writes=['cst'])
        kb.copy('dve', ident_bf[:], C(C_ID), ['cst'], ['ident_bf'])
        kb.copy('dve', caus_bf[:], C(C_CAUS), ['cst'], ['caus_bf'])
        kb.memset('dve', ones_row[:], 1.0, ['ones_row'])
        kb.dma(condT[:], cT_in, writes=['condT'])
        kb.act(condT[:], condT[:], AF.Silu, ['condT'], ['condT'])
        kb.dma(fnw_b[:], fnw_in.partition_broadcast(128), writes=['fnw_b'])

        def rstd_from_ss(ss_ap, out_ap, n, rkeys, wkeys, tmp_ap):
            kb.act(tmp_ap, ss_ap, AF.Ln, rkeys, [('tmpln',)], scale=1.0 / n, bias=EPS)
            kb.act(out_ap, tmp_ap, AF.Exp, [('tmpln',)], wkeys, scale=-0.5)

        def adaln(L, A_b, B_b, stack):
            with contextlib.ExitStack() as s2:
                adaw = [sb(s2, "adaw%d" % i, [128, 8, 256]) for i in range(2)]
                modrow = sb(s2, "modrow", [1, 3 * D])
                nwrow = sb(s2, "nwrow", [1, D])
                arow = sb(s2, "arow", [1, D])
                kb.dma(modrow[:], adab_in[L:L + 1, :], writes=[('modrow', i) for i in range(6)])
                kb.dma(nwrow[:], normw_in[L:L + 1, :], writes=['nwrow'])
                wv = adaw_in[L].rearrange("(kc p) f -> p kc f", p=128)
                for fb in range(12):
                    b = fb % 2
                    kb.dma(adaw[b][:], wv[:, :, fb * 256:(fb + 1) * 256], writes=[('adaw', b)])
                    bank = fb % 2
                    kb.newgen(bank)
                    for kc in range(8):
                        kb.mm(bank, ps[bank][0:1, 0:256], condT[:, kc:kc + 1], adaw[b][:, kc, :],
                              ['condT', ('adaw', b)], PK(bank), halves=(0,), last=(kc == 7))
                    kb.tt('dve', modrow[0:1, fb * 256:(fb + 1) * 256], ps[bank][0:1, 0:256], modrow[0:1, fb * 256:(fb + 1) * 256],
                          ALU.add, PK(bank) + [('modrow', fb // 2)], [('modrow', fb // 2)])
                kb.stt(arow[0:1, :], modrow[0:1, D:2 * D], 1.0, nwrow[0:1, :], ALU.add, ALU.mult,
                       [('modrow', 2), ('modrow', 3), 'nwrow'], ['arow'])
                srcs = [(arow[0:1, :], A_b, ['arow'], 'A_b'), (modrow[0:1, 0:D], B_b, [('modrow', 0), ('modrow', 1)], 'B_b'),
                        (modrow[0:1, 2 * D:3 * D], gate_b, [('modrow', 4), ('modrow', 5)], 'gate_b')]
                n = 0
                for (src, dst, rk, dname) in srcs:
                    for half in range(2):
                        bank = 2 + (n % 2)
                        n += 1
                        kb.newgen(bank)
                        kb.mm(bank, ps[bank][:, :], ones_row[0:1, :], src[0:1, half * 512:(half + 1) * 512],
                              rk + ['ones_row'], PK(bank))
                        kb.copy('act', dst[:, half * 512:(half + 1) * 512], ps[bank][:, :], PK(bank), [(dname, half)])
                kb.barrier()

        def norm_phase(L, xsrc, xkey, hT, A_b, B_b, stack):
            hn = sb(stack, "hn", [128, D])
            hb = [sb(stack, "hb%d" % i, [128, D], BF16) for i in range(2)]

            def st1(t):
                b = t % 2
                kb.dma(xt[b][:], xsrc[t * 128:(t + 1) * 128, :], reads=[(xkey, t)], writes=[('xt', b)])
                kb.act(junk[:], xt[b][:], AF.Square, [('xt', b)], ['junk'])
                kb.op('dve', lambda g: g.tensor_reduce(out=sm[:, 0:1], in_=junk[:], axis=AX.X, op=ALU.add), ['junk'], [('sm', 0)])
                rstd_from_ss(sm[:, 0:1], sm[:, 2:3], D, [('sm', 0)], [('sm', 2)], sm[:, 1:2])
                kb.stt(hn[:], xt[b][:], sm[:, 2:3], A_b[:], ALU.mult, ALU.mult, [('xt', b), ('sm', 2), ('A_b', 0), ('A_b', 1)], ['hn'])
                kb.tt('pool', hb[b][:], hn[:], B_b[:], ALU.add, ['hn', ('B_b', 0), ('B_b', 1)], [('hb', b)])

            def st2(t):
                b = t % 2
                bank = 4 + b
                for kc in range(8):
                    kb.tr(psbf(bank)[:, kc * 128:(kc + 1) * 128], hb[b][:, kc * 128:(kc + 1) * 128], ident_bf[:],
                          [('hb', b), 'ident_bf'], PK(bank))
                kb.copy('act', hT[:, :, t * 128:(t + 1) * 128], psbf(bank).rearrange("p (k t) -> p k t", k=8),
                        PK(bank), [('hT', t // 4)])

            st1(0)
            for t in range(NT):
                if t + 1 < NT:
                    st1(t + 1)
                st2(t)

        def outproj_tile(L, t, ogT, KC, wout_bf, xsrc, xkey, last_layer, ykeys):
            b = t % 2
            kb.dma(xt[b][:], xsrc[t * 128:(t + 1) * 128, :], reads=[(xkey, t)], writes=[('xt', b)])
            for fb in range(2):
                bank = 6 + fb
                kb.newgen(bank)
                for kc in range(KC):
                    kb.mm(bank, ps[bank][:, :], ogT[:, kc, :], wout_bf[:, kc, fb * 512:(fb + 1) * 512],
                          ykeys + ['wout_bf'], PK(bank), last=(kc == KC - 1))
                kb.tt('dve', junk[:, fb * 512:(fb + 1) * 512], ps[bank][:, :], gate_b[:, fb * 512:(fb + 1) * 512], ALU.mult,
                      PK(bank) + [('gate_b', fb)], [('junkh', fb)])
                kb.tt('pool', xt[b][:, fb * 512:(fb + 1) * 512], junk[:, fb * 512:(fb + 1) * 512], xt[b][:, fb * 512:(fb + 1) * 512],
                      ALU.add, [('junkh', fb), ('xt', b)], [('xt', b)])
            if not last_layer:
                kb.dma(xres[t * 128:(t + 1) * 128, :], xt[b][:], reads=[('xt', b)], writes=[('xres', t)])
            else:
                kb.act(junk[:], xt[b][:], AF.Square, [('xt', b)], [('junkh', 0), ('junkh', 1)])
                kb.op('dve', lambda g: g.tensor_reduce(out=sm[:, 4:5], in_=junk[:], axis=AX.X, op=ALU.add),
                      [('junkh', 0), ('junkh', 1)], [('sm', 4)])
                rstd_from_ss(sm[:, 4:5], sm[:, 6:7], D, [('sm', 4)], [('sm', 6)], sm[:, 5:6])
                kb.stt(xt[b][:], xt[b][:], sm[:, 6:7], fnw_b[:], ALU.mult, ALU.mult, [('xt', b), ('sm', 6), 'fnw_b'], [('xt', b)])
                kb.dma(out_d[t * 128:(t + 1) * 128, :], xt[b][:], reads=[('xt', b)], writes=[('out', t)])

        def load_wout(wout_dram, KC, wout_bf, stack):
            with contextlib.ExitStack() as s2:
                wst = [sb(s2, "wost%d" % i, [128, D]) for i in range(2)]
                wv = wout_dram.rearrange("(kc p) f -> p kc f", p=128)
                for g in range(KC):
                    b = g % 2
                    kb.dma(wst[b][:], wv[:, g, :], writes=[('wost', b)])
                    kb.copy('pool', wout_bf[:, g, :], wst[b][:], [('wost', b)], ['wout_bf'])
                kb.barrier()

        def proj_fm(w2d, col0, nch, modes, hT, scratch, stack, convw=None, pp_scalars=None, nred=128, ones_ap=None):
            with contextlib.ExitStack() as s2:
                wst = [sb(s2, "wst%d" % i, [128, 8, 256]) for i in range(2)]
                wbf = [sb(s2, "wbf%d" % i, [128, 8, 256], BF16) for i in range(2)]
                NBUF = 6 if any(m.startswith('conv') for m in modes) else 3
                obuf = [sb(s2, "obuf%d" % i, [128, 512], BF16) for i in range(NBUF)]
                pre = [sb(s2, "pre%d" % i, [128, 515]) for i in range(3)] if any(m.startswith('conv') for m in modes) else None
                acc = [sb(s2, "acc%d" % i, [128, 512]) for i in range(NBUF)]
                sq2 = [sb(s2, "sq2%d" % i, [128, 512]) for i in range(NBUF)]
                lnv = sb(s2, "lnv", [128, 512])
                rn = [sb(s2, "rn%d" % i, [128, 512]) for i in range(2)]
                wv = w2d.rearrange("(kc p) f -> p kc f", p=128)
                nslab = (nch + 1) // 2
                blocks = [(c, tb) for c in range(nch) for tb in range(8)]
                NB = len(blocks)

                def load_slab(sl):
                    sbf = sl % 2
                    ncs = min(2, nch - sl * 2)
                    kb.dma(wst[sbf][:, :, 0:ncs * 128], wv[:, :, col0 + sl * 256: col0 + sl * 256 + ncs * 128], writes=[('wst', sbf)])
                    kb.copy('pool', wbf[sbf][:, :, 0:ncs * 128], wst[sbf][:, :, 0:ncs * 128], [('wst', sbf)], [('wbf', sbf)])

                def stageA(n):
                    c, tb = blocks[n]
                    sl, ci = c // 2, c % 2
                    sbf = sl % 2
                    if ci == 0 and tb == 0 and sl + 1 < nslab:
                        load_slab(sl + 1)
                    bank = n % 4
                    kb.newgen(bank)
                    for kc in range(8):
                        kb.mm(bank, ps[bank][:, :], wbf[sbf][:, kc, ci * 128:(ci + 1) * 128], hT[:, kc, tb * 512:(tb + 1) * 512],
                              [('wbf', sbf), ('hT', tb)], PK(bank), last=(kc == 7))

                def out_dma(n):
                    c, tb = blocks[n]
                    ob = n % NBUF
                    kb.dma(scratch[c][:, tb * 512:(tb + 1) * 512], obuf[ob][:], reads=[('obuf', ob)], writes=[('scr', c, tb)])

                def stageB1(n):
                    c, tb = blocks[n]
                    mode = modes[c]
                    bank = n % 4
                    ob = n % NBUF
                    osl = obuf[ob][:, :]
                    okey = [('obuf', ob)]
                    a = acc[n % NBUF]
                    ak = ('acc', n % NBUF)
                    q2 = sq2[n % NBUF]
                    qk = ('sq2', n % NBUF)
                    if mode == 'silu':
                        kb.act(osl, ps[bank][:, :], AF.Silu, PK(bank), okey)
                        out_dma(n)
                    elif mode == 'rms':
                        kb.copy('act', a[:], ps[bank][:, :], PK(bank), [ak])
                        kb.tt('pool', q2[:], a[:], a[:], ALU.mult, [ak], [qk])
                    else:
                        p = pre[n % 3]
                        pk = ('pre', n % 3)
                        pprev = pre[(n - 1) % 3]
                        pkprev = ('pre', (n - 1) % 3)
                        kb.copy('act', p[:, 3:515], ps[bank][:, :], PK(bank), [pk])
                        kb.act(a[:], ps[bank][:, :], AF.Identity, PK(bank) + ['convw'], [ak], scale=convw[:, c, 3:4])
                        if tb == 0:
                            kb.memset('pool', p[:, 0:3], 0.0, [pk])
                        else:
                            kb.copy('pool', p[:, 0:3], pprev[:, 512:515], [pkprev], [pk])
                        for jj in (2, 1, 0):
                            kb.stt(a[:], p[:, jj:jj + 512], convw[:, c, jj:jj + 1], a[:], ALU.mult, ALU.add, [pk, 'convw', ak], [ak])
                        if mode == 'conv_v':
                            kb.act(osl, a[:], AF.Silu, [ak], okey)
                            out_dma(n)
                        else:
                            kb.act(a[:], a[:], AF.Silu, [ak], [ak])
                            kb.tt('pool', q2[:], a[:], a[:], ALU.mult, [ak], [qk])

                def stageB2(n):
                    c, tb = blocks[n]
                    mode = modes[c]
                    if mode in ('silu', 'conv_v'):
                        return
                    ob = n % NBUF
                    osl = obuf[ob][:, :]
                    okey = [('obuf', ob)]
                    a = acc[n % NBUF]
                    ak = ('acc', n % NBUF)
                    q2 = sq2[n % NBUF]
                    qk = ('sq2', n % NBUF)
                    nb = 4 + n % 2
                    r = rn[n % 2]
                    rk = ('rn', n % 2)
                    kb.newgen(nb)
                    if mode == 'rms':
                        kb.mm(nb, ps[nb][:, :], ones_ap, q2[:], [qk, 'cst'], PK(nb))
                        rstd_from_ss(ps[nb][:, :], r[:], nred, PK(nb), [rk], lnv[:])
                        kb.stt(osl, a[:], pp_scalars[c], r[:], ALU.mult, ALU.mult, [ak, rk, 'ppsc'], okey)
                    else:
                        kb.mm(nb, ps[nb][:, :], C(C_ONES), q2[:], [qk, 'cst'], PK(nb))
                        kb.act(lnv[:], ps[nb][:, :], AF.Ln, PK(nb), [('tmpln',)], bias=EPS)
                        kb.act(r[:], lnv[:], AF.Exp, [('tmpln',)], [rk], scale=-0.5)
                        sc = (128.0 ** -0.5) if mode == 'conv_q' else 1.0
                        kb.stt(osl, a[:], sc, r[:], ALU.mult, ALU.mult, [ak, rk], okey)
                    out_dma(n)

                load_slab(0)
                for n in range(NB + 7):
                    if n < NB:
                        stageA(n)
                    if NBUF == 6:
                        if n >= 6 and (n - 6) % 4 == 0:
                            for m in range(n - 6, n - 2):
                                if 0 <= m < NB:
                                    stageB2(m)
                    elif 0 <= n - 3 < NB:
                        stageB2(n - 3)
                    if 0 <= n - 1 < NB:
                        stageB1(n - 1)
                kb.barrier()

        def gdn_layer(L, j, xsrc, xkey, last_layer):
            with contextlib.ExitStack() as sl:
                G = sb(sl, "G", [128, NT, 6, 16])
                glb = sb(sl, "glb", [128, NT, 2, 16])
                with contextlib.ExitStack() as s1:
                    A_b = sb(s1, "A_b", [128, D])
                    B_b = sb(s1, "B_b", [128, D])
                    adaln(L, A_b, B_b, s1)
                    hT = sb(s1, "hT", [128, 8, S], BF16)
                    with contextlib.ExitStack() as s2:
                        norm_phase(L, xsrc, xkey, hT, A_b, B_b, s2)
                        kb.barrier()
                    if dbg and L == 0:
                        kb.dma(dbg_d['hT'], hT[:], reads=[('hT', i) for i in range(8)], writes=['dbg_hT'])
                    if stop == 'norm':
                        kb.barrier()
                        return True
                    with contextlib.ExitStack() as s2:
                        wbast = sb(s2, "wbast", [128, 8, 32])
                        wba = sb(s2, "wba", [128, 8, 32], BF16)
                        dtb = sb(s2, "dtb", [128, 16])
                        negA = sb(s2, "negA", [128, 16])
                        gt = sb(s2, "gt", [128, 8, 16])
                        kb.dma(wbast[:], a_win[j].rearrange("(kc p) f -> p kc f", p=128)[:, :, 6144:6176], writes=['wbast'])
                        kb.copy('dve', wba[:], wbast[:], ['wbast'], ['wba'])
                        kb.dma(dtb[:], a_dtb[j:j + 1, :].partition_broadcast(128), writes=['dtb'])
                        kb.dma(negA[:], a_Alog[j:j + 1, :].partition_broadcast(128), writes=['negA'])
                        kb.act(negA[:], negA[:], AF.Exp, ['negA'], ['negA'])
                        kb.ts('dve', negA[:], negA[:], -1.0, None, ALU.mult, None, ['negA'], ['negA'])
                        for t in range(NT):
                            bank = t % 2
                            kb.newgen(bank)
                            for kc in range(8):
                                kb.mm(bank, ps[bank][:, 0:32], hT[:, kc, t * 128:(t + 1) * 128], wba[:, kc, :],
                                      [('hT', t // 4), 'wba'], PK(bank), last=(kc == 7))
                            gk = ('G', t)
                            kb.tt('dve', gt[:, 0, :], ps[bank][:, 16:32], dtb[:], ALU.add, PK(bank) + ['dtb'], ['gt0'])
                            kb.act(gt[:, 0, :], gt[:, 0, :], AF.Exp, ['gt0'], ['gt0'])
                            kb.act(gt[:, 0, :], gt[:, 0, :], AF.Ln, ['gt0'], ['gt0'], bias=1.0)
                            kb.tt('dve', G[:, t, 0, :], gt[:, 0, :], negA[:], ALU.mult, ['gt0', 'negA'], [gk])
                            kb.act(gt[:, 1, :], ps[bank][:, 0:16], AF.Exp, PK(bank), ['gt1'], scale=-1.0)
                            kb.act(gt[:, 1, :], gt[:, 1, :], AF.Ln, ['gt1'], ['gt1'], bias=1.0)
                            kb.act(G[:, t, 2, :], gt[:, 1, :], AF.Exp, ['gt1'], [gk], scale=-1.0)
                            kb.ts('dve', G[:, t, 1, :], gt[:, 1, :], -1.0, None, ALU.mult, None, ['gt1'], [gk])
                            b2 = 2 + t % 2
                            kb.newgen(b2)
                            kb.mm(b2, ps[b2][:, 0:16], C(C_TRIBD), G[:, t, 0, :], ['cst', gk], PK(b2))
                            kb.copy('dve', G[:, t, 3, :], ps[b2][:, 0:16], PK(b2), [gk])
                            kb.mm(b2, ps[b2][:, 16:32], C(C_SELC), G[:, t, 3, :], ['cst', gk], PK(b2))
                            kb.mm(b2, ps[b2][:, 32:48], C(C_SELA), G[:, t, 3, :], ['cst', gk], PK(b2))
                            kb.mm(b2, ps[b2][:, 48:64], C(C_SELB), G[:, t, 3, :], ['cst', gk], PK(b2))
                            kb.tt('dve', gt[:, 2, :], ps[b2][:, 16:32], G[:, t, 3, :], ALU.subtract, PK(b2) + [gk], ['gt2'])
                            kb.act(G[:, t, 5, :], gt[:, 2, :], AF.Exp, ['gt2'], [gk])
                            kb.act(glb[:, t, :, :], ps[b2][:, 32:64].rearrange("p (c h) -> p c h", c=2), AF.Exp, PK(b2), [('glb', t)])
                            kb.act(gt[:, 3, :], G[:, t, 3, :], AF.Exp, [gk], ['gt3'])
                            kb.tt('dve', G[:, t, 4, :], gt[:, 3, :], G[:, t, 2, :], ALU.mult, ['gt3', gk], [gk])
                        kb.barrier()
                    if dbg and L == 0:
                        kb.dma(dbg_d['gates'], G[:], reads=[('G', t) for t in range(NT)], writes=['dbg_gates'])
                    if stop == 'gates':
                        kb.barrier()
                        return True
                    with contextlib.ExitStack() as s2:
                        convw = sb(s2, "convw", [128, 32, 4])
                        kb.dma(convw[:], a_convT[j], writes=['convw'])
                        modes = ['conv_q'] * 8 + ['conv_k'] * 8 + ['conv_v'] * 16 + ['silu'] * 16
                        proj_fm(a_win[j], 0, 48, modes, hT, qkvz, s2, convw=convw)
                    kb.barrier()
                if stop == 'proj':
                    return True
                with contextlib.ExitStack() as s1:
                    wout_bf = sb(s1, "wout_bf", [128, 16, D], BF16)
                    load_wout(a_wout[j], 16, wout_bf, s1)
                    rr = gdn_tiles(L, j, G, glb, wout_bf, xsrc, xkey, last_layer, s1)
                    kb.barrier()
                    return rr

        def gdn_tiles(L, j, G, glb, wout_bf, xsrc, xkey, last_layer, st):
            HG = 8
            Sf = sb(st, "Sf", [128, 16, 128])
            Sb = sb(st, "Sb", [128, 16, 128], BF16)
            nw = sb(st, "nw", [128, 1])
            maskbf = sb(st, "maskbf", [128, 384], BF16)
            qT = [sb(st, "qT%d" % i, [128, 8, 128], BF16) for i in range(2)]
            kT = [sb(st, "kT%d" % i, [128, 8, 128], BF16) for i in range(2)]
            vT = [sb(st, "vT%d" % i, [128, 16, 128], BF16) for i in range(2)]
            zs = [sb(st, "zs%d" % i, [128, 16, 128], BF16) for i in range(2)]
            AA = [sb(st, "AA%d" % i, [128, 256]) for i in range(2)]
            Zg = sb(st, "Zg", [128, HG * 128])
            gam8 = sb(st, "gam8", [128, HG * 128])
            E3 = [sb(st, "E3%d" % i, [128, 384]) for i in range(2)]
            XY = sb(st, "XY", [128, HG, 2, 128])
            Pm = sb(st, "Pm", [128, HG, 128])
            vb = sb(st, "vb", [128, HG, 128], BF16)
            kbg = sb(st, "kbg", [128, HG, 128], BF16)
            TT = sb(st, "TT", [128, HG, 128], BF16)
            attnT = [sb(st, "attnT%d" % i, [128, HG, 128], BF16) for i in range(2)]
            kdec = [sb(st, "kdec%d" % i, [128, HG, 128], BF16) for i in range(2)]
            qdT = [sb(st, "qdT%d" % i, [128, HG, 128], BF16) for i in range(2)]
            usb = [sb(st, "usb%d" % i, [128, HG, 128]) for i in range(2)]
            wTb = [sb(st, "wTb%d" % i, [128, HG, 128], BF16) for i in range(2)]
            vnew = sb(st, "vnew", [128, HG, 128], BF16)
            oT = sb(st, "oT", [128, HG, 128])
            osq = sb(st, "osq", [128, 512])
            rst = sb(st, "rst", [128, 512])
            lnt = sb(st, "lnt", [128, 512])
            ogT = [sb(st, "ogT%d" % i, [128, 16, 128], BF16) for i in range(2)]
            kb.memset('dve', Sf[:], 0.0, [('Sf', h) for h in range(16)])
            kb.memset('pool', Sb[:], 0.0, [('Sb', h) for h in range(16)])
            kb.dma(nw[:], a_nw[j], writes=['nw'])
            kb.copy('dve', maskbf[:, 0:128], C(C_MINCLT), ['cst'], ['maskbf'])
            kb.copy('dve', maskbf[:, 128:256], C(C_MSTRT), ['cst'], ['maskbf'])
            kb.copy('dve', maskbf[:, 256:384], C(C_MSTR), ['cst'], ['maskbf'])
            qv = qkvz[0:8].rearrange("c p t -> p c t")
            kv = qkvz[8:16].rearrange("c p t -> p c t")
            vv = qkvz[16:32].rearrange("c p t -> p c t")
            zv = qkvz[32:48].rearrange("c p t -> p c t")

            def load_qkv(t):
                b = t % 2
                tsl = slice(t * 128, (t + 1) * 128)
                kb.dma(qT[b][:], qv[:, :, tsl], writes=[('qT', b)])
                kb.dma(kT[b][:], kv[:, :, tsl], writes=[('kT', b)])
                kb.dma(vT[b][:], vv[:, :, tsl], writes=[('vT', b)])

            def load_zs(t):
                b = t % 2
                kb.dma(zs[b][:], zv[:, :, t * 128:(t + 1) * 128], writes=[('zs', b)])

            B_G, B_T = 0, 2
            B_D = (1, 3)

            def s1(t, hg, bf):
                b = t % 2
                gk = ('G', t)
                if hg == 0 and t + 1 < NT:
                    load_qkv(t + 1)
                def g_stage(hp):
                    kb.newgen(B_G)
                    kb.mm(B_G, ps[B_G][:, 0:128], kT[b][:, hp, :], kT[b][:, hp, :], [('kT', b)], PK(B_G), inc=False)
                    kb.mm(B_G, ps[B_G][:, 128:256], kT[b][:, hp, :], qT[b][:, hp, :], [('kT', b), ('qT', b)], PK(B_G))
                    ksl = hp % 2
                    kb.tr(psbf(B_T)[:, ksl * 128:(ksl + 1) * 128], kT[b][:, hp, :], ident_bf[:], [('kT', b), 'ident_bf'], PK(B_T))

                def alpha(hl):
                    h = hg * HG + hl
                    aa = AA[h % 2]
                    kb.ts('pool', aa[:, 0:128], C(C_UT), G[:, t, 0, h:h + 1], None, ALU.mult, None, ['cst', gk], [('Ag', h % 2)])
                    kb.stt(aa[:, 128:256], C(C_ID), G[:, t, 1, h:h + 1], aa[:, 0:128], ALU.mult, ALU.add, ['cst', gk, ('Ag', h % 2)], [('Agp', h % 2)])
                    db = B_D[h % 2]
                    kb.newgen(db)
                    dk = PK(db)
                    kb.mm(db, ps[db][:, 0:256], C(C_SL), aa[:, :], ['cst', ('Ag', h % 2), ('Agp', h % 2)], dk, last=False, inc=False)
                    kb.mm(db, ps[db][:, 0:256], ident_bf[:], maskbf[:, 0:256], ['ident_bf', 'maskbf'], dk, inc=False)
                    kb.mm(db, ps[db][:, 256:384], aa[:, 128:256], C(C_SL), ['cst', ('Agp', h % 2)], dk, last=False, inc=False)
                    kb.mm(db, ps[db][:, 256:384], ident_bf[:], maskbf[:, 256:384], ['ident_bf', 'maskbf'], dk)
                    vs = 2 + h % 2
                    kb.tr(psbf(B_T)[:, vs * 128:(vs + 1) * 128], vT[b][:, h, :], ident_bf[:], [('vT', b), 'ident_bf'], PK(B_T))

                def beta(hl):
                    h = hg * HG + hl
                    hp = h // 2
                    db = B_D[h % 2]
                    dk = PK(db)
                    Gps = ps[B_G][:, 0:128]
                    QKps = ps[B_G][:, 128:256]
                    ksl = hp % 2
                    psK = psbf(B_T)[:, ksl * 128:(ksl + 1) * 128]
                    vs = 2 + h % 2
                    psV = psbf(B_T)[:, vs * 128:(vs + 1) * 128]
                    e3 = E3[h % 2]
                    kb.act(e3[:], ps[db][:, 0:384], AF.Exp, dk, [('E3', h % 2)])
                    kb.tt('dve', XY[:, hl, 0, :], e3[:, 128:256], Gps, ALU.mult, [('E3', h % 2)] + PK(B_G), [('XY', hl)])
                    kb.tt('dve', XY[:, hl, 1, :], e3[:, 256:384], Gps, ALU.mult, [('E3', h % 2)] + PK(B_G), [('XY', hl)])
                    kb.tt('dve', attnT[bf][:, hl, :], e3[:, 0:128], QKps, ALU.mult, [('E3', h % 2)] + PK(B_G), [('attnT', bf, hl)])
                    kb.stt(Pm[:, hl, :], XY[:, hl, 0, :], -1.0, C(C_ID), ALU.mult, ALU.add, [('XY', hl), 'cst'], [('Pm', hl)])
                    kb.tt('pool', qdT[bf][:, hl, :], qT[b][:, hp, :], gam8[:, hl * 128:(hl + 1) * 128], ALU.mult, [('qT', b), ('gam8', hl // 4)], [('qdT', bf, hl)])
                    kb.act(vb[:, hl, :], psV, AF.Identity, PK(B_T) + [gk], [('vb', hl)], scale=G[:, t, 2, h:h + 1])
                    kb.act(kbg[:, hl, :], psK, AF.Identity, PK(B_T) + [gk], [('kbg', hl)], scale=G[:, t, 4, h:h + 1])
                    kb.ts('dve', kdec[bf][:, hl, :], psK, G[:, t, 5, h:h + 1], None, ALU.mult, None, PK(B_T) + [gk], [('kdec', bf, hl)])

                for hl in range(HG):
                    h = hg * HG + hl
                    kb.ts('pool', Zg[:, hl * 128:(hl + 1) * 128], C(C_ID), G[:, t, 3, h:h + 1], None, ALU.mult, None, ['cst', gk], [('Zg', hl // 4)])
                for i4 in range(2):
                    db = B_D[i4]
                    kb.newgen(db)
                    kb.mm(db, ps[db][:, :], C(C_ONES), Zg[:, i4 * 512:(i4 + 1) * 512], ['cst', ('Zg', i4)], PK(db))
                    kb.act(gam8[:, i4 * 512:(i4 + 1) * 512], ps[db][:, :], AF.Exp, PK(db), [('gam8', i4)])
                yield
                g_stage((hg * HG) // 2)
                alpha(0)
                for hl in range(HG):
                    if hl + 1 < HG and (hl + 1) % 2 == 1:
                        alpha(hl + 1)
                    beta(hl)
                    if hl + 1 < HG and (hl + 1) % 2 == 0:
                        g_stage((hg * HG + hl + 1) // 2)
                        alpha(hl + 1)
                    yield
                for lvl in range(NEU_LO, 6):
                    for pr in range(HG // 2):
                        bank = pr % 2
                        kb.newgen(bank)
                        for u_ in range(2):
                            hl = pr * 2 + u_
                            X = XY[:, hl, 0, :]
                            Y = XY[:, hl, 1, :]
                            o0 = u_ * 256
                            if lvl < 5:
                                kb.mm(bank, ps[bank][:, o0:o0 + 128], Y, X, [('XY', hl)], PK(bank), inc=False)
                            kb.mm(bank, ps[bank][:, o0 + 128:o0 + 256], X, Y, [('XY', hl)], PK(bank), inc=(u_ == 1))
                        if lvl < 5:
                            kb.copy('act', XY[:, pr * 2:pr * 2 + 2, :, :], ps[bank][:, :].rearrange("p (h x c) -> p h x c", h=2, x=2),
                                    PK(bank), [('XY', pr * 2), ('XY', pr * 2 + 1)])
                        else:
                            kb.copy('act', XY[:, pr * 2:pr * 2 + 2, 1, :], ps[bank][:, :].rearrange("p (h x c) -> p h x c", h=2, x=2)[:, :, 1, :],
                                    PK(bank), [('XY', pr * 2), ('XY', pr * 2 + 1)])
                        yield
                    for q4 in range(HG // 4):
                        bank = 2 + q4 % 2
                        kb.newgen(bank)
                        for u_ in range(4):
                            hl = q4 * 4 + u_
                            kb.mm(bank, ps[bank][:, u_ * 128:(u_ + 1) * 128], XY[:, hl, 1, :], Pm[:, hl, :], [('XY', hl), ('Pm', hl)], PK(bank), inc=(u_ == 3))
                        hs = slice(q4 * 4, q4 * 4 + 4)
                        pk = [('Pm', q4 * 4 + u_) for u_ in range(4)]
                        if lvl < 5:
                            kb.tt('dve', Pm[:, hs, :], Pm[:, hs, :], ps[bank][:, :].rearrange("p (h c) -> p h c", h=4), ALU.add, PK(bank) + pk, pk)
                        else:
                            kb.tt('dve', TT[:, hs, :], Pm[:, hs, :], ps[bank][:, :].rearrange("p (h c) -> p h c", h=4), ALU.add,
                                  PK(bank) + pk, [('TT', q4 * 4 + u_) for u_ in range(4)])
                        yield
                for pr in range(HG // 2):
                    bank = pr % 2
                    kb.newgen(bank)
                    for u_ in range(2):
                        hl = pr * 2 + u_
                        kb.mm(bank, ps[bank][:, u_ * 128:(u_ + 1) * 128], TT[:, hl, :], vb[:, hl, :], [('TT', hl), ('vb', hl)], PK(bank), inc=False)
                        kb.mm(bank, ps[bank][:, 256 + u_ * 128:256 + (u_ + 1) * 128], kbg[:, hl, :], TT[:, hl, :], [('TT', hl), ('kbg', hl)], PK(bank), inc=(u_ == 1))
                    kb.copy('act', usb[bf][:, pr * 2:pr * 2 + 2, :], ps[bank][:, 0:256].rearrange("p (h c) -> p h c", h=2), PK(bank),
                            [('usb', bf, pr * 2), ('usb', bf, pr * 2 + 1)])
                    kb.copy('dve', wTb[bf][:, pr * 2:pr * 2 + 2, :], ps[bank][:, 256:512].rearrange("p (h c) -> p h c", h=2), PK(bank),
                            [('wTb', bf, pr * 2), ('wTb', bf, pr * 2 + 1)])
                    yield

            B_WO = (4, 5)
            B_SS = (6, 7)
            B_N = 4

            def s2(t, hg, bf):
                b = t % 2
                if hg == 0 and t + 1 < NT:
                    load_zs(t + 1)
                for ch in range(2):
                    rs = slice(ch * 64, (ch + 1) * 64)
                    for q4 in range(HG // 4):
                        bw = B_WO[q4]
                        kb.newgen(bw)
                        for u_ in range(4):
                            hl = q4 * 4 + u_
                            h = hg * HG + hl
                            kb.mm(bw, ps[bw][rs, u_ * 128:(u_ + 1) * 128], wTb[bf][:, hl, rs], Sb[:, h, :], [('wTb', bf, hl), ('Sb', h)], PK(bw),
                                  halves=(ch,), inc=(u_ == 3))
                    for q4 in range(HG // 4):
                        bw = B_WO[q4]
                        hs = slice(q4 * 4, q4 * 4 + 4)
                        vk = [('vnew', q4 * 4 + u_) for u_ in range(4)]
                        kb.tt('dve', vnew[rs, hs, :], usb[bf][rs, hs, :], ps[bw][rs, :].rearrange("p (h c) -> p h c", h=4), ALU.subtract,
                              PK(bw) + [('usb', bf, q4 * 4 + u_) for u_ in range(4)], vk)
                    yield
                    for q4 in range(HG // 4):
                        bw = B_WO[q4]
                        bs = B_SS[q4]
                        kb.newgen(bw)
                        kb.newgen(bs)
                        for u_ in range(4):
                            hl = q4 * 4 + u_
                            h = hg * HG + hl
                            kb.mm(bw, ps[bw][:, u_ * 64:(u_ + 1) * 64], Sb[:, h, :], qdT[bf][:, hl, rs], [('Sb', h), ('qdT', bf, hl)], PK(bw), last=False, inc=False)
                            kb.mm(bw, ps[bw][:, u_ * 64:(u_ + 1) * 64], vnew[rs, hl, :], attnT[bf][rs, hl, rs], [('vnew', hl), ('attnT', bf, hl)], PK(bw), inc=(u_ == 3))
                        for u_ in range(4):
                            hl = q4 * 4 + u_
                            kb.mm(bs, ps[bs][:, u_ * 128:(u_ + 1) * 128], kdec[bf][rs, hl, :], vnew[rs, hl, :], [('kdec', bf, hl), ('vnew', hl)], PK(bs), inc=(u_ == 3))
                    yield
                    for q4 in range(HG // 4):
                        bw = B_WO[q4]
                        bs = B_SS[q4]
                        hs = slice(q4 * 4, q4 * 4 + 4)
                        kb.copy('act', oT[:, hs, rs], ps[bw][:, 0:256].rearrange("p (h c) -> p h c", h=4), PK(bw),
                                [('oT', q4 * 4 + u_) for u_ in range(4)])
                        for u_ in range(4):
                            hl = q4 * 4 + u_
                            h = hg * HG + hl
                            kb.stt(Sf[:, h, :], Sf[:, h, :], glb[:, t, ch, h:h + 1], ps[bs][:, u_ * 128:(u_ + 1) * 128], ALU.mult, ALU.add,
                                   [('Sf', h), ('glb', t)] + PK(bs), [('Sf', h)])
                        h0 = hg * HG + q4 * 4
                        kb.copy('act', Sb[:, h0:h0 + 4, :], Sf[:, h0:h0 + 4, :], [('Sf', h0 + u_) for u_ in range(4)], [('Sb', h0 + u_) for u_ in range(4)])
                    yield
                for q4 in range(HG // 4):
                    hs = slice(q4 * 4, q4 * 4 + 4)
                    h0 = hg * HG + q4 * 4
                    ok = [('oT', q4 * 4 + u_) for u_ in range(4)]
                    kb.act(osq[:].rearrange("p (h c) -> p h c", h=4), oT[:, hs, :], AF.Square, ok, ['osq'])
                    kb.newgen(B_N)
                    kb.mm(B_N, ps[B_N][:, :], C(C_ONES), osq[:], ['cst', 'osq'], PK(B_N))
                    rstd_from_ss(ps[B_N][:, :], rst[:], 128, PK(B_N), ['rst'], lnt[:])
                    kb.tt('dve', osq[:].rearrange("p (h c) -> p h c", h=4), oT[:, hs, :], rst[:].rearrange("p (h c) -> p h c", h=4), ALU.mult,
                          ok + ['rst', 'osq'], ['osq'])
                    kb.stt(ogT[b][:, h0:h0 + 4, :], osq[:].rearrange("p (h c) -> p h c", h=4), nw[:, 0:1], zs[b][:, h0:h0 + 4, :], ALU.mult, ALU.mult,
                           ['osq', 'nw', ('zs', b)], [('ogT', b)])
                    yield
                if hg == 1:
                    outproj_tile(L, t, ogT[b], 16, wout_bf, xsrc, xkey, last_layer, [('ogT', b)])
                    yield

            steps = [(t, hg) for t in range(NT) for hg in range(2)]
            load_qkv(0)
            load_zs(0)
            for _ in s1(0, 0, 0):
                pass
            RATIO = GRATIO
            for k in range(len(steps)):
                g1 = s1(steps[k + 1][0], steps[k + 1][1], (k + 1) % 2) if k + 1 < len(steps) else None
                g2 = s2(steps[k][0], steps[k][1], k % 2)
                while g1 is not None or g2 is not None:
                    if g1 is not None:
                        for _ in range(RATIO):
                            try:
                                next(g1)
                            except StopIteration:
                                g1 = None
                                break
                    if g2 is not None:
                        try:
                            next(g2)
                        except StopIteration:
                            g2 = None

        def fox_layer(L, j, xsrc, xkey, last_layer):
            with contextlib.ExitStack() as sl:
                Vall = sb(sl, "Vall", [128, NT, 16, 65], BF16)
                cumT = sb(sl, "cumT", [128, NT, 16])
                with contextlib.ExitStack() as s1:
                    hT = sb(s1, "hT", [128, 8, S], BF16)
                    with contextlib.ExitStack() as s2:
                        A_b = sb(s2, "A_b", [128, D])
                        B_b = sb(s2, "B_b", [128, D])
                        adaln(L, A_b, B_b, s2)
                        norm_phase(L, xsrc, xkey, hT, A_b, B_b, s2)
                        kb.barrier()
                    with contextlib.ExitStack() as s2:
                        wfst = sb(s2, "wfst", [128, 8, 16])
                        wf = sb(s2, "wf", [128, 8, 16], BF16)
                        nfb = sb(s2, "nfb", [16, 1])
                        spl = sb(s2, "spl", [16, 2048])
                        cums = sb(s2, "cums", [16, S])
                        onesr = sb(s2, "onesr", [16, 2048], BF16)
                        c1b = sb(s2, "c1b", [16, S], BF16)
                        kb.dma(wfst[:], b_win[j].rearrange("(kc p) f -> p kc f", p=128)[:, :, 4096:4112], writes=['wfst'])
                        kb.copy('dve', wf[:], wfst[:], ['wfst'], ['wf'])
                        kb.dma(nfb[:], b_fb[j], writes=['nfb'])
                        kb.ts('dve', nfb[:], nfb[:], -1.0, None, ALU.mult, None, ['nfb'], ['nfb'])
                        kb.memset('pool', onesr[:], 1.0, ['onesr'])
                        for half in range(2):
                            for tl in range(4):
                                tb = half * 4 + tl
                                bank = tb % 2
                                kb.newgen(bank)
                                for kc in range(8):
                                    kb.mm(bank, ps[bank][0:16, :], wf[:, kc, :], hT[:, kc, tb * 512:(tb + 1) * 512], ['wf', ('hT', tb)], PK(bank),
                                          halves=(0,), last=(kc == 7))
                                kb.act(spl[:, tl * 512:(tl + 1) * 512], ps[bank][0:16, :], AF.Exp, PK(bank) + ['nfb'], [('spl', tl)], scale=-1.0, bias=nfb[:, 0:1])
                                kb.act(spl[:, tl * 512:(tl + 1) * 512], spl[:, tl * 512:(tl + 1) * 512], AF.Ln, [('spl', tl)], [('spl', tl)], bias=1.0)
                            init = 0.0 if half == 0 else cums[:, 2047:2048]
                            kb.op('dve', lambda g: g.tensor_tensor_scan(out=cums[:, half * 2048:(half + 1) * 2048], data0=onesr[:], data1=spl[:], initial=init,
                                                                      op0=ALU.mult, op1=ALU.add),
                                  [('spl', tl) for tl in range(4)] + ['onesr', 'cums'], ['cums'])
                        kb.ts('dve', c1b[:], cums[:], -1.0, None, ALU.mult, None, ['cums'], ['c1b'])
                        kb.dma(c1s, c1b[:], reads=['c1b'], writes=['c1s'])
                        for t in range(NT):
                            bank = 2 + t % 2
                            kb.newgen(bank)
                            kb.mm(bank, ps[bank][:, 0:16], cums[:, t * 128:(t + 1) * 128], cst[0:16, C_ID, 0:16], ['cums', 'cst'], PK(bank))
                            kb.copy('dve', cumT[:, t, :], ps[bank][:, 0:16], PK(bank), [('cumT', t)])
                        kb.barrier()
                    if stop == 'f1':
                        return True
                    with contextlib.ExitStack() as s2:
                        qn = sb(s2, "qn", [128, 1])
                        kn = sb(s2, "kn", [128, 1])
                        kb.dma(qn[:], b_qn2[j], writes=['ppsc'])
                        kb.dma(kn[:], b_kn2[j], writes=['ppsc'])
                        kb.ts('dve', qn[:], qn[:], 0.125, None, ALU.mult, None, ['ppsc'], ['ppsc'])
                        proj_fm(b_win[j], 0, 16, ['rms'] * 16, hT, qks, s2, pp_scalars=[qn[:, 0:1]] * 8 + [kn[:, 0:1]] * 8, nred=64, ones_ap=C(C_ONESBD))
                    if stop == 'f2':
                        return True
                    with contextlib.ExitStack() as s2:
                        wst = [sb(s2, "wvst%d" % i, [128, 8, 128]) for i in range(2)]
                        wvz = sb(s2, "wvz", [128, 8, 2048], BF16)
                        zt = [sb(s2, "zt%d" % i, [128, D], BF16) for i in range(2)]
                        wv = b_win[j].rearrange("(kc p) f -> p kc f", p=128)
                        for g in range(16):
                            b = g % 2
                            kb.dma(wst[b][:], wv[:, :, 2048 + g * 128:2048 + (g + 1) * 128], writes=[('wvst', b)])
                            kb.copy('pool', wvz[:, :, g * 128:(g + 1) * 128], wst[b][:], [('wvst', b)], [('wvz', g // 4)])
                        kb.memset('dve', Vall[:, :, :, 64:65], 1.0, [('Vall1',)])
                        for t in range(NT):
                            for fb in range(4):
                                bank = (t * 4 + fb) % 4
                                kb.newgen(bank)
                                for kc in range(8):
                                    kb.mm(bank, ps[bank][:, :], hT[:, kc, t * 128:(t + 1) * 128], wvz[:, kc, fb * 512:(fb + 1) * 512],
                                          [('hT', t // 4), ('wvz', fb)], PK(bank), last=(kc == 7))
                                if fb < 2:
                                    kb.copy('dve', Vall[:, t, fb * 8:(fb + 1) * 8, 0:64], ps[bank][:, :].rearrange("p (h d) -> p h d", h=8), PK(bank), [('Vall', t)])
                                else:
                                    kb.act(zt[t % 2][:, (fb - 2) * 512:(fb - 1) * 512], ps[bank][:, :], AF.Silu, PK(bank), [('zt', t % 2)])
                            kb.dma(zss[t * 128:(t + 1) * 128, :], zt[t % 2][:], reads=[('zt', t % 2)], writes=[('zss', t)])
                        kb.barrier()
                if stop == 'f3':
                    return True
                with contextlib.ExitStack() as s1:
                    Oall = sb(s1, "Oall", [128, NT, D], BF16)
                    with contextlib.ExitStack() as s2:
                        QA = [sb(s2, "QA%d" % i, [65, S], BF16) for i in range(2)]
                        KA = [sb(s2, "KA%d" % i, [65, S], BF16) for i in range(2)]
                        PT = [sb(s2, "PT%d" % i, [128, 512], BF16) for i in range(4)]
                        rl = sb(s2, "rl", [128, 4])
                        for i in range(2):
                            kb.memset('dve', KA[i][64:65, :], 1.0, [('KA1', i)])

                        def load_head(h):
                            b = h % 2
                            r0 = (h % 2) * 64
                            kb.dma(QA[b][0:64, :], qks[h // 2][r0:r0 + 64, :], writes=[('QA', b)])
                            kb.dma(QA[b][64:65, :], c1s[h:h + 1, :], reads=['c1s'], writes=[('QA', b)])
                            kb.dma(KA[b][0:64, :], qks[8 + h // 2][r0:r0 + 64, :], writes=[('KA', b)])

                        load_head(0)
                        pairs = [(h, qb, kt) for h in range(16) for qb in range(8) for kt in range(4 * (qb + 1))]
                        NSB = 4
                        LA = 2

                        def stageA(n):
                            h, qb, kt = pairs[n]
                            b = h % 2
                            if qb == 0 and kt == 0 and h + 1 < 16:
                                load_head(h + 1)
                            i0 = max(0, kt - 4 * qb)
                            sbk = n % NSB
                            kb.newgen(sbk)
                            kb.mm(sbk, ps[sbk][:, i0 * 128:512], KA[b][0:65, kt * 128:(kt + 1) * 128], QA[b][0:65, qb * 512 + i0 * 128:(qb + 1) * 512],
                                  [('KA', b), ('KA1', b), ('QA', b)], PK(sbk))

                        def stageB(n):
                            h, qb, kt = pairs[n]
                            nkt = 4 * (qb + 1)
                            jd = kt - 4 * qb
                            i0 = max(0, jd)
                            sbk = n % NSB
                            pt = PT[n % 4]
                            ptk = ('PT', n % 4)
                            obk = 4 + (h * 8 + qb) % 2
                            if kt == 0:
                                kb.newgen(obk)
                            kb.act(pt[:, i0 * 128:512], ps[sbk][:, i0 * 128:512], AF.Exp, PK(sbk) + [('cumT', kt)], [ptk], bias=cumT[:, kt, h:h + 1])
                            if jd >= 0:
                                kb.tt('pool', pt[:, jd * 128:(jd + 1) * 128], pt[:, jd * 128:(jd + 1) * 128], caus_bf[:], ALU.mult, [ptk, 'caus_bf'], [ptk])
                            for i in range(i0, 4):
                                kb.mm(obk, ps[obk][:, i * 65:(i + 1) * 65], pt[:, i * 128:(i + 1) * 128], Vall[:, kt, h, :],
                                      [ptk, ('Vall', kt), ('Vall1',)], PK(obk), last=(kt == nkt - 1), inc=(i == 3))
                            if kt == nkt - 1:
                                ov = ps[obk][:, 0:260].rearrange("p (i d) -> p i d", i=4)
                                kb.op('dve', lambda g: g.reciprocal(out=rl[:], in_=ov[:, :, 64]), PK(obk), ['rl'])
                                kb.tt('dve', Oall[:, qb * 4:(qb + 1) * 4, h * 64:(h + 1) * 64], ov[:, :, 0:64], rl[:].unsqueeze(2).broadcast_to([128, 4, 64]),
                                      ALU.mult, PK(obk) + ['rl'], [('Oall', qb)])

                        for n in range(len(pairs) + LA):
                            if n < len(pairs):
                                stageA(n)
                            if n - LA >= 0:
                                stageB(n - LA)
                        kb.barrier()
                    if stop == 'f4':
                        return True
                    with contextlib.ExitStack() as s2:
                        wout_bf = sb(s2, "wout_bf", [128, 8, D], BF16)
                        load_wout(b_wout[j], 8, wout_bf, s2)
                        zt = [sb(s2, "zt%d" % i, [128, D], BF16) for i in range(2)]
                        og = [sb(s2, "og%d" % i, [128, D], BF16) for i in range(2)]
                        ogT = [sb(s2, "ogT%d" % i, [128, 8, 128], BF16) for i in range(2)]
                        def fin1(t):
                            b = t % 2
                            kb.dma(zt[b][:], zss[t * 128:(t + 1) * 128, :], reads=[('zss', t)], writes=[('zt', b)])
                            kb.tt('pool', og[b][:], Oall[:, t, :], zt[b][:], ALU.mult, [('Oall', t // 4), ('zt', b)], [('og', b)])
                            bank = 4 + b
                            for kc in range(8):
                                kb.tr(psbf(bank)[:, kc * 128:(kc + 1) * 128], og[b][:, kc * 128:(kc + 1) * 128], ident_bf[:], [('og', b), 'ident_bf'], PK(bank))
                            kb.copy('act', ogT[b][:], psbf(bank).rearrange("p (k t) -> p k t", k=8), PK(bank), [('ogT', b)])

                        fin1(0)
                        for t in range(NT):
                            if t + 1 < NT:
                                fin1(t + 1)
                            outproj_tile(L, t, ogT[t % 2], 8, wout_bf, xsrc, xkey, last_layer, [('ogT', t % 2)])
                        kb.barrier()

        xsrc, xkey = x_in, 'xin'
        for L in range(n_layers):
            last = (L == n_layers - 1)
            if L % 2 == 0:
                stopped = gdn_layer(L, L // 2, xsrc, xkey, last)
            else:
                stopped = fox_layer(L, L // 2, xsrc, xkey, last)
            xsrc, xkey = xres, 'xres'
            kb.barrier()
            if stopped:
                break
        if dbg:
            kb.dma(dbg_d['xres'], xres, reads=[('xres', t) for t in range(NT)], writes=['dbg_x'])
            kb.dma(dbg_d['qkvz'], qkvz, writes=['dbg_q'])
        kb.barrier()
        print("instructions emitted:", kb.ninst, {k: v for k, v in kb.cnt.items()})
    return nc


def make_in_maps(inputs):
    consts = make_consts()
    f = lambda a: np.ascontiguousarray(np.asarray(a, dtype=np.float32))
    x = f(inputs["x"])
    c = f(inputs["c"])
    shared = {
        "norm_w": f(inputs["norm_w"]),
        "final_norm_w": f(inputs["final_norm_w"]).reshape(1, D),
        "ada_w": f(inputs["ada_w"]),
        "ada_b": f(inputs["ada_b"]),
        "a_w_in": f(inputs["a_w_in"]),
        "a_convT": f(np.transpose(f(inputs["a_conv_w"]), (0, 2, 1)).reshape(2, 32, 128, 4).transpose(0, 2, 1, 3)),
        "a_A_log": f(inputs["a_A_log"]),
        "a_dt_bias": f(inputs["a_dt_bias"]),
        "a_norm_w": f(inputs["a_norm_w"]).reshape(2, 128, 1),
        "a_w_out": f(inputs["a_w_out"]),
        "b_w_in": f(inputs["b_w_in"]),
        "b_f_bias": f(inputs["b_f_bias"]).reshape(2, 16, 1),
        "b_qn2": f(np.tile(f(inputs["b_qn_w"]), (1, 2))).reshape(2, 128, 1),
        "b_kn2": f(np.tile(f(inputs["b_kn_w"]), (1, 2))).reshape(2, 128, 1),
        "b_w_out": f(inputs["b_w_out"]),
        "consts": consts,
    }
    maps = []
    for b in range(8):
        m = dict(shared)
        m["x"] = x[b]
        m["cT"] = f(c[b].reshape(8, 128).T)
        maps.append(m)
    return maps


_NC_CACHE = {}


def kernel(**inputs):
    if 'nc' not in _NC_CACHE:
        _NC_CACHE['nc'] = build_program()
    nc = _NC_CACHE['nc']
    in_maps = make_in_maps(inputs)
    res = run_bass_kernel_spmd(nc, in_maps, core_ids=list(range(8)))
    out = np.stack([np.asarray(r["out"], dtype=np.float32) for r in res.results], axis=0)
    return out
```

```python
import contextlib
import numpy as np
import concourse.bass as bass
import concourse.mybir as mybir
from concourse.bass_utils import run_bass_kernel_spmd

F32 = mybir.dt.float32
BF16 = mybir.dt.bfloat16
AF = mybir.ActivationFunctionType
ALU = mybir.AluOpType
AX = mybir.AxisListType

S = 4096
D = 1024
NT = 32
EPS = 1e-6
NEG = -30000.0
DEPTH = 4
GDN_IN = 6176
FOX_IN = 4112

C_ID, C_ONES, C_UT, C_SL, C_MINCLT, C_MSTRT, C_MSTR, C_TRIBD, C_SELC, C_SELA, C_SELB, C_CAUS, C_ONESBD = range(13)
NCONST = 13


def make_consts():
    i = np.arange(128)
    r = i[:, None]
    c = i[None, :]
    same = (r // 64) == (c // 64)
    m = np.zeros((NCONST, 128, 128), np.float32)
    m[C_ID] = (r == c)
    m[C_ONES] = 1.0
    m[C_UT] = (r <= c)
    m[C_SL] = (r > c)
    m[C_MINCLT] = np.where(same & (r <= c), 0.0, NEG)
    m[C_MSTRT] = np.where(same & (r < c), 0.0, NEG)
    m[C_MSTR] = np.where(same & (r > c), 0.0, NEG)
    m[C_TRIBD] = (same & (r <= c))
    m[C_SELC] = (r == (c // 64) * 64 + 63)
    m[C_SELA] = (r == 63) * np.ones((1, 128))
    m[C_SELB] = (r == 127) * np.ones((1, 128))
    m[C_CAUS] = (r <= c)
    m[C_ONESBD] = same
    return m.astype(np.float32)


class KB:
    NS = 24

    def __init__(self, nc, es):
        self.nc = nc
        self.eng = {'pe': nc.tensor, 'act': nc.scalar, 'dve': nc.vector, 'pool': nc.gpsimd, 'sp': nc.sync}
        self.sem = {k: es.enter_context(nc.semaphore("s_" + k)) for k in ['pe', 'act', 'dve', 'pool']}
        self.cnt = {k: 0 for k in self.sem}
        self.seen = {k: {} for k in self.eng}
        self.dsem = [es.enter_context(nc.semaphore("d%d" % i)) for i in range(self.NS)]
        self.dval = [0] * self.NS
        self.dnext = 0
        self.lastw = {}
        self.readers = {}
        self.fresh = {}
        self.ninst = 0
        self.relax = False

    def _wait(self, e, tok):
        sk, v = tok
        if sk == e and e == 'pe':
            return
        if self.seen[e].get(sk, 0) >= v:
            return
        sem = self.sem[sk] if isinstance(sk, str) else self.dsem[sk[1]]
        self.eng[e].wait_ge(sem, v)
        self.seen[e][sk] = v

    def _deps(self, e, reads, writes):
        for k in reads:
            t = self.lastw.get(k)
            if t is not None and not (self.relax and t[0] == e):
                self._wait(e, t)
            if isinstance(k, tuple) and k[0] == 'ps':
                for sk, t in self.readers.get(k, {}).items():
                    if sk != e:
                        self._wait(e, t)
        for k in writes:
            t = self.lastw.get(k)
            if t is not None and not (self.relax and t[0] == e):
                self._wait(e, t)
            for t in self.readers.get(k, {}).values():
                if not (self.relax and t[0] == e):
                    self._wait(e, t)

    def _record(self, tok, reads, writes):
        for k in reads:
            self.readers.setdefault(k, {})[tok[0]] = tok
        for k in writes:
            self.lastw[k] = tok
            self.readers[k] = {}

    def op(self, e, fn, reads=(), writes=(), inc=True):
        self._deps(e, reads, writes)
        ins = fn(self.eng[e])
        self.ninst += 1
        if inc:
            self.cnt[e] += 1
            ins.then_inc(self.sem[e], 1)
            tok = (e, self.cnt[e])
        else:
            tok = (e, self.cnt[e] + 1)
        self._record(tok, reads, writes)
        return tok

    def dma(self, out, in_, reads=(), writes=(), q='sp'):
        i = self.dnext
        self.dnext = (self.dnext + 1) % self.NS
        if self.dval[i] > 0:
            self._wait(q, (('d', i), self.dval[i]))
        self._deps(q, reads, writes)
        ins = self.eng[q].dma_start(out=out, in_=in_)
        self.ninst += 1
        self.dval[i] += 16
        ins.then_inc(self.dsem[i], 16)
        tok = (('d', i), self.dval[i])
        self._record(tok, reads, writes)
        return tok

    def barrier(self):
        for e in self.eng:
            for o in self.sem:
                if self.cnt[o] > 0:
                    self._wait(e, (o, self.cnt[o]))
            for i in range(self.NS):
                if self.dval[i] > 0:
                    self._wait(e, (('d', i), self.dval[i]))
        self.lastw = {}
        self.readers = {}

    def newgen(self, bank):
        self.fresh[(bank, 0)] = True
        self.fresh[(bank, 1)] = True

    def mm(self, bank, out, lhsT, rhs, reads, writes, halves=(0, 1), last=True, inc=None):
        st = False
        for h in halves:
            if self.fresh.get((bank, h), True):
                st = True
            self.fresh[(bank, h)] = False
        if inc is None:
            inc = last

        def fn(e):
            return e.matmul(out, lhsT=lhsT, rhs=rhs, start=st, stop=last, skip_group_check=True)
        return self.op('pe', fn, reads, writes, inc=inc)

    def tr(self, out, in_, ident, reads, writes):
        return self.op('pe', lambda e: e.transpose(out, in_, ident), reads, writes)

    def act(self, out, in_, func, reads, writes, scale=None, bias=None):
        def fn(e):
            kw = {}
            if scale is not None:
                kw['scale'] = scale
            if bias is not None:
                kw['bias'] = bias
            return e.activation(out=out, in_=in_, func=func, **kw)
        return self.op('act', fn, reads, writes)

    def tt(self, e, out, in0, in1, op, reads, writes):
        return self.op(e, lambda g: g.tensor_tensor(out=out, in0=in0, in1=in1, op=op), reads, writes)

    def ts(self, e, out, in0, s1, s2, op0, op1, reads, writes):
        if op1 is None and e == 'pool' and op0 == ALU.mult:
            op1, s2 = ALU.add, 0.0
        if op1 is None:
            return self.op(e, lambda g: g.tensor_scalar(out=out, in0=in0, scalar1=s1, scalar2=None, op0=op0), reads, writes)
        return self.op(e, lambda g: g.tensor_scalar(out=out, in0=in0, scalar1=s1, scalar2=s2, op0=op0, op1=op1), reads, writes)

    def stt(self, out, in0, scalar, in1, op0, op1, reads, writes):
        return self.op('dve', lambda g: g.scalar_tensor_tensor(out=out, in0=in0, scalar=scalar, in1=in1, op0=op0, op1=op1), reads, writes)

    def copy(self, e, out, in_, reads, writes):
        if e == 'act':
            return self.op('act', lambda g: g.copy(out=out, in_=in_), reads, writes)
        return self.op(e, lambda g: g.tensor_copy(out=out, in_=in_), reads, writes)

    def memset(self, e, ap, val, writes):
        return self.op(e, lambda g: g.memset(ap, val), (), writes)


class _Stop(Exception):
    pass


UWB = 7
GRATIO = 3
NEU_LO = 1


def build_program(n_layers=DEPTH, dbg=False, stop=None):
    nc = bass.Bass("TRN2", target_bir_lowering=False)

    def din(name, shape, dt=F32):
        return nc.dram_tensor(name, list(shape), dt, kind="ExternalInput").ap()

    def dscr(name, shape, dt):
        return nc.dram_tensor(name, list(shape), dt, kind="Internal").ap()

    x_in = din("x", [S, D])
    cT_in = din("cT", [128, 8])
    normw_in = din("norm_w", [DEPTH, D])
    fnw_in = din("final_norm_w", [1, D])
    adaw_in = din("ada_w", [DEPTH, D, 3 * D])
    adab_in = din("ada_b", [DEPTH, 3 * D])
    a_win = din("a_w_in", [2, D, GDN_IN])
    a_convT = din("a_convT", [2, 128, 32, 4])
    a_Alog = din("a_A_log", [2, 16])
    a_dtb = din("a_dt_bias", [2, 16])
    a_nw = din("a_norm_w", [2, 128, 1])
    a_wout = din("a_w_out", [2, 2048, D])
    b_win = din("b_w_in", [2, D, FOX_IN])
    b_fb = din("b_f_bias", [2, 16, 1])
    b_qn2 = din("b_qn2", [2, 128, 1])
    b_kn2 = din("b_kn2", [2, 128, 1])
    b_wout = din("b_w_out", [2, D, D])
    consts_in = din("consts", [NCONST, 128, 128])
    out_d = nc.dram_tensor("out", [S, D], F32, kind="ExternalOutput").ap()

    xres = dscr("xres", [S, D], F32)
    qkvz = dscr("qkvz", [48, 128, S], BF16)
    qks = dscr("qks", [16, 128, S], BF16)
    zss = dscr("zss", [S, D], BF16)
    c1s = dscr("c1s", [16, S], BF16)
    dbg_d = {}
    if dbg:
        dbg_d['hT'] = nc.dram_tensor("dbg_hT", [128, 8, S], BF16, kind="ExternalOutput").ap()
        dbg_d['xres'] = nc.dram_tensor("dbg_xres", [S, D], F32, kind="ExternalOutput").ap()
        dbg_d['qkvz'] = nc.dram_tensor("dbg_qkvz", [48, 128, S], BF16, kind="ExternalOutput").ap()
        dbg_d['gates'] = nc.dram_tensor("dbg_gates", [128, NT, 6, 16], F32, kind="ExternalOutput").ap()

    es = contextlib.ExitStack()
    with es:
        kb = KB(nc, es)

        uid = [0]

        def sb(stack, name, shape, dt=F32):
            uid[0] += 1
            return stack.enter_context(nc.sbuf_tensor("%s_%d" % (name, uid[0]), list(shape), dt))

        ps = [es.enter_context(nc.psum_tensor("ps%d" % i, [128, 512], F32)) for i in range(8)]

        def PK(bank, lo=0, hi=512):
            return [('ps', bank)]

        def psbf(i):
            return ps[i][:].bitcast(BF16)

        cst = sb(es, "cst", [128, NCONST, 128])
        ident_bf = sb(es, "ident_bf", [128, 128], BF16)
        caus_bf = sb(es, "caus_bf", [128, 128], BF16)
        condT = sb(es, "condT", [128, 8])
        gate_b = sb(es, "gate_b", [128, D])
        fnw_b = sb(es, "fnw_b", [128, D])
        xt = [sb(es, "xt%d" % i, [128, D]) for i in range(2)]
        junk = sb(es, "junk", [128, D])
        sm = sb(es, "sm", [128, 8])
        ones_row = sb(es, "ones_row", [1, 128])

        def C(i):
            return cst[:, i, :]

        kb.dma(cst[:], consts_in.rearrange("n p f -> p n f"), writes=['cst'])
        kb.copy('dve', ident_bf[:], C(C_ID), ['cst'], ['ident_bf'])
        kb.copy('dve', caus_bf[:], C(C_CAUS), ['cst'], ['caus_bf'])
        kb.memset('dve', ones_row[:], 1.0, ['ones_row'])
        kb.dma(condT[:], cT_in, writes=['condT'])
        kb.act(condT[:], condT[:], AF.Silu, ['condT'], ['condT'])
        kb.dma(fnw_b[:], fnw_in.partition_broadcast(128), writes=['fnw_b'])

        def rstd_from_ss(ss_ap, out_ap, n, rkeys, wkeys, tmp_ap, big=False):
            kb.act(tmp_ap, ss_ap, AF.Ln, rkeys, [('tmpln',)], scale=1.0 / n, bias=EPS)
            kb.relax = big
            kb.act(out_ap, tmp_ap, AF.Exp, [('tmpln',)], wkeys, scale=-0.5)
            kb.relax = False

        def adaln(L, A_b, B_b, stack):
            with contextlib.ExitStack() as s2:
                adaw = [sb(s2, "adaw%d" % i, [128, 8, 256]) for i in range(2)]
                modrow = sb(s2, "modrow", [1, 3 * D])
                nwrow = sb(s2, "nwrow", [1, D])
                arow = sb(s2, "arow", [1, D])
                kb.dma(modrow[:], adab_in[L:L + 1, :], writes=[('modrow', i) for i in range(6)])
                kb.dma(nwrow[:], normw_in[L:L + 1, :], writes=['nwrow'])
                wv = adaw_in[L].rearrange("(kc p) f -> p kc f", p=128)
                for fb in range(12):
                    b = fb % 2
                    kb.dma(adaw[b][:], wv[:, :, fb * 256:(fb + 1) * 256], writes=[('adaw', b)])
                    bank = fb % 2
                    kb.newgen(bank)
                    for kc in range(8):
                        kb.mm(bank, ps[bank][0:1, 0:256], condT[:, kc:kc + 1], adaw[b][:, kc, :],
                              ['condT', ('adaw', b)], PK(bank), halves=(0,), last=(kc == 7))
                    kb.tt('dve', modrow[0:1, fb * 256:(fb + 1) * 256], ps[bank][0:1, 0:256], modrow[0:1, fb * 256:(fb + 1) * 256],
                          ALU.add, PK(bank) + [('modrow', fb // 2)], [('modrow', fb // 2)])
                kb.stt(arow[0:1, :], modrow[0:1, D:2 * D], 1.0, nwrow[0:1, :], ALU.add, ALU.mult,
                       [('modrow', 2), ('modrow', 3), 'nwrow'], ['arow'])
                srcs = [(arow[0:1, :], A_b, ['arow'], 'A_b'), (modrow[0:1, 0:D], B_b, [('modrow', 0), ('modrow', 1)], 'B_b'),
                        (modrow[0:1, 2 * D:3 * D], gate_b, [('modrow', 4), ('modrow', 5)], 'gate_b')]
                n = 0
                for (src, dst, rk, dname) in srcs:
                    for half in range(2):
                        bank = 2 + (n % 2)
                        n += 1
                        kb.newgen(bank)
                        kb.mm(bank, ps[bank][:, :], ones_row[0:1, :], src[0:1, half * 512:(half + 1) * 512],
                              rk + ['ones_row'], PK(bank))
                        kb.copy('act', dst[:, half * 512:(half + 1) * 512], ps[bank][:, :], PK(bank), [(dname, half)])
                kb.barrier()

        def norm_phase(L, xsrc, xkey, hT, A_b, B_b, stack):
            hn = sb(stack, "hn", [128, D])
            hb = [sb(stack, "hb%d" % i, [128, D], BF16) for i in range(2)]

            def st1(t):
                b = t % 2
                kb.dma(xt[b][:], xsrc[t * 128:(t + 1) * 128, :], reads=[(xkey, t)], writes=[('xt', b)])
                kb.act(junk[:], xt[b][:], AF.Square, [('xt', b)], ['junk'])
                kb.op('dve', lambda g: g.tensor_reduce(out=sm[:, 0:1], in_=junk[:], axis=AX.X, op=ALU.add), ['junk'], [('sm', 0)])
                rstd_from_ss(sm[:, 0:1], sm[:, 2:3], D, [('sm', 0)], [('sm', 2)], sm[:, 1:2])
                kb.stt(hn[:], xt[b][:], sm[:, 2:3], A_b[:], ALU.mult, ALU.mult, [('xt', b), ('sm', 2), ('A_b', 0), ('A_b', 1)], ['hn'])
                kb.tt('pool', hb[b][:], hn[:], B_b[:], ALU.add, ['hn', ('B_b', 0), ('B_b', 1)], [('hb', b)])

            def st2(t):
                b = t % 2
                bank = 4 + b
                for kc in range(8):
                    kb.tr(psbf(bank)[:, kc * 128:(kc + 1) * 128], hb[b][:, kc * 128:(kc + 1) * 128], ident_bf[:],
                          [('hb', b), 'ident_bf'], PK(bank))
                kb.copy('act', hT[:, :, t * 128:(t + 1) * 128], psbf(bank).rearrange("p (k t) -> p k t", k=8),
                        PK(bank), [('hT', t // 4)])

            st1(0)
            for t in range(NT):
                if t + 1 < NT:
                    st1(t + 1)
                st2(t)

        def outproj_tile(L, t, ogT, KC, wout_bf, xsrc, xkey, last_layer, ykeys):
            b = t % 2
            kb.dma(xt[b][:], xsrc[t * 128:(t + 1) * 128, :], reads=[(xkey, t)], writes=[('xt', b)])
            for fb in range(2):
                bank = 6 + fb
                kb.newgen(bank)
                for kc in range(KC):
                    kb.mm(bank, ps[bank][:, :], ogT[:, kc, :], wout_bf[:, kc, fb * 512:(fb + 1) * 512],
                          ykeys + ['wout_bf'], PK(bank), last=(kc == KC - 1))
                kb.tt('dve', junk[:, fb * 512:(fb + 1) * 512], ps[bank][:, :], gate_b[:, fb * 512:(fb + 1) * 512], ALU.mult,
                      PK(bank) + [('gate_b', fb)], [('junkh', fb)])
                kb.tt('pool', xt[b][:, fb * 512:(fb + 1) * 512], junk[:, fb * 512:(fb + 1) * 512], xt[b][:, fb * 512:(fb + 1) * 512],
                      ALU.add, [('junkh', fb), ('xt', b)], [('xt', b)])
            if not last_layer:
                kb.dma(xres[t * 128:(t + 1) * 128, :], xt[b][:], reads=[('xt', b)], writes=[('xres', t)])
            else:
                kb.act(junk[:], xt[b][:], AF.Square, [('xt', b)], [('junkh', 0), ('junkh', 1)])
                kb.op('dve', lambda g: g.tensor_reduce(out=sm[:, 4:5], in_=junk[:], axis=AX.X, op=ALU.add),
                      [('junkh', 0), ('junkh', 1)], [('sm', 4)])
                rstd_from_ss(sm[:, 4:5], sm[:, 6:7], D, [('sm', 4)], [('sm', 6)], sm[:, 5:6])
                kb.stt(xt[b][:], xt[b][:], sm[:, 6:7], fnw_b[:], ALU.mult, ALU.mult, [('xt', b), ('sm', 6), 'fnw_b'], [('xt', b)])
                kb.dma(out_d[t * 128:(t + 1) * 128, :], xt[b][:], reads=[('xt', b)], writes=[('out', t)])

        def load_wout(wout_dram, KC, wout_bf, stack):
            with contextlib.ExitStack() as s2:
                wst = [sb(s2, "wost%d" % i, [128, D]) for i in range(2)]
                wv = wout_dram.rearrange("(kc p) f -> p kc f", p=128)
                for g in range(KC):
                    b = g % 2
                    kb.dma(wst[b][:], wv[:, g, :], writes=[('wost', b)])
                    kb.copy('pool', wout_bf[:, g, :], wst[b][:], [('wost', b)], ['wout_bf'])
                kb.barrier()

        def proj_fm(w2d, col0, nch, modes, hT, scratch, stack, convw=None, pp_scalars=None, nred=128, ones_ap=None):
            with contextlib.ExitStack() as s2:
                wst = [sb(s2, "wst%d" % i, [128, 8, 256]) for i in range(2)]
                wbf = [sb(s2, "wbf%d" % i, [128, 8, 256], BF16) for i in range(2)]
                NBUF = 6 if any(m.startswith('conv') for m in modes) else 3
                obuf = [sb(s2, "obuf%d" % i, [128, 512], BF16) for i in range(NBUF)]
                pre = [sb(s2, "pre%d" % i, [128, 515]) for i in range(3)] if any(m.startswith('conv') for m in modes) else None
                acc = [sb(s2, "acc%d" % i, [128, 512]) for i in range(NBUF)]
                sq2 = [sb(s2, "sq2%d" % i, [128, 512]) for i in range(NBUF)]
                lnv = sb(s2, "lnv", [128, 512])
                rn = [sb(s2, "rn%d" % i, [128, 512]) for i in range(2)]
                wv = w2d.rearrange("(kc p) f -> p kc f", p=128)
                nslab = (nch + 1) // 2
                blocks = [(c, tb) for c in range(nch) for tb in range(8)]
                NB = len(blocks)

                def load_slab(sl):
                    sbf = sl % 2
                    ncs = min(2, nch - sl * 2)
                    kb.dma(wst[sbf][:, :, 0:ncs * 128], wv[:, :, col0 + sl * 256: col0 + sl * 256 + ncs * 128], writes=[('wst', sbf)])
                    kb.copy('pool', wbf[sbf][:, :, 0:ncs * 128], wst[sbf][:, :, 0:ncs * 128], [('wst', sbf)], [('wbf', sbf)])

                def stageA(n):
                    c, tb = blocks[n]
                    sl, ci = c // 2, c % 2
                    sbf = sl % 2
                    if ci == 0 and tb == 0 and sl + 1 < nslab:
                        load_slab(sl + 1)
                    bank = n % 4
                    kb.newgen(bank)
                    for kc in range(8):
                        kb.mm(bank, ps[bank][:, :], wbf[sbf][:, kc, ci * 128:(ci + 1) * 128], hT[:, kc, tb * 512:(tb + 1) * 512],
                              [('wbf', sbf), ('hT', tb)], PK(bank), last=(kc == 7))

                def out_dma(n):
                    c, tb = blocks[n]
                    ob = n % NBUF
                    kb.dma(scratch[c][:, tb * 512:(tb + 1) * 512], obuf[ob][:], reads=[('obuf', ob)], writes=[('scr', c, tb)])

                def stageB1(n):
                    c, tb = blocks[n]
                    mode = modes[c]
                    bank = n % 4
                    ob = n % NBUF
                    osl = obuf[ob][:, :]
                    okey = [('obuf', ob)]
                    a = acc[n % NBUF]
                    ak = ('acc', n % NBUF)
                    q2 = sq2[n % NBUF]
                    qk = ('sq2', n % NBUF)
                    if mode == 'silu':
                        kb.act(osl, ps[bank][:, :], AF.Silu, PK(bank), okey)
                        out_dma(n)
                    elif mode == 'rms':
                        kb.copy('act', a[:], ps[bank][:, :], PK(bank), [ak])
                        kb.tt('pool', q2[:], a[:], a[:], ALU.mult, [ak], [qk])
                    else:
                        p = pre[n % 3]
                        pk = ('pre', n % 3)
                        pprev = pre[(n - 1) % 3]
                        pkprev = ('pre', (n - 1) % 3)
                        kb.copy('act', p[:, 3:515], ps[bank][:, :], PK(bank), [pk])
                        kb.act(a[:], ps[bank][:, :], AF.Identity, PK(bank) + ['convw'], [ak], scale=convw[:, c, 3:4])
                        if tb == 0:
                            kb.memset('pool', p[:, 0:3], 0.0, [pk])
                        else:
                            kb.copy('pool', p[:, 0:3], pprev[:, 512:515], [pkprev], [pk])
                        for jj in (2, 1, 0):
                            kb.relax = (jj != 2)
                            kb.stt(a[:], p[:, jj:jj + 512], convw[:, c, jj:jj + 1], a[:], ALU.mult, ALU.add, [pk, 'convw', ak], [ak])
                            kb.relax = False

                def stageB1b(n):
                    c, tb = blocks[n]
                    mode = modes[c]
                    if not mode.startswith('conv'):
                        return
                    ob = n % NBUF
                    osl = obuf[ob][:, :]
                    okey = [('obuf', ob)]
                    a = acc[n % NBUF]
                    ak = ('acc', n % NBUF)
                    q2 = sq2[n % NBUF]
                    qk = ('sq2', n % NBUF)
                    if mode == 'conv_v':
                        kb.act(osl, a[:], AF.Silu, [ak], okey)
                        out_dma(n)
                    else:
                        kb.act(a[:], a[:], AF.Silu, [ak], [ak])
                        kb.tt('pool', q2[:], a[:], a[:], ALU.mult, [ak], [qk])

                def stageB2(n):
                    c, tb = blocks[n]
                    mode = modes[c]
                    if mode in ('silu', 'conv_v'):
                        return
                    ob = n % NBUF
                    osl = obuf[ob][:, :]
                    okey = [('obuf', ob)]
                    a = acc[n % NBUF]
                    ak = ('acc', n % NBUF)
                    q2 = sq2[n % NBUF]
                    qk = ('sq2', n % NBUF)
                    nb = 4 + n % 2
                    r = rn[n % 2]
                    rk = ('rn', n % 2)
                    kb.newgen(nb)
                    if mode == 'rms':
                        kb.mm(nb, ps[nb][:, :], ones_ap, q2[:], [qk, 'cst'], PK(nb))
                        rstd_from_ss(ps[nb][:, :], r[:], nred, PK(nb), [rk], lnv[:], big=True)
                        kb.stt(osl, a[:], pp_scalars[c], r[:], ALU.mult, ALU.mult, [ak, rk, 'ppsc'], okey)
                    else:
                        kb.mm(nb, ps[nb][:, :], C(C_ONES), q2[:], [qk, 'cst'], PK(nb))
                        kb.act(lnv[:], ps[nb][:, :], AF.Ln, PK(nb), [('tmpln',)], bias=EPS)
                        kb.relax = True
                        kb.act(r[:], lnv[:], AF.Exp, [('tmpln',)], [rk], scale=-0.5)
                        kb.relax = False
                        sc = (128.0 ** -0.5) if mode == 'conv_q' else 1.0
                        kb.stt(osl, a[:], sc, r[:], ALU.mult, ALU.mult, [ak, rk], okey)
                    out_dma(n)

                load_slab(0)
                for n in range(NB + 8):
                    if n < NB:
                        stageA(n)
                    if NBUF == 6:
                        if n >= 7 and (n - 7) % 4 == 0:
                            for m in range(n - 7, n - 3):
                                if 0 <= m < NB:
                                    stageB2(m)
                    elif 0 <= n - 3 < NB:
                        stageB2(n - 3)
                    if 0 <= n - 2 < NB:
                        stageB1b(n - 2)
                    if 0 <= n - 1 < NB:
                        stageB1(n - 1)
                kb.barrier()

        def gdn_layer(L, j, xsrc, xkey, last_layer):
            with contextlib.ExitStack() as sl:
                G = sb(sl, "G", [128, NT, 6, 16])
                glb = sb(sl, "glb", [128, NT, 2, 16])
                with contextlib.ExitStack() as s1:
                    A_b = sb(s1, "A_b", [128, D])
                    B_b = sb(s1, "B_b", [128, D])
                    adaln(L, A_b, B_b, s1)
                    hT = sb(s1, "hT", [128, 8, S], BF16)
                    with contextlib.ExitStack() as s2:
                        norm_phase(L, xsrc, xkey, hT, A_b, B_b, s2)
                        kb.barrier()
                    if dbg and L == 0:
                        kb.dma(dbg_d['hT'], hT[:], reads=[('hT', i) for i in range(8)], writes=['dbg_hT'])
                    if stop == 'norm':
                        kb.barrier()
                        return True
                    with contextlib.ExitStack() as s2:
                        wbast = sb(s2, "wbast", [128, 8, 32])
                        wba = sb(s2, "wba", [128, 8, 32], BF16)
                        dtb = sb(s2, "dtb", [128, 16])
                        negA = sb(s2, "negA", [128, 16])
                        gt = sb(s2, "gt", [128, 8, 16])
                        kb.dma(wbast[:], a_win[j].rearrange("(kc p) f -> p kc f", p=128)[:, :, 6144:6176], writes=['wbast'])
                        kb.copy('dve', wba[:], wbast[:], ['wbast'], ['wba'])
                        kb.dma(dtb[:], a_dtb[j:j + 1, :].partition_broadcast(128), writes=['dtb'])
                        kb.dma(negA[:], a_Alog[j:j + 1, :].partition_broadcast(128), writes=['negA'])
                        kb.act(negA[:], negA[:], AF.Exp, ['negA'], ['negA'])
                        kb.ts('dve', negA[:], negA[:], -1.0, None, ALU.mult, None, ['negA'], ['negA'])
                        for t in range(NT):
                            bank = t % 2
                            kb.newgen(bank)
                            for kc in range(8):
                                kb.mm(bank, ps[bank][:, 0:32], hT[:, kc, t * 128:(t + 1) * 128], wba[:, kc, :],
                                      [('hT', t // 4), 'wba'], PK(bank), last=(kc == 7))
                            gk = ('G', t)
                            kb.tt('dve', gt[:, 0, :], ps[bank][:, 16:32], dtb[:], ALU.add, PK(bank) + ['dtb'], ['gt0'])
                            kb.act(gt[:, 0, :], gt[:, 0, :], AF.Exp, ['gt0'], ['gt0'])
                            kb.act(gt[:, 0, :], gt[:, 0, :], AF.Ln, ['gt0'], ['gt0'], bias=1.0)
                            kb.tt('dve', G[:, t, 0, :], gt[:, 0, :], negA[:], ALU.mult, ['gt0', 'negA'], [gk])
                            kb.act(gt[:, 1, :], ps[bank][:, 0:16], AF.Exp, PK(bank), ['gt1'], scale=-1.0)
                            kb.act(gt[:, 1, :], gt[:, 1, :], AF.Ln, ['gt1'], ['gt1'], bias=1.0)
                            kb.act(G[:, t, 2, :], gt[:, 1, :], AF.Exp, ['gt1'], [gk], scale=-1.0)
                            kb.ts('dve', G[:, t, 1, :], gt[:, 1, :], -1.0, None, ALU.mult, None, ['gt1'], [gk])
                            b2 = 2 + t % 2
                            kb.newgen(b2)
                            kb.mm(b2, ps[b2][:, 0:16], C(C_TRIBD), G[:, t, 0, :], ['cst', gk], PK(b2))
                            kb.copy('dve', G[:, t, 3, :], ps[b2][:, 0:16], PK(b2), [gk])
                            kb.mm(b2, ps[b2][:, 16:32], C(C_SELC), G[:, t, 3, :], ['cst', gk], PK(b2))
                            kb.mm(b2, ps[b2][:, 32:48], C(C_SELA), G[:, t, 3, :], ['cst', gk], PK(b2))
                            kb.mm(b2, ps[b2][:, 48:64], C(C_SELB), G[:, t, 3, :], ['cst', gk], PK(b2))
                            kb.tt('dve', gt[:, 2, :], ps[b2][:, 16:32], G[:, t, 3, :], ALU.subtract, PK(b2) + [gk], ['gt2'])
                            kb.act(G[:, t, 5, :], gt[:, 2, :], AF.Exp, ['gt2'], [gk])
                            kb.act(glb[:, t, :, :], ps[b2][:, 32:64].rearrange("p (c h) -> p c h", c=2), AF.Exp, PK(b2), [('glb', t)])
                            kb.act(gt[:, 3, :], G[:, t, 3, :], AF.Exp, [gk], ['gt3'])
                            kb.tt('dve', G[:, t, 4, :], gt[:, 3, :], G[:, t, 2, :], ALU.mult, ['gt3', gk], [gk])
                        kb.barrier()
                    if dbg and L == 0:
                        kb.dma(dbg_d['gates'], G[:], reads=[('G', t) for t in range(NT)], writes=['dbg_gates'])
                    if stop == 'gates':
                        kb.barrier()
                        return True
                    with contextlib.ExitStack() as s2:
                        convw = sb(s2, "convw", [128, 32, 4])
                        kb.dma(convw[:], a_convT[j], writes=['convw'])
                        modes = ['conv_q'] * 8 + ['conv_k'] * 8 + ['conv_v'] * 16 + ['silu'] * 16
                        proj_fm(a_win[j], 0, 48, modes, hT, qkvz, s2, convw=convw)
                    kb.barrier()
                if stop == 'proj':
                    return True
                with contextlib.ExitStack() as s1:
                    wout_bf = sb(s1, "wout_bf", [128, 16, D], BF16)
                    load_wout(a_wout[j], 16, wout_bf, s1)
                    rr = gdn_tiles(L, j, G, glb, wout_bf, xsrc, xkey, last_layer, s1)
                    kb.barrier()
                    return rr

        def gdn_tiles(L, j, G, glb, wout_bf, xsrc, xkey, last_layer, st):
            HG = 8
            Sf = sb(st, "Sf", [128, 16, 128])
            Sb = sb(st, "Sb", [128, 16, 128], BF16)
            nw = sb(st, "nw", [128, 1])
            maskbf = sb(st, "maskbf", [128, 384], BF16)
            qT = [sb(st, "qT%d" % i, [128, 8, 128], BF16) for i in range(2)]
            kT = [sb(st, "kT%d" % i, [128, 8, 128], BF16) for i in range(2)]
            vT = [sb(st, "vT%d" % i, [128, 16, 128], BF16) for i in range(2)]
            zs = [sb(st, "zs%d" % i, [128, 16, 128], BF16) for i in range(2)]
            AA = [sb(st, "AA%d" % i, [128, 256]) for i in range(2)]
            Zg = sb(st, "Zg", [128, HG * 128])
            gam8 = sb(st, "gam8", [128, HG * 128])
            E3 = [sb(st, "E3%d" % i, [128, 384]) for i in range(2)]
            XY = sb(st, "XY", [128, HG, 2, 128])
            Pm = sb(st, "Pm", [128, HG, 128])
            vb = sb(st, "vb", [128, HG, 128], BF16)
            kbg = sb(st, "kbg", [128, HG, 128], BF16)
            TT = sb(st, "TT", [128, HG, 128], BF16)
            attnT = [sb(st, "attnT%d" % i, [128, HG, 128], BF16) for i in range(2)]
            kdec = [sb(st, "kdec%d" % i, [128, HG, 128], BF16) for i in range(2)]
            qdT = [sb(st, "qdT%d" % i, [128, HG, 128], BF16) for i in range(2)]
            usb = [sb(st, "usb%d" % i, [128, HG, 128]) for i in range(2)]
            wTb = [sb(st, "wTb%d" % i, [128, HG, 128], BF16) for i in range(2)]
            vnew = sb(st, "vnew", [128, HG, 128], BF16)
            oT = sb(st, "oT", [128, HG, 128])
            osq = sb(st, "osq", [128, 512])
            rst = sb(st, "rst", [128, 512])
            lnt = sb(st, "lnt", [128, 512])
            ogT = [sb(st, "ogT%d" % i, [128, 16, 128], BF16) for i in range(2)]
            kb.memset('dve', Sf[:], 0.0, [('Sf', h) for h in range(16)])
            kb.memset('pool', Sb[:], 0.0, [('Sb', h) for h in range(16)])
            kb.dma(nw[:], a_nw[j], writes=['nw'])
            kb.copy('dve', maskbf[:, 0:128], C(C_MINCLT), ['cst'], ['maskbf'])
            kb.copy('dve', maskbf[:, 128:256], C(C_MSTRT), ['cst'], ['maskbf'])
            kb.copy('dve', maskbf[:, 256:384], C(C_MSTR), ['cst'], ['maskbf'])
            qv = qkvz[0:8].rearrange("c p t -> p c t")
            kv = qkvz[8:16].rearrange("c p t -> p c t")
            vv = qkvz[16:32].rearrange("c p t -> p c t")
            zv = qkvz[32:48].rearrange("c p t -> p c t")

            def load_qkv(t):
                b = t % 2
                tsl = slice(t * 128, (t + 1) * 128)
                kb.dma(qT[b][:], qv[:, :, tsl], writes=[('qT', b)])
                kb.dma(kT[b][:], kv[:, :, tsl], writes=[('kT', b)])
                kb.dma(vT[b][:], vv[:, :, tsl], writes=[('vT', b)])

            def load_zs(t):
                b = t % 2
                kb.dma(zs[b][:], zv[:, :, t * 128:(t + 1) * 128], writes=[('zs', b)])

            B_G, B_T = 0, 2
            B_D = (1, 3)

            def s1(t, hg, bf):
                b = t % 2
                gk = ('G', t)
                if hg == 0 and t + 1 < NT:
                    load_qkv(t + 1)
                def g_stage(hp):
                    kb.newgen(B_G)
                    kb.mm(B_G, ps[B_G][:, 0:128], kT[b][:, hp, :], kT[b][:, hp, :], [('kT', b)], PK(B_G), inc=False)
                    kb.mm(B_G, ps[B_G][:, 128:256], kT[b][:, hp, :], qT[b][:, hp, :], [('kT', b), ('qT', b)], PK(B_G))
                    ksl = hp % 2
                    kb.tr(psbf(B_T)[:, ksl * 128:(ksl + 1) * 128], kT[b][:, hp, :], ident_bf[:], [('kT', b), 'ident_bf'], PK(B_T))

                def alpha(hl):
                    h = hg * HG + hl
                    aa = AA[h % 2]
                    kb.ts('pool', aa[:, 0:128], C(C_UT), G[:, t, 0, h:h + 1], None, ALU.mult, None, ['cst', gk], [('Ag', h % 2)])
                    kb.stt(aa[:, 128:256], C(C_ID), G[:, t, 1, h:h + 1], aa[:, 0:128], ALU.mult, ALU.add, ['cst', gk, ('Ag', h % 2)], [('Agp', h % 2)])
                    db = B_D[h % 2]
                    kb.newgen(db)
                    dk = PK(db)
                    kb.mm(db, ps[db][:, 0:256], C(C_SL), aa[:, :], ['cst', ('Ag', h % 2), ('Agp', h % 2)], dk, last=False, inc=False)
                    kb.mm(db, ps[db][:, 0:256], ident_bf[:], maskbf[:, 0:256], ['ident_bf', 'maskbf'], dk, inc=False)
                    kb.mm(db, ps[db][:, 256:384], aa[:, 128:256], C(C_SL), ['cst', ('Agp', h % 2)], dk, last=False, inc=False)
                    kb.mm(db, ps[db][:, 256:384], ident_bf[:], maskbf[:, 256:384], ['ident_bf', 'maskbf'], dk)
                    vs = 2 + h % 2
                    kb.tr(psbf(B_T)[:, vs * 128:(vs + 1) * 128], vT[b][:, h, :], ident_bf[:], [('vT', b), 'ident_bf'], PK(B_T))

                def beta(hl):
                    h = hg * HG + hl
                    hp = h // 2
                    db = B_D[h % 2]
                    dk = PK(db)
                    Gps = ps[B_G][:, 0:128]
                    QKps = ps[B_G][:, 128:256]
                    ksl = hp % 2
                    psK = psbf(B_T)[:, ksl * 128:(ksl + 1) * 128]
                    vs = 2 + h % 2
                    psV = psbf(B_T)[:, vs * 128:(vs + 1) * 128]
                    e3 = E3[h % 2]
                    kb.act(e3[:], ps[db][:, 0:384], AF.Exp, dk, [('E3', h % 2)])
                    kb.tt('dve', XY[:, hl, 0, :], e3[:, 128:256], Gps, ALU.mult, [('E3', h % 2)] + PK(B_G), [('XY', hl)])
                    kb.tt('dve', XY[:, hl, 1, :], e3[:, 256:384], Gps, ALU.mult, [('E3', h % 2)] + PK(B_G), [('XY', hl)])
                    kb.tt('dve', attnT[bf][:, hl, :], e3[:, 0:128], QKps, ALU.mult, [('E3', h % 2)] + PK(B_G), [('attnT', bf, hl)])
                    kb.stt(Pm[:, hl, :], XY[:, hl, 0, :], -1.0, C(C_ID), ALU.mult, ALU.add, [('XY', hl), 'cst'], [('Pm', hl)])
                    kb.tt('pool', qdT[bf][:, hl, :], qT[b][:, hp, :], gam8[:, hl * 128:(hl + 1) * 128], ALU.mult, [('qT', b), ('gam8', hl // 4)], [('qdT', bf, hl)])
                    kb.act(vb[:, hl, :], psV, AF.Identity, PK(B_T) + [gk], [('vb', hl)], scale=G[:, t, 2, h:h + 1])
                    kb.act(kbg[:, hl, :], psK, AF.Identity, PK(B_T) + [gk], [('kbg', hl)], scale=G[:, t, 4, h:h + 1])
                    kb.ts('dve', kdec[bf][:, hl, :], psK, G[:, t, 5, h:h + 1], None, ALU.mult, None, PK(B_T) + [gk], [('kdec', bf, hl)])

                for hl in range(HG):
                    h = hg * HG + hl
                    kb.ts('pool', Zg[:, hl * 128:(hl + 1) * 128], C(C_ID), G[:, t, 3, h:h + 1], None, ALU.mult, None, ['cst', gk], [('Zg', hl // 4)])
                for i4 in range(2):
                    db = B_D[i4]
                    kb.newgen(db)
                    kb.mm(db, ps[db][:, :], C(C_ONES), Zg[:, i4 * 512:(i4 + 1) * 512], ['cst', ('Zg', i4)], PK(db))
                    kb.act(gam8[:, i4 * 512:(i4 + 1) * 512], ps[db][:, :], AF.Exp, PK(db), [('gam8', i4)])
                yield
                g_stage((hg * HG) // 2)
                alpha(0)
                for hl in range(HG):
                    if hl + 1 < HG and (hl + 1) % 2 == 1:
                        alpha(hl + 1)
                    beta(hl)
                    if hl + 1 < HG and (hl + 1) % 2 == 0:
                        g_stage((hg * HG + hl + 1) // 2)
                        alpha(hl + 1)
                    yield
                for lvl in range(NEU_LO, 6):
                    for pr in range(HG // 2):
                        bank = pr % 2
                        kb.newgen(bank)
                        for u_ in range(2):
                            hl = pr * 2 + u_
                            X = XY[:, hl, 0, :]
                            Y = XY[:, hl, 1, :]
                            o0 = u_ * 256
                            if lvl < 5:
                                kb.mm(bank, ps[bank][:, o0:o0 + 128], Y, X, [('XY', hl)], PK(bank), inc=False)
                            kb.mm(bank, ps[bank][:, o0 + 128:o0 + 256], X, Y, [('XY', hl)], PK(bank), inc=(u_ == 1))
                        if lvl < 5:
                            kb.copy('act', XY[:, pr * 2:pr * 2 + 2, :, :], ps[bank][:, :].rearrange("p (h x c) -> p h x c", h=2, x=2),
                                    PK(bank), [('XY', pr * 2), ('XY', pr * 2 + 1)])
                        else:
                            kb.copy('act', XY[:, pr * 2:pr * 2 + 2, 1, :], ps[bank][:, :].rearrange("p (h x c) -> p h x c", h=2, x=2)[:, :, 1, :],
                                    PK(bank), [('XY', pr * 2), ('XY', pr * 2 + 1)])
                        yield
                    for q4 in range(HG // 4):
                        bank = 2 + q4 % 2
                        kb.newgen(bank)
                        for u_ in range(4):
                            hl = q4 * 4 + u_
                            kb.mm(bank, ps[bank][:, u_ * 128:(u_ + 1) * 128], XY[:, hl, 1, :], Pm[:, hl, :], [('XY', hl), ('Pm', hl)], PK(bank), inc=(u_ == 3))
                        hs = slice(q4 * 4, q4 * 4 + 4)
                        pk = [('Pm', q4 * 4 + u_) for u_ in range(4)]
                        if lvl < 5:
                            kb.tt('dve', Pm[:, hs, :], Pm[:, hs, :], ps[bank][:, :].rearrange("p (h c) -> p h c", h=4), ALU.add, PK(bank) + pk, pk)
                        else:
                            kb.tt('dve', TT[:, hs, :], Pm[:, hs, :], ps[bank][:, :].rearrange("p (h c) -> p h c", h=4), ALU.add,
                                  PK(bank) + pk, [('TT', q4 * 4 + u_) for u_ in range(4)])
                        yield
                for pr in range(HG // 2):
                    bank = pr % 2
                    kb.newgen(bank)
                    for u_ in range(2):
                        hl = pr * 2 + u_
                        kb.mm(bank, ps[bank][:, u_ * 128:(u_ + 1) * 128], TT[:, hl, :], vb[:, hl, :], [('TT', hl), ('vb', hl)], PK(bank), inc=False)
                        kb.mm(bank, ps[bank][:, 256 + u_ * 128:256 + (u_ + 1) * 128], kbg[:, hl, :], TT[:, hl, :], [('TT', hl), ('kbg', hl)], PK(bank), inc=(u_ == 1))
                    kb.copy('act', usb[bf][:, pr * 2:pr * 2 + 2, :], ps[bank][:, 0:256].rearrange("p (h c) -> p h c", h=2), PK(bank),
                            [('usb', bf, pr * 2), ('usb', bf, pr * 2 + 1)])
                    kb.copy('dve', wTb[bf][:, pr * 2:pr * 2 + 2, :], ps[bank][:, 256:512].rearrange("p (h c) -> p h c", h=2), PK(bank),
                            [('wTb', bf, pr * 2), ('wTb', bf, pr * 2 + 1)])
                    yield

            B_WO = (4, 5)
            B_SS = (6, 7)
            B_N = 4

            def s2(t, hg, bf):
                b = t % 2
                if hg == 0 and t + 1 < NT:
                    load_zs(t + 1)
                for ch in range(2):
                    rs = slice(ch * 64, (ch + 1) * 64)
                    for q4 in range(HG // 4):
                        bw = B_WO[q4]
                        kb.newgen(bw)
                        for u_ in range(4):
                            hl = q4 * 4 + u_
                            h = hg * HG + hl
                            kb.mm(bw, ps[bw][rs, u_ * 128:(u_ + 1) * 128], wTb[bf][:, hl, rs], Sb[:, h, :], [('wTb', bf, hl), ('Sb', h)], PK(bw),
                                  halves=(ch,), inc=(u_ == 3))
                    for q4 in range(HG // 4):
                        bw = B_WO[q4]
                        hs = slice(q4 * 4, q4 * 4 + 4)
                        vk = [('vnew', q4 * 4 + u_) for u_ in range(4)]
                        kb.tt('dve', vnew[rs, hs, :], usb[bf][rs, hs, :], ps[bw][rs, :].rearrange("p (h c) -> p h c", h=4), ALU.subtract,
                              PK(bw) + [('usb', bf, q4 * 4 + u_) for u_ in range(4)], vk)
                    yield
                    for q4 in range(HG // 4):
                        bw = B_WO[q4]
                        bs = B_SS[q4]
                        kb.newgen(bw)
                        kb.newgen(bs)
                        for u_ in range(4):
                            hl = q4 * 4 + u_
                            h = hg * HG + hl
                            kb.mm(bw, ps[bw][:, u_ * 64:(u_ + 1) * 64], Sb[:, h, :], qdT[bf][:, hl, rs], [('Sb', h), ('qdT', bf, hl)], PK(bw), last=False, inc=False)
                            kb.mm(bw, ps[bw][:, u_ * 64:(u_ + 1) * 64], vnew[rs, hl, :], attnT[bf][rs, hl, rs], [('vnew', hl), ('attnT', bf, hl)], PK(bw), inc=(u_ == 3))
                        for u_ in range(4):
                            hl = q4 * 4 + u_
                            kb.mm(bs, ps[bs][:, u_ * 128:(u_ + 1) * 128], kdec[bf][rs, hl, :], vnew[rs, hl, :], [('kdec', bf, hl), ('vnew', hl)], PK(bs), inc=(u_ == 3))
                    yield
                    for q4 in range(HG // 4):
                        bw = B_WO[q4]
                        bs = B_SS[q4]
                        hs = slice(q4 * 4, q4 * 4 + 4)
                        kb.copy('act', oT[:, hs, rs], ps[bw][:, 0:256].rearrange("p (h c) -> p h c", h=4), PK(bw),
                                [('oT', q4 * 4 + u_) for u_ in range(4)])
                        for u_ in range(4):
                            hl = q4 * 4 + u_
                            h = hg * HG + hl
                            kb.stt(Sf[:, h, :], Sf[:, h, :], glb[:, t, ch, h:h + 1], ps[bs][:, u_ * 128:(u_ + 1) * 128], ALU.mult, ALU.add,
                                   [('Sf', h), ('glb', t)] + PK(bs), [('Sf', h)])
                        h0 = hg * HG + q4 * 4
                        kb.copy('act', Sb[:, h0:h0 + 4, :], Sf[:, h0:h0 + 4, :], [('Sf', h0 + u_) for u_ in range(4)], [('Sb', h0 + u_) for u_ in range(4)])
                    yield
                for q4 in range(HG // 4):
                    hs = slice(q4 * 4, q4 * 4 + 4)
                    h0 = hg * HG + q4 * 4
                    ok = [('oT', q4 * 4 + u_) for u_ in range(4)]
                    kb.act(osq[:].rearrange("p (h c) -> p h c", h=4), oT[:, hs, :], AF.Square, ok, ['osq'])
                    kb.newgen(B_N)
                    kb.mm(B_N, ps[B_N][:, :], C(C_ONES), osq[:], ['cst', 'osq'], PK(B_N))
                    rstd_from_ss(ps[B_N][:, :], rst[:], 128, PK(B_N), ['rst'], lnt[:], big=True)
                    kb.tt('dve', osq[:].rearrange("p (h c) -> p h c", h=4), oT[:, hs, :], rst[:].rearrange("p (h c) -> p h c", h=4), ALU.mult,
                          ok + ['rst', 'osq'], ['osq'])
                    kb.stt(ogT[b][:, h0:h0 + 4, :], osq[:].rearrange("p (h c) -> p h c", h=4), nw[:, 0:1], zs[b][:, h0:h0 + 4, :], ALU.mult, ALU.mult,
                           ['osq', 'nw', ('zs', b)], [('ogT', b)])
                    yield
                if hg == 1:
                    outproj_tile(L, t, ogT[b], 16, wout_bf, xsrc, xkey, last_layer, [('ogT', b)])
                    yield

            steps = [(t, hg) for t in range(NT) for hg in range(2)]
            load_qkv(0)
            load_zs(0)
            for _ in s1(0, 0, 0):
                pass
            RATIO = GRATIO
            for k in range(len(steps)):
                g1 = s1(steps[k + 1][0], steps[k + 1][1], (k + 1) % 2) if k + 1 < len(steps) else None
                g2 = s2(steps[k][0], steps[k][1], k % 2)
                while g1 is not None or g2 is not None:
                    if g1 is not None:
                        for _ in range(RATIO):
                            try:
                                next(g1)
                            except StopIteration:
                                g1 = None
                                break
                    if g2 is not None:
                        try:
                            next(g2)
                        except StopIteration:
                            g2 = None

        def fox_layer(L, j, xsrc, xkey, last_layer):
            with contextlib.ExitStack() as sl:
                Vall = sb(sl, "Vall", [128, NT, 16, 65], BF16)
                cumT = sb(sl, "cumT", [128, NT, 16])
                with contextlib.ExitStack() as s1:
                    hT = sb(s1, "hT", [128, 8, S], BF16)
                    with contextlib.ExitStack() as s2:
                        A_b = sb(s2, "A_b", [128, D])
                        B_b = sb(s2, "B_b", [128, D])
                        adaln(L, A_b, B_b, s2)
                        norm_phase(L, xsrc, xkey, hT, A_b, B_b, s2)
                        kb.barrier()
                    with contextlib.ExitStack() as s2:
                        wfst = sb(s2, "wfst", [128, 8, 16])
                        wf = sb(s2, "wf", [128, 8, 16], BF16)
                        nfb = sb(s2, "nfb", [16, 1])
                        spl = sb(s2, "spl", [16, 2048])
                        cums = sb(s2, "cums", [16, S])
                        onesr = sb(s2, "onesr", [16, 2048], BF16)
                        c1b = sb(s2, "c1b", [16, S], BF16)
                        kb.dma(wfst[:], b_win[j].rearrange("(kc p) f -> p kc f", p=128)[:, :, 4096:4112], writes=['wfst'])
                        kb.copy('dve', wf[:], wfst[:], ['wfst'], ['wf'])
                        kb.dma(nfb[:], b_fb[j], writes=['nfb'])
                        kb.ts('dve', nfb[:], nfb[:], -1.0, None, ALU.mult, None, ['nfb'], ['nfb'])
                        kb.memset('pool', onesr[:], 1.0, ['onesr'])
                        for half in range(2):
                            for tl in range(4):
                                tb = half * 4 + tl
                                bank = tb % 2
                                kb.newgen(bank)
                                for kc in range(8):
                                    kb.mm(bank, ps[bank][0:16, :], wf[:, kc, :], hT[:, kc, tb * 512:(tb + 1) * 512], ['wf', ('hT', tb)], PK(bank),
                                          halves=(0,), last=(kc == 7))
                                kb.act(spl[:, tl * 512:(tl + 1) * 512], ps[bank][0:16, :], AF.Exp, PK(bank) + ['nfb'], [('spl', tl)], scale=-1.0, bias=nfb[:, 0:1])
                                kb.act(spl[:, tl * 512:(tl + 1) * 512], spl[:, tl * 512:(tl + 1) * 512], AF.Ln, [('spl', tl)], [('spl', tl)], bias=1.0)
                            init = 0.0 if half == 0 else cums[:, 2047:2048]
                            kb.op('dve', lambda g: g.tensor_tensor_scan(out=cums[:, half * 2048:(half + 1) * 2048], data0=onesr[:], data1=spl[:], initial=init,
                                                                      op0=ALU.mult, op1=ALU.add),
                                  [('spl', tl) for tl in range(4)] + ['onesr', 'cums'], ['cums'])
                        kb.ts('dve', c1b[:], cums[:], -1.0, None, ALU.mult, None, ['cums'], ['c1b'])
                        kb.dma(c1s, c1b[:], reads=['c1b'], writes=['c1s'])
                        for t in range(NT):
                            bank = 2 + t % 2
                            kb.newgen(bank)
                            kb.mm(bank, ps[bank][:, 0:16], cums[:, t * 128:(t + 1) * 128], cst[0:16, C_ID, 0:16], ['cums', 'cst'], PK(bank))
                            kb.copy('dve', cumT[:, t, :], ps[bank][:, 0:16], PK(bank), [('cumT', t)])
                        kb.barrier()
                    if stop == 'f1':
                        return True
                    with contextlib.ExitStack() as s2:
                        qn = sb(s2, "qn", [128, 1])
                        kn = sb(s2, "kn", [128, 1])
                        kb.dma(qn[:], b_qn2[j], writes=['ppsc'])
                        kb.dma(kn[:], b_kn2[j], writes=['ppsc'])
                        kb.ts('dve', qn[:], qn[:], 0.125, None, ALU.mult, None, ['ppsc'], ['ppsc'])
                        proj_fm(b_win[j], 0, 16, ['rms'] * 16, hT, qks, s2, pp_scalars=[qn[:, 0:1]] * 8 + [kn[:, 0:1]] * 8, nred=64, ones_ap=C(C_ONESBD))
                    if stop == 'f2':
                        return True
                    with contextlib.ExitStack() as s2:
                        wst = [sb(s2, "wvst%d" % i, [128, 8, 128]) for i in range(2)]
                        wvz = sb(s2, "wvz", [128, 8, 2048], BF16)
                        zt = [sb(s2, "zt%d" % i, [128, D], BF16) for i in range(2)]
                        wv = b_win[j].rearrange("(kc p) f -> p kc f", p=128)
                        for g in range(16):
                            b = g % 2
                            kb.dma(wst[b][:], wv[:, :, 2048 + g * 128:2048 + (g + 1) * 128], writes=[('wvst', b)])
                            kb.copy('pool', wvz[:, :, g * 128:(g + 1) * 128], wst[b][:], [('wvst', b)], [('wvz', g // 4)])
                        kb.memset('dve', Vall[:, :, :, 64:65], 1.0, [('Vall1',)])
                        for t in range(NT):
                            for fb in range(4):
                                bank = (t * 4 + fb) % 4
                                kb.newgen(bank)
                                for kc in range(8):
                                    kb.mm(bank, ps[bank][:, :], hT[:, kc, t * 128:(t + 1) * 128], wvz[:, kc, fb * 512:(fb + 1) * 512],
                                          [('hT', t // 4), ('wvz', fb)], PK(bank), last=(kc == 7))
                                if fb < 2:
                                    kb.copy('dve', Vall[:, t, fb * 8:(fb + 1) * 8, 0:64], ps[bank][:, :].rearrange("p (h d) -> p h d", h=8), PK(bank), [('Vall', t)])
                                else:
                                    kb.act(zt[t % 2][:, (fb - 2) * 512:(fb - 1) * 512], ps[bank][:, :], AF.Silu, PK(bank), [('zt', t % 2)])
                            kb.dma(zss[t * 128:(t + 1) * 128, :], zt[t % 2][:], reads=[('zt', t % 2)], writes=[('zss', t)])
                        kb.barrier()
                if stop == 'f3':
                    return True
                with contextlib.ExitStack() as s1:
                    Oall = sb(s1, "Oall", [128, NT, D], BF16)
                    with contextlib.ExitStack() as s2:
                        QA = [sb(s2, "QA%d" % i, [65, S], BF16) for i in range(2)]
                        KA = [sb(s2, "KA%d" % i, [65, S], BF16) for i in range(2)]
                        PT = [sb(s2, "PT%d" % i, [128, 512], BF16) for i in range(4)]
                        rl = sb(s2, "rl", [128, 4])
                        for i in range(2):
                            kb.memset('dve', KA[i][64:65, :], 1.0, [('KA1', i)])

                        def load_head(h):
                            b = h % 2
                            r0 = (h % 2) * 64
                            kb.dma(QA[b][0:64, :], qks[h // 2][r0:r0 + 64, :], writes=[('QA', b)])
                            kb.dma(QA[b][64:65, :], c1s[h:h + 1, :], reads=['c1s'], writes=[('QA', b)])
                            kb.dma(KA[b][0:64, :], qks[8 + h // 2][r0:r0 + 64, :], writes=[('KA', b)])

                        load_head(0)
                        pairs = [(h, qb, kt) for h in range(16) for qb in range(8) for kt in range(4 * (qb + 1))]
                        NSB = 4
                        LA = 2

                        def stageA(n):
                            h, qb, kt = pairs[n]
                            b = h % 2
                            if qb == 0 and kt == 0 and h + 1 < 16:
                                load_head(h + 1)
                            i0 = max(0, kt - 4 * qb)
                            sbk = n % NSB
                            kb.newgen(sbk)
                            kb.mm(sbk, ps[sbk][:, i0 * 128:512], KA[b][0:65, kt * 128:(kt + 1) * 128], QA[b][0:65, qb * 512 + i0 * 128:(qb + 1) * 512],
                                  [('KA', b), ('KA1', b), ('QA', b)], PK(sbk))

                        def stageB(n):
                            h, qb, kt = pairs[n]
                            nkt = 4 * (qb + 1)
                            jd = kt - 4 * qb
                            i0 = max(0, jd)
                            sbk = n % NSB
                            pt = PT[n % 4]
                            ptk = ('PT', n % 4)
                            obk = 4 + (h * 8 + qb) % 2
                            if kt == 0:
                                kb.newgen(obk)
                            kb.act(pt[:, i0 * 128:512], ps[sbk][:, i0 * 128:512], AF.Exp, PK(sbk) + [('cumT', kt)], [ptk], bias=cumT[:, kt, h:h + 1])
                            if jd >= 0:
                                kb.tt('pool', pt[:, jd * 128:(jd + 1) * 128], pt[:, jd * 128:(jd + 1) * 128], caus_bf[:], ALU.mult, [ptk, 'caus_bf'], [ptk])
                            for i in range(i0, 4):
                                kb.mm(obk, ps[obk][:, i * 65:(i + 1) * 65], pt[:, i * 128:(i + 1) * 128], Vall[:, kt, h, :],
                                      [ptk, ('Vall', kt), ('Vall1',)], PK(obk), last=(kt == nkt - 1), inc=(i == 3))
                            if kt == nkt - 1:
                                ov = ps[obk][:, 0:260].rearrange("p (i d) -> p i d", i=4)
                                kb.op('dve', lambda g: g.reciprocal(out=rl[:], in_=ov[:, :, 64]), PK(obk), ['rl'])
                                kb.tt('dve', Oall[:, qb * 4:(qb + 1) * 4, h * 64:(h + 1) * 64], ov[:, :, 0:64], rl[:].unsqueeze(2).broadcast_to([128, 4, 64]),
                                      ALU.mult, PK(obk) + ['rl'], [('Oall', qb)])

                        for n in range(len(pairs) + LA):
                            if n < len(pairs):
                                stageA(n)
                            if n - LA >= 0:
                                stageB(n - LA)
                        kb.barrier()
                    if stop == 'f4':
                        return True
                    with contextlib.ExitStack() as s2:
                        wout_bf = sb(s2, "wout_bf", [128, 8, D], BF16)
                        load_wout(b_wout[j], 8, wout_bf, s2)
                        zt = [sb(s2, "zt%d" % i, [128, D], BF16) for i in range(2)]
                        og = [sb(s2, "og%d" % i, [128, D], BF16) for i in range(2)]
                        ogT = [sb(s2, "ogT%d" % i, [128, 8, 128], BF16) for i in range(2)]
                        def fin1(t):
                            b = t % 2
                            kb.dma(zt[b][:], zss[t * 128:(t + 1) * 128, :], reads=[('zss', t)], writes=[('zt', b)])
                            kb.tt('pool', og[b][:], Oall[:, t, :], zt[b][:], ALU.mult, [('Oall', t // 4), ('zt', b)], [('og', b)])
                            bank = 4 + b
                            for kc in range(8):
                                kb.tr(psbf(bank)[:, kc * 128:(kc + 1) * 128], og[b][:, kc * 128:(kc + 1) * 128], ident_bf[:], [('og', b), 'ident_bf'], PK(bank))
                            kb.copy('act', ogT[b][:], psbf(bank).rearrange("p (k t) -> p k t", k=8), PK(bank), [('ogT', b)])

                        fin1(0)
                        for t in range(NT):
                            if t + 1 < NT:
                                fin1(t + 1)
                            outproj_tile(L, t, ogT[t % 2], 8, wout_bf, xsrc, xkey, last_layer, [('ogT', t % 2)])
                        kb.barrier()

        xsrc, xkey = x_in, 'xin'
        for L in range(n_layers):
            last = (L == n_layers - 1)
            if L % 2 == 0:
                stopped = gdn_layer(L, L // 2, xsrc, xkey, last)
            else:
                stopped = fox_layer(L, L // 2, xsrc, xkey, last)
            xsrc, xkey = xres, 'xres'
            kb.barrier()
            if stopped:
                break
        if dbg:
            kb.dma(dbg_d['xres'], xres, reads=[('xres', t) for t in range(NT)], writes=['dbg_x'])
            kb.dma(dbg_d['qkvz'], qkvz, writes=['dbg_q'])
        kb.barrier()
        print("instructions emitted:", kb.ninst, {k: v for k, v in kb.cnt.items()})
    return nc


def make_in_maps(inputs):
    consts = make_consts()
    f = lambda a: np.ascontiguousarray(np.asarray(a, dtype=np.float32))
    x = f(inputs["x"])
    c = f(inputs["c"])
    shared = {
        "norm_w": f(inputs["norm_w"]),
        "final_norm_w": f(inputs["final_norm_w"]).reshape(1, D),
        "ada_w": f(inputs["ada_w"]),
        "ada_b": f(inputs["ada_b"]),
        "a_w_in": f(inputs["a_w_in"]),
        "a_convT": f(np.transpose(f(inputs["a_conv_w"]), (0, 2, 1)).reshape(2, 32, 128, 4).transpose(0, 2, 1, 3)),
        "a_A_log": f(inputs["a_A_log"]),
        "a_dt_bias": f(inputs["a_dt_bias"]),
        "a_norm_w": f(inputs["a_norm_w"]).reshape(2, 128, 1),
        "a_w_out": f(inputs["a_w_out"]),
        "b_w_in": f(inputs["b_w_in"]),
        "b_f_bias": f(inputs["b_f_bias"]).reshape(2, 16, 1),
        "b_qn2": f(np.tile(f(inputs["b_qn_w"]), (1, 2))).reshape(2, 128, 1),
        "b_kn2": f(np.tile(f(inputs["b_kn_w"]), (1, 2))).reshape(2, 128, 1),
        "b_w_out": f(inputs["b_w_out"]),
        "consts": consts,
    }
    maps = []
    for b in range(8):
        m = dict(shared)
        m["x"] = x[b]
        m["cT"] = f(c[b].reshape(8, 128).T)
        maps.append(m)
    return maps


_NC_CACHE = {}


def kernel(**inputs):
    if 'nc' not in _NC_CACHE:
        _NC_CACHE['nc'] = build_program()
    nc = _NC_CACHE['nc']
    in_maps = make_in_maps(inputs)
    res = run_bass_kernel_spmd(nc, in_maps, core_ids=list(range(8)))
    out = np.stack([np.asarray(r["out"], dtype=np.float32) for r in res.results], axis=0)
    return out
```
